# Optimizing a Trainium2 kernel written in Bass

```python
import math
import jax, jax.numpy as jnp
from jax import lax
import numpy as np

D_MODEL = 1024
BATCH = 1
SEQ = 16384
DEPTH = 1

CHUNK = 64
Q_BLOCK = 128
PLE_DIM = 256
ROPE_THETA = 10000.0
DA_HEADS = 4
DA_QK_DIM = 64
DA_V_DIM = 128
DA_QK_W = DA_HEADS * 2 * DA_QK_DIM
DA_V_W = DA_HEADS * DA_V_DIM
ML_HEADS = 4
ML_DIM = 128
ML_W = ML_HEADS * ML_DIM
ML_CONV = 4
D_FF = 2816
FF_CONV = 3
EPS = 1e-6
IN_SIZES = (DA_QK_W, DA_QK_W, DA_V_W, 2 * ML_W, ML_W, ML_W, 2 * ML_HEADS, D_MODEL, D_MODEL)
N_IN = DA_QK_W * 2 + DA_V_W + 2 * ML_W + ML_W + ML_W + 2 * ML_HEADS + 2 * D_MODEL

kernel_name = "diffattn_mlstm_gated_hybrid_block"


def rmsnorm(x, g):
    xf = x.astype(jnp.float32)
    y = xf * lax.rsqrt(jnp.mean(xf * xf, axis=-1, keepdims=True) + EPS)
    return (y * g.astype(jnp.float32)).astype(x.dtype)


def head_layernorm(x, g):
    xf = x.astype(jnp.float32)
    mu = jnp.mean(xf, axis=-1, keepdims=True)
    var = jnp.mean(jnp.square(xf - mu), axis=-1, keepdims=True)
    return ((xf - mu) * lax.rsqrt(var + EPS) * g.astype(jnp.float32)).astype(x.dtype)


def causal_dwconv(x, w):
    K, C = w.shape
    return lax.conv_general_dilated(x, w[:, None, :].astype(x.dtype), window_strides=(1,),
                                    padding=[(K - 1, 0)],
                                    dimension_numbers=('NWC', 'WIO', 'NWC'),
                                    feature_group_count=C)


def rope_tables(S, dim):
    inv_freq = ROPE_THETA ** (-jnp.arange(0, dim, 2, dtype=jnp.float32) / dim)
    ang = jnp.arange(S, dtype=jnp.float32)[:, None] * inv_freq[None, :]
    return jnp.cos(ang), jnp.sin(ang)


def apply_rope(t, cos, sin):
    tf = t.astype(jnp.float32)
    half = tf.shape[-1] // 2
    c = cos[None, :, None, None, :]
    s = sin[None, :, None, None, :]
    t1, t2 = tf[..., :half], tf[..., half:]
    return jnp.concatenate([t1 * c - t2 * s, t2 * c + t1 * s], axis=-1).astype(t.dtype)


def diff_attention(q, k, v, lam):
    B, S, H, _, dh = q.shape
    dv = v.shape[-1]
    nb = S // Q_BLOCK
    qb = q.reshape(B, nb, Q_BLOCK, H, 2, dh).transpose(1, 0, 2, 3, 4, 5)
    k_chunk = jnp.arange(S) // CHUNK
    scale = dh ** -0.5

    def block(args):
        q_blk, bi = args
        q_chunk = (bi * Q_BLOCK + jnp.arange(Q_BLOCK)) // CHUNK
        mask = k_chunk[None, :] <= q_chunk[:, None]
        s = jnp.einsum('bqhmd,bkhmd->bhmqk', q_blk, k).astype(jnp.float32) * scale
        s = jnp.where(mask, s, -jnp.inf)
        a = jax.nn.softmax(s, axis=-1)
        w = a[:, :, 0] - lam * a[:, :, 1]
        return jnp.einsum('bhqk,bkhe->bqhe', w.astype(v.dtype), v)

    out = lax.map(block, (qb, jnp.arange(nb)))
    return out.transpose(1, 0, 2, 3, 4).reshape(B, S, H, dv)


def mlstm_chunkwise(q, k, v, i_pre, logf):
    B, S, H, dk = q.shape
    dv = v.shape[-1]
    nc = S // CHUNK
    out_dtype = v.dtype

    def chunks(t):
        return t.astype(jnp.float32).reshape(B, nc, CHUNK, H, -1).transpose(1, 0, 3, 2, 4)

    qc, kc, vc = chunks(q), chunks(k), chunks(v)
    ic = chunks(i_pre[..., None])[..., 0]
    fc = chunks(logf[..., None])[..., 0]
    causal = jnp.tril(jnp.ones((CHUNK, CHUNK), dtype=bool))

    def step(carry, inp):
        C, n, m = carry
        q_, k_, v_, i_, f_ = inp
        b = jnp.cumsum(f_, axis=-1)
        D = b[..., :, None] - b[..., None, :] + i_[..., None, :]
        D = jnp.where(causal, D, -jnp.inf)
        inter = b + m[..., None]
        m_t = jnp.maximum(inter, jnp.max(D, axis=-1))
        w_intra = jnp.exp(D - m_t[..., None])
        w_inter = jnp.exp(inter - m_t)
        s = jnp.einsum('bhtd,bhsd->bhts', q_, k_) * w_intra
        num = (w_inter[..., None] * jnp.einsum('bhtd,bhde->bhte', q_, C)
               + jnp.einsum('bhts,bhse->bhte', s, v_))
        den = w_inter * jnp.einsum('bhtd,bhd->bht', q_, n) + jnp.sum(s, axis=-1)
        h = num / jnp.maximum(jnp.abs(den), jnp.exp(-m_t))[..., None]
        bL = b[..., -1]
        g = bL[..., None] - b + i_
        m_new = jnp.maximum(bL + m, jnp.max(g, axis=-1))
        wk = jnp.exp(g - m_new[..., None])
        decay = jnp.exp(bL + m - m_new)
        C_new = decay[..., None, None] * C + jnp.einsum('bhs,bhsd,bhse->bhde', wk, k_, v_)
        n_new = decay[..., None] * n + jnp.einsum('bhs,bhsd->bhd', wk, k_)
        return (C_new, n_new, m_new), h

    init = (jnp.zeros((B, H, dk, dv), jnp.float32), jnp.zeros((B, H, dk), jnp.float32),
            jnp.zeros((B, H), jnp.float32))
    _, hs = lax.scan(step, init, (qc, kc, vc, ic, fc))
    return hs.transpose(1, 0, 3, 2, 4).reshape(B, S, H, dv).astype(out_dtype)


def setup_inputs(seed: int = 0) -> dict:
    key = jax.random.key(seed)
    ks = jax.random.split(key, 24)
    nrm = lambda k, shape, fan: jax.random.normal(k, shape, jnp.float32) * fan ** -0.5
    gain = lambda k, shape: 1.0 + 0.02 * jax.random.normal(k, shape, jnp.float32)
    b_if = jnp.concatenate([
        0.1 * jax.random.normal(ks[3], (DEPTH, ML_HEADS), jnp.float32),
        jax.random.uniform(ks[4], (DEPTH, ML_HEADS), jnp.float32, 3.0, 6.0)], axis=-1)
    return {
        "x": jax.random.normal(ks[0], (BATCH, SEQ, D_MODEL), jnp.float32),
        "p": jax.random.normal(ks[1], (DEPTH, BATCH, SEQ, PLE_DIM), jnp.float32),
        "norm1_g": gain(ks[2], (DEPTH, D_MODEL)),
        "w_in": nrm(ks[5], (DEPTH, D_MODEL, N_IN), D_MODEL),
        "b_if": b_if,
        "conv_m_w": nrm(ks[6], (DEPTH, ML_CONV, 2 * ML_W), ML_CONV),
        "lam_q1": 0.1 * jax.random.normal(ks[7], (DEPTH, DA_QK_DIM), jnp.float32),
        "lam_k1": 0.1 * jax.random.normal(ks[8], (DEPTH, DA_QK_DIM), jnp.float32),
        "lam_q2": 0.1 * jax.random.normal(ks[9], (DEPTH, DA_QK_DIM), jnp.float32),
        "lam_k2": 0.1 * jax.random.normal(ks[10], (DEPTH, DA_QK_DIM), jnp.float32),
        "da_norm_g": gain(ks[11], (DEPTH, DA_V_DIM)),
        "ml_norm_g": gain(ks[12], (DEPTH, ML_DIM)),
        "w_ya": nrm(ks[13], (DEPTH, DA_V_W, D_MODEL), DA_V_W),
        "w_yb": nrm(ks[14], (DEPTH, ML_W, D_MODEL), ML_W),
        "w_o": nrm(ks[15], (DEPTH, D_MODEL, D_MODEL), D_MODEL),
        "norm2_g": gain(ks[16], (DEPTH, D_MODEL)),
        "w_up": nrm(ks[17], (DEPTH, D_MODEL, 2 * D_FF), D_MODEL),
        "conv_f_w": nrm(ks[18], (DEPTH, FF_CONV, D_FF), FF_CONV),
        "conv_f_b": 0.02 * jax.random.normal(ks[19], (DEPTH, D_FF), jnp.float32),
        "w_down": nrm(ks[20], (DEPTH, D_FF, D_MODEL), D_FF),
        "w_ple": nrm(ks[21], (DEPTH, PLE_DIM, D_MODEL), PLE_DIM),
        "w_pg": nrm(ks[22], (DEPTH, D_MODEL, D_MODEL), D_MODEL),
        "final_g": gain(ks[23], (D_MODEL,)),
    }


def reference(x, p, norm1_g, w_in, b_if, conv_m_w, lam_q1, lam_k1, lam_q2, lam_k2,
              da_norm_g, ml_norm_g, w_ya, w_yb, w_o, norm2_g, w_up, conv_f_w, conv_f_b,
              w_down, w_ple, w_pg, final_g):
    B, S, _ = x.shape
    cos, sin = rope_tables(S, DA_QK_DIM)
    split_pts = list(np.cumsum(IN_SIZES)[:-1])
    for i in range(DEPTH):
        lam_init = 0.8 - 0.6 * math.exp(-0.3 * i)
        h = rmsnorm(x, norm1_g[i])
        z = h @ w_in[i]
        qa, ka, va, qk_m, vm, om, if_pre, ga, gb = jnp.split(z, split_pts, axis=-1)

        qa = apply_rope(qa.reshape(B, S, DA_HEADS, 2, DA_QK_DIM), cos, sin)
        ka = apply_rope(ka.reshape(B, S, DA_HEADS, 2, DA_QK_DIM), cos, sin)
        va = va.reshape(B, S, DA_HEADS, DA_V_DIM)
        lam = (jnp.exp(jnp.sum(lam_q1[i].astype(jnp.float32) * lam_k1[i].astype(jnp.float32)))
               - jnp.exp(jnp.sum(lam_q2[i].astype(jnp.float32) * lam_k2[i].astype(jnp.float32)))
               + lam_init)
        ao = diff_attention(qa, ka, va, lam)
        ao = rmsnorm(ao, da_norm_g[i]) * (1.0 - lam_init)
        y_a = ao.reshape(B, S, DA_V_W) @ w_ya[i]

        qk_m = jax.nn.silu(causal_dwconv(qk_m, conv_m_w[i]))
        qm, km = jnp.split(qk_m, 2, axis=-1)
        gates = if_pre.astype(jnp.float32) + b_if[i].astype(jnp.float32)
        i_pre = gates[..., :ML_HEADS]
        logf = jax.nn.log_sigmoid(gates[..., ML_HEADS:])
        hm = mlstm_chunkwise(qm.reshape(B, S, ML_HEADS, ML_DIM),
                             km.reshape(B, S, ML_HEADS, ML_DIM) * (ML_DIM ** -0.5),
                             vm.reshape(B, S, ML_HEADS, ML_DIM), i_pre, logf)
        hm = head_layernorm(hm, ml_norm_g[i]) * jax.nn.sigmoid(om).reshape(B, S, ML_HEADS, ML_DIM)
        y_b = hm.reshape(B, S, ML_W) @ w_yb[i]

        merged = jax.nn.sigmoid(ga) * y_a + jax.nn.sigmoid(gb) * y_b
        x = x + merged @ w_o[i]

        h2 = rmsnorm(x, norm2_g[i])
        a, g = jnp.split(h2 @ w_up[i], 2, axis=-1)
        a = causal_dwconv(a, conv_f_w[i]) + conv_f_b[i]
        x = x + (jax.nn.gelu(a) * g) @ w_down[i]

        x = x + (p[i] @ w_ple[i]) * jax.nn.sigmoid(x @ w_pg[i])
    return rmsnorm(x, final_g)
```

```python
import math
from contextlib import ExitStack

import numpy as np
import ml_dtypes

import concourse.bass as bass
import concourse.mybir as mybir
from concourse.bass_utils import run_bass_kernel_spmd

F32 = mybir.dt.float32
BF16 = mybir.dt.bfloat16
AF = mybir.ActivationFunctionType
ALU = mybir.AluOpType

NCORES = 8
NSLOT = 8
D = 1024
NIN = 5640
DFF = 2816
NFC = DFF // 128
PLE = 256
H = 4
EPS = 1e-6
LAM_INIT = 0.8 - 0.6 * math.exp(-0.3 * 0)
QA, KA, VA, QM, KM, VM, OM, IFG, GA, GB = 0, 512, 1024, 1536, 2048, 2560, 3072, 3584, 3592, 4616
NEG = -30000.0


class Res:
    __slots__ = ("name", "w", "r", "ro")

    def __init__(self, name, ro=False):
        self.name = name
        self.w = None
        self.r = []
        self.ro = ro


class Prog:
    ENGS = ("pe", "act", "dve", "pool", "sp")

    def __init__(self, nc):
        self.nc = nc
        self.q = {e: [] for e in self.ENGS}
        self.sem = {e: nc.alloc_semaphore(f"c_{e}") for e in self.ENGS}
        self.cnt = {e: 0 for e in self.ENGS}
        self.seen = {e: {} for e in self.ENGS}
        self.dsem = {}
        self.dknow = {}
        self.hist = {e: {} for e in self.ENGS}
        self.n_wait = 0
        self.n_ins = 0

    def _merge(self, eng, know):
        se = self.seen[eng]
        for k, v in know.items():
            if se.get(k, 0) < v:
                se[k] = v

    def _need(self, eng, toks):
        se = self.seen[eng]
        cand = []
        for t in toks:
            if t is None:
                continue
            kind, key, val = t
            if kind == "d":
                val = self.dsem[key][1]
            cand.append((kind, key, val))
        cand.sort(key=lambda t: (t[0] == "c" and t[1] == eng, -t[2]))
        out = []
        for kind, key, val in cand:
            k = (kind, key)
            if se.get(k, 0) >= val:
                continue
            if kind == "c":
                sem = self.sem[key]
                know = self.hist[key].get(val)
            else:
                sem = self.dsem[key][0]
                know = self.dknow.get(key)
            se[k] = val
            if know:
                self._merge(eng, know)
            out.append((sem, val))
        return out

    def _deps(self, eng, reads, writes):
        toks = []
        for r in reads:
            toks.append(r.w)
        for w in writes:
            if not (eng == "pe" and w.w is not None and w.w[0] == "c" and w.w[1] == "pe"):
                toks.append(w.w)
            toks.extend(w.r)
        return toks

    def _record(self, tok, reads, writes):
        for r in reads:
            if not r.ro:
                r.r.append(tok)
        for w in writes:
            w.w = tok
            w.r = []

    def op(self, eng, fn, reads=(), writes=()):
        waits = self._need(eng, self._deps(eng, reads, writes))
        self.cnt[eng] += 1
        n = self.cnt[eng]
        tok = ("c", eng, n)
        snap = dict(self.seen[eng])
        snap[("c", eng)] = n
        self.hist[eng][n] = snap
        self.q[eng].append((waits, fn, self.sem[eng], 1))
        self.n_wait += len(waits)
        self.n_ins += 1
        self._record(tok, reads, writes)
        return tok

    def dma(self, eng, semname, fn, reads=(), writes=()):
        if semname not in self.dsem:
            self.dsem[semname] = [self.nc.alloc_semaphore(f"d_{semname}"), 0]
            self.dknow[semname] = {}
        waits = self._need(eng, self._deps(eng, reads, writes))
        ent = self.dsem[semname]
        ent[1] += 16
        tok = ("d", semname, ent[1])
        dk = self.dknow[semname]
        for k, v in self.seen[eng].items():
            if dk.get(k, 0) < v:
                dk[k] = v
        self.q[eng].append((waits, fn, ent[0], 16))
        self.n_wait += len(waits)
        self.n_ins += 1
        self._record(tok, reads, writes)
        return tok

    def wait_all(self, eng, ress):
        toks = []
        for r in ress:
            toks.append(r.w)
            toks.extend(r.r)
        waits = self._need(eng, toks)
        self.q[eng].append((waits, None, None, 0))

    def barrier(self):
        for eng in self.ENGS:
            toks = [("c", e2, self.cnt[e2]) for e2 in self.ENGS if self.cnt[e2] > 0]
            toks += [("d", name, ent[1]) for name, ent in self.dsem.items() if ent[1] > 0]
            waits = self._need(eng, toks)
            self.q[eng].append((waits, None, None, 0))

    def run(self):
        q = self.q

        def play(engobj, items):
            for waits, fn, sem, inc in items:
                if fn is None:
                    for (s, v) in waits:
                        engobj.wait_ge(s, v)
                    continue
                for (s, v) in waits[:-1]:
                    engobj.wait_ge(s, v)
                ins = fn(engobj)
                if waits:
                    ins._wait_ge(*waits[-1])
                if sem is not None:
                    ins.then_inc(sem, inc)

        with self.nc.Block() as block:
            @block.tensor
            def _(e):
                play(e, q["pe"])

            @block.scalar
            def _(e):
                play(e, q["act"])

            @block.vector
            def _(e):
                play(e, q["dve"])

            @block.gpsimd
            def _(e):
                play(e, q["pool"])

            @block.sync
            def _(e):
                play(e, q["sp"])


def build_program(NTS):
    NT = NSLOT * NTS
    NOWN = NTS + 1
    G0 = NT - NOWN
    SLOT = NTS * 128
    BT = min(4, NTS)
    blocks = [[0]] + [list(range(1 + b, 1 + b + BT)) for b in range(0, NTS, BT)]
    NOWNTOK = NOWN * 128

    nc = bass.Bass("TRN2", target_bir_lowering=False)
    P = Prog(nc)

    def din(name, shape, dt=F32):
        return nc.dram_tensor(name, list(shape), dt, kind="ExternalInput").ap()

    def dscr(name, shape, dt):
        return nc.dram_tensor(name, list(shape), dt).ap(), Res(name)

    xa = din("xa", [NT * 128, D])
    pT_d = din("pT", [128, 2, SLOT])
    w_in_d = din("w_in", [D, NIN])
    w_ya_d = din("w_ya", [512, D])
    w_yb_d = din("w_yb", [512, D])
    w_o_d = din("w_o", [D, D])
    w_up_d = din("w_up", [D, 2 * DFF])
    w_dn_d = din("w_down", [DFF, D])
    w_ple_d = din("w_ple", [PLE, D])
    w_pg_d = din("w_pg", [D, D])
    g1_d = din("g1", [1, D])
    g2_d = din("g2", [1, D])
    gF_d = din("gF", [1, D])
    gda_d = din("gda", [1, 128])
    gml_d = din("gml", [1, 128])
    bif_d = din("b_if", [1, 8])
    lam_d = din("lamv", [1, 256])
    cmw_d = din("conv_m_w", [1, 4 * D])
    cfw_d = din("cfw", [128, NFC, 3])
    cfb_d = din("cfb", [128, NFC])
    identb_d = din("identb", [128, 128], BF16)
    tri_d = din("tri", [128, 128])
    ones_d = din("ones", [128, 128])
    shifts_d = din("shifts", [128, 7, 128], BF16)
    pat_d = din("pat", [128, 4, 512], BF16)
    ropec_d = din("ropec", [128, NT, 32])
    ropes_d = din("ropes", [128, NT, 32])
    biast_d = din("biast", [128, NSLOT])
    y_d = nc.dram_tensor("y", [SLOT, D], F32, kind="ExternalOutput").ap()
    r_y = Res("y")

    KT_d, r_KT = dscr("KT", [H, 128, NT * 128], BF16)
    VV_d, r_VV = dscr("VV", [H, 128, NT, 130], BF16)
    QT_d, r_QT = dscr("QT", [H, 128, NOWNTOK], BF16)
    SG_d, r_SG = dscr("SG", [NOWNTOK, 2 * D], BF16)
    HMT_d, r_HMT = dscr("HMT", [128, H, NOWNTOK], BF16)
    AOT_d, r_AOT = dscr("AOT", [128, H, NOWNTOK], BF16)
    WUP_d, r_WUP = dscr("WUPb", [128, 8, 2 * DFF], BF16)
    WDN_d, r_WDN = dscr("WDNb", [128, NFC, D], BF16)
    WYA_d, r_WYA = dscr("WYAb", [128, 4, D], BF16)
    WYB_d, r_WYB = dscr("WYBb", [128, 4, D], BF16)
    WO_d, r_WO = dscr("WOb", [128, 8, D], BF16)
    WPG_d, r_WPG = dscr("WPGb", [128, 8, D], BF16)
    WPL_d, r_WPL = dscr("WPLb", [128, 2, D], BF16)

    def mm(out, lhsT, rhs, start, stop, reads, writes, skip=False):
        if skip:
            P.op("pe", lambda e: e.matmul(out, lhsT=lhsT, rhs=rhs, start=start, stop=stop, skip_group_check=True),
                 reads, writes)
        else:
            P.op("pe", lambda e: e.matmul(out, lhsT=lhsT, rhs=rhs, start=start, stop=stop), reads, writes)

    def tr(out, in_, ident, reads, writes):
        P.op("pe", lambda e: e.transpose(out=out, in_=in_, identity=ident), reads, writes)

    def act(out, in_, func, reads, writes, bias=0.0, scale=1.0, accum_out=None, eng="act"):
        if accum_out is None:
            P.op(eng, lambda e: e.activation(out=out, in_=in_, func=func, bias=bias, scale=scale), reads, writes)
        else:
            P.op(eng, lambda e: e.activation(out=out, in_=in_, func=func, bias=bias, scale=scale,
                                             accum_out=accum_out), reads, writes)

    def acopy(out, in_, reads, writes):
        P.op("act", lambda e: e.copy(out=out, in_=in_), reads, writes)

    def tcopy(eng, out, in_, reads, writes):
        if eng == "act":
            P.op(eng, lambda e: e.copy(out=out, in_=in_), reads, writes)
        else:
            P.op(eng, lambda e: e.tensor_copy(out=out, in_=in_), reads, writes)

    def tt(eng, out, in0, in1, op, reads, writes):
        P.op(eng, lambda e: e.tensor_tensor(out=out, in0=in0, in1=in1, op=op), reads, writes)

    def ts(eng, out, in0, s1, s2, op0, op1, reads, writes):
        if s2 is None:
            P.op(eng, lambda e: e.tensor_scalar(out=out, in0=in0, scalar1=s1, scalar2=None, op0=op0), reads, writes)
        else:
            P.op(eng, lambda e: e.tensor_scalar(out=out, in0=in0, scalar1=s1, scalar2=s2, op0=op0, op1=op1),
                 reads, writes)

    def stt(eng, out, in0, scalar, in1, op0, op1, reads, writes):
        P.op(eng, lambda e: e.scalar_tensor_tensor(out=out, in0=in0, scalar=scalar, in1=in1, op0=op0, op1=op1),
             reads, writes)

    def recip(out, in_, reads, writes):
        P.op("dve", lambda e: e.reciprocal(out=out, in_=in_), reads, writes)

    def memset(eng, ap, val, writes):
        P.op(eng, lambda e: e.memset(ap, val), (), writes)

    def dma(semname, out, in_, reads, writes, eng="sp"):
        P.dma(eng, semname, lambda e: e.dma_start(out=out, in_=in_), reads, writes)

    with ExitStack() as es_top:
        def sbuf(es, name, shape, dt=F32, ro=False):
            t = es.enter_context(nc.sbuf_tensor("s_" + name, list(shape), dt))
            return t, Res(name, ro=ro)

        banks = []
        for i in range(4):
            t = es_top.enter_context(nc.psum_tensor(f"pb{i}", [128, 2, 512], F32))
            banks.append((t, Res(f"pb{i}a")))
            banks.append((t, Res(f"pb{i}b")))

        def bank(i):
            t, r = banks[i]
            return t[:, i % 2, :], r

        def bank2(i):
            t, ra = banks[i]
            _, rb = banks[i + 1]
            return t, ra, rb

        esg = es_top
        identb, r_identb = sbuf(esg, "identb", [128, 128], BF16, ro=True)
        tri, r_tri = sbuf(esg, "tri", [128, 128], F32, ro=True)
        ones, r_ones = sbuf(esg, "ones", [128, 128], F32, ro=True)
        gFb, r_gFb = sbuf(esg, "gFb", [128, D], F32, ro=True)
        lam_t, r_lam = sbuf(esg, "lam_t", [128, 4, 64], F32)
        lam_s, r_lams = sbuf(esg, "lam_s", [128, 4], F32)
        nlam, r_nlam = sbuf(esg, "nlam", [128, 1], F32, ro=True)
        ssq, r_ssq = sbuf(esg, "ssq", [128, 8], F32)
        junk, r_junk = sbuf(esg, "junk", [128, D], F32)

        dma("identb", identb[:], identb_d, [], [r_identb])
        dma("tri", tri[:], tri_d, [], [r_tri])
        dma("ones", ones[:], ones_d, [], [r_ones])
        dma("gFb", gFb[:], gF_d.partition_broadcast(128), [], [r_gFb])
        dma("lam_t", lam_t[:].rearrange("p a b -> p (a b)"),
            lam_d.partition_broadcast(128), [], [r_lam])
        for i in range(2):
            tt("dve", lam_t[:, 2 * i, :], lam_t[:, 2 * i, :], lam_t[:, 2 * i + 1, :], ALU.mult, [r_lam], [r_lam])
            act(lam_t[:, 2 * i + 1, :], lam_t[:, 2 * i, :], AF.Copy, [r_lam], [r_lam, r_lams],
                accum_out=lam_s[:, i:i + 1])
        act(lam_s[:, 2:4], lam_s[:, 0:2], AF.Exp, [r_lams], [r_lams])
        tt("dve", lam_s[:, 0:1], lam_s[:, 3:4], lam_s[:, 2:3], ALU.subtract, [r_lams], [r_lams])
        ts("dve", nlam[:], lam_s[:, 0:1], -LAM_INIT, None, ALU.add, None, [r_lams], [r_nlam])

        def rms_stats(src_ap, n, r_src, col=0):
            act(junk[:, 0:n], src_ap, AF.Square, [r_src], [r_junk, r_ssq], accum_out=ssq[:, col:col + 1])
            act(ssq[:, col:col + 1], ssq[:, col:col + 1], AF.Sqrt, [r_ssq], [r_ssq], bias=EPS, scale=1.0 / n)
            recip(ssq[:, col:col + 1], ssq[:, col:col + 1], [r_ssq], [r_ssq])
            return ssq[:, col:col + 1]

        def transpose8(src_bf, r_src, dst_ap, r_dst, nblk, bidx=0):
            bk, r_bk = bank(bidx)
            bb = bk.bitcast(BF16)
            for k in range(nblk):
                tr(bb[:, k * 128:(k + 1) * 128], src_bf[:, k * 128:(k + 1) * 128], identb[:],
                   [r_src, r_identb], [r_bk])
            acopy(dst_ap, bb[:, 0:nblk * 128].rearrange("p (a b) -> p a b", a=nblk), [r_bk], [r_dst])

        with ExitStack() as es1:
            Wb, r_Wb = sbuf(es1, "Wb", [128, 8, NIN], BF16, ro=True)
            g1b, r_g1b = sbuf(es1, "g1b", [128, D], F32, ro=True)
            cmwb, r_cmwb = sbuf(es1, "cmwb", [128, 4, D], F32, ro=True)
            bifb, r_bifb = sbuf(es1, "bifb", [128, 8], F32, ro=True)
            gmlb, r_gmlb = sbuf(es1, "gmlb", [128, 128], F32, ro=True)
            shifts, r_shifts = sbuf(es1, "shifts", [128, 7, 128], BF16, ro=True)
            CU = 1024
            stage = [sbuf(es1, f"stage{i}", [128, CU], F32) for i in range(2)]
            stageb = [sbuf(es1, f"stageb{i}", [128, CU], BF16) for i in range(2)]
            NXT = 3
            xt = [sbuf(es1, f"xt{i}", [128, D], F32) for i in range(NXT)]
            rc = [sbuf(es1, f"rc{i}", [128, 32], F32) for i in range(NXT)]
            rs = [sbuf(es1, f"rs{i}", [128, 32], F32) for i in range(NXT)]
            st1 = [sbuf(es1, f"st1_{i}", [128, 2], F32) for i in range(NXT)]
            hbs = [sbuf(es1, f"hb{i}", [128, D], BF16) for i in range(2)]
            hTs = [sbuf(es1, f"hT{i}", [128, 8, 128], BF16) for i in range(2)]
            ra, r_ra = sbuf(es1, "ra", [128, 512], F32)
            rb1, r_rb1 = sbuf(es1, "rb1", [128, 8, 32], F32)
            rb2, r_rb2 = sbuf(es1, "rb2", [128, 8, 32], F32)
            krot, r_krot = sbuf(es1, "krot", [128, 512], BF16)
            ktT, r_ktT = sbuf(es1, "ktT", [128, 4, 128], BF16)
            vaug = [sbuf(es1, f"vaug{i}", [128, 4, 130], BF16) for i in range(2)]
            xwk = [(sbuf(es1, f"xwk{i}", [128, 4, 512], BF16)[0], [Res(f"xwk{i}_{j}") for j in range(4)]) for i in range(2)]
            xwq = [(sbuf(es1, f"xwq{i}", [128, 4, 512], BF16)[0], [Res(f"xwq{i}_{j}") for j in range(4)]) for i in range(2)]
            sig, r_sig = sbuf(es1, "sig", [128, 512], F32)
            kmtoks = [sbuf(es1, f"kmtok{i}", [128, 512], BF16) for i in range(2)]
            qmtok, r_qmtok = sbuf(es1, "qmtok", [128, 512], BF16)
            kmT, r_kmT = sbuf(es1, "kmT", [128, 4, 128], BF16)
            qmT, r_qmT = sbuf(es1, "qmT", [128, 4, 128], BF16)
            gt, r_gt = sbuf(es1, "gt", [128, 8], F32)
            gs, r_gs = sbuf(es1, "gs", [128, 24], F32)
            vts = [sbuf(es1, f"vt{i}", [128, 4, 129], BF16) for i in range(2)]
            Cst, r_C = sbuf(es1, "Cst", [128, 4, 129], F32)
            Chat, r_Chat = sbuf(es1, "Chat", [128, 4, 129], F32)
            Chb, r_Chb = sbuf(es1, "Chb", [128, 4, 129], BF16)
            ATs, r_ATs = sbuf(es1, "ATs", [128, 4, 128], BF16)
            hmf, r_hmf = sbuf(es1, "hmf", [128, 4, 128], F32)
            hmbs = [sbuf(es1, f"hmb{i}", [128, 512], BF16) for i in range(2)]
            hmTs, r_hmTs = sbuf(es1, "hmTs", [128, 4, 128], BF16)
            oms, r_oms = sbuf(es1, "oms", [128, 512], F32)
            sgs, r_sgs = sbuf(es1, "sgs", [128, 2 * D], BF16)
            mst, r_mst = sbuf(es1, "mst", [128, 4, 8], F32)
            rr, r_rr = sbuf(es1, "rr", [128, 8], F32)

            dma("g1b", g1b[:], g1_d.partition_broadcast(128), [], [r_g1b])
            dma("cmwb", cmwb[:].rearrange("p a b -> p (a b)"), cmw_d.partition_broadcast(128), [], [r_cmwb])
            dma("bifb", bifb[:], bif_d.partition_broadcast(128), [], [r_bifb])
            dma("gmlb", gmlb[:], gml_d.partition_broadcast(128), [], [r_gmlb])
            dma("shifts", shifts[:], shifts_d, [], [r_shifts])
            memset("pool", Cst[:], 0.0, [r_C])
            for i in range(2):
                memset("pool", xwk[i][0][:], 0.0, xwk[i][1])
                memset("pool", xwq[i][0][:], 0.0, xwq[i][1])
                memset("pool", vaug[i][0][:, :, 128:129], 1.0, [vaug[i][1]])
                memset("pool", vaug[i][0][:, :, 129:130], 0.0, [vaug[i][1]])

            cast_i = [0]

            def cast_unit(src_ap, ncols_free, dst_kind, dst_ap, r_dst, ceng="pool", dq="sp"):
                i = cast_i[0] % 2
                cast_i[0] += 1
                st, r_st = stage[i]
                a, b = src_ap.shape[1], src_ap.shape[2]
                stv = st[:, 0:ncols_free].rearrange("p (a b) -> p a b", a=a)
                dma(f"stage{i}", stv, src_ap, [], [r_st], eng=dq)
                if dst_kind == "sbuf":
                    tcopy(ceng, dst_ap, stv, [r_st], [r_dst])
                else:
                    sbt, r_sbt = stageb[i]
                    sbv = sbt[:, 0:ncols_free].rearrange("p (a b) -> p a b", a=a)
                    tcopy(ceng, sbv, stv, [r_st], [r_sbt])
                    dma(f"stageb{i}", dst_ap, sbv, [r_sbt], [r_dst], eng=dq)

            w_in_v = w_in_d.rearrange("(k p) n -> p k n", p=128)
            GW = CU // 8
            wgrp = {}
            col_groups = []
            for c0 in (KA, VA, IFG, KM, VM, QA, QM, OM, GA, GA + 512, GB, GB + 512):
                n_tot = 8 if c0 == IFG else 512
                for cc in range(c0, c0 + n_tot, GW):
                    col_groups.append((cc, min(GW, c0 + n_tot - cc)))
            engs = ["pool", "dve", "act"]
            late_units = []
            n_early = 4 * 4 + 1
            for gi, (c0, n) in enumerate(col_groups):
                wgrp[c0] = (Res(f"Wb{c0}", ro=True), n)
                if gi < n_early:
                    cast_unit(w_in_v[:, :, c0:c0 + n], 8 * n, "sbuf", Wb[:, :, c0:c0 + n], wgrp[c0][0],
                              ceng=engs[gi % 3])
                else:
                    late_units.append((w_in_v[:, :, c0:c0 + n], 8 * n, Wb[:, :, c0:c0 + n], wgrp[c0][0], "sbuf"))

            def wres(c0, n):
                return [r for c, (r, m) in wgrp.items() if c < c0 + n and c + m > c0]

            prep = list(late_units)
            w_up_v = w_up_d.rearrange("(k p) n -> p k n", p=128)
            for c0 in range(0, 2 * DFF, GW):
                prep.append((w_up_v[:, :, c0:c0 + GW], CU, WUP_d[:, :, c0:c0 + GW], r_WUP))
            w_dn_v = w_dn_d.rearrange("(k p) n -> p k n", p=128)
            for k0 in range(NFC):
                prep.append((w_dn_v[:, k0:k0 + 1, :], D, WDN_d[:, k0:k0 + 1, :], r_WDN))
            w_ya_v = w_ya_d.rearrange("(k p) n -> p k n", p=128)
            w_yb_v = w_yb_d.rearrange("(k p) n -> p k n", p=128)
            for k0 in range(4):
                prep.append((w_ya_v[:, k0:k0 + 1, :], D, WYA_d[:, k0:k0 + 1, :], r_WYA))
                prep.append((w_yb_v[:, k0:k0 + 1, :], D, WYB_d[:, k0:k0 + 1, :], r_WYB))
            w_o_v = w_o_d.rearrange("(k p) n -> p k n", p=128)
            w_pg_v = w_pg_d.rearrange("(k p) n -> p k n", p=128)
            for k0 in range(8):
                prep.append((w_o_v[:, k0:k0 + 1, :], D, WO_d[:, k0:k0 + 1, :], r_WO))
                prep.append((w_pg_v[:, k0:k0 + 1, :], D, WPG_d[:, k0:k0 + 1, :], r_WPG))
            w_pl_v = w_ple_d.rearrange("(k p) n -> p k n", p=128)
            for k0 in range(2):
                prep.append((w_pl_v[:, k0:k0 + 1, :], D, WPL_d[:, k0:k0 + 1, :], r_WPL))
            prep_per_tile = -(-len(prep) // max(1, NT - NOWN))
            prep_pos = [0]

            def emit_prep(k):
                for _ in range(k):
                    if prep_pos[0] < len(prep):
                        pr = prep[prep_pos[0]]
                        prep_pos[0] += 1
                        cast_unit(pr[0], pr[1], pr[4] if len(pr) > 4 else "dram", pr[2], pr[3], dq="pool")

            def proj(hT_t, r_hT_, c0, n, bidx, col0=0):
                bk, r_bk = bank(bidx)
                for kc in range(8):
                    mm(bk[:, col0:col0 + n], hT_t[:, kc, :], Wb[:, kc, c0:c0 + n], kc == 0, kc == 7,
                       [r_hT_] + wres(c0, n), [r_bk])
                return bk, r_bk

            def rope_tm(bk, r_bk, cosap, sinap, r_c, r_s, out_bf, r_out):
                v4 = lambda ap: ap.rearrange("p (a b c) -> p a b c", a=8, b=2, c=32)
                kv = v4(bk[:, 0:512])
                av = v4(ra[:, :])
                ov = v4(out_bf[:, :])
                cb4 = cosap.unsqueeze(1).unsqueeze(1).to_broadcast([128, 8, 2, 32])
                sb3 = sinap.unsqueeze(1).to_broadcast([128, 8, 32])
                tt("dve", av, kv, cb4, ALU.mult, [r_bk, r_c], [r_ra])
                tt("dve", rb1[:], kv[:, :, 1, :], sb3, ALU.mult, [r_bk, r_s], [r_rb1])
                tt("dve", rb2[:], kv[:, :, 0, :], sb3, ALU.mult, [r_bk, r_s], [r_rb2])
                tt("dve", ov[:, :, 0, :], av[:, :, 0, :], rb1[:], ALU.subtract, [r_ra, r_rb1], [r_out])
                tt("dve", ov[:, :, 1, :], av[:, :, 1, :], rb2[:], ALU.add, [r_ra, r_rb2], [r_out])

            def conv_premul(bk, r_bk, wcol0, xw_cur):
                xc, r_xc = xw_cur
                for j in range(4):
                    tt("dve", xc[:, j, :], bk[:, 0:512], cmwb[:, j, wcol0:wcol0 + 512], ALU.mult,
                       [r_bk, r_cmwb], [r_xc[j]])

            def conv_mm(xw_cur, xw_prev, bidx_out):
                xc, r_xc = xw_cur
                xp, r_xp = xw_prev
                ob, r_ob = bank(bidx_out)
                for j in range(4):
                    mm(ob[:, 0:512], shifts[:, j, :], xc[:, j, :], j == 0, False, [r_shifts, r_xc[j]], [r_ob])
                for j in range(3):
                    mm(ob[:, 0:512], shifts[:, 4 + j, :], xp[:, j, :], False, j == 2, [r_shifts, r_xp[j]], [r_ob])
                return ob, r_ob

            def silu_tanh(cb, r_cb, out_bf, r_out, c):
                act(sig[:], cb[:, 0:512], AF.Exp, [r_cb], [r_sig], scale=-1.0 / c)
                act(sig[:], sig[:], AF.Ln, [r_sig], [r_sig], bias=1.0)
                act(sig[:], sig[:], AF.Exp, [r_sig], [r_sig], scale=-1.0)
                tt("dve", out_bf[:], cb[:, 0:512], sig[:], ALU.mult, [r_cb, r_sig], [r_out])

            km_scale = 128.0 ** -0.5
            CK = km_scale
            CQ = 1.0
            ts("dve", cmwb[:, :, 512:1024], cmwb[:, :, 512:1024], CK, None, ALU.mult, None, [r_cmwb], [r_cmwb])

            def stage_A1(g):
                x_t, r_x = xt[g % NXT]
                c_t, r_c = rc[g % NXT]
                s_t, r_s = rs[g % NXT]
                s1, r_s1 = st1[g % NXT]
                hb, r_hb = hbs[g % 2]
                dma(f"xt{g % NXT}", x_t[:], xa[g * 128:(g + 1) * 128, :], [], [r_x])
                dma(f"rc{g % NXT}", c_t[:], ropec_d[:, g, :], [], [r_c])
                dma(f"rs{g % NXT}", s_t[:], ropes_d[:, g, :], [], [r_s])
                act(junk[:, 0:D], x_t[:], AF.Square, [r_x], [r_junk, r_s1], accum_out=s1[:, 0:1])
                act(s1[:, 1:2], s1[:, 0:1], AF.Ln, [r_s1], [r_s1], bias=EPS, scale=1.0 / D)
                act(s1[:, 1:2], s1[:, 1:2], AF.Exp, [r_s1], [r_s1], scale=-0.5)
                stt("dve", hb[:], x_t[:], s1[:, 1:2], g1b[:], ALU.mult, ALU.mult, [r_x, r_s1, r_g1b], [r_hb])

            def stage_A2(g):
                hb, r_hb = hbs[g % 2]
                hT, r_hT = hTs[g % 2]
                transpose8(hb, r_hb, hT[:], r_hT, 8, bidx=0)

            def stage_D(g):
                u = g - G0
                hmb, r_hmb = hmbs[g % 2]
                transpose8(hmb, r_hmb, hmTs[:], r_hmTs, 4, bidx=0)
                dma("hmTs", HMT_d[:, :, u * 128:(u + 1) * 128], hmTs[:], [r_hmTs], [r_HMT])

            def stage_C(g):
                kmtok, r_kmtok = kmtoks[g % 2]
                vt, r_vt = vts[g % 2]
                for hp in range(2):
                    cbk, r_cbk = bank(5)
                    for hh in range(2):
                        h = 2 * hp + hh
                        mm(cbk[:, hh * 129:(hh + 1) * 129], kmtok[:, h * 128:(h + 1) * 128], vt[:, h, :], True, True,
                           [r_kmtok, r_vt], [r_cbk])
                    tt("dve", Cst[:, 2 * hp:2 * hp + 2, :], Chat[:, 2 * hp:2 * hp + 2, :],
                       cbk[:, 0:258].rearrange("p (a b) -> p a b", a=2), ALU.add, [r_Chat, r_cbk], [r_C])

            def stage_B(g):
                own = g >= G0
                u = g - G0
                hT, r_hT = hTs[g % 2]
                kmtok, r_kmtok = kmtoks[g % 2]
                vt, r_vt = vts[g % 2]
                c_t, r_c = rc[g % NXT]
                s_t, r_s = rs[g % NXT]
                bkG, r_bkG = proj(hT, r_hT, IFG, 8, 4, col0=16)
                tt("dve", gt[:], bkG[:, 16:24], bifb[:], ALU.add, [r_bkG, r_bifb], [r_gt])
                act(gs[:, 0:4], gt[:, 4:8], AF.Exp, [r_gt], [r_gs], scale=-1.0)
                act(gs[:, 0:4], gs[:, 0:4], AF.Ln, [r_gs], [r_gs], bias=1.0)
                bkM, r_bkM = proj(hT, r_hT, KM, 512, 3)
                bkK, r_bkK = proj(hT, r_hT, KA, 512, 1)
                bkV, r_bkV = proj(hT, r_hT, VA, 512, 2)
                bkW, r_bkW = proj(hT, r_hT, VM, 512, 6)
                if g - 1 >= G0:
                    stage_D(g - 1)
                conv_premul(bkM, r_bkM, 512, xwk[g % 2])
                rope_tm(bkK, r_bkK, c_t[:, :], s_t[:, :], r_c, r_s, krot, r_krot)
                va, r_va = vaug[g % 2]
                acopy(va[:, :, 0:128], bkV[:, 0:512].rearrange("p (a b) -> p a b", a=4), [r_bkV], [r_va])
                dma(f"vaug{g % 2}", VV_d[:, :, g, :].rearrange("h p c -> p h c"), va[:], [r_va], [r_VV])
                bk4, r_bk4 = bank(4)
                mm(bk4[:, 0:4], tri[:], gs[:, 0:4], True, True, [r_tri, r_gs], [r_bk4])
                mm(bk4[:, 8:12], ones[:], gs[:, 0:4], True, True, [r_ones, r_gs], [r_bk4])
                cb, r_cb = conv_mm(xwk[g % 2], xwk[(g + 1) % 2], 7)
                transpose8(krot, r_krot, ktT[:], r_ktT, 4, bidx=0)
                dma("ktT", KT_d[:, :, g * 128:(g + 1) * 128].rearrange("h p t -> p h t"), ktT[:], [r_ktT], [r_KT])
                if g > 0:
                    stage_C(g - 1)
                tt("dve", gs[:, 8:12], bk4[:, 0:4], gt[:, 0:4], ALU.add, [r_bk4, r_gt], [r_gs])
                tt("dve", gs[:, 8:12], gs[:, 8:12], bk4[:, 8:12], ALU.subtract, [r_gs, r_bk4], [r_gs])
                act(gs[:, 8:12], gs[:, 8:12], AF.Exp, [r_gs], [r_gs])
                act(gs[:, 12:16], bk4[:, 8:12], AF.Exp, [r_bk4], [r_gs], scale=-1.0)
                if own:
                    tcopy("dve", gs[:, 4:8], bk4[:, 8:12], [r_bk4], [r_gs])
                    tt("dve", gs[:, 16:20], gs[:, 4:8], bk4[:, 0:4], ALU.subtract, [r_gs, r_bk4], [r_gs])
                    act(gs[:, 16:20], gs[:, 16:20], AF.Exp, [r_gs], [r_gs])
                wk = gs[:, 8:12]
                dd = gs[:, 12:16]
                wp = gs[:, 16:20]
                silu_tanh(cb, r_cb, kmtok, r_kmtok, CK)
                tt("dve", vt[:, :, 0:128], bkW[:, 0:512].rearrange("p (a b) -> p a b", a=4),
                   wk.unsqueeze(2).to_broadcast([128, 4, 128]), ALU.mult, [r_bkW, r_gs], [r_vt])
                tcopy("dve", vt[:, :, 128:129], wk.unsqueeze(2), [r_gs], [r_vt])
                tt("dve", Chat[:], Cst[:], dd.unsqueeze(2).to_broadcast([128, 4, 129]), ALU.mult,
                   [r_C, r_gs], [r_Chat])

                if own:
                    hTo = hT
                    bkq, r_bkq = proj(hTo, r_hT, QA, 512, 1)
                    bkqm, r_bkqm = proj(hTo, r_hT, QM, 512, 2)
                    bkom, r_bkom = proj(hTo, r_hT, OM, 512, 3)
                    conv_premul(bkqm, r_bkqm, 0, xwq[g % 2])
                    rope_tm(bkq, r_bkq, c_t[:, :], s_t[:, :], r_c, r_s, krot, r_krot)
                    cbq, r_cbq = conv_mm(xwq[g % 2], xwq[(g + 1) % 2], 7)
                    transpose8(krot, r_krot, ktT[:], r_ktT, 4, bidx=0)
                    dma("ktT", QT_d[:, :, u * 128:(u + 1) * 128].rearrange("h p t -> p h t"), ktT[:],
                        [r_ktT], [r_QT])
                    bkga0, r_bkga0 = proj(hTo, r_hT, GA, 512, 1)
                    bkga1, r_bkga1 = proj(hTo, r_hT, GA + 512, 512, 2)
                    bkgb0, r_bkgb0 = proj(hTo, r_hT, GB, 512, 5)
                    silu_tanh(cbq, r_cbq, qmtok, r_qmtok, CQ)
                    act(oms[:], bkom[:, 0:512], AF.Sigmoid, [r_bkom], [r_oms])
                    act(sgs[:, 0:512], bkga0[:, 0:512], AF.Sigmoid, [r_bkga0], [r_sgs])
                    act(sgs[:, 512:1024], bkga1[:, 0:512], AF.Sigmoid, [r_bkga1], [r_sgs])
                    act(sgs[:, 1024:1536], bkgb0[:, 0:512], AF.Sigmoid, [r_bkgb0], [r_sgs])
                    transpose8(qmtok, r_qmtok, qmT[:], r_qmT, 4, bidx=0)
                    transpose8(kmtok, r_kmtok, kmT[:], r_kmT, 4, bidx=0)
                    bkgb1, r_bkgb1 = proj(hTo, r_hT, GB + 512, 512, 3)
                    act(sgs[:, 1536:2048], bkgb1[:, 0:512], AF.Sigmoid, [r_bkgb1], [r_sgs])
                    dma("sgs", SG_d[u * 128:(u + 1) * 128, :], sgs[:], [r_sgs], [r_SG])
                    bk7, r_bk7 = bank(7)
                    for h in range(H):
                        mm(bk7[:, h * 128:(h + 1) * 128], kmT[:, h, :], qmT[:, h, :], True, True,
                           [r_kmT, r_qmT], [r_bk7])
                    tt("dve", ATs[:], bk7[:, 0:512].rearrange("p (a b) -> p a b", a=4),
                       tri[:, :].unsqueeze(1).to_broadcast([128, 4, 128]), ALU.mult, [r_bk7, r_tri], [r_ATs])
                    acopy(Chb[:], Chat[:], [r_Chat], [r_Chb])
                    obs = [bank(6), bank(4)]
                    for hp in range(2):
                        ob, r_ob = obs[hp]
                        for hh in range(2):
                            h = 2 * hp + hh
                            oap = ob[:, hh * 129:(hh + 1) * 129]
                            mm(oap, qmT[:, h, :], Chb[:, h, :], True, False, [r_qmT, r_Chb], [r_ob])
                            mm(oap, ATs[:, h, :], vt[:, h, :], False, True, [r_ATs, r_vt], [r_ob])
                    for hp in range(2):
                        ob, r_ob = obs[hp]
                        for hh in range(2):
                            h = 2 * hp + hh
                            oap = ob[:, hh * 129:(hh + 1) * 129]
                            act(rr[:, 4 * hp + hh:4 * hp + hh + 1], oap[:, 128:129], AF.Abs, [r_ob, r_gs], [r_rr],
                                scale=wp[:, h:h + 1])
                    for hp in range(2):
                        ob, r_ob = obs[hp]
                        for hh in range(2):
                            h = 2 * hp + hh
                            oap = ob[:, hh * 129:(hh + 1) * 129]
                            c0 = 4 * hp + hh
                            ts("dve", rr[:, c0:c0 + 1], rr[:, c0:c0 + 1], 1.0, None, ALU.max, None, [r_rr], [r_rr])
                            recip(rr[:, c0:c0 + 1], rr[:, c0:c0 + 1], [r_rr], [r_rr])
                            tt("dve", rr[:, c0 + 2:c0 + 3], rr[:, c0:c0 + 1], wp[:, h:h + 1], ALU.mult, [r_rr, r_gs], [r_rr])
                            ts("dve", hmf[:, h, :], oap[:, 0:128], rr[:, c0 + 2:c0 + 3], None, ALU.mult, None,
                               [r_ob, r_rr], [r_hmf])
                    for h in range(H):
                        act(junk[:, 0:128], hmf[:, h, :], AF.Copy, [r_hmf], [r_junk, r_mst],
                            accum_out=mst[:, h, 0:1])
                        act(junk[:, 0:128], hmf[:, h, :], AF.Square, [r_hmf], [r_junk, r_mst],
                            accum_out=mst[:, h, 1:2])
                    ts("dve", mst[:, :, 2:3], mst[:, :, 0:1], 1.0 / 128, None, ALU.mult, None, [r_mst], [r_mst])
                    tt("dve", mst[:, :, 3:4], mst[:, :, 2:3], mst[:, :, 2:3], ALU.mult, [r_mst], [r_mst])
                    stt("dve", mst[:, :, 4:5], mst[:, :, 1:2], 1.0 / 128, mst[:, :, 3:4], ALU.mult, ALU.subtract,
                        [r_mst], [r_mst])
                    act(mst[:, :, 4:5], mst[:, :, 4:5], AF.Ln, [r_mst], [r_mst], bias=EPS)
                    act(mst[:, :, 5:6], mst[:, :, 4:5], AF.Exp, [r_mst], [r_mst], scale=-0.5)
                    for h in range(H):
                        ts("dve", hmf[:, h, :], hmf[:, h, :], mst[:, h, 2:3], mst[:, h, 5:6], ALU.subtract, ALU.mult,
                           [r_hmf, r_mst], [r_hmf])
                    tt("dve", hmf[:], hmf[:], gmlb[:, :].unsqueeze(1).to_broadcast([128, 4, 128]), ALU.mult,
                       [r_hmf, r_gmlb], [r_hmf])
                    hmb, r_hmb = hmbs[g % 2]
                    tt("dve", hmb[:], hmf[:].rearrange("p a b -> p (a b)"), oms[:], ALU.mult, [r_hmf, r_oms], [r_hmb])

            stage_A1(0)
            if NT > 1:
                stage_A1(1)
            stage_A2(0)
            for g in range(NT):
                if g + 2 < NT:
                    stage_A1(g + 2)
                if g + 1 < NT:
                    stage_A2(g + 1)
                if g < NT - NOWN:
                    target = -(-(g + 1) * len(prep) // (NT - NOWN))
                    emit_prep(max(0, target - prep_pos[0]))
                stage_B(g)
            stage_D(NT - 1)
            stage_C(NT - 1)
            emit_prep(len(prep))

        scale = 64.0 ** -0.5
        P.barrier()
        es23 = ExitStack()
        es23.__enter__()
        Wya, r_Wya = sbuf(es23, "Wya", [128, 4, D], BF16, ro=True)
        Wyb, r_Wyb = sbuf(es23, "Wyb", [128, 4, D], BF16, ro=True)
        Wo, r_Wo = sbuf(es23, "Wo", [128, 8, D], BF16, ro=True)
        Wpg, r_Wpg = sbuf(es23, "Wpg", [128, 8, D], BF16, ro=True)
        Wpl, r_Wpl = sbuf(es23, "Wpl", [128, 2, D], BF16, ro=True)
        g2b, r_g2b = sbuf(es23, "g2b", [128, D], F32, ro=True)
        cfw, r_cfw = sbuf(es23, "cfw", [128, NFC, 3], F32, ro=True)
        cfb, r_cfb = sbuf(es23, "cfb", [128, NFC], F32, ro=True)
        dma("Wya", Wya[:], WYA_d, [r_WYA], [r_Wya])
        dma("Wyb", Wyb[:], WYB_d, [r_WYB], [r_Wyb])
        dma("Wo", Wo[:], WO_d, [r_WO], [r_Wo])
        dma("Wpg", Wpg[:], WPG_d, [r_WPG], [r_Wpg])
        dma("Wpl", Wpl[:], WPL_d, [r_WPL], [r_Wpl])
        dma("g2b", g2b[:], g2_d.partition_broadcast(128), [], [r_g2b])
        dma("cfw", cfw[:], cfw_d, [], [r_cfw])
        dma("cfb", cfb[:], cfb_d, [], [r_cfb])
        with ExitStack() as es2:
            biast, r_biast = sbuf(es2, "biast", [128, NSLOT], F32, ro=True)
            pat, r_pat = sbuf(es2, "pat", [128, 4, 512], BF16, ro=True)
            gdab, r_gdab = sbuf(es2, "gdab", [128, 128], F32, ro=True)
            qt_s = [sbuf(es2, f"qts{i}", [128, 512], BF16) for i in range(2)]
            kg = [sbuf(es2, f"kg{i}", [128, NTS * 128], BF16) for i in range(2)]
            vg = [sbuf(es2, f"vg{i}", [128, NTS, 130], BF16) for i in range(2)]
            pts = [sbuf(es2, f"pts{i}", [128, 2, 512], BF16) for i in range(3)]
            aof, r_aof = sbuf(es2, "aof", [128, BT, 4, 128], F32)
            aob, r_aob = sbuf(es2, "aob", [128, 512], BF16)
            aoTs, r_aoTs = sbuf(es2, "aoTs", [128, 4, 128], BF16)
            tmpa, r_tmpa = sbuf(es2, "tmpa", [128, 128], F32)
            rl, r_rl = sbuf(es2, "rl", [128, 8], F32)
            dma("biast", biast[:], biast_d, [], [r_biast])
            dma("pat", pat[:], pat_d, [], [r_pat])
            dma("gdab", gdab[:], gda_d.partition_broadcast(128), [], [r_gdab])
            ts("dve", gdab[:], gdab[:], 1.0 - LAM_INIT, None, ALU.mult, None, [r_gdab], [r_gdab])

            S_t = [bank2(0), bank2(2)]
            O_bk = [[bank(4), bank(5)], [bank(6), bank(7)]]
            ld_i = 0
            pt_i = 0
            s_i = 0
            for bi, tiles in enumerate(blocks):
                nq = len(tiles)
                N = nq * 128
                t0 = tiles[0]
                klist = []
                for kt in range(NT):
                    ku = kt - G0
                    if bi == 0:
                        if kt < G0:
                            klist.append((kt, None))
                        elif kt == G0:
                            klist.append((kt, 0))
                    else:
                        if ku < t0:
                            klist.append((kt, None))
                        elif ku < t0 + nq:
                            klist.append((kt, ku - t0))
                groups = {}
                for kt, kind in klist:
                    groups.setdefault(kt // NTS, []).append((kt, kind))
                gkeys = sorted(groups)
                for h in range(H):
                    q_s, r_q = qt_s[(bi * H + h) % 2]
                    dma(f"qts{(bi * H + h) % 2}", q_s[:, 0:N], QT_d[h, :, t0 * 128:t0 * 128 + N], [r_QT], [r_q])
                    total_k = len(klist)
                    loaded = {}
                    seq = []
                    for gk in gkeys:
                        for (kt, kind) in groups[gk]:
                            seq.append((gk, kt, kind))

                    def ensure_group(gk):
                        nonlocal ld_i
                        if gk not in loaded:
                            k_s, r_k = kg[ld_i % 2]
                            v_s, r_v = vg[ld_i % 2]
                            dma(f"kg{ld_i % 2}", k_s[:], KT_d[h, :, gk * NTS * 128:(gk + 1) * NTS * 128], [r_KT], [r_k])
                            dma(f"vg{ld_i % 2}", v_s[:], VV_d[h, :, gk * NTS:(gk + 1) * NTS, :], [r_VV], [r_v])
                            ld_i += 1
                            loaded[gk] = (k_s, r_k, v_s, r_v)
                        return loaded[gk]

                    def emit_s(idx):
                        nonlocal s_i, pt_i
                        gk, kt, kind = seq[idx]
                        k_s, r_k, v_s, r_v = ensure_group(gk)
                        kl = kt - gk * NTS
                        St, r_sa, r_sb = S_t[s_i % 2]
                        s_i += 1
                        mm(St[:, 0, 0:N], k_s[0:64, kl * 128:(kl + 1) * 128], q_s[0:64, 0:N], True, True,
                           [r_k, r_q], [r_sa])
                        mm(St[:, 1, 0:N], k_s[64:128, kl * 128:(kl + 1) * 128], q_s[64:128, 0:N], True, True,
                           [r_k, r_q], [r_sb])
                        p_s, r_p = pts[pt_i % 3]
                        pt_i += 1
                        act(p_s[:, :, 0:N], St[:, :, 0:N], AF.Exp, [r_sa, r_sb, r_biast], [r_p],
                            bias=biast[:, gk:gk + 1], scale=scale)
                        if kind is not None:
                            tt("pool", p_s[:, :, 0:N], p_s[:, :, 0:N],
                               pat[:, kind, 0:N].unsqueeze(1).to_broadcast([128, 2, N]), ALU.mult,
                               [r_p, r_pat], [r_p])
                        return p_s, r_p

                    def emit_pv(idx, p_s, r_p):
                        gk, kt, kind = seq[idx]
                        k_s, r_k, v_s, r_v = loaded[gk]
                        kl = kt - gk * NTS
                        for sub in range(nq):
                            for m in range(2):
                                ob, r_ob = O_bk[m][sub // 2]
                                mm(ob[:, (sub % 2) * 129:(sub % 2 + 1) * 129],
                                   p_s[:, m, sub * 128:(sub + 1) * 128], v_s[:, kl, 0:129],
                                   idx == 0 and (sub % 2 == 0), idx == total_k - 1, [r_p, r_v], [r_ob], skip=True)

                    pend = emit_s(0)
                    for idx in range(total_k):
                        nxt = emit_s(idx + 1) if idx + 1 < total_k else None
                        emit_pv(idx, *pend)
                        pend = nxt
                    for sub in range(nq):
                        o0, r_o0 = O_bk[0][sub // 2]
                        o1, r_o1 = O_bk[1][sub // 2]
                        o0 = o0[:, (sub % 2) * 129:(sub % 2 + 1) * 129]
                        o1 = o1[:, (sub % 2) * 129:(sub % 2 + 1) * 129]
                        ts("dve", rl[:, 0:1], o0[:, 128:129], 1e-30, None, ALU.max, None, [r_o0], [r_rl])
                        ts("dve", rl[:, 1:2], o1[:, 128:129], 1e-30, None, ALU.max, None, [r_o1], [r_rl])
                        recip(rl[:, 2:4], rl[:, 0:2], [r_rl], [r_rl])
                        tt("dve", rl[:, 4:5], rl[:, 3:4], nlam[:, 0:1], ALU.mult, [r_rl, r_nlam], [r_rl])
                        ts("dve", tmpa[:], o1[:, 0:128], rl[:, 4:5], None, ALU.mult, None, [r_o1, r_rl], [r_tmpa])
                        stt("dve", aof[:, sub, h, :], o0[:, 0:128], rl[:, 2:3], tmpa[:], ALU.mult, ALU.add,
                            [r_o0, r_rl, r_tmpa], [r_aof])
                for sub in range(nq):
                    for h in range(H):
                        act(junk[:, 0:128], aof[:, sub, h, :], AF.Square, [r_aof], [r_junk, r_ssq],
                            accum_out=ssq[:, 4 + h:5 + h])
                    act(ssq[:, 4:8], ssq[:, 4:8], AF.Sqrt, [r_ssq], [r_ssq], bias=EPS, scale=1.0 / 128)
                    recip(ssq[:, 4:8], ssq[:, 4:8], [r_ssq], [r_ssq])
                    for h in range(H):
                        stt("dve", aob[:, h * 128:(h + 1) * 128], aof[:, sub, h, :], ssq[:, 4 + h:5 + h], gdab[:],
                            ALU.mult, ALU.mult, [r_aof, r_ssq, r_gdab], [r_aob])
                    transpose8(aob, r_aob, aoTs[:], r_aoTs, 4, bidx=0)
                    u = t0 + sub
                    dma("aoTs", AOT_d[:, :, u * 128:(u + 1) * 128], aoTs[:], [r_aoTs], [r_AOT])

        P.barrier()
        with ExitStack() as es3:
            wdn = [sbuf(es3, f"wdn{i}", [128, 2, D], BF16) for i in range(2)]
            wup = [sbuf(es3, f"wup{i}", [128, 8, 2, 256], BF16) for i in range(2)]
            aoT, r_aoT = sbuf(es3, "aoT", [128, 4, 512], BF16)
            hmT, r_hmT = sbuf(es3, "hmT", [128, 4, 512], BF16)
            sg3s = [sbuf(es3, f"sg3_{i}", [128, 2 * D], BF16) for i in range(2)]
            x3t = [sbuf(es3, f"x3t{i}", [128, D], F32) for i in range(2)]
            m2, r_m2 = sbuf(es3, "m2", [128, D], F32)
            mbs = [sbuf(es3, f"mb{i}", [128, D], BF16) for i in range(2)]
            mT, r_mT = sbuf(es3, "mT", [128, 8, 128], BF16)
            x1b, r_x1b = sbuf(es3, "x1b", [128, BT, D], F32)
            h2bs = [sbuf(es3, f"h2b{i}", [128, D], BF16) for i in range(2)]
            h2T, r_h2T = sbuf(es3, "h2T", [128, 8, 512], BF16)
            aext, r_aext = sbuf(es3, "aext", [128, 2 + 512], F32)
            halo = [sbuf(es3, f"halo{i}", [128, NFC, 2], F32) for i in range(2)]
            acc, r_acc = sbuf(es3, "acc", [128, 512], F32)
            gel, r_gel = sbuf(es3, "gel", [128, 512], F32)
            uT, r_uT = sbuf(es3, "uT", [128, NFC, 512], BF16)
            x2bf, r_x2bf = sbuf(es3, "x2bf", [128, D], BF16)
            x2Ts = [sbuf(es3, f"x2T{i}", [128, 8, 128], BF16) for i in range(2)]
            gsb, r_gsb = sbuf(es3, "gsb", [128, D], F32)
            pf, r_pf = sbuf(es3, "pf", [128, 2, 128], F32)
            pbs = [sbuf(es3, f"pb{i}", [128, 2, 128], BF16) for i in range(2)]
            m1, r_m1 = gsb, r_gsb
            yo = x3t

            for i in range(2):
                memset("pool", halo[i][0][:], 0.0, [halo[i][1]])

            wu_i = 0
            wd_i = 0
            yo_i = 0
            x_i = 0
            for bi, tiles in enumerate(blocks):
                nq = len(tiles)
                N = nq * 128
                t0 = tiles[0]
                last_is_halo = bi == 0
                dma("aoT", aoT[:, :, 0:N], AOT_d[:, :, t0 * 128:t0 * 128 + N], [r_AOT], [r_aoT])
                dma("hmT", hmT[:, :, 0:N], HMT_d[:, :, t0 * 128:t0 * 128 + N], [r_HMT], [r_hmT])
                xbuf = {}

                def stage_MA(ti):
                    nonlocal x_i
                    u = tiles[ti]
                    x_t, r_x = x3t[x_i % 2]
                    xbuf[ti] = (x_t, r_x)
                    dma(f"x3t{x_i % 2}", x_t[:], xa[(G0 + u) * 128:(G0 + u + 1) * 128, :], [], [r_x])
                    x_i += 1
                    sg3, r_sg3 = sg3s[ti % 2]
                    dma(f"sg3_{ti % 2}", sg3[:], SG_d[u * 128:(u + 1) * 128, :], [r_SG], [r_sg3])
                    ya, r_ya0, r_ya1 = bank2(0)
                    yb, r_yb0, r_yb1 = bank2(2)
                    for half, (ra_, rb_) in enumerate(((r_ya0, r_yb0), (r_ya1, r_yb1))):
                        for c in range(4):
                            mm(ya[:, half, :], aoT[:, c, ti * 128:(ti + 1) * 128], Wya[:, c, half * 512:(half + 1) * 512],
                               c == 0, c == 3, [r_aoT, r_Wya], [ra_])
                        for c in range(4):
                            mm(yb[:, half, :], hmT[:, c, ti * 128:(ti + 1) * 128], Wyb[:, c, half * 512:(half + 1) * 512],
                               c == 0, c == 3, [r_hmT, r_Wyb], [rb_])
                    mb, r_mb = mbs[ti % 2]
                    tt("dve", m1[:].rearrange("p (a b) -> p a b", a=2), ya[:, :, :],
                       sg3[:, 0:D].rearrange("p (a b) -> p a b", a=2), ALU.mult, [r_ya0, r_ya1, r_sg3], [r_m1])
                    tt("dve", m2[:].rearrange("p (a b) -> p a b", a=2), yb[:, :, :],
                       sg3[:, D:2 * D].rearrange("p (a b) -> p a b", a=2), ALU.mult, [r_yb0, r_yb1, r_sg3], [r_m2])
                    tt("dve", mb[:], m1[:], m2[:], ALU.add, [r_m1, r_m2], [r_mb])

                def stage_MB(ti):
                    mb, r_mb = mbs[ti % 2]
                    x_t, r_x = xbuf[ti]
                    h2b, r_h2b = h2bs[ti % 2]
                    transpose8(mb, r_mb, mT[:], r_mT, 8, bidx=4)
                    xo, r_xo0, r_xo1 = bank2(6)
                    for half, rr_ in enumerate((r_xo0, r_xo1)):
                        for kc in range(8):
                            mm(xo[:, half, :], mT[:, kc, :], Wo[:, kc, half * 512:(half + 1) * 512], kc == 0, kc == 7,
                               [r_mT, r_Wo], [rr_])
                    tt("dve", x1b[:, ti, :].rearrange("p (a b) -> p a b", a=2), xo[:, :, :],
                       x_t[:].rearrange("p (a b) -> p a b", a=2), ALU.add, [r_xo0, r_xo1, r_x], [r_x1b])
                    rstd = rms_stats(x1b[:, ti, :], D, r_x1b)
                    stt("dve", h2b[:], x1b[:, ti, :], rstd, g2b[:], ALU.mult, ALU.mult, [r_x1b, r_ssq, r_g2b], [r_h2b])

                def stage_MC(ti):
                    h2b, r_h2b = h2bs[ti % 2]
                    transpose8(h2b, r_h2b, h2T[:, :, ti * 128:(ti + 1) * 128], r_h2T, 8, bidx=5)

                stage_MA(0)
                for ti in range(nq):
                    if ti + 1 < nq:
                        stage_MA(ti + 1)
                    stage_MB(ti)
                    if ti >= 1:
                        stage_MC(ti - 1)
                stage_MC(nq - 1)

                h_in, r_hin = halo[bi % 2]
                h_out, r_hout = halo[(bi + 1) % 2]
                for fc0 in range(0, NFC, 2):
                    w_s, r_w = wup[wu_i % 2]
                    wu_i += 1
                    dma(f"wup{(wu_i - 1) % 2}", w_s[:, :, 0, :], WUP_d[:, :, fc0 * 128:fc0 * 128 + 256], [r_WUP], [r_w])
                    if not last_is_halo:
                        dma(f"wup{(wu_i - 1) % 2}", w_s[:, :, 1, :], WUP_d[:, :, DFF + fc0 * 128:DFF + fc0 * 128 + 256],
                            [r_WUP], [r_w])
                    for fi in range(2):
                        fc = fc0 + fi
                        ab, r_ab = bank(2 * fi)
                        for kc in range(8):
                            mm(ab[:, 0:N], w_s[:, kc, 0, fi * 128:(fi + 1) * 128], h2T[:, kc, 0:N], kc == 0, kc == 7,
                               [r_w, r_h2T], [r_ab])
                        acopy(aext[:, 2:2 + N], ab[:, 0:N], [r_ab], [r_aext])
                        tcopy("dve", aext[:, 0:2], h_in[:, fc, :], [r_hin], [r_aext])
                        tcopy("dve", h_out[:, fc, :], aext[:, N:N + 2], [r_aext], [r_hout])
                        if last_is_halo:
                            continue
                        gb_, r_gb = bank(2 * fi + 1)
                        for kc in range(8):
                            mm(gb_[:, 0:N], w_s[:, kc, 1, fi * 128:(fi + 1) * 128], h2T[:, kc, 0:N], kc == 0, kc == 7,
                               [r_w, r_h2T], [r_gb])
                        ts("dve", acc[:, 0:N], aext[:, 0:N], cfw[:, fc, 0:1], cfb[:, fc:fc + 1], ALU.mult, ALU.add,
                           [r_aext, r_cfw, r_cfb], [r_acc])
                        stt("dve", acc[:, 0:N], aext[:, 1:N + 1], cfw[:, fc, 1:2], acc[:, 0:N], ALU.mult, ALU.add,
                            [r_aext, r_cfw, r_acc], [r_acc])
                        stt("dve", acc[:, 0:N], aext[:, 2:N + 2], cfw[:, fc, 2:3], acc[:, 0:N], ALU.mult, ALU.add,
                            [r_aext, r_cfw, r_acc], [r_acc])
                        act(gel[:, 0:N], acc[:, 0:N], AF.Gelu_apprx_tanh, [r_acc], [r_gel])
                        tt("dve", uT[:, fc, 0:N], gel[:, 0:N], gb_[:, 0:N], ALU.mult, [r_gel, r_gb], [r_uT])
                if last_is_halo:
                    continue

                for fc0 in range(0, NFC, 2):
                    wd, r_wd = wdn[wd_i % 2]
                    wd_i += 1
                    dma(f"wdn{(wd_i - 1) % 2}", wd[:], WDN_d[:, fc0:fc0 + 2, :], [r_WDN], [r_wd])
                    for fi in range(2):
                        fc = fc0 + fi
                        for ti in range(nq):
                            for half in range(2):
                                bkx, r_bkx = bank(2 * ti + half)
                                mm(bkx[:, :], uT[:, fc, ti * 128:(ti + 1) * 128], wd[:, fi, half * 512:(half + 1) * 512],
                                   fc == 0, fc == NFC - 1, [r_uT, r_wd], [r_bkx])
                for ti in range(nq):
                    xd, r_xd0, r_xd1 = bank2(2 * ti)
                    tt("dve", x1b[:, ti, :].rearrange("p (a b) -> p a b", a=2), xd[:, :, :],
                       x1b[:, ti, :].rearrange("p (a b) -> p a b", a=2), ALU.add, [r_xd0, r_xd1, r_x1b], [r_x1b])
                def stage_PA(ti):
                    u = tiles[ti]
                    x2 = x1b[:, ti, :]
                    x2T, r_x2T = x2Ts[ti % 2]
                    pb, r_pb = pbs[ti % 2]
                    tcopy("act", x2bf[:], x2, [r_x1b], [r_x2bf])
                    transpose8(x2bf, r_x2bf, x2T[:], r_x2T, 8, bidx=6)
                    tok0 = (u - 1) * 128
                    dma("pf", pf[:], pT_d[:, :, tok0:tok0 + 128], [], [r_pf])
                    tcopy("dve", pb[:], pf[:], [r_pf], [r_pb])

                def stage_PB(ti):
                    nonlocal yo_i
                    u = tiles[ti]
                    x2 = x1b[:, ti, :]
                    r_x2 = r_x1b
                    x2T, r_x2T = x2Ts[ti % 2]
                    pb, r_pb = pbs[ti % 2]
                    tok0 = (u - 1) * 128
                    gp, r_gp0, r_gp1 = bank2(0)
                    pp, r_pp0, r_pp1 = bank2(2)
                    for half, (rg_, rp_) in enumerate(((r_gp0, r_pp0), (r_gp1, r_pp1))):
                        for kc in range(8):
                            mm(gp[:, half, :], x2T[:, kc, :], Wpg[:, kc, half * 512:(half + 1) * 512], kc == 0, kc == 7,
                               [r_x2T, r_Wpg], [rg_])
                        for c in range(2):
                            mm(pp[:, half, :], pb[:, c, :], Wpl[:, c, half * 512:(half + 1) * 512], c == 0, c == 1,
                               [r_pb, r_Wpl], [rp_])
                    act(gsb[:].rearrange("p (a b) -> p a b", a=2), gp[:, :, :], AF.Sigmoid, [r_gp0, r_gp1], [r_gsb])
                    tt("dve", gsb[:].rearrange("p (a b) -> p a b", a=2), pp[:, :, :],
                       gsb[:].rearrange("p (a b) -> p a b", a=2), ALU.mult, [r_pp0, r_pp1, r_gsb], [r_gsb])
                    tt("dve", x2, x2, gsb[:], ALU.add, [r_x2, r_gsb], [r_x2])
                    rstd = rms_stats(x2, D, r_x2)
                    y_t, r_yt = yo[yo_i % 2]
                    yo_i += 1
                    stt("dve", y_t[:], x2, rstd, gFb[:], ALU.mult, ALU.mult, [r_x2, r_ssq, r_gFb], [r_yt])
                    dma(f"x3t{(yo_i - 1) % 2}", y_d[tok0:tok0 + 128, :], y_t[:], [r_yt], [r_y])

                stage_PA(0)
                for ti in range(nq):
                    if ti + 1 < nq:
                        stage_PA(ti + 1)
                    stage_PB(ti)

            P.wait_all("sp", [r_y, yo[0][1], yo[1][1]])
            P.wait_all("pool", [r_y, yo[0][1], yo[1][1]])
            P.wait_all("act", [r_y, yo[0][1], yo[1][1]])
        es23.close()
        P.run()
    return nc, P


def host_constants(NTS):
    NT = NSLOT * NTS
    bf = ml_dtypes.bfloat16
    c = {}
    c["identb"] = np.eye(128, dtype=np.float32).astype(bf)
    s = np.arange(128)
    c["tri"] = (s[:, None] <= s[None, :]).astype(np.float32)
    c["ones"] = np.ones((128, 128), np.float32)
    sh = np.zeros((128, 7, 128), np.float32)
    for j in range(4):
        for t in range(128):
            sidx = t - 3 + j
            if 0 <= sidx < 128:
                sh[sidx, j, t] = 1.0
    for j in range(3):
        for t in range(128):
            sidx = 128 + t - 3 + j
            if 0 <= sidx < 128:
                sh[sidx, 4 + j, t] = 1.0
    c["shifts"] = sh.astype(bf)
    pat = np.zeros((128, 4, 512), np.float32)
    k = np.arange(128)
    q = np.arange(512)
    for o in range(4):
        pat[:, o, :] = (((128 * o + k) // 64)[:, None] <= (q // 64)[None, :]).astype(np.float32)
    c["pat"] = pat.astype(bf)
    inv_freq = (10000.0 ** (-np.arange(0, 64, 2, dtype=np.float32) / 64)).astype(np.float32)
    pos = (np.arange(NT)[None, :] * 128 + np.arange(128)[:, None]).astype(np.float32)
    ang = pos[:, :, None] * inv_freq[None, None, :]
    c["ropec"] = np.cos(ang).astype(np.float32)
    c["ropes"] = np.sin(ang).astype(np.float32)
    return c


_PROG_CACHE = {}


def make_in_maps(inputs, NTS, cores=None):
    SLOT = NTS * 128
    S = NSLOT * SLOT
    NT = NSLOT * NTS
    f32 = np.float32
    x = np.asarray(inputs["x"], f32).reshape(S, D)
    p = np.asarray(inputs["p"], f32).reshape(S, PLE)
    consts = host_constants(NTS)
    shared = {
        "w_in": np.ascontiguousarray(np.asarray(inputs["w_in"], f32)[0]),
        "w_ya": np.ascontiguousarray(np.asarray(inputs["w_ya"], f32)[0]),
        "w_yb": np.ascontiguousarray(np.asarray(inputs["w_yb"], f32)[0]),
        "w_o": np.ascontiguousarray(np.asarray(inputs["w_o"], f32)[0]),
        "w_up": np.ascontiguousarray(np.asarray(inputs["w_up"], f32)[0]),
        "w_down": np.ascontiguousarray(np.asarray(inputs["w_down"], f32)[0]),
        "w_ple": np.ascontiguousarray(np.asarray(inputs["w_ple"], f32)[0]),
        "w_pg": np.ascontiguousarray(np.asarray(inputs["w_pg"], f32)[0]),
        "g1": np.asarray(inputs["norm1_g"], f32).reshape(1, D),
        "g2": np.asarray(inputs["norm2_g"], f32).reshape(1, D),
        "gF": np.asarray(inputs["final_g"], f32).reshape(1, D),
        "gda": np.asarray(inputs["da_norm_g"], f32).reshape(1, 128),
        "gml": np.asarray(inputs["ml_norm_g"], f32).reshape(1, 128),
        "b_if": np.asarray(inputs["b_if"], f32).reshape(1, 8),
        "lamv": np.stack([np.asarray(inputs[k], f32).reshape(64) for k in ("lam_q1", "lam_k1", "lam_q2", "lam_k2")]).reshape(1, 256),
        "conv_m_w": np.ascontiguousarray(np.asarray(inputs["conv_m_w"], f32)[0]).reshape(1, 4 * D),
        "cfw": np.ascontiguousarray(np.asarray(inputs["conv_f_w"], f32)[0].T.reshape(NFC, 128, 3).transpose(1, 0, 2)),
        "cfb": np.ascontiguousarray(np.asarray(inputs["conv_f_b"], f32)[0].reshape(NFC, 128).T),
    }
    shared.update(consts)
    in_maps = []
    for r in (range(NCORES) if cores is None else cores):
        xa = np.zeros((NT * 128, D), f32)
        lo = (r - (NSLOT - 1)) * SLOT
        hi = (r + 1) * SLOT
        src_lo = max(lo, 0)
        xa[src_lo - lo:, :] = x[src_lo:hi]
        pT = np.ascontiguousarray(p[r * SLOT:(r + 1) * SLOT].T.reshape(2, 128, SLOT).transpose(1, 0, 2))
        biast = np.zeros((128, NSLOT), f32)
        for s_ in range(NSLOT):
            if s_ < NSLOT - 1 - r:
                biast[:, s_] = NEG
        m = dict(shared)
        m["xa"] = xa
        m["pT"] = pT
        m["biast"] = biast
        in_maps.append(m)
    return in_maps


def kernel(**inputs):
    S = int(np.asarray(inputs["x"]).shape[1])
    NTS = S // (NSLOT * 128)
    if NTS not in _PROG_CACHE:
        _PROG_CACHE[NTS] = build_program(NTS)[0]
    nc = _PROG_CACHE[NTS]
    in_maps = make_in_maps(inputs, NTS)
    res = run_bass_kernel_spmd(nc, in_maps, core_ids=list(range(NCORES)))
    out = np.concatenate([np.asarray(r["y"], np.float32) for r in res.results], axis=0)
    return out.reshape(1, S, D)
```

```python
import math
from contextlib import ExitStack

import numpy as np
import ml_dtypes

import concourse.bass as bass
import concourse.mybir as mybir
from concourse.bass_utils import run_bass_kernel_spmd

F32 = mybir.dt.float32
BF16 = mybir.dt.bfloat16
AF = mybir.ActivationFunctionType
ALU = mybir.AluOpType

NCORES = 8
NSLOT = 8
D = 1024
NIN = 5640
DFF = 2816
NFC = DFF // 128
PLE = 256
H = 4
EPS = 1e-6
LAM_INIT = 0.8 - 0.6 * math.exp(-0.3 * 0)
QA, KA, VA, QM, KM, VM, OM, IFG, GA, GB = 0, 512, 1024, 1536, 2048, 2560, 3072, 3584, 3592, 4616
NEG = -30000.0


class Res:
    __slots__ = ("name", "w", "r", "ro")

    def __init__(self, name, ro=False):
        self.name = name
        self.w = None
        self.r = []
        self.ro = ro


class Prog:
    ENGS = ("pe", "act", "dve", "pool", "sp")

    def __init__(self, nc):
        self.nc = nc
        self.q = {e: [] for e in self.ENGS}
        self.sem = {e: nc.alloc_semaphore(f"c_{e}") for e in self.ENGS}
        self.cnt = {e: 0 for e in self.ENGS}
        self.seen = {e: {} for e in self.ENGS}
        self.dsem = {}
        self.dknow = {}
        self.hist = {e: {} for e in self.ENGS}
        self.n_wait = 0
        self.n_ins = 0

    def _merge(self, eng, know):
        se = self.seen[eng]
        for k, v in know.items():
            if se.get(k, 0) < v:
                se[k] = v

    def _need(self, eng, toks):
        se = self.seen[eng]
        cand = []
        for t in toks:
            if t is None:
                continue
            kind, key, val = t
            if kind == "d":
                val = self.dsem[key][1]
            cand.append((kind, key, val))
        cand.sort(key=lambda t: (t[0] == "c" and t[1] == eng, -t[2]))
        out = []
        for kind, key, val in cand:
            k = (kind, key)
            if se.get(k, 0) >= val:
                continue
            if kind == "c":
                sem = self.sem[key]
                know = self.hist[key].get(val)
            else:
                sem = self.dsem[key][0]
                know = self.dknow.get(key)
            se[k] = val
            if know:
                self._merge(eng, know)
            out.append((sem, val))
        return out

    def _deps(self, eng, reads, writes):
        toks = []
        for r in reads:
            toks.append(r.w)
        for w in writes:
            if not (eng == "pe" and w.w is not None and w.w[0] == "c" and w.w[1] == "pe"):
                toks.append(w.w)
            toks.extend(w.r)
        return toks

    def _record(self, tok, reads, writes):
        for r in reads:
            if not r.ro:
                r.r.append(tok)
        for w in writes:
            w.w = tok
            w.r = []

    def op(self, eng, fn, reads=(), writes=()):
        waits = self._need(eng, self._deps(eng, reads, writes))
        self.cnt[eng] += 1
        n = self.cnt[eng]
        tok = ("c", eng, n)
        snap = dict(self.seen[eng])
        snap[("c", eng)] = n
        self.hist[eng][n] = snap
        self.q[eng].append((waits, fn, self.sem[eng], 1))
        self.n_wait += len(waits)
        self.n_ins += 1
        self._record(tok, reads, writes)
        return tok

    def dma(self, eng, semname, fn, reads=(), writes=()):
        if semname not in self.dsem:
            self.dsem[semname] = [self.nc.alloc_semaphore(f"d_{semname}"), 0]
            self.dknow[semname] = {}
        waits = self._need(eng, self._deps(eng, reads, writes))
        ent = self.dsem[semname]
        ent[1] += 16
        tok = ("d", semname, ent[1])
        dk = self.dknow[semname]
        for k, v in self.seen[eng].items():
            if dk.get(k, 0) < v:
                dk[k] = v
        self.q[eng].append((waits, fn, ent[0], 16))
        self.n_wait += len(waits)
        self.n_ins += 1
        self._record(tok, reads, writes)
        return tok

    def wait_all(self, eng, ress):
        toks = []
        for r in ress:
            toks.append(r.w)
            toks.extend(r.r)
        waits = self._need(eng, toks)
        self.q[eng].append((waits, None, None, 0))

    def barrier(self):
        for eng in self.ENGS:
            toks = [("c", e2, self.cnt[e2]) for e2 in self.ENGS if self.cnt[e2] > 0]
            toks += [("d", name, ent[1]) for name, ent in self.dsem.items() if ent[1] > 0]
            waits = self._need(eng, toks)
            self.q[eng].append((waits, None, None, 0))

    def run(self):
        q = self.q

        def play(engobj, items):
            for waits, fn, sem, inc in items:
                if fn is None:
                    for (s, v) in waits:
                        engobj.wait_ge(s, v)
                    continue
                for (s, v) in waits[:-1]:
                    engobj.wait_ge(s, v)
                ins = fn(engobj)
                if waits:
                    ins._wait_ge(*waits[-1])
                if sem is not None:
                    ins.then_inc(sem, inc)

        with self.nc.Block() as block:
            @block.tensor
            def _(e):
                play(e, q["pe"])

            @block.scalar
            def _(e):
                play(e, q["act"])

            @block.vector
            def _(e):
                play(e, q["dve"])

            @block.gpsimd
            def _(e):
                play(e, q["pool"])

            @block.sync
            def _(e):
                play(e, q["sp"])


def build_program(NTS):
    NT = NSLOT * NTS
    NOWN = NTS + 1
    G0 = NT - NOWN
    SLOT = NTS * 128
    BT = min(4, NTS)
    blocks = [[0]] + [list(range(1 + b, 1 + b + BT)) for b in range(0, NTS, BT)]
    NOWNTOK = NOWN * 128

    nc = bass.Bass("TRN2", target_bir_lowering=False)
    P = Prog(nc)

    def din(name, shape, dt=F32):
        return nc.dram_tensor(name, list(shape), dt, kind="ExternalInput").ap()

    def dscr(name, shape, dt):
        return nc.dram_tensor(name, list(shape), dt).ap(), Res(name)

    xa = din("xa", [NT * 128, D])
    pT_d = din("pT", [128, 2, SLOT])
    w_in_d = din("w_in", [D, NIN])
    w_ya_d = din("w_ya", [512, D])
    w_yb_d = din("w_yb", [512, D])
    w_o_d = din("w_o", [D, D])
    w_up_d = din("w_up", [D, 2 * DFF])
    w_dn_d = din("w_down", [DFF, D])
    w_ple_d = din("w_ple", [PLE, D])
    w_pg_d = din("w_pg", [D, D])
    g1_d = din("g1", [1, D])
    g2_d = din("g2", [1, D])
    gF_d = din("gF", [1, D])
    gda_d = din("gda", [1, 128])
    gml_d = din("gml", [1, 128])
    bif_d = din("b_if", [1, 8])
    lam_d = din("lamv", [1, 256])
    cmw_d = din("conv_m_w", [1, 4 * D])
    cfw_d = din("cfw", [128, NFC, 3])
    cfb_d = din("cfb", [128, NFC])
    identb_d = din("identb", [128, 128], BF16)
    tri_d = din("tri", [128, 128])
    ones_d = din("ones", [128, 128])
    shifts_d = din("shifts", [128, 7, 128], BF16)
    pat_d = din("pat", [128, 4, 512], BF16)
    ropec_d = din("ropec", [128, NT, 32])
    ropes_d = din("ropes", [128, NT, 32])
    biast_d = din("biast", [128, NSLOT])
    y_d = nc.dram_tensor("y", [SLOT, D], F32, kind="ExternalOutput").ap()
    r_y = Res("y")

    KT_d, r_KT = dscr("KT", [H, 128, NT * 128], BF16)
    VV_d, r_VV = dscr("VV", [H, 128, NT, 130], BF16)
    QT_d, r_QT = dscr("QT", [H, 128, NOWNTOK], BF16)
    SG_d, r_SG = dscr("SG", [NOWNTOK, 2 * D], BF16)
    HMT_d, r_HMT = dscr("HMT", [128, H, NOWNTOK], BF16)
    AOT_d, r_AOT = dscr("AOT", [128, H, NOWNTOK], BF16)
    WUP_d, r_WUP = dscr("WUPb", [128, 8, 2 * DFF], BF16)
    WDN_d, r_WDN = dscr("WDNb", [128, NFC, D], BF16)
    WYA_d, r_WYA = dscr("WYAb", [128, 4, D], BF16)
    WYB_d, r_WYB = dscr("WYBb", [128, 4, D], BF16)
    WO_d, r_WO = dscr("WOb", [128, 8, D], BF16)
    WPG_d, r_WPG = dscr("WPGb", [128, 8, D], BF16)
    WPL_d, r_WPL = dscr("WPLb", [128, 2, D], BF16)

    def mm(out, lhsT, rhs, start, stop, reads, writes, skip=False):
        if skip:
            P.op("pe", lambda e: e.matmul(out, lhsT=lhsT, rhs=rhs, start=start, stop=stop, skip_group_check=True),
                 reads, writes)
        else:
            P.op("pe", lambda e: e.matmul(out, lhsT=lhsT, rhs=rhs, start=start, stop=stop), reads, writes)

    def tr(out, in_, ident, reads, writes):
        P.op("pe", lambda e: e.transpose(out=out, in_=in_, identity=ident), reads, writes)

    def act(out, in_, func, reads, writes, bias=0.0, scale=1.0, accum_out=None, eng="act"):
        if accum_out is None:
            P.op(eng, lambda e: e.activation(out=out, in_=in_, func=func, bias=bias, scale=scale), reads, writes)
        else:
            P.op(eng, lambda e: e.activation(out=out, in_=in_, func=func, bias=bias, scale=scale,
                                             accum_out=accum_out), reads, writes)

    def acopy(out, in_, reads, writes):
        P.op("act", lambda e: e.copy(out=out, in_=in_), reads, writes)

    def tcopy(eng, out, in_, reads, writes):
        if eng == "act":
            P.op(eng, lambda e: e.copy(out=out, in_=in_), reads, writes)
        else:
            P.op(eng, lambda e: e.tensor_copy(out=out, in_=in_), reads, writes)

    def tt(eng, out, in0, in1, op, reads, writes):
        P.op(eng, lambda e: e.tensor_tensor(out=out, in0=in0, in1=in1, op=op), reads, writes)

    def ts(eng, out, in0, s1, s2, op0, op1, reads, writes):
        if s2 is None:
            P.op(eng, lambda e: e.tensor_scalar(out=out, in0=in0, scalar1=s1, scalar2=None, op0=op0), reads, writes)
        else:
            P.op(eng, lambda e: e.tensor_scalar(out=out, in0=in0, scalar1=s1, scalar2=s2, op0=op0, op1=op1),
                 reads, writes)

    def stt(eng, out, in0, scalar, in1, op0, op1, reads, writes):
        P.op(eng, lambda e: e.scalar_tensor_tensor(out=out, in0=in0, scalar=scalar, in1=in1, op0=op0, op1=op1),
             reads, writes)

    def recip(out, in_, reads, writes):
        P.op("dve", lambda e: e.reciprocal(out=out, in_=in_), reads, writes)

    def memset(eng, ap, val, writes):
        P.op(eng, lambda e: e.memset(ap, val), (), writes)

    def dma(semname, out, in_, reads, writes, eng="sp"):
        P.dma(eng, semname, lambda e: e.dma_start(out=out, in_=in_), reads, writes)

    with ExitStack() as es_top:
        def sbuf(es, name, shape, dt=F32, ro=False):
            t = es.enter_context(nc.sbuf_tensor("s_" + name, list(shape), dt))
            return t, Res(name, ro=ro)

        banks = []
        for i in range(4):
            t = es_top.enter_context(nc.psum_tensor(f"pb{i}", [128, 2, 512], F32))
            banks.append((t, Res(f"pb{i}a")))
            banks.append((t, Res(f"pb{i}b")))

        def bank(i):
            t, r = banks[i]
            return t[:, i % 2, :], r

        def bank2(i):
            t, ra = banks[i]
            _, rb = banks[i + 1]
            return t, ra, rb

        esg = es_top
        identb, r_identb = sbuf(esg, "identb", [128, 128], BF16, ro=True)
        tri, r_tri = sbuf(esg, "tri", [128, 128], F32, ro=True)
        ones, r_ones = sbuf(esg, "ones", [128, 128], F32, ro=True)
        gFb, r_gFb = sbuf(esg, "gFb", [128, D], F32, ro=True)
        lam_t, r_lam = sbuf(esg, "lam_t", [128, 4, 64], F32)
        lam_s, r_lams = sbuf(esg, "lam_s", [128, 4], F32)
        nlam, r_nlam = sbuf(esg, "nlam", [128, 1], F32, ro=True)
        ssq, r_ssq = sbuf(esg, "ssq", [128, 8], F32)
        junk, r_junk = sbuf(esg, "junk", [128, D], F32)

        dma("identb", identb[:], identb_d, [], [r_identb])
        dma("tri", tri[:], tri_d, [], [r_tri])
        dma("ones", ones[:], ones_d, [], [r_ones])
        dma("gFb", gFb[:], gF_d.partition_broadcast(128), [], [r_gFb])
        dma("lam_t", lam_t[:].rearrange("p a b -> p (a b)"),
            lam_d.partition_broadcast(128), [], [r_lam])
        for i in range(2):
            tt("dve", lam_t[:, 2 * i, :], lam_t[:, 2 * i, :], lam_t[:, 2 * i + 1, :], ALU.mult, [r_lam], [r_lam])
            act(lam_t[:, 2 * i + 1, :], lam_t[:, 2 * i, :], AF.Copy, [r_lam], [r_lam, r_lams],
                accum_out=lam_s[:, i:i + 1])
        act(lam_s[:, 2:4], lam_s[:, 0:2], AF.Exp, [r_lams], [r_lams])
        tt("dve", lam_s[:, 0:1], lam_s[:, 3:4], lam_s[:, 2:3], ALU.subtract, [r_lams], [r_lams])
        ts("dve", nlam[:], lam_s[:, 0:1], -LAM_INIT, None, ALU.add, None, [r_lams], [r_nlam])

        def rms_stats(src_ap, n, r_src, col=0):
            act(junk[:, 0:n], src_ap, AF.Square, [r_src], [r_junk, r_ssq], accum_out=ssq[:, col:col + 1])
            act(ssq[:, col:col + 1], ssq[:, col:col + 1], AF.Sqrt, [r_ssq], [r_ssq], bias=EPS, scale=1.0 / n)
            recip(ssq[:, col:col + 1], ssq[:, col:col + 1], [r_ssq], [r_ssq])
            return ssq[:, col:col + 1]

        def transpose8(src_bf, r_src, dst_ap, r_dst, nblk, bidx=0):
            bk, r_bk = bank(bidx)
            bb = bk.bitcast(BF16)
            for k in range(nblk):
                tr(bb[:, k * 128:(k + 1) * 128], src_bf[:, k * 128:(k + 1) * 128], identb[:],
                   [r_src, r_identb], [r_bk])
            acopy(dst_ap, bb[:, 0:nblk * 128].rearrange("p (a b) -> p a b", a=nblk), [r_bk], [r_dst])

        with ExitStack() as es1:
            Wb, r_Wb = sbuf(es1, "Wb", [128, 8, NIN], BF16, ro=True)
            g1b, r_g1b = sbuf(es1, "g1b", [128, D], F32, ro=True)
            cmwb, r_cmwb = sbuf(es1, "cmwb", [128, 4, D], F32, ro=True)
            bifb, r_bifb = sbuf(es1, "bifb", [128, 8], F32, ro=True)
            gmlb, r_gmlb = sbuf(es1, "gmlb", [128, 128], F32, ro=True)
            shifts, r_shifts = sbuf(es1, "shifts", [128, 7, 128], BF16, ro=True)
            CU = 1024
            stage = [sbuf(es1, f"stage{i}", [128, CU], F32) for i in range(2)]
            stageb = [sbuf(es1, f"stageb{i}", [128, CU], BF16) for i in range(2)]
            NXT = 3
            xt = [sbuf(es1, f"xt{i}", [128, D], F32) for i in range(NXT)]
            rc = [sbuf(es1, f"rc{i}", [128, 32], F32) for i in range(NXT)]
            rs = [sbuf(es1, f"rs{i}", [128, 32], F32) for i in range(NXT)]
            st1 = [sbuf(es1, f"st1_{i}", [128, 2], F32) for i in range(NXT)]
            hbs = [sbuf(es1, f"hb{i}", [128, D], BF16) for i in range(2)]
            hTs = [sbuf(es1, f"hT{i}", [128, 8, 128], BF16) for i in range(2)]
            ra, r_ra = sbuf(es1, "ra", [128, 512], F32)
            rb1, r_rb1 = sbuf(es1, "rb1", [128, 8, 32], F32)
            rb2, r_rb2 = sbuf(es1, "rb2", [128, 8, 32], F32)
            krot, r_krot = sbuf(es1, "krot", [128, 512], BF16)
            ktT, r_ktT = sbuf(es1, "ktT", [128, 4, 128], BF16)
            vaug = [sbuf(es1, f"vaug{i}", [128, 4, 130], BF16) for i in range(2)]
            xwk = [(sbuf(es1, f"xwk{i}", [128, 4, 512], BF16)[0], [Res(f"xwk{i}_{j}") for j in range(4)]) for i in range(2)]
            xwq = [(sbuf(es1, f"xwq{i}", [128, 4, 512], BF16)[0], [Res(f"xwq{i}_{j}") for j in range(4)]) for i in range(2)]
            sig, r_sig = sbuf(es1, "sig", [128, 512], F32)
            kmtoks = [sbuf(es1, f"kmtok{i}", [128, 512], BF16) for i in range(2)]
            qmtok, r_qmtok = sbuf(es1, "qmtok", [128, 512], BF16)
            kmT, r_kmT = sbuf(es1, "kmT", [128, 4, 128], BF16)
            qmT, r_qmT = sbuf(es1, "qmT", [128, 4, 128], BF16)
            gt, r_gt = sbuf(es1, "gt", [128, 8], F32)
            gs, r_gs = sbuf(es1, "gs", [128, 24], F32)
            vts = [sbuf(es1, f"vt{i}", [128, 4, 129], BF16) for i in range(2)]
            Cst, r_C = sbuf(es1, "Cst", [128, 4, 129], F32)
            Chat, r_Chat = sbuf(es1, "Chat", [128, 4, 129], F32)
            Chb, r_Chb = sbuf(es1, "Chb", [128, 4, 129], BF16)
            ATs, r_ATs = sbuf(es1, "ATs", [128, 4, 128], BF16)
            hmf, r_hmf = sbuf(es1, "hmf", [128, 4, 128], F32)
            hmbs = [sbuf(es1, f"hmb{i}", [128, 512], BF16) for i in range(2)]
            hmTs, r_hmTs = sbuf(es1, "hmTs", [128, 4, 128], BF16)
            oms, r_oms = sbuf(es1, "oms", [128, 512], F32)
            sgs, r_sgs = sbuf(es1, "sgs", [128, 2 * D], BF16)
            mst, r_mst = sbuf(es1, "mst", [128, 4, 8], F32)
            rr, r_rr = sbuf(es1, "rr", [128, 8], F32)

            dma("g1b", g1b[:], g1_d.partition_broadcast(128), [], [r_g1b])
            dma("cmwb", cmwb[:].rearrange("p a b -> p (a b)"), cmw_d.partition_broadcast(128), [], [r_cmwb])
            dma("bifb", bifb[:], bif_d.partition_broadcast(128), [], [r_bifb])
            dma("gmlb", gmlb[:], gml_d.partition_broadcast(128), [], [r_gmlb])
            dma("shifts", shifts[:], shifts_d, [], [r_shifts])
            memset("pool", Cst[:], 0.0, [r_C])
            for i in range(2):
                memset("pool", xwk[i][0][:], 0.0, xwk[i][1])
                memset("pool", xwq[i][0][:], 0.0, xwq[i][1])
                memset("pool", vaug[i][0][:, :, 128:129], 1.0, [vaug[i][1]])
                memset("pool", vaug[i][0][:, :, 129:130], 0.0, [vaug[i][1]])

            cast_i = [0]

            def cast_unit(src_ap, ncols_free, dst_kind, dst_ap, r_dst, ceng="pool", dq="sp"):
                i = cast_i[0] % 2
                cast_i[0] += 1
                st, r_st = stage[i]
                a, b = src_ap.shape[1], src_ap.shape[2]
                stv = st[:, 0:ncols_free].rearrange("p (a b) -> p a b", a=a)
                dma(f"{dq}stage{i}", stv, src_ap, [], [r_st], eng=dq)
                if dst_kind == "sbuf":
                    tcopy(ceng, dst_ap, stv, [r_st], [r_dst])
                else:
                    sbt, r_sbt = stageb[i]
                    sbv = sbt[:, 0:ncols_free].rearrange("p (a b) -> p a b", a=a)
                    tcopy(ceng, sbv, stv, [r_st], [r_sbt])
                    dma(f"{dq}stageb{i}", dst_ap, sbv, [r_sbt], [r_dst], eng=dq)

            w_in_v = w_in_d.rearrange("(k p) n -> p k n", p=128)
            GW = CU // 8
            wgrp = {}
            col_groups = []
            for c0 in (KA, VA, IFG, KM, VM, QA, QM, OM, GA, GA + 512, GB, GB + 512):
                n_tot = 8 if c0 == IFG else 512
                for cc in range(c0, c0 + n_tot, GW):
                    col_groups.append((cc, min(GW, c0 + n_tot - cc)))
            engs = ["pool", "dve", "act"]
            late_units = []
            n_early = 4 * 4 + 1
            for gi, (c0, n) in enumerate(col_groups):
                wgrp[c0] = (Res(f"Wb{c0}", ro=True), n)
                if gi < n_early:
                    cast_unit(w_in_v[:, :, c0:c0 + n], 8 * n, "sbuf", Wb[:, :, c0:c0 + n], wgrp[c0][0],
                              ceng=engs[gi % 3])
                else:
                    late_units.append((w_in_v[:, :, c0:c0 + n], 8 * n, Wb[:, :, c0:c0 + n], wgrp[c0][0], "sbuf"))

            def wres(c0, n):
                return [r for c, (r, m) in wgrp.items() if c < c0 + n and c + m > c0]

            prep = list(late_units)
            w_up_v = w_up_d.rearrange("(k p) n -> p k n", p=128)
            for c0 in range(0, 2 * DFF, GW):
                prep.append((w_up_v[:, :, c0:c0 + GW], CU, WUP_d[:, :, c0:c0 + GW], r_WUP))
            w_dn_v = w_dn_d.rearrange("(k p) n -> p k n", p=128)
            for k0 in range(NFC):
                prep.append((w_dn_v[:, k0:k0 + 1, :], D, WDN_d[:, k0:k0 + 1, :], r_WDN))
            w_ya_v = w_ya_d.rearrange("(k p) n -> p k n", p=128)
            w_yb_v = w_yb_d.rearrange("(k p) n -> p k n", p=128)
            for k0 in range(4):
                prep.append((w_ya_v[:, k0:k0 + 1, :], D, WYA_d[:, k0:k0 + 1, :], r_WYA))
                prep.append((w_yb_v[:, k0:k0 + 1, :], D, WYB_d[:, k0:k0 + 1, :], r_WYB))
            w_o_v = w_o_d.rearrange("(k p) n -> p k n", p=128)
            w_pg_v = w_pg_d.rearrange("(k p) n -> p k n", p=128)
            for k0 in range(8):
                prep.append((w_o_v[:, k0:k0 + 1, :], D, WO_d[:, k0:k0 + 1, :], r_WO))
                prep.append((w_pg_v[:, k0:k0 + 1, :], D, WPG_d[:, k0:k0 + 1, :], r_WPG))
            w_pl_v = w_ple_d.rearrange("(k p) n -> p k n", p=128)
            for k0 in range(2):
                prep.append((w_pl_v[:, k0:k0 + 1, :], D, WPL_d[:, k0:k0 + 1, :], r_WPL))
            prep_per_tile = -(-len(prep) // max(1, NT - NOWN))
            prep_pos = [0]

            def emit_prep(k):
                for _ in range(k):
                    if prep_pos[0] < len(prep):
                        pr = prep[prep_pos[0]]
                        prep_pos[0] += 1
                        cast_unit(pr[0], pr[1], pr[4] if len(pr) > 4 else "dram", pr[2], pr[3], dq="pool")

            def proj(hT_t, r_hT_, c0, n, bidx, col0=0):
                bk, r_bk = bank(bidx)
                for kc in range(8):
                    mm(bk[:, col0:col0 + n], hT_t[:, kc, :], Wb[:, kc, c0:c0 + n], kc == 0, kc == 7,
                       [r_hT_] + wres(c0, n), [r_bk])
                return bk, r_bk

            def rope_tm(bk, r_bk, cosap, sinap, r_c, r_s, out_bf, r_out):
                v4 = lambda ap: ap.rearrange("p (a b c) -> p a b c", a=8, b=2, c=32)
                kv = v4(bk[:, 0:512])
                av = v4(ra[:, :])
                ov = v4(out_bf[:, :])
                cb4 = cosap.unsqueeze(1).unsqueeze(1).to_broadcast([128, 8, 2, 32])
                sb3 = sinap.unsqueeze(1).to_broadcast([128, 8, 32])
                tt("dve", av, kv, cb4, ALU.mult, [r_bk, r_c], [r_ra])
                tt("dve", rb1[:], kv[:, :, 1, :], sb3, ALU.mult, [r_bk, r_s], [r_rb1])
                tt("dve", rb2[:], kv[:, :, 0, :], sb3, ALU.mult, [r_bk, r_s], [r_rb2])
                tt("dve", ov[:, :, 0, :], av[:, :, 0, :], rb1[:], ALU.subtract, [r_ra, r_rb1], [r_out])
                tt("dve", ov[:, :, 1, :], av[:, :, 1, :], rb2[:], ALU.add, [r_ra, r_rb2], [r_out])

            def conv_premul(bk, r_bk, wcol0, xw_cur):
                xc, r_xc = xw_cur
                for j in range(4):
                    tt("dve", xc[:, j, :], bk[:, 0:512], cmwb[:, j, wcol0:wcol0 + 512], ALU.mult,
                       [r_bk, r_cmwb], [r_xc[j]])

            def conv_mm(xw_cur, xw_prev, bidx_out):
                xc, r_xc = xw_cur
                xp, r_xp = xw_prev
                ob, r_ob = bank(bidx_out)
                for j in range(4):
                    mm(ob[:, 0:512], shifts[:, j, :], xc[:, j, :], j == 0, False, [r_shifts, r_xc[j]], [r_ob])
                for j in range(3):
                    mm(ob[:, 0:512], shifts[:, 4 + j, :], xp[:, j, :], False, j == 2, [r_shifts, r_xp[j]], [r_ob])
                return ob, r_ob

            def silu_tanh(cb, r_cb, out_bf, r_out, c):
                act(sig[:], cb[:, 0:512], AF.Exp, [r_cb], [r_sig], scale=-1.0 / c)
                act(sig[:], sig[:], AF.Ln, [r_sig], [r_sig], bias=1.0)
                act(sig[:], sig[:], AF.Exp, [r_sig], [r_sig], scale=-1.0)
                tt("dve", out_bf[:], cb[:, 0:512], sig[:], ALU.mult, [r_cb, r_sig], [r_out])

            km_scale = 128.0 ** -0.5
            CK = km_scale
            CQ = 1.0
            ts("dve", cmwb[:, :, 512:1024], cmwb[:, :, 512:1024], CK, None, ALU.mult, None, [r_cmwb], [r_cmwb])

            def stage_A1(g):
                x_t, r_x = xt[g % NXT]
                c_t, r_c = rc[g % NXT]
                s_t, r_s = rs[g % NXT]
                s1, r_s1 = st1[g % NXT]
                hb, r_hb = hbs[g % 2]
                dma(f"xt{g % NXT}", x_t[:], xa[g * 128:(g + 1) * 128, :], [], [r_x])
                dma(f"rc{g % NXT}", c_t[:], ropec_d[:, g, :], [], [r_c])
                dma(f"rs{g % NXT}", s_t[:], ropes_d[:, g, :], [], [r_s])
                act(junk[:, 0:D], x_t[:], AF.Square, [r_x], [r_junk, r_s1], accum_out=s1[:, 0:1])
                act(s1[:, 1:2], s1[:, 0:1], AF.Ln, [r_s1], [r_s1], bias=EPS, scale=1.0 / D)
                act(s1[:, 1:2], s1[:, 1:2], AF.Exp, [r_s1], [r_s1], scale=-0.5)
                stt("dve", hb[:], x_t[:], s1[:, 1:2], g1b[:], ALU.mult, ALU.mult, [r_x, r_s1, r_g1b], [r_hb])

            def stage_A2(g):
                hb, r_hb = hbs[g % 2]
                hT, r_hT = hTs[g % 2]
                transpose8(hb, r_hb, hT[:], r_hT, 8, bidx=0)

            def stage_D(g):
                u = g - G0
                hmb, r_hmb = hmbs[g % 2]
                transpose8(hmb, r_hmb, hmTs[:], r_hmTs, 4, bidx=0)
                dma("hmTs", HMT_d[:, :, u * 128:(u + 1) * 128], hmTs[:], [r_hmTs], [r_HMT])

            def stage_C(g):
                kmtok, r_kmtok = kmtoks[g % 2]
                vt, r_vt = vts[g % 2]
                for hp in range(2):
                    cbk, r_cbk = bank(5)
                    for hh in range(2):
                        h = 2 * hp + hh
                        mm(cbk[:, hh * 129:(hh + 1) * 129], kmtok[:, h * 128:(h + 1) * 128], vt[:, h, :], True, True,
                           [r_kmtok, r_vt], [r_cbk])
                    tt("dve", Cst[:, 2 * hp:2 * hp + 2, :], Chat[:, 2 * hp:2 * hp + 2, :],
                       cbk[:, 0:258].rearrange("p (a b) -> p a b", a=2), ALU.add, [r_Chat, r_cbk], [r_C])

            def stage_B(g):
                own = g >= G0
                u = g - G0
                hT, r_hT = hTs[g % 2]
                kmtok, r_kmtok = kmtoks[g % 2]
                vt, r_vt = vts[g % 2]
                c_t, r_c = rc[g % NXT]
                s_t, r_s = rs[g % NXT]
                bkG, r_bkG = proj(hT, r_hT, IFG, 8, 4, col0=16)
                tt("dve", gt[:], bkG[:, 16:24], bifb[:], ALU.add, [r_bkG, r_bifb], [r_gt])
                act(gs[:, 0:4], gt[:, 4:8], AF.Exp, [r_gt], [r_gs], scale=-1.0)
                act(gs[:, 0:4], gs[:, 0:4], AF.Ln, [r_gs], [r_gs], bias=1.0)
                bkM, r_bkM = proj(hT, r_hT, KM, 512, 3)
                bkK, r_bkK = proj(hT, r_hT, KA, 512, 1)
                bkV, r_bkV = proj(hT, r_hT, VA, 512, 2)
                bkW, r_bkW = proj(hT, r_hT, VM, 512, 6)
                if g - 1 >= G0:
                    stage_D(g - 1)
                conv_premul(bkM, r_bkM, 512, xwk[g % 2])
                rope_tm(bkK, r_bkK, c_t[:, :], s_t[:, :], r_c, r_s, krot, r_krot)
                va, r_va = vaug[g % 2]
                acopy(va[:, :, 0:128], bkV[:, 0:512].rearrange("p (a b) -> p a b", a=4), [r_bkV], [r_va])
                dma(f"vaug{g % 2}", VV_d[:, :, g, :].rearrange("h p c -> p h c"), va[:], [r_va], [r_VV])
                bk4, r_bk4 = bank(4)
                mm(bk4[:, 0:4], tri[:], gs[:, 0:4], True, True, [r_tri, r_gs], [r_bk4])
                mm(bk4[:, 8:12], ones[:], gs[:, 0:4], True, True, [r_ones, r_gs], [r_bk4])
                cb, r_cb = conv_mm(xwk[g % 2], xwk[(g + 1) % 2], 7)
                transpose8(krot, r_krot, ktT[:], r_ktT, 4, bidx=0)
                dma("ktT", KT_d[:, :, g * 128:(g + 1) * 128].rearrange("h p t -> p h t"), ktT[:], [r_ktT], [r_KT])
                if g > 0:
                    stage_C(g - 1)
                tt("dve", gs[:, 8:12], bk4[:, 0:4], gt[:, 0:4], ALU.add, [r_bk4, r_gt], [r_gs])
                tt("dve", gs[:, 8:12], gs[:, 8:12], bk4[:, 8:12], ALU.subtract, [r_gs, r_bk4], [r_gs])
                act(gs[:, 8:12], gs[:, 8:12], AF.Exp, [r_gs], [r_gs])
                act(gs[:, 12:16], bk4[:, 8:12], AF.Exp, [r_bk4], [r_gs], scale=-1.0)
                if own:
                    tcopy("dve", gs[:, 4:8], bk4[:, 8:12], [r_bk4], [r_gs])
                    tt("dve", gs[:, 16:20], gs[:, 4:8], bk4[:, 0:4], ALU.subtract, [r_gs, r_bk4], [r_gs])
                    act(gs[:, 16:20], gs[:, 16:20], AF.Exp, [r_gs], [r_gs])
                wk = gs[:, 8:12]
                dd = gs[:, 12:16]
                wp = gs[:, 16:20]
                silu_tanh(cb, r_cb, kmtok, r_kmtok, CK)
                tt("dve", vt[:, :, 0:128], bkW[:, 0:512].rearrange("p (a b) -> p a b", a=4),
                   wk.unsqueeze(2).to_broadcast([128, 4, 128]), ALU.mult, [r_bkW, r_gs], [r_vt])
                tcopy("dve", vt[:, :, 128:129], wk.unsqueeze(2), [r_gs], [r_vt])
                tt("dve", Chat[:], Cst[:], dd.unsqueeze(2).to_broadcast([128, 4, 129]), ALU.mult,
                   [r_C, r_gs], [r_Chat])

                if own:
                    hTo = hT
                    bkq, r_bkq = proj(hTo, r_hT, QA, 512, 1)
                    bkqm, r_bkqm = proj(hTo, r_hT, QM, 512, 2)
                    bkom, r_bkom = proj(hTo, r_hT, OM, 512, 3)
                    conv_premul(bkqm, r_bkqm, 0, xwq[g % 2])
                    rope_tm(bkq, r_bkq, c_t[:, :], s_t[:, :], r_c, r_s, krot, r_krot)
                    cbq, r_cbq = conv_mm(xwq[g % 2], xwq[(g + 1) % 2], 7)
                    transpose8(krot, r_krot, ktT[:], r_ktT, 4, bidx=0)
                    dma("ktT", QT_d[:, :, u * 128:(u + 1) * 128].rearrange("h p t -> p h t"), ktT[:],
                        [r_ktT], [r_QT])
                    bkga0, r_bkga0 = proj(hTo, r_hT, GA, 512, 1)
                    bkga1, r_bkga1 = proj(hTo, r_hT, GA + 512, 512, 2)
                    bkgb0, r_bkgb0 = proj(hTo, r_hT, GB, 512, 5)
                    silu_tanh(cbq, r_cbq, qmtok, r_qmtok, CQ)
                    act(oms[:], bkom[:, 0:512], AF.Sigmoid, [r_bkom], [r_oms])
                    act(sgs[:, 0:512], bkga0[:, 0:512], AF.Sigmoid, [r_bkga0], [r_sgs])
                    act(sgs[:, 512:1024], bkga1[:, 0:512], AF.Sigmoid, [r_bkga1], [r_sgs])
                    act(sgs[:, 1024:1536], bkgb0[:, 0:512], AF.Sigmoid, [r_bkgb0], [r_sgs])
                    transpose8(qmtok, r_qmtok, qmT[:], r_qmT, 4, bidx=0)
                    transpose8(kmtok, r_kmtok, kmT[:], r_kmT, 4, bidx=0)
                    bkgb1, r_bkgb1 = proj(hTo, r_hT, GB + 512, 512, 3)
                    act(sgs[:, 1536:2048], bkgb1[:, 0:512], AF.Sigmoid, [r_bkgb1], [r_sgs])
                    dma("sgs", SG_d[u * 128:(u + 1) * 128, :], sgs[:], [r_sgs], [r_SG])
                    bk7, r_bk7 = bank(7)
                    for h in range(H):
                        mm(bk7[:, h * 128:(h + 1) * 128], kmT[:, h, :], qmT[:, h, :], True, True,
                           [r_kmT, r_qmT], [r_bk7])
                    tt("dve", ATs[:], bk7[:, 0:512].rearrange("p (a b) -> p a b", a=4),
                       tri[:, :].unsqueeze(1).to_broadcast([128, 4, 128]), ALU.mult, [r_bk7, r_tri], [r_ATs])
                    acopy(Chb[:], Chat[:], [r_Chat], [r_Chb])
                    obs = [bank(6), bank(4)]
                    for hp in range(2):
                        ob, r_ob = obs[hp]
                        for hh in range(2):
                            h = 2 * hp + hh
                            oap = ob[:, hh * 129:(hh + 1) * 129]
                            mm(oap, qmT[:, h, :], Chb[:, h, :], True, False, [r_qmT, r_Chb], [r_ob])
                            mm(oap, ATs[:, h, :], vt[:, h, :], False, True, [r_ATs, r_vt], [r_ob])
                    for hp in range(2):
                        ob, r_ob = obs[hp]
                        for hh in range(2):
                            h = 2 * hp + hh
                            oap = ob[:, hh * 129:(hh + 1) * 129]
                            act(rr[:, 4 * hp + hh:4 * hp + hh + 1], oap[:, 128:129], AF.Abs, [r_ob, r_gs], [r_rr],
                                scale=wp[:, h:h + 1])
                    for hp in range(2):
                        ob, r_ob = obs[hp]
                        for hh in range(2):
                            h = 2 * hp + hh
                            oap = ob[:, hh * 129:(hh + 1) * 129]
                            c0 = 4 * hp + hh
                            ts("dve", rr[:, c0:c0 + 1], rr[:, c0:c0 + 1], 1.0, None, ALU.max, None, [r_rr], [r_rr])
                            recip(rr[:, c0:c0 + 1], rr[:, c0:c0 + 1], [r_rr], [r_rr])
                            tt("dve", rr[:, c0 + 2:c0 + 3], rr[:, c0:c0 + 1], wp[:, h:h + 1], ALU.mult, [r_rr, r_gs], [r_rr])
                            ts("dve", hmf[:, h, :], oap[:, 0:128], rr[:, c0 + 2:c0 + 3], None, ALU.mult, None,
                               [r_ob, r_rr], [r_hmf])
                    for h in range(H):
                        act(junk[:, 0:128], hmf[:, h, :], AF.Copy, [r_hmf], [r_junk, r_mst],
                            accum_out=mst[:, h, 0:1])
                        act(junk[:, 0:128], hmf[:, h, :], AF.Square, [r_hmf], [r_junk, r_mst],
                            accum_out=mst[:, h, 1:2])
                    ts("dve", mst[:, :, 2:3], mst[:, :, 0:1], 1.0 / 128, None, ALU.mult, None, [r_mst], [r_mst])
                    tt("dve", mst[:, :, 3:4], mst[:, :, 2:3], mst[:, :, 2:3], ALU.mult, [r_mst], [r_mst])
                    stt("dve", mst[:, :, 4:5], mst[:, :, 1:2], 1.0 / 128, mst[:, :, 3:4], ALU.mult, ALU.subtract,
                        [r_mst], [r_mst])
                    act(mst[:, :, 4:5], mst[:, :, 4:5], AF.Ln, [r_mst], [r_mst], bias=EPS)
                    act(mst[:, :, 5:6], mst[:, :, 4:5], AF.Exp, [r_mst], [r_mst], scale=-0.5)
                    for h in range(H):
                        ts("dve", hmf[:, h, :], hmf[:, h, :], mst[:, h, 2:3], mst[:, h, 5:6], ALU.subtract, ALU.mult,
                           [r_hmf, r_mst], [r_hmf])
                    tt("dve", hmf[:], hmf[:], gmlb[:, :].unsqueeze(1).to_broadcast([128, 4, 128]), ALU.mult,
                       [r_hmf, r_gmlb], [r_hmf])
                    hmb, r_hmb = hmbs[g % 2]
                    tt("dve", hmb[:], hmf[:].rearrange("p a b -> p (a b)"), oms[:], ALU.mult, [r_hmf, r_oms], [r_hmb])

            stage_A1(0)
            if NT > 1:
                stage_A1(1)
            stage_A2(0)
            for g in range(NT):
                if g + 2 < NT:
                    stage_A1(g + 2)
                if g + 1 < NT:
                    stage_A2(g + 1)
                if g < NT - NOWN:
                    target = -(-(g + 1) * len(prep) // (NT - NOWN))
                    emit_prep(max(0, target - prep_pos[0]))
                stage_B(g)
            stage_D(NT - 1)
            stage_C(NT - 1)
            emit_prep(len(prep))

        scale = 64.0 ** -0.5
        P.barrier()
        es23 = ExitStack()
        es23.__enter__()
        Wya, r_Wya = sbuf(es23, "Wya", [128, 4, D], BF16, ro=True)
        Wyb, r_Wyb = sbuf(es23, "Wyb", [128, 4, D], BF16, ro=True)
        Wo, r_Wo = sbuf(es23, "Wo", [128, 8, D], BF16, ro=True)
        Wpg, r_Wpg = sbuf(es23, "Wpg", [128, 8, D], BF16, ro=True)
        Wpl, r_Wpl = sbuf(es23, "Wpl", [128, 2, D], BF16, ro=True)
        g2b, r_g2b = sbuf(es23, "g2b", [128, D], F32, ro=True)
        cfw, r_cfw = sbuf(es23, "cfw", [128, NFC, 3], F32, ro=True)
        cfb, r_cfb = sbuf(es23, "cfb", [128, NFC], F32, ro=True)
        dma("Wya", Wya[:], WYA_d, [r_WYA], [r_Wya])
        dma("Wyb", Wyb[:], WYB_d, [r_WYB], [r_Wyb])
        dma("Wo", Wo[:], WO_d, [r_WO], [r_Wo])
        dma("Wpg", Wpg[:], WPG_d, [r_WPG], [r_Wpg])
        dma("Wpl", Wpl[:], WPL_d, [r_WPL], [r_Wpl])
        dma("g2b", g2b[:], g2_d.partition_broadcast(128), [], [r_g2b])
        dma("cfw", cfw[:], cfw_d, [], [r_cfw])
        dma("cfb", cfb[:], cfb_d, [], [r_cfb])
        with ExitStack() as es2:
            biast, r_biast = sbuf(es2, "biast", [128, NSLOT], F32, ro=True)
            pat, r_pat = sbuf(es2, "pat", [128, 4, 512], BF16, ro=True)
            gdab, r_gdab = sbuf(es2, "gdab", [128, 128], F32, ro=True)
            qt_s = [sbuf(es2, f"qts{i}", [128, 512], BF16) for i in range(2)]
            kg = [sbuf(es2, f"kg{i}", [128, NTS * 128], BF16) for i in range(2)]
            vg = [sbuf(es2, f"vg{i}", [128, NTS, 130], BF16) for i in range(2)]
            pts = [sbuf(es2, f"pts{i}", [128, 2, 512], BF16) for i in range(3)]
            aof, r_aof = sbuf(es2, "aof", [128, BT, 4, 128], F32)
            aob, r_aob = sbuf(es2, "aob", [128, 512], BF16)
            aoTs, r_aoTs = sbuf(es2, "aoTs", [128, 4, 128], BF16)
            tmpa, r_tmpa = sbuf(es2, "tmpa", [128, 128], F32)
            rl, r_rl = sbuf(es2, "rl", [128, 8], F32)
            dma("biast", biast[:], biast_d, [], [r_biast])
            dma("pat", pat[:], pat_d, [], [r_pat])
            dma("gdab", gdab[:], gda_d.partition_broadcast(128), [], [r_gdab])
            ts("dve", gdab[:], gdab[:], 1.0 - LAM_INIT, None, ALU.mult, None, [r_gdab], [r_gdab])

            S_t = [bank2(0), bank2(2)]
            O_bk = [[bank(4), bank(5)], [bank(6), bank(7)]]
            ld_i = 0
            pt_i = 0
            s_i = 0
            for bi, tiles in enumerate(blocks):
                nq = len(tiles)
                N = nq * 128
                t0 = tiles[0]
                klist = []
                for kt in range(NT):
                    ku = kt - G0
                    if bi == 0:
                        if kt < G0:
                            klist.append((kt, None))
                        elif kt == G0:
                            klist.append((kt, 0))
                    else:
                        if ku < t0:
                            klist.append((kt, None))
                        elif ku < t0 + nq:
                            klist.append((kt, ku - t0))
                groups = {}
                for kt, kind in klist:
                    groups.setdefault(kt // NTS, []).append((kt, kind))
                gkeys = sorted(groups)
                for h in range(H):
                    q_s, r_q = qt_s[(bi * H + h) % 2]
                    dma(f"qts{(bi * H + h) % 2}", q_s[:, 0:N], QT_d[h, :, t0 * 128:t0 * 128 + N], [r_QT], [r_q])
                    total_k = len(klist)
                    loaded = {}
                    seq = []
                    for gk in gkeys:
                        for (kt, kind) in groups[gk]:
                            seq.append((gk, kt, kind))

                    def ensure_group(gk):
                        nonlocal ld_i
                        if gk not in loaded:
                            k_s, r_k = kg[ld_i % 2]
                            v_s, r_v = vg[ld_i % 2]
                            dma(f"kg{ld_i % 2}", k_s[:], KT_d[h, :, gk * NTS * 128:(gk + 1) * NTS * 128], [r_KT], [r_k])
                            dma(f"vg{ld_i % 2}", v_s[:], VV_d[h, :, gk * NTS:(gk + 1) * NTS, :], [r_VV], [r_v])
                            ld_i += 1
                            loaded[gk] = (k_s, r_k, v_s, r_v)
                        return loaded[gk]

                    def emit_s(idx):
                        nonlocal s_i, pt_i
                        gk, kt, kind = seq[idx]
                        k_s, r_k, v_s, r_v = ensure_group(gk)
                        kl = kt - gk * NTS
                        St, r_sa, r_sb = S_t[s_i % 2]
                        s_i += 1
                        mm(St[:, 0, 0:N], k_s[0:64, kl * 128:(kl + 1) * 128], q_s[0:64, 0:N], True, True,
                           [r_k, r_q], [r_sa])
                        mm(St[:, 1, 0:N], k_s[64:128, kl * 128:(kl + 1) * 128], q_s[64:128, 0:N], True, True,
                           [r_k, r_q], [r_sb])
                        p_s, r_p = pts[pt_i % 3]
                        pt_i += 1
                        act(p_s[:, :, 0:N], St[:, :, 0:N], AF.Exp, [r_sa, r_sb, r_biast], [r_p],
                            bias=biast[:, gk:gk + 1], scale=scale)
                        if kind is not None:
                            tt("pool", p_s[:, :, 0:N], p_s[:, :, 0:N],
                               pat[:, kind, 0:N].unsqueeze(1).to_broadcast([128, 2, N]), ALU.mult,
                               [r_p, r_pat], [r_p])
                        return p_s, r_p

                    def emit_pv(idx, p_s, r_p):
                        gk, kt, kind = seq[idx]
                        k_s, r_k, v_s, r_v = loaded[gk]
                        kl = kt - gk * NTS
                        for sub in range(nq):
                            for m in range(2):
                                ob, r_ob = O_bk[m][sub // 2]
                                mm(ob[:, (sub % 2) * 129:(sub % 2 + 1) * 129],
                                   p_s[:, m, sub * 128:(sub + 1) * 128], v_s[:, kl, 0:129],
                                   idx == 0 and (sub % 2 == 0), idx == total_k - 1, [r_p, r_v], [r_ob], skip=True)

                    pend = emit_s(0)
                    for idx in range(total_k):
                        nxt = emit_s(idx + 1) if idx + 1 < total_k else None
                        emit_pv(idx, *pend)
                        pend = nxt
                    for sub in range(nq):
                        o0, r_o0 = O_bk[0][sub // 2]
                        o1, r_o1 = O_bk[1][sub // 2]
                        o0 = o0[:, (sub % 2) * 129:(sub % 2 + 1) * 129]
                        o1 = o1[:, (sub % 2) * 129:(sub % 2 + 1) * 129]
                        ts("dve", rl[:, 0:1], o0[:, 128:129], 1e-30, None, ALU.max, None, [r_o0], [r_rl])
                        ts("dve", rl[:, 1:2], o1[:, 128:129], 1e-30, None, ALU.max, None, [r_o1], [r_rl])
                        recip(rl[:, 2:4], rl[:, 0:2], [r_rl], [r_rl])
                        tt("dve", rl[:, 4:5], rl[:, 3:4], nlam[:, 0:1], ALU.mult, [r_rl, r_nlam], [r_rl])
                        ts("dve", tmpa[:], o1[:, 0:128], rl[:, 4:5], None, ALU.mult, None, [r_o1, r_rl], [r_tmpa])
                        stt("dve", aof[:, sub, h, :], o0[:, 0:128], rl[:, 2:3], tmpa[:], ALU.mult, ALU.add,
                            [r_o0, r_rl, r_tmpa], [r_aof])
                for sub in range(nq):
                    for h in range(H):
                        act(junk[:, 0:128], aof[:, sub, h, :], AF.Square, [r_aof], [r_junk, r_ssq],
                            accum_out=ssq[:, 4 + h:5 + h])
                    act(ssq[:, 4:8], ssq[:, 4:8], AF.Sqrt, [r_ssq], [r_ssq], bias=EPS, scale=1.0 / 128)
                    recip(ssq[:, 4:8], ssq[:, 4:8], [r_ssq], [r_ssq])
                    for h in range(H):
                        stt("dve", aob[:, h * 128:(h + 1) * 128], aof[:, sub, h, :], ssq[:, 4 + h:5 + h], gdab[:],
                            ALU.mult, ALU.mult, [r_aof, r_ssq, r_gdab], [r_aob])
                    transpose8(aob, r_aob, aoTs[:], r_aoTs, 4, bidx=0)
                    u = t0 + sub
                    dma("aoTs", AOT_d[:, :, u * 128:(u + 1) * 128], aoTs[:], [r_aoTs], [r_AOT])

        P.barrier()
        with ExitStack() as es3:
            wdn = [sbuf(es3, f"wdn{i}", [128, 2, D], BF16) for i in range(2)]
            wup = [sbuf(es3, f"wup{i}", [128, 8, 2, 256], BF16) for i in range(2)]
            aoT, r_aoT = sbuf(es3, "aoT", [128, 4, 512], BF16)
            hmT, r_hmT = sbuf(es3, "hmT", [128, 4, 512], BF16)
            sg3s = [sbuf(es3, f"sg3_{i}", [128, 2 * D], BF16) for i in range(2)]
            x3t = [sbuf(es3, f"x3t{i}", [128, D], F32) for i in range(2)]
            m2, r_m2 = sbuf(es3, "m2", [128, D], F32)
            mbs = [sbuf(es3, f"mb{i}", [128, D], BF16) for i in range(2)]
            mT, r_mT = sbuf(es3, "mT", [128, 8, 128], BF16)
            x1b, r_x1b = sbuf(es3, "x1b", [128, BT, D], F32)
            h2bs = [sbuf(es3, f"h2b{i}", [128, D], BF16) for i in range(2)]
            h2T, r_h2T = sbuf(es3, "h2T", [128, 8, 512], BF16)
            aext, r_aext = sbuf(es3, "aext", [128, 2 + 512], F32)
            halo = [sbuf(es3, f"halo{i}", [128, NFC, 2], F32) for i in range(2)]
            acc, r_acc = sbuf(es3, "acc", [128, 512], F32)
            gel, r_gel = sbuf(es3, "gel", [128, 512], F32)
            uT, r_uT = sbuf(es3, "uT", [128, NFC, 512], BF16)
            x2bf, r_x2bf = sbuf(es3, "x2bf", [128, D], BF16)
            x2Ts = [sbuf(es3, f"x2T{i}", [128, 8, 128], BF16) for i in range(2)]
            gsb, r_gsb = sbuf(es3, "gsb", [128, D], F32)
            pf, r_pf = sbuf(es3, "pf", [128, 2, 128], F32)
            pbs = [sbuf(es3, f"pb{i}", [128, 2, 128], BF16) for i in range(2)]
            m1, r_m1 = gsb, r_gsb
            yo = x3t

            for i in range(2):
                memset("pool", halo[i][0][:], 0.0, [halo[i][1]])

            wu_i = 0
            wd_i = 0
            yo_i = 0
            x_i = 0
            for bi, tiles in enumerate(blocks):
                nq = len(tiles)
                N = nq * 128
                t0 = tiles[0]
                last_is_halo = bi == 0
                dma("aoT", aoT[:, :, 0:N], AOT_d[:, :, t0 * 128:t0 * 128 + N], [r_AOT], [r_aoT])
                dma("hmT", hmT[:, :, 0:N], HMT_d[:, :, t0 * 128:t0 * 128 + N], [r_HMT], [r_hmT])
                xbuf = {}

                def stage_MA(ti):
                    nonlocal x_i
                    u = tiles[ti]
                    x_t, r_x = x3t[x_i % 2]
                    xbuf[ti] = (x_t, r_x)
                    dma(f"x3t{x_i % 2}", x_t[:], xa[(G0 + u) * 128:(G0 + u + 1) * 128, :], [], [r_x])
                    x_i += 1
                    sg3, r_sg3 = sg3s[ti % 2]
                    dma(f"sg3_{ti % 2}", sg3[:], SG_d[u * 128:(u + 1) * 128, :], [r_SG], [r_sg3])
                    ya, r_ya0, r_ya1 = bank2(0)
                    yb, r_yb0, r_yb1 = bank2(2)
                    for half, (ra_, rb_) in enumerate(((r_ya0, r_yb0), (r_ya1, r_yb1))):
                        for c in range(4):
                            mm(ya[:, half, :], aoT[:, c, ti * 128:(ti + 1) * 128], Wya[:, c, half * 512:(half + 1) * 512],
                               c == 0, c == 3, [r_aoT, r_Wya], [ra_])
                        for c in range(4):
                            mm(yb[:, half, :], hmT[:, c, ti * 128:(ti + 1) * 128], Wyb[:, c, half * 512:(half + 1) * 512],
                               c == 0, c == 3, [r_hmT, r_Wyb], [rb_])
                    mb, r_mb = mbs[ti % 2]
                    tt("dve", m1[:].rearrange("p (a b) -> p a b", a=2), ya[:, :, :],
                       sg3[:, 0:D].rearrange("p (a b) -> p a b", a=2), ALU.mult, [r_ya0, r_ya1, r_sg3], [r_m1])
                    tt("dve", m2[:].rearrange("p (a b) -> p a b", a=2), yb[:, :, :],
                       sg3[:, D:2 * D].rearrange("p (a b) -> p a b", a=2), ALU.mult, [r_yb0, r_yb1, r_sg3], [r_m2])
                    tt("dve", mb[:], m1[:], m2[:], ALU.add, [r_m1, r_m2], [r_mb])

                def stage_MB(ti):
                    mb, r_mb = mbs[ti % 2]
                    x_t, r_x = xbuf[ti]
                    h2b, r_h2b = h2bs[ti % 2]
                    transpose8(mb, r_mb, mT[:], r_mT, 8, bidx=4)
                    xo, r_xo0, r_xo1 = bank2(6)
                    for half, rr_ in enumerate((r_xo0, r_xo1)):
                        for kc in range(8):
                            mm(xo[:, half, :], mT[:, kc, :], Wo[:, kc, half * 512:(half + 1) * 512], kc == 0, kc == 7,
                               [r_mT, r_Wo], [rr_])
                    tt("dve", x1b[:, ti, :].rearrange("p (a b) -> p a b", a=2), xo[:, :, :],
                       x_t[:].rearrange("p (a b) -> p a b", a=2), ALU.add, [r_xo0, r_xo1, r_x], [r_x1b])
                    rstd = rms_stats(x1b[:, ti, :], D, r_x1b)
                    stt("dve", h2b[:], x1b[:, ti, :], rstd, g2b[:], ALU.mult, ALU.mult, [r_x1b, r_ssq, r_g2b], [r_h2b])

                def stage_MC(ti):
                    h2b, r_h2b = h2bs[ti % 2]
                    transpose8(h2b, r_h2b, h2T[:, :, ti * 128:(ti + 1) * 128], r_h2T, 8, bidx=5)

                stage_MA(0)
                for ti in range(nq):
                    if ti + 1 < nq:
                        stage_MA(ti + 1)
                    stage_MB(ti)
                    if ti >= 1:
                        stage_MC(ti - 1)
                stage_MC(nq - 1)

                h_in, r_hin = halo[bi % 2]
                h_out, r_hout = halo[(bi + 1) % 2]
                for fc0 in range(0, NFC, 2):
                    w_s, r_w = wup[wu_i % 2]
                    wu_i += 1
                    dma(f"wup{(wu_i - 1) % 2}", w_s[:, :, 0, :], WUP_d[:, :, fc0 * 128:fc0 * 128 + 256], [r_WUP], [r_w])
                    if not last_is_halo:
                        dma(f"wup{(wu_i - 1) % 2}", w_s[:, :, 1, :], WUP_d[:, :, DFF + fc0 * 128:DFF + fc0 * 128 + 256],
                            [r_WUP], [r_w])
                    for fi in range(2):
                        fc = fc0 + fi
                        ab, r_ab = bank(2 * fi)
                        for kc in range(8):
                            mm(ab[:, 0:N], w_s[:, kc, 0, fi * 128:(fi + 1) * 128], h2T[:, kc, 0:N], kc == 0, kc == 7,
                               [r_w, r_h2T], [r_ab])
                        acopy(aext[:, 2:2 + N], ab[:, 0:N], [r_ab], [r_aext])
                        tcopy("dve", aext[:, 0:2], h_in[:, fc, :], [r_hin], [r_aext])
                        tcopy("dve", h_out[:, fc, :], aext[:, N:N + 2], [r_aext], [r_hout])
                        if last_is_halo:
                            continue
                        gb_, r_gb = bank(2 * fi + 1)
                        for kc in range(8):
                            mm(gb_[:, 0:N], w_s[:, kc, 1, fi * 128:(fi + 1) * 128], h2T[:, kc, 0:N], kc == 0, kc == 7,
                               [r_w, r_h2T], [r_gb])
                        ts("dve", acc[:, 0:N], aext[:, 0:N], cfw[:, fc, 0:1], cfb[:, fc:fc + 1], ALU.mult, ALU.add,
                           [r_aext, r_cfw, r_cfb], [r_acc])
                        stt("dve", acc[:, 0:N], aext[:, 1:N + 1], cfw[:, fc, 1:2], acc[:, 0:N], ALU.mult, ALU.add,
                            [r_aext, r_cfw, r_acc], [r_acc])
                        stt("dve", acc[:, 0:N], aext[:, 2:N + 2], cfw[:, fc, 2:3], acc[:, 0:N], ALU.mult, ALU.add,
                            [r_aext, r_cfw, r_acc], [r_acc])
                        act(gel[:, 0:N], acc[:, 0:N], AF.Gelu_apprx_tanh, [r_acc], [r_gel])
                        tt("dve", uT[:, fc, 0:N], gel[:, 0:N], gb_[:, 0:N], ALU.mult, [r_gel, r_gb], [r_uT])
                if last_is_halo:
                    continue

                for fc0 in range(0, NFC, 2):
                    wd, r_wd = wdn[wd_i % 2]
                    wd_i += 1
                    dma(f"wdn{(wd_i - 1) % 2}", wd[:], WDN_d[:, fc0:fc0 + 2, :], [r_WDN], [r_wd])
                    for fi in range(2):
                        fc = fc0 + fi
                        for ti in range(nq):
                            for half in range(2):
                                bkx, r_bkx = bank(2 * ti + half)
                                mm(bkx[:, :], uT[:, fc, ti * 128:(ti + 1) * 128], wd[:, fi, half * 512:(half + 1) * 512],
                                   fc == 0, fc == NFC - 1, [r_uT, r_wd], [r_bkx])
                for ti in range(nq):
                    xd, r_xd0, r_xd1 = bank2(2 * ti)
                    tt("dve", x1b[:, ti, :].rearrange("p (a b) -> p a b", a=2), xd[:, :, :],
                       x1b[:, ti, :].rearrange("p (a b) -> p a b", a=2), ALU.add, [r_xd0, r_xd1, r_x1b], [r_x1b])
                def stage_PA(ti):
                    u = tiles[ti]
                    x2 = x1b[:, ti, :]
                    x2T, r_x2T = x2Ts[ti % 2]
                    pb, r_pb = pbs[ti % 2]
                    tcopy("act", x2bf[:], x2, [r_x1b], [r_x2bf])
                    transpose8(x2bf, r_x2bf, x2T[:], r_x2T, 8, bidx=6)
                    tok0 = (u - 1) * 128
                    dma("pf", pf[:], pT_d[:, :, tok0:tok0 + 128], [], [r_pf])
                    tcopy("dve", pb[:], pf[:], [r_pf], [r_pb])

                def stage_PB(ti):
                    nonlocal yo_i
                    u = tiles[ti]
                    x2 = x1b[:, ti, :]
                    r_x2 = r_x1b
                    x2T, r_x2T = x2Ts[ti % 2]
                    pb, r_pb = pbs[ti % 2]
                    tok0 = (u - 1) * 128
                    gp, r_gp0, r_gp1 = bank2(0)
                    pp, r_pp0, r_pp1 = bank2(2)
                    for half, (rg_, rp_) in enumerate(((r_gp0, r_pp0), (r_gp1, r_pp1))):
                        for kc in range(8):
                            mm(gp[:, half, :], x2T[:, kc, :], Wpg[:, kc, half * 512:(half + 1) * 512], kc == 0, kc == 7,
                               [r_x2T, r_Wpg], [rg_])
                        for c in range(2):
                            mm(pp[:, half, :], pb[:, c, :], Wpl[:, c, half * 512:(half + 1) * 512], c == 0, c == 1,
                               [r_pb, r_Wpl], [rp_])
                    act(gsb[:].rearrange("p (a b) -> p a b", a=2), gp[:, :, :], AF.Sigmoid, [r_gp0, r_gp1], [r_gsb])
                    tt("dve", gsb[:].rearrange("p (a b) -> p a b", a=2), pp[:, :, :],
                       gsb[:].rearrange("p (a b) -> p a b", a=2), ALU.mult, [r_pp0, r_pp1, r_gsb], [r_gsb])
                    tt("dve", x2, x2, gsb[:], ALU.add, [r_x2, r_gsb], [r_x2])
                    rstd = rms_stats(x2, D, r_x2)
                    y_t, r_yt = yo[yo_i % 2]
                    yo_i += 1
                    stt("dve", y_t[:], x2, rstd, gFb[:], ALU.mult, ALU.mult, [r_x2, r_ssq, r_gFb], [r_yt])
                    dma(f"x3t{(yo_i - 1) % 2}", y_d[tok0:tok0 + 128, :], y_t[:], [r_yt], [r_y])

                stage_PA(0)
                for ti in range(nq):
                    if ti + 1 < nq:
                        stage_PA(ti + 1)
                    stage_PB(ti)

            P.wait_all("sp", [r_y, yo[0][1], yo[1][1]])
            P.wait_all("pool", [r_y, yo[0][1], yo[1][1]])
            P.wait_all("act", [r_y, yo[0][1], yo[1][1]])
        es23.close()
        P.run()
    return nc, P


def host_constants(NTS):
    NT = NSLOT * NTS
    bf = ml_dtypes.bfloat16
    c = {}
    c["identb"] = np.eye(128, dtype=np.float32).astype(bf)
    s = np.arange(128)
    c["tri"] = (s[:, None] <= s[None, :]).astype(np.float32)
    c["ones"] = np.ones((128, 128), np.float32)
    sh = np.zeros((128, 7, 128), np.float32)
    for j in range(4):
        for t in range(128):
            sidx = t - 3 + j
            if 0 <= sidx < 128:
                sh[sidx, j, t] = 1.0
    for j in range(3):
        for t in range(128):
            sidx = 128 + t - 3 + j
            if 0 <= sidx < 128:
                sh[sidx, 4 + j, t] = 1.0
    c["shifts"] = sh.astype(bf)
    pat = np.zeros((128, 4, 512), np.float32)
    k = np.arange(128)
    q = np.arange(512)
    for o in range(4):
        pat[:, o, :] = (((128 * o + k) // 64)[:, None] <= (q // 64)[None, :]).astype(np.float32)
    c["pat"] = pat.astype(bf)
    inv_freq = (10000.0 ** (-np.arange(0, 64, 2, dtype=np.float32) / 64)).astype(np.float32)
    pos = (np.arange(NT)[None, :] * 128 + np.arange(128)[:, None]).astype(np.float32)
    ang = pos[:, :, None] * inv_freq[None, None, :]
    c["ropec"] = np.cos(ang).astype(np.float32)
    c["ropes"] = np.sin(ang).astype(np.float32)
    return c


_PROG_CACHE = {}


def make_in_maps(inputs, NTS, cores=None):
    SLOT = NTS * 128
    S = NSLOT * SLOT
    NT = NSLOT * NTS
    f32 = np.float32
    x = np.asarray(inputs["x"], f32).reshape(S, D)
    p = np.asarray(inputs["p"], f32).reshape(S, PLE)
    consts = host_constants(NTS)
    shared = {
        "w_in": np.ascontiguousarray(np.asarray(inputs["w_in"], f32)[0]),
        "w_ya": np.ascontiguousarray(np.asarray(inputs["w_ya"], f32)[0]),
        "w_yb": np.ascontiguousarray(np.asarray(inputs["w_yb"], f32)[0]),
        "w_o": np.ascontiguousarray(np.asarray(inputs["w_o"], f32)[0]),
        "w_up": np.ascontiguousarray(np.asarray(inputs["w_up"], f32)[0]),
        "w_down": np.ascontiguousarray(np.asarray(inputs["w_down"], f32)[0]),
        "w_ple": np.ascontiguousarray(np.asarray(inputs["w_ple"], f32)[0]),
        "w_pg": np.ascontiguousarray(np.asarray(inputs["w_pg"], f32)[0]),
        "g1": np.asarray(inputs["norm1_g"], f32).reshape(1, D),
        "g2": np.asarray(inputs["norm2_g"], f32).reshape(1, D),
        "gF": np.asarray(inputs["final_g"], f32).reshape(1, D),
        "gda": np.asarray(inputs["da_norm_g"], f32).reshape(1, 128),
        "gml": np.asarray(inputs["ml_norm_g"], f32).reshape(1, 128),
        "b_if": np.asarray(inputs["b_if"], f32).reshape(1, 8),
        "lamv": np.stack([np.asarray(inputs[k], f32).reshape(64) for k in ("lam_q1", "lam_k1", "lam_q2", "lam_k2")]).reshape(1, 256),
        "conv_m_w": np.ascontiguousarray(np.asarray(inputs["conv_m_w"], f32)[0]).reshape(1, 4 * D),
        "cfw": np.ascontiguousarray(np.asarray(inputs["conv_f_w"], f32)[0].T.reshape(NFC, 128, 3).transpose(1, 0, 2)),
        "cfb": np.ascontiguousarray(np.asarray(inputs["conv_f_b"], f32)[0].reshape(NFC, 128).T),
    }
    shared.update(consts)
    in_maps = []
    for r in (range(NCORES) if cores is None else cores):
        xa = np.zeros((NT * 128, D), f32)
        lo = (r - (NSLOT - 1)) * SLOT
        hi = (r + 1) * SLOT
        src_lo = max(lo, 0)
        xa[src_lo - lo:, :] = x[src_lo:hi]
        pT = np.ascontiguousarray(p[r * SLOT:(r + 1) * SLOT].T.reshape(2, 128, SLOT).transpose(1, 0, 2))
        biast = np.zeros((128, NSLOT), f32)
        for s_ in range(NSLOT):
            if s_ < NSLOT - 1 - r:
                biast[:, s_] = NEG
        m = dict(shared)
        m["xa"] = xa
        m["pT"] = pT
        m["biast"] = biast
        in_maps.append(m)
    return in_maps


def kernel(**inputs):
    S = int(np.asarray(inputs["x"]).shape[1])
    NTS = S // (NSLOT * 128)
    if NTS not in _PROG_CACHE:
        _PROG_CACHE[NTS] = build_program(NTS)[0]
    nc = _PROG_CACHE[NTS]
    in_maps = make_in_maps(inputs, NTS)
    res = run_bass_kernel_spmd(nc, in_maps, core_ids=list(range(NCORES)))
    out = np.concatenate([np.asarray(r["y"], np.float32) for r in res.results], axis=0)
    return out.reshape(1, S, D)
```

```python
import math
from contextlib import ExitStack

import numpy as np
import ml_dtypes

import concourse.bass as bass
import concourse.mybir as mybir
from concourse.bass_utils import run_bass_kernel_spmd

F32 = mybir.dt.float32
BF16 = mybir.dt.bfloat16
AF = mybir.ActivationFunctionType
ALU = mybir.AluOpType

NCORES = 8
NSLOT = 8
D = 1024
NIN = 5640
DFF = 2816
NFC = DFF // 128
PLE = 256
H = 4
EPS = 1e-6
LAM_INIT = 0.8 - 0.6 * math.exp(-0.3 * 0)
QA, KA, VA, QM, KM, VM, OM, IFG, GA, GB = 0, 512, 1024, 1536, 2048, 2560, 3072, 3584, 3592, 4616
NEG = -30000.0


class Res:
    __slots__ = ("name", "w", "r", "ro")

    def __init__(self, name, ro=False):
        self.name = name
        self.w = None
        self.r = []
        self.ro = ro


class Prog:
    ENGS = ("pe", "act", "dve", "pool", "sp")

    def __init__(self, nc):
        self.nc = nc
        self.q = {e: [] for e in self.ENGS}
        self.sem = {e: nc.alloc_semaphore(f"c_{e}") for e in self.ENGS}
        self.cnt = {e: 0 for e in self.ENGS}
        self.seen = {e: {} for e in self.ENGS}
        self.dsem = {}
        self.dknow = {}
        self.hist = {e: {} for e in self.ENGS}
        self.n_wait = 0
        self.n_ins = 0

    def _merge(self, eng, know):
        se = self.seen[eng]
        for k, v in know.items():
            if se.get(k, 0) < v:
                se[k] = v

    def _need(self, eng, toks):
        se = self.seen[eng]
        cand = []
        for t in toks:
            if t is None:
                continue
            kind, key, val = t
            if kind == "d":
                val = self.dsem[key][1]
            cand.append((kind, key, val))
        cand.sort(key=lambda t: (t[0] == "c" and t[1] == eng, -t[2]))
        out = []
        for kind, key, val in cand:
            k = (kind, key)
            if se.get(k, 0) >= val:
                continue
            if kind == "c":
                sem = self.sem[key]
                know = self.hist[key].get(val)
            else:
                sem = self.dsem[key][0]
                know = self.dknow.get(key)
            se[k] = val
            if know:
                self._merge(eng, know)
            out.append((sem, val))
        return out

    def _deps(self, eng, reads, writes):
        toks = []
        for r in reads:
            toks.append(r.w)
        for w in writes:
            if not (eng == "pe" and w.w is not None and w.w[0] == "c" and w.w[1] == "pe"):
                toks.append(w.w)
            toks.extend(w.r)
        return toks

    def _record(self, tok, reads, writes):
        for r in reads:
            if not r.ro:
                r.r.append(tok)
        for w in writes:
            w.w = tok
            w.r = []

    def op(self, eng, fn, reads=(), writes=()):
        waits = self._need(eng, self._deps(eng, reads, writes))
        self.cnt[eng] += 1
        n = self.cnt[eng]
        tok = ("c", eng, n)
        snap = dict(self.seen[eng])
        snap[("c", eng)] = n
        self.hist[eng][n] = snap
        self.q[eng].append((waits, fn, self.sem[eng], 1))
        self.n_wait += len(waits)
        self.n_ins += 1
        self._record(tok, reads, writes)
        return tok

    def dma(self, eng, semname, fn, reads=(), writes=()):
        if semname not in self.dsem:
            self.dsem[semname] = [self.nc.alloc_semaphore(f"d_{semname}"), 0]
            self.dknow[semname] = {}
        waits = self._need(eng, self._deps(eng, reads, writes))
        ent = self.dsem[semname]
        ent[1] += 16
        tok = ("d", semname, ent[1])
        dk = self.dknow[semname]
        for k, v in self.seen[eng].items():
            if dk.get(k, 0) < v:
                dk[k] = v
        self.q[eng].append((waits, fn, ent[0], 16))
        self.n_wait += len(waits)
        self.n_ins += 1
        self._record(tok, reads, writes)
        return tok

    def wait_all(self, eng, ress):
        toks = []
        for r in ress:
            toks.append(r.w)
            toks.extend(r.r)
        waits = self._need(eng, toks)
        self.q[eng].append((waits, None, None, 0))

    def barrier(self):
        for eng in self.ENGS:
            toks = [("c", e2, self.cnt[e2]) for e2 in self.ENGS if self.cnt[e2] > 0]
            toks += [("d", name, ent[1]) for name, ent in self.dsem.items() if ent[1] > 0]
            waits = self._need(eng, toks)
            self.q[eng].append((waits, None, None, 0))

    def run(self):
        q = self.q

        def play(engobj, items):
            for waits, fn, sem, inc in items:
                if fn is None:
                    for (s, v) in waits:
                        engobj.wait_ge(s, v)
                    continue
                for (s, v) in waits[:-1]:
                    engobj.wait_ge(s, v)
                ins = fn(engobj)
                if waits:
                    ins._wait_ge(*waits[-1])
                if sem is not None:
                    ins.then_inc(sem, inc)

        with self.nc.Block() as block:
            @block.tensor
            def _(e):
                play(e, q["pe"])

            @block.scalar
            def _(e):
                play(e, q["act"])

            @block.vector
            def _(e):
                play(e, q["dve"])

            @block.gpsimd
            def _(e):
                play(e, q["pool"])

            @block.sync
            def _(e):
                play(e, q["sp"])


def build_program(NTS):
    NT = NSLOT * NTS
    NOWN = NTS + 1
    G0 = NT - NOWN
    SLOT = NTS * 128
    BT = min(4, NTS)
    blocks = [[0]] + [list(range(1 + b, 1 + b + BT)) for b in range(0, NTS, BT)]
    NOWNTOK = NOWN * 128

    nc = bass.Bass("TRN2", target_bir_lowering=False)
    P = Prog(nc)

    def din(name, shape, dt=F32):
        return nc.dram_tensor(name, list(shape), dt, kind="ExternalInput").ap()

    def dscr(name, shape, dt):
        return nc.dram_tensor(name, list(shape), dt).ap(), Res(name)

    xa = din("xa", [NT * 128, D])
    pT_d = din("pT", [128, 2, SLOT])
    w_in_d = din("w_in", [D, NIN])
    w_ya_d = din("w_ya", [512, D])
    w_yb_d = din("w_yb", [512, D])
    w_o_d = din("w_o", [D, D])
    w_up_d = din("w_up", [D, 2 * DFF])
    w_dn_d = din("w_down", [DFF, D])
    w_ple_d = din("w_ple", [PLE, D])
    w_pg_d = din("w_pg", [D, D])
    g1_d = din("g1", [1, D])
    g2_d = din("g2", [1, D])
    gF_d = din("gF", [1, D])
    gda_d = din("gda", [1, 128])
    gml_d = din("gml", [1, 128])
    bif_d = din("b_if", [1, 8])
    lam_d = din("lamv", [1, 256])
    cmw_d = din("conv_m_w", [1, 4 * D])
    cfw_d = din("cfw", [128, NFC, 3])
    cfb_d = din("cfb", [128, NFC])
    identb_d = din("identb", [128, 128], BF16)
    tri_d = din("tri", [128, 128])
    ones_d = din("ones", [128, 128])
    shifts_d = din("shifts", [128, 7, 128], BF16)
    pat_d = din("pat", [128, 4, 512], BF16)
    ropec_d = din("ropec", [128, NT, 32])
    ropes_d = din("ropes", [128, NT, 32])
    biast_d = din("biast", [128, NSLOT])
    y_d = nc.dram_tensor("y", [SLOT, D], F32, kind="ExternalOutput").ap()
    r_y = Res("y")

    KT_d, r_KT = dscr("KT", [H, 128, NT * 128], BF16)
    VV_d, r_VV = dscr("VV", [H, 128, NT, 130], BF16)
    QT_d, r_QT = dscr("QT", [H, 128, NOWNTOK], BF16)
    SG_d, r_SG = dscr("SG", [NOWNTOK, 2 * D], BF16)
    HMT_d, r_HMT = dscr("HMT", [128, H, NOWNTOK], BF16)
    AOT_d, r_AOT = dscr("AOT", [128, H, NOWNTOK], BF16)
    WUP_d, r_WUP = dscr("WUPb", [128, 8, 2 * DFF], BF16)
    WDN_d, r_WDN = dscr("WDNb", [128, NFC, D], BF16)
    WYA_d, r_WYA = dscr("WYAb", [128, 4, D], BF16)
    WYB_d, r_WYB = dscr("WYBb", [128, 4, D], BF16)
    WO_d, r_WO = dscr("WOb", [128, 8, D], BF16)
    WPG_d, r_WPG = dscr("WPGb", [128, 8, D], BF16)
    WPL_d, r_WPL = dscr("WPLb", [128, 2, D], BF16)

    def mm(out, lhsT, rhs, start, stop, reads, writes, skip=False):
        if skip:
            P.op("pe", lambda e: e.matmul(out, lhsT=lhsT, rhs=rhs, start=start, stop=stop, skip_group_check=True),
                 reads, writes)
        else:
            P.op("pe", lambda e: e.matmul(out, lhsT=lhsT, rhs=rhs, start=start, stop=stop), reads, writes)

    def tr(out, in_, ident, reads, writes):
        P.op("pe", lambda e: e.transpose(out=out, in_=in_, identity=ident), reads, writes)

    def act(out, in_, func, reads, writes, bias=0.0, scale=1.0, accum_out=None, eng="act"):
        if accum_out is None:
            P.op(eng, lambda e: e.activation(out=out, in_=in_, func=func, bias=bias, scale=scale), reads, writes)
        else:
            P.op(eng, lambda e: e.activation(out=out, in_=in_, func=func, bias=bias, scale=scale,
                                             accum_out=accum_out), reads, writes)

    def acopy(out, in_, reads, writes):
        P.op("act", lambda e: e.copy(out=out, in_=in_), reads, writes)

    def tcopy(eng, out, in_, reads, writes):
        if eng == "act":
            P.op(eng, lambda e: e.copy(out=out, in_=in_), reads, writes)
        else:
            P.op(eng, lambda e: e.tensor_copy(out=out, in_=in_), reads, writes)

    def tt(eng, out, in0, in1, op, reads, writes):
        P.op(eng, lambda e: e.tensor_tensor(out=out, in0=in0, in1=in1, op=op), reads, writes)

    def ts(eng, out, in0, s1, s2, op0, op1, reads, writes):
        if s2 is None:
            P.op(eng, lambda e: e.tensor_scalar(out=out, in0=in0, scalar1=s1, scalar2=None, op0=op0), reads, writes)
        else:
            P.op(eng, lambda e: e.tensor_scalar(out=out, in0=in0, scalar1=s1, scalar2=s2, op0=op0, op1=op1),
                 reads, writes)

    def stt(eng, out, in0, scalar, in1, op0, op1, reads, writes):
        P.op(eng, lambda e: e.scalar_tensor_tensor(out=out, in0=in0, scalar=scalar, in1=in1, op0=op0, op1=op1),
             reads, writes)

    def recip(out, in_, reads, writes):
        P.op("dve", lambda e: e.reciprocal(out=out, in_=in_), reads, writes)

    def memset(eng, ap, val, writes):
        P.op(eng, lambda e: e.memset(ap, val), (), writes)

    def dma(semname, out, in_, reads, writes, eng="sp"):
        P.dma(eng, semname, lambda e: e.dma_start(out=out, in_=in_), reads, writes)

    with ExitStack() as es_top:
        def sbuf(es, name, shape, dt=F32, ro=False):
            t = es.enter_context(nc.sbuf_tensor("s_" + name, list(shape), dt))
            return t, Res(name, ro=ro)

        banks = []
        for i in range(4):
            t = es_top.enter_context(nc.psum_tensor(f"pb{i}", [128, 2, 512], F32))
            banks.append((t, Res(f"pb{i}a")))
            banks.append((t, Res(f"pb{i}b")))

        def bank(i):
            t, r = banks[i]
            return t[:, i % 2, :], r

        def bank2(i):
            t, ra = banks[i]
            _, rb = banks[i + 1]
            return t, ra, rb

        esg = es_top
        identb, r_identb = sbuf(esg, "identb", [128, 128], BF16, ro=True)
        tri, r_tri = sbuf(esg, "tri", [128, 128], F32, ro=True)
        ones, r_ones = sbuf(esg, "ones", [128, 128], F32, ro=True)
        gFb, r_gFb = sbuf(esg, "gFb", [128, D], F32, ro=True)
        lam_t, r_lam = sbuf(esg, "lam_t", [128, 4, 64], F32)
        lam_s, r_lams = sbuf(esg, "lam_s", [128, 4], F32)
        nlam, r_nlam = sbuf(esg, "nlam", [128, 1], F32, ro=True)
        ssq, r_ssq = sbuf(esg, "ssq", [128, 8], F32)
        junk, r_junk = sbuf(esg, "junk", [128, D], F32)

        dma("identb", identb[:], identb_d, [], [r_identb])
        dma("tri", tri[:], tri_d, [], [r_tri])
        dma("ones", ones[:], ones_d, [], [r_ones])
        dma("gFb", gFb[:], gF_d.partition_broadcast(128), [], [r_gFb])
        dma("lam_t", lam_t[:].rearrange("p a b -> p (a b)"),
            lam_d.partition_broadcast(128), [], [r_lam])
        for i in range(2):
            tt("dve", lam_t[:, 2 * i, :], lam_t[:, 2 * i, :], lam_t[:, 2 * i + 1, :], ALU.mult, [r_lam], [r_lam])
            act(lam_t[:, 2 * i + 1, :], lam_t[:, 2 * i, :], AF.Copy, [r_lam], [r_lam, r_lams],
                accum_out=lam_s[:, i:i + 1])
        act(lam_s[:, 2:4], lam_s[:, 0:2], AF.Exp, [r_lams], [r_lams])
        tt("dve", lam_s[:, 0:1], lam_s[:, 3:4], lam_s[:, 2:3], ALU.subtract, [r_lams], [r_lams])
        ts("dve", nlam[:], lam_s[:, 0:1], -LAM_INIT, None, ALU.add, None, [r_lams], [r_nlam])

        def rms_stats(src_ap, n, r_src, col=0):
            act(junk[:, 0:n], src_ap, AF.Square, [r_src], [r_junk, r_ssq], accum_out=ssq[:, col:col + 1])
            act(ssq[:, col:col + 1], ssq[:, col:col + 1], AF.Sqrt, [r_ssq], [r_ssq], bias=EPS, scale=1.0 / n)
            recip(ssq[:, col:col + 1], ssq[:, col:col + 1], [r_ssq], [r_ssq])
            return ssq[:, col:col + 1]

        def transpose8(src_bf, r_src, dst_ap, r_dst, nblk, bidx=0):
            bk, r_bk = bank(bidx)
            bb = bk.bitcast(BF16)
            for k in range(nblk):
                tr(bb[:, k * 128:(k + 1) * 128], src_bf[:, k * 128:(k + 1) * 128], identb[:],
                   [r_src, r_identb], [r_bk])
            acopy(dst_ap, bb[:, 0:nblk * 128].rearrange("p (a b) -> p a b", a=nblk), [r_bk], [r_dst])

        with ExitStack() as es1:
            Wb, r_Wb = sbuf(es1, "Wb", [128, 8, NIN], BF16, ro=True)
            g1b, r_g1b = sbuf(es1, "g1b", [128, D], F32, ro=True)
            cmwb, r_cmwb = sbuf(es1, "cmwb", [128, 4, D], F32, ro=True)
            bifb, r_bifb = sbuf(es1, "bifb", [128, 8], F32, ro=True)
            gmlb, r_gmlb = sbuf(es1, "gmlb", [128, 128], F32, ro=True)
            shifts, r_shifts = sbuf(es1, "shifts", [128, 7, 128], BF16, ro=True)
            CU = 1024
            stage = [sbuf(es1, f"stage{i}", [128, CU], F32) for i in range(2)]
            stageb = [sbuf(es1, f"stageb{i}", [128, CU], BF16) for i in range(2)]
            NXT = 3
            xt = [sbuf(es1, f"xt{i}", [128, D], F32) for i in range(NXT)]
            rc = [sbuf(es1, f"rc{i}", [128, 32], F32) for i in range(NXT)]
            rs = [sbuf(es1, f"rs{i}", [128, 32], F32) for i in range(NXT)]
            st1 = [sbuf(es1, f"st1_{i}", [128, 2], F32) for i in range(NXT)]
            hbs = [sbuf(es1, f"hb{i}", [128, D], BF16) for i in range(2)]
            hTs = [sbuf(es1, f"hT{i}", [128, 8, 128], BF16) for i in range(2)]
            ra, r_ra = sbuf(es1, "ra", [128, 512], F32)
            rb1, r_rb1 = sbuf(es1, "rb1", [128, 8, 32], F32)
            rb2, r_rb2 = sbuf(es1, "rb2", [128, 8, 32], F32)
            krot, r_krot = sbuf(es1, "krot", [128, 512], BF16)
            ktT, r_ktT = sbuf(es1, "ktT", [128, 4, 128], BF16)
            vaug = [sbuf(es1, f"vaug{i}", [128, 4, 130], BF16) for i in range(2)]
            xwk = [(sbuf(es1, f"xwk{i}", [128, 4, 512], BF16)[0], [Res(f"xwk{i}_{j}") for j in range(4)]) for i in range(2)]
            xwq = [(sbuf(es1, f"xwq{i}", [128, 4, 512], BF16)[0], [Res(f"xwq{i}_{j}") for j in range(4)]) for i in range(2)]
            sig, r_sig = sbuf(es1, "sig", [128, 512], F32)
            kmtoks = [sbuf(es1, f"kmtok{i}", [128, 512], BF16) for i in range(2)]
            qmtok, r_qmtok = sbuf(es1, "qmtok", [128, 512], BF16)
            kmT, r_kmT = sbuf(es1, "kmT", [128, 4, 128], BF16)
            qmT, r_qmT = sbuf(es1, "qmT", [128, 4, 128], BF16)
            gt, r_gt = sbuf(es1, "gt", [128, 8], F32)
            gs, r_gs = sbuf(es1, "gs", [128, 24], F32)
            vts = [sbuf(es1, f"vt{i}", [128, 4, 129], BF16) for i in range(2)]
            Cst, r_C = sbuf(es1, "Cst", [128, 4, 129], F32)
            Chat, r_Chat = sbuf(es1, "Chat", [128, 4, 129], F32)
            Chb, r_Chb = sbuf(es1, "Chb", [128, 4, 129], BF16)
            ATs, r_ATs = sbuf(es1, "ATs", [128, 4, 128], BF16)
            hmf, r_hmf = sbuf(es1, "hmf", [128, 4, 128], F32)
            hmbs = [sbuf(es1, f"hmb{i}", [128, 512], BF16) for i in range(2)]
            hmTs, r_hmTs = sbuf(es1, "hmTs", [128, 4, 128], BF16)
            oms, r_oms = sbuf(es1, "oms", [128, 512], F32)
            sgs, r_sgs = sbuf(es1, "sgs", [128, 2 * D], BF16)
            mst, r_mst = sbuf(es1, "mst", [128, 4, 8], F32)
            rr, r_rr = sbuf(es1, "rr", [128, 8], F32)

            dma("g1b", g1b[:], g1_d.partition_broadcast(128), [], [r_g1b])
            dma("cmwb", cmwb[:].rearrange("p a b -> p (a b)"), cmw_d.partition_broadcast(128), [], [r_cmwb])
            dma("bifb", bifb[:], bif_d.partition_broadcast(128), [], [r_bifb])
            dma("gmlb", gmlb[:], gml_d.partition_broadcast(128), [], [r_gmlb])
            dma("shifts", shifts[:], shifts_d, [], [r_shifts])
            memset("pool", Cst[:], 0.0, [r_C])
            for i in range(2):
                memset("pool", xwk[i][0][:], 0.0, xwk[i][1])
                memset("pool", xwq[i][0][:], 0.0, xwq[i][1])
                memset("pool", vaug[i][0][:, :, 128:129], 1.0, [vaug[i][1]])
                memset("pool", vaug[i][0][:, :, 129:130], 0.0, [vaug[i][1]])

            cast_i = [0]

            def cast_unit(src_ap, ncols_free, dst_kind, dst_ap, r_dst, ceng="pool"):
                i = cast_i[0] % 2
                cast_i[0] += 1
                st, r_st = stage[i]
                a, b = src_ap.shape[1], src_ap.shape[2]
                stv = st[:, 0:ncols_free].rearrange("p (a b) -> p a b", a=a)
                dma(f"stage{i}", stv, src_ap, [], [r_st])
                if dst_kind == "sbuf":
                    tcopy(ceng, dst_ap, stv, [r_st], [r_dst])
                else:
                    sbt, r_sbt = stageb[i]
                    sbv = sbt[:, 0:ncols_free].rearrange("p (a b) -> p a b", a=a)
                    tcopy(ceng, sbv, stv, [r_st], [r_sbt])
                    dma(f"stageb{i}", dst_ap, sbv, [r_sbt], [r_dst])

            w_in_v = w_in_d.rearrange("(k p) n -> p k n", p=128)
            GW = CU // 8
            wgrp = {}
            col_groups = []
            for c0 in (KA, VA, IFG, KM, VM, QA, QM, OM, GA, GA + 512, GB, GB + 512):
                n_tot = 8 if c0 == IFG else 512
                for cc in range(c0, c0 + n_tot, GW):
                    col_groups.append((cc, min(GW, c0 + n_tot - cc)))
            engs = ["pool", "dve", "act"]
            for gi, (c0, n) in enumerate(col_groups):
                wgrp[c0] = (Res(f"Wb{c0}", ro=True), n)
                cast_unit(w_in_v[:, :, c0:c0 + n], 8 * n, "sbuf", Wb[:, :, c0:c0 + n], wgrp[c0][0],
                          ceng=engs[gi % 3] if gi < 40 else "pool")

            def wres(c0, n):
                return [r for c, (r, m) in wgrp.items() if c < c0 + n and c + m > c0]

            prep = []
            w_up_v = w_up_d.rearrange("(k p) n -> p k n", p=128)
            for c0 in range(0, 2 * DFF, GW):
                prep.append((w_up_v[:, :, c0:c0 + GW], CU, WUP_d[:, :, c0:c0 + GW], r_WUP))
            w_dn_v = w_dn_d.rearrange("(k p) n -> p k n", p=128)
            for k0 in range(NFC):
                prep.append((w_dn_v[:, k0:k0 + 1, :], D, WDN_d[:, k0:k0 + 1, :], r_WDN))
            w_ya_v = w_ya_d.rearrange("(k p) n -> p k n", p=128)
            w_yb_v = w_yb_d.rearrange("(k p) n -> p k n", p=128)
            for k0 in range(4):
                prep.append((w_ya_v[:, k0:k0 + 1, :], D, WYA_d[:, k0:k0 + 1, :], r_WYA))
                prep.append((w_yb_v[:, k0:k0 + 1, :], D, WYB_d[:, k0:k0 + 1, :], r_WYB))
            w_o_v = w_o_d.rearrange("(k p) n -> p k n", p=128)
            w_pg_v = w_pg_d.rearrange("(k p) n -> p k n", p=128)
            for k0 in range(8):
                prep.append((w_o_v[:, k0:k0 + 1, :], D, WO_d[:, k0:k0 + 1, :], r_WO))
                prep.append((w_pg_v[:, k0:k0 + 1, :], D, WPG_d[:, k0:k0 + 1, :], r_WPG))
            w_pl_v = w_ple_d.rearrange("(k p) n -> p k n", p=128)
            for k0 in range(2):
                prep.append((w_pl_v[:, k0:k0 + 1, :], D, WPL_d[:, k0:k0 + 1, :], r_WPL))
            prep_per_tile = -(-len(prep) // max(1, NT - NOWN))
            prep_pos = [0]

            def emit_prep(k):
                for _ in range(k):
                    if prep_pos[0] < len(prep):
                        pr = prep[prep_pos[0]]
                        prep_pos[0] += 1
                        cast_unit(pr[0], pr[1], "dram", pr[2], pr[3])

            def proj(hT_t, r_hT_, c0, n, bidx, col0=0):
                bk, r_bk = bank(bidx)
                for kc in range(8):
                    mm(bk[:, col0:col0 + n], hT_t[:, kc, :], Wb[:, kc, c0:c0 + n], kc == 0, kc == 7,
                       [r_hT_] + wres(c0, n), [r_bk])
                return bk, r_bk

            def rope_tm(bk, r_bk, cosap, sinap, r_c, r_s, out_bf, r_out):
                v4 = lambda ap: ap.rearrange("p (a b c) -> p a b c", a=8, b=2, c=32)
                kv = v4(bk[:, 0:512])
                av = v4(ra[:, :])
                ov = v4(out_bf[:, :])
                cb4 = cosap.unsqueeze(1).unsqueeze(1).to_broadcast([128, 8, 2, 32])
                sb3 = sinap.unsqueeze(1).to_broadcast([128, 8, 32])
                tt("dve", av, kv, cb4, ALU.mult, [r_bk, r_c], [r_ra])
                tt("dve", rb1[:], kv[:, :, 1, :], sb3, ALU.mult, [r_bk, r_s], [r_rb1])
                tt("dve", rb2[:], kv[:, :, 0, :], sb3, ALU.mult, [r_bk, r_s], [r_rb2])
                tt("dve", ov[:, :, 0, :], av[:, :, 0, :], rb1[:], ALU.subtract, [r_ra, r_rb1], [r_out])
                tt("dve", ov[:, :, 1, :], av[:, :, 1, :], rb2[:], ALU.add, [r_ra, r_rb2], [r_out])

            def conv_premul(bk, r_bk, wcol0, xw_cur):
                xc, r_xc = xw_cur
                for j in range(4):
                    tt("dve", xc[:, j, :], bk[:, 0:512], cmwb[:, j, wcol0:wcol0 + 512], ALU.mult,
                       [r_bk, r_cmwb], [r_xc[j]])

            def conv_mm(xw_cur, xw_prev, bidx_out):
                xc, r_xc = xw_cur
                xp, r_xp = xw_prev
                ob, r_ob = bank(bidx_out)
                for j in range(4):
                    mm(ob[:, 0:512], shifts[:, j, :], xc[:, j, :], j == 0, False, [r_shifts, r_xc[j]], [r_ob])
                for j in range(3):
                    mm(ob[:, 0:512], shifts[:, 4 + j, :], xp[:, j, :], False, j == 2, [r_shifts, r_xp[j]], [r_ob])
                return ob, r_ob

            def silu_tanh(cb, r_cb, out_bf, r_out, c):
                act(sig[:], cb[:, 0:512], AF.Exp, [r_cb], [r_sig], scale=-1.0 / c)
                act(sig[:], sig[:], AF.Ln, [r_sig], [r_sig], bias=1.0)
                act(sig[:], sig[:], AF.Exp, [r_sig], [r_sig], scale=-1.0)
                tt("dve", out_bf[:], cb[:, 0:512], sig[:], ALU.mult, [r_cb, r_sig], [r_out])

            km_scale = 128.0 ** -0.5
            CK = km_scale
            CQ = 1.0
            ts("dve", cmwb[:, :, 512:1024], cmwb[:, :, 512:1024], CK, None, ALU.mult, None, [r_cmwb], [r_cmwb])

            def stage_A1(g):
                x_t, r_x = xt[g % NXT]
                c_t, r_c = rc[g % NXT]
                s_t, r_s = rs[g % NXT]
                s1, r_s1 = st1[g % NXT]
                hb, r_hb = hbs[g % 2]
                dma(f"xt{g % NXT}", x_t[:], xa[g * 128:(g + 1) * 128, :], [], [r_x])
                dma(f"rc{g % NXT}", c_t[:], ropec_d[:, g, :], [], [r_c])
                dma(f"rs{g % NXT}", s_t[:], ropes_d[:, g, :], [], [r_s])
                act(junk[:, 0:D], x_t[:], AF.Square, [r_x], [r_junk, r_s1], accum_out=s1[:, 0:1])
                act(s1[:, 1:2], s1[:, 0:1], AF.Ln, [r_s1], [r_s1], bias=EPS, scale=1.0 / D)
                act(s1[:, 1:2], s1[:, 1:2], AF.Exp, [r_s1], [r_s1], scale=-0.5)
                stt("dve", hb[:], x_t[:], s1[:, 1:2], g1b[:], ALU.mult, ALU.mult, [r_x, r_s1, r_g1b], [r_hb])

            def stage_A2(g):
                hb, r_hb = hbs[g % 2]
                hT, r_hT = hTs[g % 2]
                transpose8(hb, r_hb, hT[:], r_hT, 8, bidx=0)

            def stage_D(g):
                u = g - G0
                hmb, r_hmb = hmbs[g % 2]
                transpose8(hmb, r_hmb, hmTs[:], r_hmTs, 4, bidx=0)
                dma("hmTs", HMT_d[:, :, u * 128:(u + 1) * 128], hmTs[:], [r_hmTs], [r_HMT])

            def stage_C(g):
                kmtok, r_kmtok = kmtoks[g % 2]
                vt, r_vt = vts[g % 2]
                for hp in range(2):
                    cbk, r_cbk = bank(5)
                    for hh in range(2):
                        h = 2 * hp + hh
                        mm(cbk[:, hh * 129:(hh + 1) * 129], kmtok[:, h * 128:(h + 1) * 128], vt[:, h, :], True, True,
                           [r_kmtok, r_vt], [r_cbk])
                    tt("dve", Cst[:, 2 * hp:2 * hp + 2, :], Chat[:, 2 * hp:2 * hp + 2, :],
                       cbk[:, 0:258].rearrange("p (a b) -> p a b", a=2), ALU.add, [r_Chat, r_cbk], [r_C])

            def stage_B(g):
                own = g >= G0
                u = g - G0
                hT, r_hT = hTs[g % 2]
                kmtok, r_kmtok = kmtoks[g % 2]
                vt, r_vt = vts[g % 2]
                c_t, r_c = rc[g % NXT]
                s_t, r_s = rs[g % NXT]
                bkG, r_bkG = proj(hT, r_hT, IFG, 8, 4, col0=16)
                tt("dve", gt[:], bkG[:, 16:24], bifb[:], ALU.add, [r_bkG, r_bifb], [r_gt])
                act(gs[:, 0:4], gt[:, 4:8], AF.Exp, [r_gt], [r_gs], scale=-1.0)
                act(gs[:, 0:4], gs[:, 0:4], AF.Ln, [r_gs], [r_gs], bias=1.0)
                bkM, r_bkM = proj(hT, r_hT, KM, 512, 3)
                bkK, r_bkK = proj(hT, r_hT, KA, 512, 1)
                bkV, r_bkV = proj(hT, r_hT, VA, 512, 2)
                bkW, r_bkW = proj(hT, r_hT, VM, 512, 6)
                if g - 1 >= G0:
                    stage_D(g - 1)
                conv_premul(bkM, r_bkM, 512, xwk[g % 2])
                rope_tm(bkK, r_bkK, c_t[:, :], s_t[:, :], r_c, r_s, krot, r_krot)
                va, r_va = vaug[g % 2]
                acopy(va[:, :, 0:128], bkV[:, 0:512].rearrange("p (a b) -> p a b", a=4), [r_bkV], [r_va])
                dma(f"vaug{g % 2}", VV_d[:, :, g, :].rearrange("h p c -> p h c"), va[:], [r_va], [r_VV])
                bk4, r_bk4 = bank(4)
                mm(bk4[:, 0:4], tri[:], gs[:, 0:4], True, True, [r_tri, r_gs], [r_bk4])
                mm(bk4[:, 8:12], ones[:], gs[:, 0:4], True, True, [r_ones, r_gs], [r_bk4])
                cb, r_cb = conv_mm(xwk[g % 2], xwk[(g + 1) % 2], 7)
                transpose8(krot, r_krot, ktT[:], r_ktT, 4, bidx=0)
                dma("ktT", KT_d[:, :, g * 128:(g + 1) * 128].rearrange("h p t -> p h t"), ktT[:], [r_ktT], [r_KT])
                if g > 0:
                    stage_C(g - 1)
                tt("dve", gs[:, 8:12], bk4[:, 0:4], gt[:, 0:4], ALU.add, [r_bk4, r_gt], [r_gs])
                tt("dve", gs[:, 8:12], gs[:, 8:12], bk4[:, 8:12], ALU.subtract, [r_gs, r_bk4], [r_gs])
                act(gs[:, 8:12], gs[:, 8:12], AF.Exp, [r_gs], [r_gs])
                act(gs[:, 12:16], bk4[:, 8:12], AF.Exp, [r_bk4], [r_gs], scale=-1.0)
                if own:
                    tcopy("dve", gs[:, 4:8], bk4[:, 8:12], [r_bk4], [r_gs])
                    tt("dve", gs[:, 16:20], gs[:, 4:8], bk4[:, 0:4], ALU.subtract, [r_gs, r_bk4], [r_gs])
                    act(gs[:, 16:20], gs[:, 16:20], AF.Exp, [r_gs], [r_gs])
                wk = gs[:, 8:12]
                dd = gs[:, 12:16]
                wp = gs[:, 16:20]
                silu_tanh(cb, r_cb, kmtok, r_kmtok, CK)
                tt("dve", vt[:, :, 0:128], bkW[:, 0:512].rearrange("p (a b) -> p a b", a=4),
                   wk.unsqueeze(2).to_broadcast([128, 4, 128]), ALU.mult, [r_bkW, r_gs], [r_vt])
                tcopy("dve", vt[:, :, 128:129], wk.unsqueeze(2), [r_gs], [r_vt])
                tt("pool", Chat[:], Cst[:], dd.unsqueeze(2).to_broadcast([128, 4, 129]), ALU.mult,
                   [r_C, r_gs], [r_Chat])

                if own:
                    hTo = hT
                    bkq, r_bkq = proj(hTo, r_hT, QA, 512, 1)
                    bkqm, r_bkqm = proj(hTo, r_hT, QM, 512, 2)
                    bkom, r_bkom = proj(hTo, r_hT, OM, 512, 3)
                    conv_premul(bkqm, r_bkqm, 0, xwq[g % 2])
                    rope_tm(bkq, r_bkq, c_t[:, :], s_t[:, :], r_c, r_s, krot, r_krot)
                    cbq, r_cbq = conv_mm(xwq[g % 2], xwq[(g + 1) % 2], 7)
                    transpose8(krot, r_krot, ktT[:], r_ktT, 4, bidx=0)
                    dma("ktT", QT_d[:, :, u * 128:(u + 1) * 128].rearrange("h p t -> p h t"), ktT[:],
                        [r_ktT], [r_QT])
                    bkga0, r_bkga0 = proj(hTo, r_hT, GA, 512, 1)
                    bkga1, r_bkga1 = proj(hTo, r_hT, GA + 512, 512, 2)
                    bkgb0, r_bkgb0 = proj(hTo, r_hT, GB, 512, 5)
                    silu_tanh(cbq, r_cbq, qmtok, r_qmtok, CQ)
                    act(oms[:], bkom[:, 0:512], AF.Sigmoid, [r_bkom], [r_oms])
                    act(sgs[:, 0:512], bkga0[:, 0:512], AF.Sigmoid, [r_bkga0], [r_sgs])
                    act(sgs[:, 512:1024], bkga1[:, 0:512], AF.Sigmoid, [r_bkga1], [r_sgs])
                    act(sgs[:, 1024:1536], bkgb0[:, 0:512], AF.Sigmoid, [r_bkgb0], [r_sgs])
                    transpose8(qmtok, r_qmtok, qmT[:], r_qmT, 4, bidx=0)
                    transpose8(kmtok, r_kmtok, kmT[:], r_kmT, 4, bidx=0)
                    bkgb1, r_bkgb1 = proj(hTo, r_hT, GB + 512, 512, 3)
                    act(sgs[:, 1536:2048], bkgb1[:, 0:512], AF.Sigmoid, [r_bkgb1], [r_sgs])
                    dma("sgs", SG_d[u * 128:(u + 1) * 128, :], sgs[:], [r_sgs], [r_SG])
                    bk7, r_bk7 = bank(7)
                    for h in range(H):
                        mm(bk7[:, h * 128:(h + 1) * 128], kmT[:, h, :], qmT[:, h, :], True, True,
                           [r_kmT, r_qmT], [r_bk7])
                    tt("dve", ATs[:], bk7[:, 0:512].rearrange("p (a b) -> p a b", a=4),
                       tri[:, :].unsqueeze(1).to_broadcast([128, 4, 128]), ALU.mult, [r_bk7, r_tri], [r_ATs])
                    acopy(Chb[:], Chat[:], [r_Chat], [r_Chb])
                    obs = [bank(6), bank(4)]
                    for hp in range(2):
                        ob, r_ob = obs[hp]
                        for hh in range(2):
                            h = 2 * hp + hh
                            oap = ob[:, hh * 129:(hh + 1) * 129]
                            mm(oap, qmT[:, h, :], Chb[:, h, :], True, False, [r_qmT, r_Chb], [r_ob])
                            mm(oap, ATs[:, h, :], vt[:, h, :], False, True, [r_ATs, r_vt], [r_ob])
                    for hp in range(2):
                        ob, r_ob = obs[hp]
                        for hh in range(2):
                            h = 2 * hp + hh
                            oap = ob[:, hh * 129:(hh + 1) * 129]
                            act(rr[:, 4 * hp + hh:4 * hp + hh + 1], oap[:, 128:129], AF.Abs, [r_ob, r_gs], [r_rr],
                                scale=wp[:, h:h + 1])
                    for hp in range(2):
                        ob, r_ob = obs[hp]
                        for hh in range(2):
                            h = 2 * hp + hh
                            oap = ob[:, hh * 129:(hh + 1) * 129]
                            c0 = 4 * hp + hh
                            ts("dve", rr[:, c0:c0 + 1], rr[:, c0:c0 + 1], 1.0, None, ALU.max, None, [r_rr], [r_rr])
                            recip(rr[:, c0:c0 + 1], rr[:, c0:c0 + 1], [r_rr], [r_rr])
                            tt("dve", rr[:, c0 + 2:c0 + 3], rr[:, c0:c0 + 1], wp[:, h:h + 1], ALU.mult, [r_rr, r_gs], [r_rr])
                            ts("dve", hmf[:, h, :], oap[:, 0:128], rr[:, c0 + 2:c0 + 3], None, ALU.mult, None,
                               [r_ob, r_rr], [r_hmf])
                    for h in range(H):
                        act(junk[:, 0:128], hmf[:, h, :], AF.Copy, [r_hmf], [r_junk, r_mst],
                            accum_out=mst[:, h, 0:1])
                        act(junk[:, 0:128], hmf[:, h, :], AF.Square, [r_hmf], [r_junk, r_mst],
                            accum_out=mst[:, h, 1:2])
                    ts("dve", mst[:, :, 2:3], mst[:, :, 0:1], 1.0 / 128, None, ALU.mult, None, [r_mst], [r_mst])
                    tt("dve", mst[:, :, 3:4], mst[:, :, 2:3], mst[:, :, 2:3], ALU.mult, [r_mst], [r_mst])
                    stt("dve", mst[:, :, 4:5], mst[:, :, 1:2], 1.0 / 128, mst[:, :, 3:4], ALU.mult, ALU.subtract,
                        [r_mst], [r_mst])
                    act(mst[:, :, 4:5], mst[:, :, 4:5], AF.Ln, [r_mst], [r_mst], bias=EPS)
                    act(mst[:, :, 5:6], mst[:, :, 4:5], AF.Exp, [r_mst], [r_mst], scale=-0.5)
                    for h in range(H):
                        ts("dve", hmf[:, h, :], hmf[:, h, :], mst[:, h, 2:3], mst[:, h, 5:6], ALU.subtract, ALU.mult,
                           [r_hmf, r_mst], [r_hmf])
                    tt("dve", hmf[:], hmf[:], gmlb[:, :].unsqueeze(1).to_broadcast([128, 4, 128]), ALU.mult,
                       [r_hmf, r_gmlb], [r_hmf])
                    hmb, r_hmb = hmbs[g % 2]
                    tt("dve", hmb[:], hmf[:].rearrange("p a b -> p (a b)"), oms[:], ALU.mult, [r_hmf, r_oms], [r_hmb])

            stage_A1(0)
            if NT > 1:
                stage_A1(1)
            stage_A2(0)
            for g in range(NT):
                if g + 2 < NT:
                    stage_A1(g + 2)
                if g + 1 < NT:
                    stage_A2(g + 1)
                if g < NT - NOWN:
                    emit_prep(prep_per_tile)
                stage_B(g)
            stage_D(NT - 1)
            stage_C(NT - 1)
            emit_prep(len(prep))

        scale = 64.0 ** -0.5
        P.barrier()
        es23 = ExitStack()
        es23.__enter__()
        Wya, r_Wya = sbuf(es23, "Wya", [128, 4, D], BF16, ro=True)
        Wyb, r_Wyb = sbuf(es23, "Wyb", [128, 4, D], BF16, ro=True)
        Wo, r_Wo = sbuf(es23, "Wo", [128, 8, D], BF16, ro=True)
        Wpg, r_Wpg = sbuf(es23, "Wpg", [128, 8, D], BF16, ro=True)
        Wpl, r_Wpl = sbuf(es23, "Wpl", [128, 2, D], BF16, ro=True)
        g2b, r_g2b = sbuf(es23, "g2b", [128, D], F32, ro=True)
        cfw, r_cfw = sbuf(es23, "cfw", [128, NFC, 3], F32, ro=True)
        cfb, r_cfb = sbuf(es23, "cfb", [128, NFC], F32, ro=True)
        dma("Wya", Wya[:], WYA_d, [r_WYA], [r_Wya])
        dma("Wyb", Wyb[:], WYB_d, [r_WYB], [r_Wyb])
        dma("Wo", Wo[:], WO_d, [r_WO], [r_Wo])
        dma("Wpg", Wpg[:], WPG_d, [r_WPG], [r_Wpg])
        dma("Wpl", Wpl[:], WPL_d, [r_WPL], [r_Wpl])
        dma("g2b", g2b[:], g2_d.partition_broadcast(128), [], [r_g2b])
        dma("cfw", cfw[:], cfw_d, [], [r_cfw])
        dma("cfb", cfb[:], cfb_d, [], [r_cfb])
        with ExitStack() as es2:
            biast, r_biast = sbuf(es2, "biast", [128, NSLOT], F32, ro=True)
            pat, r_pat = sbuf(es2, "pat", [128, 4, 512], BF16, ro=True)
            gdab, r_gdab = sbuf(es2, "gdab", [128, 128], F32, ro=True)
            qt_s = [sbuf(es2, f"qts{i}", [128, 512], BF16) for i in range(2)]
            kg = [sbuf(es2, f"kg{i}", [128, NTS * 128], BF16) for i in range(2)]
            vg = [sbuf(es2, f"vg{i}", [128, NTS, 130], BF16) for i in range(2)]
            pts = [sbuf(es2, f"pts{i}", [128, 2, 512], BF16) for i in range(3)]
            aof, r_aof = sbuf(es2, "aof", [128, BT, 4, 128], F32)
            aob, r_aob = sbuf(es2, "aob", [128, 512], BF16)
            aoTs, r_aoTs = sbuf(es2, "aoTs", [128, 4, 128], BF16)
            tmpa, r_tmpa = sbuf(es2, "tmpa", [128, 128], F32)
            rl, r_rl = sbuf(es2, "rl", [128, 8], F32)
            dma("biast", biast[:], biast_d, [], [r_biast])
            dma("pat", pat[:], pat_d, [], [r_pat])
            dma("gdab", gdab[:], gda_d.partition_broadcast(128), [], [r_gdab])
            ts("dve", gdab[:], gdab[:], 1.0 - LAM_INIT, None, ALU.mult, None, [r_gdab], [r_gdab])

            S_t = [bank2(0), bank2(2)]
            O_bk = [[bank(4), bank(5)], [bank(6), bank(7)]]
            ld_i = 0
            pt_i = 0
            s_i = 0
            for bi, tiles in enumerate(blocks):
                nq = len(tiles)
                N = nq * 128
                t0 = tiles[0]
                klist = []
                for kt in range(NT):
                    ku = kt - G0
                    if bi == 0:
                        if kt < G0:
                            klist.append((kt, None))
                        elif kt == G0:
                            klist.append((kt, 0))
                    else:
                        if ku < t0:
                            klist.append((kt, None))
                        elif ku < t0 + nq:
                            klist.append((kt, ku - t0))
                groups = {}
                for kt, kind in klist:
                    groups.setdefault(kt // NTS, []).append((kt, kind))
                gkeys = sorted(groups)
                for h in range(H):
                    q_s, r_q = qt_s[(bi * H + h) % 2]
                    dma(f"qts{(bi * H + h) % 2}", q_s[:, 0:N], QT_d[h, :, t0 * 128:t0 * 128 + N], [r_QT], [r_q])
                    total_k = len(klist)
                    loaded = {}
                    seq = []
                    for gk in gkeys:
                        for (kt, kind) in groups[gk]:
                            seq.append((gk, kt, kind))

                    def ensure_group(gk):
                        nonlocal ld_i
                        if gk not in loaded:
                            k_s, r_k = kg[ld_i % 2]
                            v_s, r_v = vg[ld_i % 2]
                            dma(f"kg{ld_i % 2}", k_s[:], KT_d[h, :, gk * NTS * 128:(gk + 1) * NTS * 128], [r_KT], [r_k])
                            dma(f"vg{ld_i % 2}", v_s[:], VV_d[h, :, gk * NTS:(gk + 1) * NTS, :], [r_VV], [r_v])
                            ld_i += 1
                            loaded[gk] = (k_s, r_k, v_s, r_v)
                        return loaded[gk]

                    def emit_s(idx):
                        nonlocal s_i, pt_i
                        gk, kt, kind = seq[idx]
                        k_s, r_k, v_s, r_v = ensure_group(gk)
                        kl = kt - gk * NTS
                        St, r_sa, r_sb = S_t[s_i % 2]
                        s_i += 1
                        mm(St[:, 0, 0:N], k_s[0:64, kl * 128:(kl + 1) * 128], q_s[0:64, 0:N], True, True,
                           [r_k, r_q], [r_sa])
                        mm(St[:, 1, 0:N], k_s[64:128, kl * 128:(kl + 1) * 128], q_s[64:128, 0:N], True, True,
                           [r_k, r_q], [r_sb])
                        p_s, r_p = pts[pt_i % 3]
                        pt_i += 1
                        act(p_s[:, :, 0:N], St[:, :, 0:N], AF.Exp, [r_sa, r_sb, r_biast], [r_p],
                            bias=biast[:, gk:gk + 1], scale=scale)
                        if kind is not None:
                            tt("pool", p_s[:, :, 0:N], p_s[:, :, 0:N],
                               pat[:, kind, 0:N].unsqueeze(1).to_broadcast([128, 2, N]), ALU.mult,
                               [r_p, r_pat], [r_p])
                        return p_s, r_p

                    def emit_pv(idx, p_s, r_p):
                        gk, kt, kind = seq[idx]
                        k_s, r_k, v_s, r_v = loaded[gk]
                        kl = kt - gk * NTS
                        for sub in range(nq):
                            for m in range(2):
                                ob, r_ob = O_bk[m][sub // 2]
                                mm(ob[:, (sub % 2) * 129:(sub % 2 + 1) * 129],
                                   p_s[:, m, sub * 128:(sub + 1) * 128], v_s[:, kl, 0:129],
                                   idx == 0 and (sub % 2 == 0), idx == total_k - 1, [r_p, r_v], [r_ob], skip=True)

                    pend = emit_s(0)
                    for idx in range(total_k):
                        nxt = emit_s(idx + 1) if idx + 1 < total_k else None
                        emit_pv(idx, *pend)
                        pend = nxt
                    for sub in range(nq):
                        o0, r_o0 = O_bk[0][sub // 2]
                        o1, r_o1 = O_bk[1][sub // 2]
                        o0 = o0[:, (sub % 2) * 129:(sub % 2 + 1) * 129]
                        o1 = o1[:, (sub % 2) * 129:(sub % 2 + 1) * 129]
                        ts("dve", rl[:, 0:1], o0[:, 128:129], 1e-30, None, ALU.max, None, [r_o0], [r_rl])
                        ts("dve", rl[:, 1:2], o1[:, 128:129], 1e-30, None, ALU.max, None, [r_o1], [r_rl])
                        recip(rl[:, 2:4], rl[:, 0:2], [r_rl], [r_rl])
                        tt("dve", rl[:, 4:5], rl[:, 3:4], nlam[:, 0:1], ALU.mult, [r_rl, r_nlam], [r_rl])
                        ts("dve", tmpa[:], o1[:, 0:128], rl[:, 4:5], None, ALU.mult, None, [r_o1, r_rl], [r_tmpa])
                        stt("dve", aof[:, sub, h, :], o0[:, 0:128], rl[:, 2:3], tmpa[:], ALU.mult, ALU.add,
                            [r_o0, r_rl, r_tmpa], [r_aof])
                for sub in range(nq):
                    for h in range(H):
                        act(junk[:, 0:128], aof[:, sub, h, :], AF.Square, [r_aof], [r_junk, r_ssq],
                            accum_out=ssq[:, 4 + h:5 + h])
                    act(ssq[:, 4:8], ssq[:, 4:8], AF.Sqrt, [r_ssq], [r_ssq], bias=EPS, scale=1.0 / 128)
                    recip(ssq[:, 4:8], ssq[:, 4:8], [r_ssq], [r_ssq])
                    for h in range(H):
                        stt("dve", aob[:, h * 128:(h + 1) * 128], aof[:, sub, h, :], ssq[:, 4 + h:5 + h], gdab[:],
                            ALU.mult, ALU.mult, [r_aof, r_ssq, r_gdab], [r_aob])
                    transpose8(aob, r_aob, aoTs[:], r_aoTs, 4, bidx=0)
                    u = t0 + sub
                    dma("aoTs", AOT_d[:, :, u * 128:(u + 1) * 128], aoTs[:], [r_aoTs], [r_AOT])

        P.barrier()
        with ExitStack() as es3:
            wdn = [sbuf(es3, f"wdn{i}", [128, 2, D], BF16) for i in range(2)]
            wup = [sbuf(es3, f"wup{i}", [128, 8, 2, 256], BF16) for i in range(2)]
            aoT, r_aoT = sbuf(es3, "aoT", [128, 4, 512], BF16)
            hmT, r_hmT = sbuf(es3, "hmT", [128, 4, 512], BF16)
            sg3s = [sbuf(es3, f"sg3_{i}", [128, 2 * D], BF16) for i in range(2)]
            x3t = [sbuf(es3, f"x3t{i}", [128, D], F32) for i in range(2)]
            m2, r_m2 = sbuf(es3, "m2", [128, D], F32)
            mbs = [sbuf(es3, f"mb{i}", [128, D], BF16) for i in range(2)]
            mT, r_mT = sbuf(es3, "mT", [128, 8, 128], BF16)
            x1b, r_x1b = sbuf(es3, "x1b", [128, BT, D], F32)
            h2bs = [sbuf(es3, f"h2b{i}", [128, D], BF16) for i in range(2)]
            h2T, r_h2T = sbuf(es3, "h2T", [128, 8, 512], BF16)
            aext, r_aext = sbuf(es3, "aext", [128, 2 + 512], F32)
            halo = [sbuf(es3, f"halo{i}", [128, NFC, 2], F32) for i in range(2)]
            acc, r_acc = sbuf(es3, "acc", [128, 512], F32)
            gel, r_gel = sbuf(es3, "gel", [128, 512], F32)
            uT, r_uT = sbuf(es3, "uT", [128, NFC, 512], BF16)
            x2bf, r_x2bf = sbuf(es3, "x2bf", [128, D], BF16)
            x2Ts = [sbuf(es3, f"x2T{i}", [128, 8, 128], BF16) for i in range(2)]
            gsb, r_gsb = sbuf(es3, "gsb", [128, D], F32)
            pf, r_pf = sbuf(es3, "pf", [128, 2, 128], F32)
            pbs = [sbuf(es3, f"pb{i}", [128, 2, 128], BF16) for i in range(2)]
            m1, r_m1 = gsb, r_gsb
            yo = x3t

            for i in range(2):
                memset("pool", halo[i][0][:], 0.0, [halo[i][1]])

            wu_i = 0
            wd_i = 0
            yo_i = 0
            x_i = 0
            for bi, tiles in enumerate(blocks):
                nq = len(tiles)
                N = nq * 128
                t0 = tiles[0]
                last_is_halo = bi == 0
                dma("aoT", aoT[:, :, 0:N], AOT_d[:, :, t0 * 128:t0 * 128 + N], [r_AOT], [r_aoT])
                dma("hmT", hmT[:, :, 0:N], HMT_d[:, :, t0 * 128:t0 * 128 + N], [r_HMT], [r_hmT])
                xbuf = {}

                def stage_MA(ti):
                    nonlocal x_i
                    u = tiles[ti]
                    x_t, r_x = x3t[x_i % 2]
                    xbuf[ti] = (x_t, r_x)
                    dma(f"x3t{x_i % 2}", x_t[:], xa[(G0 + u) * 128:(G0 + u + 1) * 128, :], [], [r_x])
                    x_i += 1
                    sg3, r_sg3 = sg3s[ti % 2]
                    dma(f"sg3_{ti % 2}", sg3[:], SG_d[u * 128:(u + 1) * 128, :], [r_SG], [r_sg3])
                    ya, r_ya0, r_ya1 = bank2(0)
                    yb, r_yb0, r_yb1 = bank2(2)
                    for half, (ra_, rb_) in enumerate(((r_ya0, r_yb0), (r_ya1, r_yb1))):
                        for c in range(4):
                            mm(ya[:, half, :], aoT[:, c, ti * 128:(ti + 1) * 128], Wya[:, c, half * 512:(half + 1) * 512],
                               c == 0, c == 3, [r_aoT, r_Wya], [ra_])
                        for c in range(4):
                            mm(yb[:, half, :], hmT[:, c, ti * 128:(ti + 1) * 128], Wyb[:, c, half * 512:(half + 1) * 512],
                               c == 0, c == 3, [r_hmT, r_Wyb], [rb_])
                    mb, r_mb = mbs[ti % 2]
                    tt("dve", m1[:].rearrange("p (a b) -> p a b", a=2), ya[:, :, :],
                       sg3[:, 0:D].rearrange("p (a b) -> p a b", a=2), ALU.mult, [r_ya0, r_ya1, r_sg3], [r_m1])
                    tt("dve", m2[:].rearrange("p (a b) -> p a b", a=2), yb[:, :, :],
                       sg3[:, D:2 * D].rearrange("p (a b) -> p a b", a=2), ALU.mult, [r_yb0, r_yb1, r_sg3], [r_m2])
                    tt("dve", mb[:], m1[:], m2[:], ALU.add, [r_m1, r_m2], [r_mb])

                def stage_MB(ti):
                    mb, r_mb = mbs[ti % 2]
                    x_t, r_x = xbuf[ti]
                    h2b, r_h2b = h2bs[ti % 2]
                    transpose8(mb, r_mb, mT[:], r_mT, 8, bidx=4)
                    xo, r_xo0, r_xo1 = bank2(6)
                    for half, rr_ in enumerate((r_xo0, r_xo1)):
                        for kc in range(8):
                            mm(xo[:, half, :], mT[:, kc, :], Wo[:, kc, half * 512:(half + 1) * 512], kc == 0, kc == 7,
                               [r_mT, r_Wo], [rr_])
                    tt("dve", x1b[:, ti, :].rearrange("p (a b) -> p a b", a=2), xo[:, :, :],
                       x_t[:].rearrange("p (a b) -> p a b", a=2), ALU.add, [r_xo0, r_xo1, r_x], [r_x1b])
                    rstd = rms_stats(x1b[:, ti, :], D, r_x1b)
                    stt("dve", h2b[:], x1b[:, ti, :], rstd, g2b[:], ALU.mult, ALU.mult, [r_x1b, r_ssq, r_g2b], [r_h2b])

                def stage_MC(ti):
                    h2b, r_h2b = h2bs[ti % 2]
                    transpose8(h2b, r_h2b, h2T[:, :, ti * 128:(ti + 1) * 128], r_h2T, 8, bidx=5)

                stage_MA(0)
                for ti in range(nq):
                    if ti + 1 < nq:
                        stage_MA(ti + 1)
                    stage_MB(ti)
                    if ti >= 1:
                        stage_MC(ti - 1)
                stage_MC(nq - 1)

                h_in, r_hin = halo[bi % 2]
                h_out, r_hout = halo[(bi + 1) % 2]
                for fc0 in range(0, NFC, 2):
                    w_s, r_w = wup[wu_i % 2]
                    wu_i += 1
                    dma(f"wup{(wu_i - 1) % 2}", w_s[:, :, 0, :], WUP_d[:, :, fc0 * 128:fc0 * 128 + 256], [r_WUP], [r_w])
                    if not last_is_halo:
                        dma(f"wup{(wu_i - 1) % 2}", w_s[:, :, 1, :], WUP_d[:, :, DFF + fc0 * 128:DFF + fc0 * 128 + 256],
                            [r_WUP], [r_w])
                    for fi in range(2):
                        fc = fc0 + fi
                        ab, r_ab = bank(2 * fi)
                        for kc in range(8):
                            mm(ab[:, 0:N], w_s[:, kc, 0, fi * 128:(fi + 1) * 128], h2T[:, kc, 0:N], kc == 0, kc == 7,
                               [r_w, r_h2T], [r_ab])
                        acopy(aext[:, 2:2 + N], ab[:, 0:N], [r_ab], [r_aext])
                        tcopy("dve", aext[:, 0:2], h_in[:, fc, :], [r_hin], [r_aext])
                        tcopy("dve", h_out[:, fc, :], aext[:, N:N + 2], [r_aext], [r_hout])
                        if last_is_halo:
                            continue
                        gb_, r_gb = bank(2 * fi + 1)
                        for kc in range(8):
                            mm(gb_[:, 0:N], w_s[:, kc, 1, fi * 128:(fi + 1) * 128], h2T[:, kc, 0:N], kc == 0, kc == 7,
                               [r_w, r_h2T], [r_gb])
                        ts("dve", acc[:, 0:N], aext[:, 0:N], cfw[:, fc, 0:1], cfb[:, fc:fc + 1], ALU.mult, ALU.add,
                           [r_aext, r_cfw, r_cfb], [r_acc])
                        stt("dve", acc[:, 0:N], aext[:, 1:N + 1], cfw[:, fc, 1:2], acc[:, 0:N], ALU.mult, ALU.add,
                            [r_aext, r_cfw, r_acc], [r_acc])
                        stt("dve", acc[:, 0:N], aext[:, 2:N + 2], cfw[:, fc, 2:3], acc[:, 0:N], ALU.mult, ALU.add,
                            [r_aext, r_cfw, r_acc], [r_acc])
                        act(gel[:, 0:N], acc[:, 0:N], AF.Gelu_apprx_tanh, [r_acc], [r_gel])
                        tt("dve", uT[:, fc, 0:N], gel[:, 0:N], gb_[:, 0:N], ALU.mult, [r_gel, r_gb], [r_uT])
                if last_is_halo:
                    continue

                for fc0 in range(0, NFC, 2):
                    wd, r_wd = wdn[wd_i % 2]
                    wd_i += 1
                    dma(f"wdn{(wd_i - 1) % 2}", wd[:], WDN_d[:, fc0:fc0 + 2, :], [r_WDN], [r_wd])
                    for fi in range(2):
                        fc = fc0 + fi
                        for ti in range(nq):
                            for half in range(2):
                                bkx, r_bkx = bank(2 * ti + half)
                                mm(bkx[:, :], uT[:, fc, ti * 128:(ti + 1) * 128], wd[:, fi, half * 512:(half + 1) * 512],
                                   fc == 0, fc == NFC - 1, [r_uT, r_wd], [r_bkx])
                for ti in range(nq):
                    xd, r_xd0, r_xd1 = bank2(2 * ti)
                    tt("dve", x1b[:, ti, :].rearrange("p (a b) -> p a b", a=2), xd[:, :, :],
                       x1b[:, ti, :].rearrange("p (a b) -> p a b", a=2), ALU.add, [r_xd0, r_xd1, r_x1b], [r_x1b])
                def stage_PA(ti):
                    u = tiles[ti]
                    x2 = x1b[:, ti, :]
                    x2T, r_x2T = x2Ts[ti % 2]
                    pb, r_pb = pbs[ti % 2]
                    tcopy("act", x2bf[:], x2, [r_x1b], [r_x2bf])
                    transpose8(x2bf, r_x2bf, x2T[:], r_x2T, 8, bidx=6)
                    tok0 = (u - 1) * 128
                    dma("pf", pf[:], pT_d[:, :, tok0:tok0 + 128], [], [r_pf])
                    tcopy("dve", pb[:], pf[:], [r_pf], [r_pb])

                def stage_PB(ti):
                    nonlocal yo_i
                    u = tiles[ti]
                    x2 = x1b[:, ti, :]
                    r_x2 = r_x1b
                    x2T, r_x2T = x2Ts[ti % 2]
                    pb, r_pb = pbs[ti % 2]
                    tok0 = (u - 1) * 128
                    gp, r_gp0, r_gp1 = bank2(0)
                    pp, r_pp0, r_pp1 = bank2(2)
                    for half, (rg_, rp_) in enumerate(((r_gp0, r_pp0), (r_gp1, r_pp1))):
                        for kc in range(8):
                            mm(gp[:, half, :], x2T[:, kc, :], Wpg[:, kc, half * 512:(half + 1) * 512], kc == 0, kc == 7,
                               [r_x2T, r_Wpg], [rg_])
                        for c in range(2):
                            mm(pp[:, half, :], pb[:, c, :], Wpl[:, c, half * 512:(half + 1) * 512], c == 0, c == 1,
                               [r_pb, r_Wpl], [rp_])
                    act(gsb[:].rearrange("p (a b) -> p a b", a=2), gp[:, :, :], AF.Sigmoid, [r_gp0, r_gp1], [r_gsb])
                    tt("dve", gsb[:].rearrange("p (a b) -> p a b", a=2), pp[:, :, :],
                       gsb[:].rearrange("p (a b) -> p a b", a=2), ALU.mult, [r_pp0, r_pp1, r_gsb], [r_gsb])
                    tt("dve", x2, x2, gsb[:], ALU.add, [r_x2, r_gsb], [r_x2])
                    rstd = rms_stats(x2, D, r_x2)
                    y_t, r_yt = yo[yo_i % 2]
                    yo_i += 1
                    stt("dve", y_t[:], x2, rstd, gFb[:], ALU.mult, ALU.mult, [r_x2, r_ssq, r_gFb], [r_yt])
                    dma(f"x3t{(yo_i - 1) % 2}", y_d[tok0:tok0 + 128, :], y_t[:], [r_yt], [r_y])

                stage_PA(0)
                for ti in range(nq):
                    if ti + 1 < nq:
                        stage_PA(ti + 1)
                    stage_PB(ti)

            P.wait_all("sp", [r_y, yo[0][1], yo[1][1]])
            P.wait_all("pool", [r_y, yo[0][1], yo[1][1]])
            P.wait_all("act", [r_y, yo[0][1], yo[1][1]])
        es23.close()
        P.run()
    return nc, P


def host_constants(NTS):
    NT = NSLOT * NTS
    bf = ml_dtypes.bfloat16
    c = {}
    c["identb"] = np.eye(128, dtype=np.float32).astype(bf)
    s = np.arange(128)
    c["tri"] = (s[:, None] <= s[None, :]).astype(np.float32)
    c["ones"] = np.ones((128, 128), np.float32)
    sh = np.zeros((128, 7, 128), np.float32)
    for j in range(4):
        for t in range(128):
            sidx = t - 3 + j
            if 0 <= sidx < 128:
                sh[sidx, j, t] = 1.0
    for j in range(3):
        for t in range(128):
            sidx = 128 + t - 3 + j
            if 0 <= sidx < 128:
                sh[sidx, 4 + j, t] = 1.0
    c["shifts"] = sh.astype(bf)
    pat = np.zeros((128, 4, 512), np.float32)
    k = np.arange(128)
    q = np.arange(512)
    for o in range(4):
        pat[:, o, :] = (((128 * o + k) // 64)[:, None] <= (q // 64)[None, :]).astype(np.float32)
    c["pat"] = pat.astype(bf)
    inv_freq = (10000.0 ** (-np.arange(0, 64, 2, dtype=np.float32) / 64)).astype(np.float32)
    pos = (np.arange(NT)[None, :] * 128 + np.arange(128)[:, None]).astype(np.float32)
    ang = pos[:, :, None] * inv_freq[None, None, :]
    c["ropec"] = np.cos(ang).astype(np.float32)
    c["ropes"] = np.sin(ang).astype(np.float32)
    return c


_PROG_CACHE = {}


def make_in_maps(inputs, NTS, cores=None):
    SLOT = NTS * 128
    S = NSLOT * SLOT
    NT = NSLOT * NTS
    f32 = np.float32
    x = np.asarray(inputs["x"], f32).reshape(S, D)
    p = np.asarray(inputs["p"], f32).reshape(S, PLE)
    consts = host_constants(NTS)
    shared = {
        "w_in": np.ascontiguousarray(np.asarray(inputs["w_in"], f32)[0]),
        "w_ya": np.ascontiguousarray(np.asarray(inputs["w_ya"], f32)[0]),
        "w_yb": np.ascontiguousarray(np.asarray(inputs["w_yb"], f32)[0]),
        "w_o": np.ascontiguousarray(np.asarray(inputs["w_o"], f32)[0]),
        "w_up": np.ascontiguousarray(np.asarray(inputs["w_up"], f32)[0]),
        "w_down": np.ascontiguousarray(np.asarray(inputs["w_down"], f32)[0]),
        "w_ple": np.ascontiguousarray(np.asarray(inputs["w_ple"], f32)[0]),
        "w_pg": np.ascontiguousarray(np.asarray(inputs["w_pg"], f32)[0]),
        "g1": np.asarray(inputs["norm1_g"], f32).reshape(1, D),
        "g2": np.asarray(inputs["norm2_g"], f32).reshape(1, D),
        "gF": np.asarray(inputs["final_g"], f32).reshape(1, D),
        "gda": np.asarray(inputs["da_norm_g"], f32).reshape(1, 128),
        "gml": np.asarray(inputs["ml_norm_g"], f32).reshape(1, 128),
        "b_if": np.asarray(inputs["b_if"], f32).reshape(1, 8),
        "lamv": np.stack([np.asarray(inputs[k], f32).reshape(64) for k in ("lam_q1", "lam_k1", "lam_q2", "lam_k2")]).reshape(1, 256),
        "conv_m_w": np.ascontiguousarray(np.asarray(inputs["conv_m_w"], f32)[0]).reshape(1, 4 * D),
        "cfw": np.ascontiguousarray(np.asarray(inputs["conv_f_w"], f32)[0].T.reshape(NFC, 128, 3).transpose(1, 0, 2)),
        "cfb": np.ascontiguousarray(np.asarray(inputs["conv_f_b"], f32)[0].reshape(NFC, 128).T),
    }
    shared.update(consts)
    in_maps = []
    for r in (range(NCORES) if cores is None else cores):
        xa = np.zeros((NT * 128, D), f32)
        lo = (r - (NSLOT - 1)) * SLOT
        hi = (r + 1) * SLOT
        src_lo = max(lo, 0)
        xa[src_lo - lo:, :] = x[src_lo:hi]
        pT = np.ascontiguousarray(p[r * SLOT:(r + 1) * SLOT].T.reshape(2, 128, SLOT).transpose(1, 0, 2))
        biast = np.zeros((128, NSLOT), f32)
        for s_ in range(NSLOT):
            if s_ < NSLOT - 1 - r:
                biast[:, s_] = NEG
        m = dict(shared)
        m["xa"] = xa
        m["pT"] = pT
        m["biast"] = biast
        in_maps.append(m)
    return in_maps


def kernel(**inputs):
    S = int(np.asarray(inputs["x"]).shape[1])
    NTS = S // (NSLOT * 128)
    if NTS not in _PROG_CACHE:
        _PROG_CACHE[NTS] = build_program(NTS)[0]
    nc = _PROG_CACHE[NTS]
    in_maps = make_in_maps(inputs, NTS)
    res = run_bass_kernel_spmd(nc, in_maps, core_ids=list(range(NCORES)))
    out = np.concatenate([np.asarray(r["y"], np.float32) for r in res.results], axis=0)
    return out.reshape(1, S, D)
```

```python
import math
from contextlib import ExitStack

import numpy as np
import ml_dtypes

import concourse.bass as bass
import concourse.mybir as mybir
from concourse.bass_utils import run_bass_kernel_spmd

F32 = mybir.dt.float32
BF16 = mybir.dt.bfloat16
AF = mybir.ActivationFunctionType
ALU = mybir.AluOpType

NCORES = 8
NSLOT = 8
D = 1024
NIN = 5640
DFF = 2816
NFC = DFF // 128
PLE = 256
H = 4
EPS = 1e-6
LAM_INIT = 0.8 - 0.6 * math.exp(-0.3 * 0)
QA, KA, VA, QM, KM, VM, OM, IFG, GA, GB = 0, 512, 1024, 1536, 2048, 2560, 3072, 3584, 3592, 4616
NEG = -30000.0


class Res:
    __slots__ = ("name", "w", "r", "ro")

    def __init__(self, name, ro=False):
        self.name = name
        self.w = None
        self.r = []
        self.ro = ro


class Prog:
    ENGS = ("pe", "act", "dve", "pool", "sp")

    def __init__(self, nc):
        self.nc = nc
        self.q = {e: [] for e in self.ENGS}
        self.sem = {e: nc.alloc_semaphore(f"c_{e}") for e in self.ENGS}
        self.cnt = {e: 0 for e in self.ENGS}
        self.seen = {e: {} for e in self.ENGS}
        self.dsem = {}
        self.dknow = {}
        self.hist = {e: {} for e in self.ENGS}
        self.n_wait = 0
        self.n_ins = 0

    def _merge(self, eng, know):
        se = self.seen[eng]
        for k, v in know.items():
            if se.get(k, 0) < v:
                se[k] = v

    def _need(self, eng, toks):
        se = self.seen[eng]
        cand = []
        for t in toks:
            if t is None:
                continue
            kind, key, val = t
            if kind == "d":
                val = self.dsem[key][1]
            cand.append((kind, key, val))
        cand.sort(key=lambda t: (t[0] == "c" and t[1] == eng, -t[2]))
        out = []
        for kind, key, val in cand:
            k = (kind, key)
            if se.get(k, 0) >= val:
                continue
            if kind == "c":
                sem = self.sem[key]
                know = self.hist[key].get(val)
            else:
                sem = self.dsem[key][0]
                know = self.dknow.get(key)
            se[k] = val
            if know:
                self._merge(eng, know)
            out.append((sem, val))
        return out

    def _deps(self, eng, reads, writes):
        toks = []
        for r in reads:
            toks.append(r.w)
        for w in writes:
            if not (eng == "pe" and w.w is not None and w.w[0] == "c" and w.w[1] == "pe"):
                toks.append(w.w)
            toks.extend(w.r)
        return toks

    def _record(self, tok, reads, writes):
        for r in reads:
            if not r.ro:
                r.r.append(tok)
        for w in writes:
            w.w = tok
            w.r = []

    def op(self, eng, fn, reads=(), writes=()):
        waits = self._need(eng, self._deps(eng, reads, writes))
        self.cnt[eng] += 1
        n = self.cnt[eng]
        tok = ("c", eng, n)
        snap = dict(self.seen[eng])
        snap[("c", eng)] = n
        self.hist[eng][n] = snap
        self.q[eng].append((waits, fn, self.sem[eng], 1))
        self.n_wait += len(waits)
        self.n_ins += 1
        self._record(tok, reads, writes)
        return tok

    def dma(self, eng, semname, fn, reads=(), writes=()):
        if semname not in self.dsem:
            self.dsem[semname] = [self.nc.alloc_semaphore(f"d_{semname}"), 0]
            self.dknow[semname] = {}
        waits = self._need(eng, self._deps(eng, reads, writes))
        ent = self.dsem[semname]
        ent[1] += 16
        tok = ("d", semname, ent[1])
        dk = self.dknow[semname]
        for k, v in self.seen[eng].items():
            if dk.get(k, 0) < v:
                dk[k] = v
        self.q[eng].append((waits, fn, ent[0], 16))
        self.n_wait += len(waits)
        self.n_ins += 1
        self._record(tok, reads, writes)
        return tok

    def wait_all(self, eng, ress):
        toks = []
        for r in ress:
            toks.append(r.w)
            toks.extend(r.r)
        waits = self._need(eng, toks)
        self.q[eng].append((waits, None, None, 0))

    def barrier(self):
        for eng in self.ENGS:
            toks = [("c", e2, self.cnt[e2]) for e2 in self.ENGS if self.cnt[e2] > 0]
            toks += [("d", name, ent[1]) for name, ent in self.dsem.items() if ent[1] > 0]
            waits = self._need(eng, toks)
            self.q[eng].append((waits, None, None, 0))

    def run(self):
        q = self.q

        def play(engobj, items):
            for waits, fn, sem, inc in items:
                if fn is None:
                    for (s, v) in waits:
                        engobj.wait_ge(s, v)
                    continue
                for (s, v) in waits[:-1]:
                    engobj.wait_ge(s, v)
                ins = fn(engobj)
                if waits:
                    ins._wait_ge(*waits[-1])
                if sem is not None:
                    ins.then_inc(sem, inc)

        with self.nc.Block() as block:
            @block.tensor
            def _(e):
                play(e, q["pe"])

            @block.scalar
            def _(e):
                play(e, q["act"])

            @block.vector
            def _(e):
                play(e, q["dve"])

            @block.gpsimd
            def _(e):
                play(e, q["pool"])

            @block.sync
            def _(e):
                play(e, q["sp"])


def build_program(NTS):
    NT = NSLOT * NTS
    NOWN = NTS + 1
    G0 = NT - NOWN
    SLOT = NTS * 128
    BT = min(4, NTS)
    blocks = [[0]] + [list(range(1 + b, 1 + b + BT)) for b in range(0, NTS, BT)]
    NOWNTOK = NOWN * 128

    nc = bass.Bass("TRN2", target_bir_lowering=False)
    P = Prog(nc)

    def din(name, shape, dt=F32):
        return nc.dram_tensor(name, list(shape), dt, kind="ExternalInput").ap()

    def dscr(name, shape, dt):
        return nc.dram_tensor(name, list(shape), dt).ap(), Res(name)

    xa = din("xa", [NT * 128, D])
    pT_d = din("pT", [128, 2, SLOT])
    w_in_d = din("w_in", [D, NIN])
    w_ya_d = din("w_ya", [512, D])
    w_yb_d = din("w_yb", [512, D])
    w_o_d = din("w_o", [D, D])
    w_up_d = din("w_up", [D, 2 * DFF])
    w_dn_d = din("w_down", [DFF, D])
    w_ple_d = din("w_ple", [PLE, D])
    w_pg_d = din("w_pg", [D, D])
    g1_d = din("g1", [1, D])
    g2_d = din("g2", [1, D])
    gF_d = din("gF", [1, D])
    gda_d = din("gda", [1, 128])
    gml_d = din("gml", [1, 128])
    bif_d = din("b_if", [1, 8])
    lam_d = din("lamv", [1, 256])
    cmw_d = din("conv_m_w", [1, 4 * D])
    cfw_d = din("cfw", [128, NFC, 3])
    cfb_d = din("cfb", [128, NFC])
    identb_d = din("identb", [128, 128], BF16)
    tri_d = din("tri", [128, 128])
    ones_d = din("ones", [128, 128])
    shifts_d = din("shifts", [128, 7, 128], BF16)
    pat_d = din("pat", [128, 4, 512], BF16)
    ropec_d = din("ropec", [128, NT, 32])
    ropes_d = din("ropes", [128, NT, 32])
    biast_d = din("biast", [128, NSLOT])
    y_d = nc.dram_tensor("y", [SLOT, D], F32, kind="ExternalOutput").ap()
    r_y = Res("y")

    KT_d, r_KT = dscr("KT", [H, 128, NT * 128], BF16)
    VV_d, r_VV = dscr("VV", [H, 128, NT, 130], BF16)
    QT_d, r_QT = dscr("QT", [H, 128, NOWNTOK], BF16)
    SG_d, r_SG = dscr("SG", [NOWNTOK, 2 * D], BF16)
    HMT_d, r_HMT = dscr("HMT", [128, H, NOWNTOK], BF16)
    AOT_d, r_AOT = dscr("AOT", [128, H, NOWNTOK], BF16)
    WUP_d, r_WUP = dscr("WUPb", [128, 8, 2 * DFF], BF16)
    WDN_d, r_WDN = dscr("WDNb", [128, NFC, D], BF16)
    WYA_d, r_WYA = dscr("WYAb", [128, 4, D], BF16)
    WYB_d, r_WYB = dscr("WYBb", [128, 4, D], BF16)
    WO_d, r_WO = dscr("WOb", [128, 8, D], BF16)
    WPG_d, r_WPG = dscr("WPGb", [128, 8, D], BF16)
    WPL_d, r_WPL = dscr("WPLb", [128, 2, D], BF16)

    def mm(out, lhsT, rhs, start, stop, reads, writes, skip=False):
        if skip:
            P.op("pe", lambda e: e.matmul(out, lhsT=lhsT, rhs=rhs, start=start, stop=stop, skip_group_check=True),
                 reads, writes)
        else:
            P.op("pe", lambda e: e.matmul(out, lhsT=lhsT, rhs=rhs, start=start, stop=stop), reads, writes)

    def tr(out, in_, ident, reads, writes):
        P.op("pe", lambda e: e.transpose(out=out, in_=in_, identity=ident), reads, writes)

    def act(out, in_, func, reads, writes, bias=0.0, scale=1.0, accum_out=None, eng="act"):
        if accum_out is None:
            P.op(eng, lambda e: e.activation(out=out, in_=in_, func=func, bias=bias, scale=scale), reads, writes)
        else:
            P.op(eng, lambda e: e.activation(out=out, in_=in_, func=func, bias=bias, scale=scale,
                                             accum_out=accum_out), reads, writes)

    def acopy(out, in_, reads, writes):
        P.op("act", lambda e: e.copy(out=out, in_=in_), reads, writes)

    def tcopy(eng, out, in_, reads, writes):
        if eng == "act":
            P.op(eng, lambda e: e.copy(out=out, in_=in_), reads, writes)
        else:
            P.op(eng, lambda e: e.tensor_copy(out=out, in_=in_), reads, writes)

    def tt(eng, out, in0, in1, op, reads, writes):
        P.op(eng, lambda e: e.tensor_tensor(out=out, in0=in0, in1=in1, op=op), reads, writes)

    def ts(eng, out, in0, s1, s2, op0, op1, reads, writes):
        if s2 is None:
            P.op(eng, lambda e: e.tensor_scalar(out=out, in0=in0, scalar1=s1, scalar2=None, op0=op0), reads, writes)
        else:
            P.op(eng, lambda e: e.tensor_scalar(out=out, in0=in0, scalar1=s1, scalar2=s2, op0=op0, op1=op1),
                 reads, writes)

    def stt(eng, out, in0, scalar, in1, op0, op1, reads, writes):
        P.op(eng, lambda e: e.scalar_tensor_tensor(out=out, in0=in0, scalar=scalar, in1=in1, op0=op0, op1=op1),
             reads, writes)

    def recip(out, in_, reads, writes):
        P.op("dve", lambda e: e.reciprocal(out=out, in_=in_), reads, writes)

    def memset(eng, ap, val, writes):
        P.op(eng, lambda e: e.memset(ap, val), (), writes)

    def dma(semname, out, in_, reads, writes, eng="sp"):
        P.dma(eng, semname, lambda e: e.dma_start(out=out, in_=in_), reads, writes)

    with ExitStack() as es_top:
        def sbuf(es, name, shape, dt=F32, ro=False):
            t = es.enter_context(nc.sbuf_tensor("s_" + name, list(shape), dt))
            return t, Res(name, ro=ro)

        banks = []
        for i in range(4):
            t = es_top.enter_context(nc.psum_tensor(f"pb{i}", [128, 2, 512], F32))
            banks.append((t, Res(f"pb{i}a")))
            banks.append((t, Res(f"pb{i}b")))

        def bank(i):
            t, r = banks[i]
            return t[:, i % 2, :], r

        def bank2(i):
            t, ra = banks[i]
            _, rb = banks[i + 1]
            return t, ra, rb

        esg = es_top
        identb, r_identb = sbuf(esg, "identb", [128, 128], BF16, ro=True)
        tri, r_tri = sbuf(esg, "tri", [128, 128], F32, ro=True)
        ones, r_ones = sbuf(esg, "ones", [128, 128], F32, ro=True)
        gFb, r_gFb = sbuf(esg, "gFb", [128, D], F32, ro=True)
        lam_t, r_lam = sbuf(esg, "lam_t", [128, 4, 64], F32)
        lam_s, r_lams = sbuf(esg, "lam_s", [128, 4], F32)
        nlam, r_nlam = sbuf(esg, "nlam", [128, 1], F32, ro=True)
        ssq, r_ssq = sbuf(esg, "ssq", [128, 8], F32)
        junk, r_junk = sbuf(esg, "junk", [128, D], F32)

        dma("identb", identb[:], identb_d, [], [r_identb])
        dma("tri", tri[:], tri_d, [], [r_tri])
        dma("ones", ones[:], ones_d, [], [r_ones])
        dma("gFb", gFb[:], gF_d.partition_broadcast(128), [], [r_gFb])
        dma("lam_t", lam_t[:].rearrange("p a b -> p (a b)"),
            lam_d.partition_broadcast(128), [], [r_lam])
        for i in range(2):
            tt("dve", lam_t[:, 2 * i, :], lam_t[:, 2 * i, :], lam_t[:, 2 * i + 1, :], ALU.mult, [r_lam], [r_lam])
            act(lam_t[:, 2 * i + 1, :], lam_t[:, 2 * i, :], AF.Copy, [r_lam], [r_lam, r_lams],
                accum_out=lam_s[:, i:i + 1])
        act(lam_s[:, 2:4], lam_s[:, 0:2], AF.Exp, [r_lams], [r_lams])
        tt("dve", lam_s[:, 0:1], lam_s[:, 3:4], lam_s[:, 2:3], ALU.subtract, [r_lams], [r_lams])
        ts("dve", nlam[:], lam_s[:, 0:1], -LAM_INIT, None, ALU.add, None, [r_lams], [r_nlam])

        def rms_stats(src_ap, n, r_src, col=0):
            act(junk[:, 0:n], src_ap, AF.Square, [r_src], [r_junk, r_ssq], accum_out=ssq[:, col:col + 1])
            act(ssq[:, col:col + 1], ssq[:, col:col + 1], AF.Sqrt, [r_ssq], [r_ssq], bias=EPS, scale=1.0 / n)
            recip(ssq[:, col:col + 1], ssq[:, col:col + 1], [r_ssq], [r_ssq])
            return ssq[:, col:col + 1]

        def transpose8(src_bf, r_src, dst_ap, r_dst, nblk, bidx=0):
            bk, r_bk = bank(bidx)
            bb = bk.bitcast(BF16)
            for k in range(nblk):
                tr(bb[:, k * 128:(k + 1) * 128], src_bf[:, k * 128:(k + 1) * 128], identb[:],
                   [r_src, r_identb], [r_bk])
            acopy(dst_ap, bb[:, 0:nblk * 128].rearrange("p (a b) -> p a b", a=nblk), [r_bk], [r_dst])

        with ExitStack() as es1:
            Wb, r_Wb = sbuf(es1, "Wb", [128, 8, NIN], BF16, ro=True)
            g1b, r_g1b = sbuf(es1, "g1b", [128, D], F32, ro=True)
            cmwb, r_cmwb = sbuf(es1, "cmwb", [128, 4, D], F32, ro=True)
            bifb, r_bifb = sbuf(es1, "bifb", [128, 8], F32, ro=True)
            gmlb, r_gmlb = sbuf(es1, "gmlb", [128, 128], F32, ro=True)
            shifts, r_shifts = sbuf(es1, "shifts", [128, 7, 128], BF16, ro=True)
            CU = 1024
            stage = [sbuf(es1, f"stage{i}", [128, CU], F32) for i in range(2)]
            stageb = [sbuf(es1, f"stageb{i}", [128, CU], BF16) for i in range(2)]
            NXT = 3
            xt = [sbuf(es1, f"xt{i}", [128, D], F32) for i in range(NXT)]
            rc = [sbuf(es1, f"rc{i}", [128, 32], F32) for i in range(NXT)]
            rs = [sbuf(es1, f"rs{i}", [128, 32], F32) for i in range(NXT)]
            st1 = [sbuf(es1, f"st1_{i}", [128, 2], F32) for i in range(NXT)]
            hbs = [sbuf(es1, f"hb{i}", [128, D], BF16) for i in range(2)]
            hTs = [sbuf(es1, f"hT{i}", [128, 8, 128], BF16) for i in range(2)]
            ra, r_ra = sbuf(es1, "ra", [128, 512], F32)
            rb1, r_rb1 = sbuf(es1, "rb1", [128, 8, 32], F32)
            rb2, r_rb2 = sbuf(es1, "rb2", [128, 8, 32], F32)
            krot, r_krot = sbuf(es1, "krot", [128, 512], BF16)
            ktT, r_ktT = sbuf(es1, "ktT", [128, 4, 128], BF16)
            vaug = [sbuf(es1, f"vaug{i}", [128, 4, 130], BF16) for i in range(2)]
            xwk = [(sbuf(es1, f"xwk{i}", [128, 4, 512], BF16)[0], [Res(f"xwk{i}_{j}") for j in range(4)]) for i in range(2)]
            xwq = [(sbuf(es1, f"xwq{i}", [128, 4, 512], BF16)[0], [Res(f"xwq{i}_{j}") for j in range(4)]) for i in range(2)]
            sig, r_sig = sbuf(es1, "sig", [128, 512], F32)
            kmtoks = [sbuf(es1, f"kmtok{i}", [128, 512], BF16) for i in range(2)]
            qmtok, r_qmtok = sbuf(es1, "qmtok", [128, 512], BF16)
            kmT, r_kmT = sbuf(es1, "kmT", [128, 4, 128], BF16)
            qmT, r_qmT = sbuf(es1, "qmT", [128, 4, 128], BF16)
            gt, r_gt = sbuf(es1, "gt", [128, 8], F32)
            gs, r_gs = sbuf(es1, "gs", [128, 24], F32)
            vts = [sbuf(es1, f"vt{i}", [128, 4, 129], BF16) for i in range(2)]
            Cst, r_C = sbuf(es1, "Cst", [128, 4, 129], F32)
            Chat, r_Chat = sbuf(es1, "Chat", [128, 4, 129], F32)
            Chb, r_Chb = sbuf(es1, "Chb", [128, 4, 129], BF16)
            ATs, r_ATs = sbuf(es1, "ATs", [128, 4, 128], BF16)
            hmf, r_hmf = sbuf(es1, "hmf", [128, 4, 128], F32)
            hmbs = [sbuf(es1, f"hmb{i}", [128, 512], BF16) for i in range(2)]
            hmTs, r_hmTs = sbuf(es1, "hmTs", [128, 4, 128], BF16)
            oms, r_oms = sbuf(es1, "oms", [128, 512], F32)
            sgs, r_sgs = sbuf(es1, "sgs", [128, 2 * D], BF16)
            mst, r_mst = sbuf(es1, "mst", [128, 4, 8], F32)
            rr, r_rr = sbuf(es1, "rr", [128, 8], F32)

            dma("g1b", g1b[:], g1_d.partition_broadcast(128), [], [r_g1b])
            dma("cmwb", cmwb[:].rearrange("p a b -> p (a b)"), cmw_d.partition_broadcast(128), [], [r_cmwb])
            dma("bifb", bifb[:], bif_d.partition_broadcast(128), [], [r_bifb])
            dma("gmlb", gmlb[:], gml_d.partition_broadcast(128), [], [r_gmlb])
            dma("shifts", shifts[:], shifts_d, [], [r_shifts])
            memset("pool", Cst[:], 0.0, [r_C])
            for i in range(2):
                memset("pool", xwk[i][0][:], 0.0, xwk[i][1])
                memset("pool", xwq[i][0][:], 0.0, xwq[i][1])
                memset("pool", vaug[i][0][:, :, 128:129], 1.0, [vaug[i][1]])
                memset("pool", vaug[i][0][:, :, 129:130], 0.0, [vaug[i][1]])

            cast_i = [0]

            def cast_unit(src_ap, ncols_free, dst_kind, dst_ap, r_dst, ceng="pool"):
                i = cast_i[0] % 2
                cast_i[0] += 1
                st, r_st = stage[i]
                a, b = src_ap.shape[1], src_ap.shape[2]
                stv = st[:, 0:ncols_free].rearrange("p (a b) -> p a b", a=a)
                dma(f"stage{i}", stv, src_ap, [], [r_st])
                if dst_kind == "sbuf":
                    tcopy(ceng, dst_ap, stv, [r_st], [r_dst])
                else:
                    sbt, r_sbt = stageb[i]
                    sbv = sbt[:, 0:ncols_free].rearrange("p (a b) -> p a b", a=a)
                    tcopy(ceng, sbv, stv, [r_st], [r_sbt])
                    dma(f"stageb{i}", dst_ap, sbv, [r_sbt], [r_dst])

            w_in_v = w_in_d.rearrange("(k p) n -> p k n", p=128)
            GW = CU // 8
            wgrp = {}
            col_groups = []
            for c0 in (KA, VA, IFG, KM, VM, QA, QM, OM, GA, GA + 512, GB, GB + 512):
                n_tot = 8 if c0 == IFG else 512
                for cc in range(c0, c0 + n_tot, GW):
                    col_groups.append((cc, min(GW, c0 + n_tot - cc)))
            engs = ["pool", "dve", "act"]
            for gi, (c0, n) in enumerate(col_groups):
                wgrp[c0] = (Res(f"Wb{c0}", ro=True), n)
                cast_unit(w_in_v[:, :, c0:c0 + n], 8 * n, "sbuf", Wb[:, :, c0:c0 + n], wgrp[c0][0],
                          ceng=engs[gi % 3] if gi < 40 else "pool")

            def wres(c0, n):
                return [r for c, (r, m) in wgrp.items() if c < c0 + n and c + m > c0]

            prep = []
            w_up_v = w_up_d.rearrange("(k p) n -> p k n", p=128)
            for c0 in range(0, 2 * DFF, GW):
                prep.append((w_up_v[:, :, c0:c0 + GW], CU, WUP_d[:, :, c0:c0 + GW], r_WUP))
            w_dn_v = w_dn_d.rearrange("(k p) n -> p k n", p=128)
            for k0 in range(NFC):
                prep.append((w_dn_v[:, k0:k0 + 1, :], D, WDN_d[:, k0:k0 + 1, :], r_WDN))
            w_ya_v = w_ya_d.rearrange("(k p) n -> p k n", p=128)
            w_yb_v = w_yb_d.rearrange("(k p) n -> p k n", p=128)
            for k0 in range(4):
                prep.append((w_ya_v[:, k0:k0 + 1, :], D, WYA_d[:, k0:k0 + 1, :], r_WYA))
                prep.append((w_yb_v[:, k0:k0 + 1, :], D, WYB_d[:, k0:k0 + 1, :], r_WYB))
            w_o_v = w_o_d.rearrange("(k p) n -> p k n", p=128)
            w_pg_v = w_pg_d.rearrange("(k p) n -> p k n", p=128)
            for k0 in range(8):
                prep.append((w_o_v[:, k0:k0 + 1, :], D, WO_d[:, k0:k0 + 1, :], r_WO))
                prep.append((w_pg_v[:, k0:k0 + 1, :], D, WPG_d[:, k0:k0 + 1, :], r_WPG))
            w_pl_v = w_ple_d.rearrange("(k p) n -> p k n", p=128)
            for k0 in range(2):
                prep.append((w_pl_v[:, k0:k0 + 1, :], D, WPL_d[:, k0:k0 + 1, :], r_WPL))
            prep_per_tile = -(-len(prep) // max(1, NT - NOWN))
            prep_pos = [0]

            def emit_prep(k):
                for _ in range(k):
                    if prep_pos[0] < len(prep):
                        pr = prep[prep_pos[0]]
                        prep_pos[0] += 1
                        cast_unit(pr[0], pr[1], "dram", pr[2], pr[3])

            def proj(hT_t, r_hT_, c0, n, bidx, col0=0):
                bk, r_bk = bank(bidx)
                for kc in range(8):
                    mm(bk[:, col0:col0 + n], hT_t[:, kc, :], Wb[:, kc, c0:c0 + n], kc == 0, kc == 7,
                       [r_hT_] + wres(c0, n), [r_bk])
                return bk, r_bk

            def rope_tm(bk, r_bk, cosap, sinap, r_c, r_s, out_bf, r_out):
                v4 = lambda ap: ap.rearrange("p (a b c) -> p a b c", a=8, b=2, c=32)
                kv = v4(bk[:, 0:512])
                av = v4(ra[:, :])
                ov = v4(out_bf[:, :])
                cb4 = cosap.unsqueeze(1).unsqueeze(1).to_broadcast([128, 8, 2, 32])
                sb3 = sinap.unsqueeze(1).to_broadcast([128, 8, 32])
                tt("dve", av, kv, cb4, ALU.mult, [r_bk, r_c], [r_ra])
                tt("dve", rb1[:], kv[:, :, 1, :], sb3, ALU.mult, [r_bk, r_s], [r_rb1])
                tt("dve", rb2[:], kv[:, :, 0, :], sb3, ALU.mult, [r_bk, r_s], [r_rb2])
                tt("dve", ov[:, :, 0, :], av[:, :, 0, :], rb1[:], ALU.subtract, [r_ra, r_rb1], [r_out])
                tt("dve", ov[:, :, 1, :], av[:, :, 1, :], rb2[:], ALU.add, [r_ra, r_rb2], [r_out])

            def conv_premul(bk, r_bk, wcol0, xw_cur):
                xc, r_xc = xw_cur
                for j in range(4):
                    tt("dve", xc[:, j, :], bk[:, 0:512], cmwb[:, j, wcol0:wcol0 + 512], ALU.mult,
                       [r_bk, r_cmwb], [r_xc[j]])

            def conv_mm(xw_cur, xw_prev, bidx_out):
                xc, r_xc = xw_cur
                xp, r_xp = xw_prev
                ob, r_ob = bank(bidx_out)
                for j in range(4):
                    mm(ob[:, 0:512], shifts[:, j, :], xc[:, j, :], j == 0, False, [r_shifts, r_xc[j]], [r_ob])
                for j in range(3):
                    mm(ob[:, 0:512], shifts[:, 4 + j, :], xp[:, j, :], False, j == 2, [r_shifts, r_xp[j]], [r_ob])
                return ob, r_ob

            def silu_tanh(cb, r_cb, out_bf, r_out, c):
                act(sig[:], cb[:, 0:512], AF.Exp, [r_cb], [r_sig], scale=-1.0 / c)
                act(sig[:], sig[:], AF.Ln, [r_sig], [r_sig], bias=1.0)
                act(sig[:], sig[:], AF.Exp, [r_sig], [r_sig], scale=-1.0)
                tt("dve", out_bf[:], cb[:, 0:512], sig[:], ALU.mult, [r_cb, r_sig], [r_out])

            km_scale = 128.0 ** -0.5
            CK = km_scale
            CQ = 1.0
            ts("dve", cmwb[:, :, 512:1024], cmwb[:, :, 512:1024], CK, None, ALU.mult, None, [r_cmwb], [r_cmwb])

            def stage_A1(g):
                x_t, r_x = xt[g % NXT]
                c_t, r_c = rc[g % NXT]
                s_t, r_s = rs[g % NXT]
                s1, r_s1 = st1[g % NXT]
                hb, r_hb = hbs[g % 2]
                dma(f"xt{g % NXT}", x_t[:], xa[g * 128:(g + 1) * 128, :], [], [r_x])
                dma(f"rc{g % NXT}", c_t[:], ropec_d[:, g, :], [], [r_c])
                dma(f"rs{g % NXT}", s_t[:], ropes_d[:, g, :], [], [r_s])
                act(junk[:, 0:D], x_t[:], AF.Square, [r_x], [r_junk, r_s1], accum_out=s1[:, 0:1])
                act(s1[:, 1:2], s1[:, 0:1], AF.Ln, [r_s1], [r_s1], bias=EPS, scale=1.0 / D)
                act(s1[:, 1:2], s1[:, 1:2], AF.Exp, [r_s1], [r_s1], scale=-0.5)
                stt("dve", hb[:], x_t[:], s1[:, 1:2], g1b[:], ALU.mult, ALU.mult, [r_x, r_s1, r_g1b], [r_hb])

            def stage_A2(g):
                hb, r_hb = hbs[g % 2]
                hT, r_hT = hTs[g % 2]
                transpose8(hb, r_hb, hT[:], r_hT, 8, bidx=0)

            def stage_D(g):
                u = g - G0
                hmb, r_hmb = hmbs[g % 2]
                transpose8(hmb, r_hmb, hmTs[:], r_hmTs, 4, bidx=0)
                dma("hmTs", HMT_d[:, :, u * 128:(u + 1) * 128], hmTs[:], [r_hmTs], [r_HMT])

            def stage_C(g):
                kmtok, r_kmtok = kmtoks[g % 2]
                vt, r_vt = vts[g % 2]
                for hp in range(2):
                    cbk, r_cbk = bank(5)
                    for hh in range(2):
                        h = 2 * hp + hh
                        mm(cbk[:, hh * 129:(hh + 1) * 129], kmtok[:, h * 128:(h + 1) * 128], vt[:, h, :], True, True,
                           [r_kmtok, r_vt], [r_cbk])
                    tt("dve", Cst[:, 2 * hp:2 * hp + 2, :], Chat[:, 2 * hp:2 * hp + 2, :],
                       cbk[:, 0:258].rearrange("p (a b) -> p a b", a=2), ALU.add, [r_Chat, r_cbk], [r_C])

            def stage_B(g):
                own = g >= G0
                u = g - G0
                hT, r_hT = hTs[g % 2]
                kmtok, r_kmtok = kmtoks[g % 2]
                vt, r_vt = vts[g % 2]
                c_t, r_c = rc[g % NXT]
                s_t, r_s = rs[g % NXT]
                bkG, r_bkG = proj(hT, r_hT, IFG, 8, 4, col0=16)
                tt("dve", gt[:], bkG[:, 16:24], bifb[:], ALU.add, [r_bkG, r_bifb], [r_gt])
                act(gs[:, 0:4], gt[:, 4:8], AF.Exp, [r_gt], [r_gs], scale=-1.0)
                act(gs[:, 0:4], gs[:, 0:4], AF.Ln, [r_gs], [r_gs], bias=1.0)
                bkM, r_bkM = proj(hT, r_hT, KM, 512, 3)
                bkK, r_bkK = proj(hT, r_hT, KA, 512, 1)
                bkV, r_bkV = proj(hT, r_hT, VA, 512, 2)
                bkW, r_bkW = proj(hT, r_hT, VM, 512, 6)
                if g - 1 >= G0:
                    stage_D(g - 1)
                conv_premul(bkM, r_bkM, 512, xwk[g % 2])
                rope_tm(bkK, r_bkK, c_t[:, :], s_t[:, :], r_c, r_s, krot, r_krot)
                va, r_va = vaug[g % 2]
                acopy(va[:, :, 0:128], bkV[:, 0:512].rearrange("p (a b) -> p a b", a=4), [r_bkV], [r_va])
                dma(f"vaug{g % 2}", VV_d[:, :, g, :].rearrange("h p c -> p h c"), va[:], [r_va], [r_VV])
                bk4, r_bk4 = bank(4)
                mm(bk4[:, 0:4], tri[:], gs[:, 0:4], True, True, [r_tri, r_gs], [r_bk4])
                mm(bk4[:, 8:12], ones[:], gs[:, 0:4], True, True, [r_ones, r_gs], [r_bk4])
                cb, r_cb = conv_mm(xwk[g % 2], xwk[(g + 1) % 2], 7)
                transpose8(krot, r_krot, ktT[:], r_ktT, 4, bidx=0)
                dma("ktT", KT_d[:, :, g * 128:(g + 1) * 128].rearrange("h p t -> p h t"), ktT[:], [r_ktT], [r_KT])
                if g > 0:
                    stage_C(g - 1)
                tt("dve", gs[:, 8:12], bk4[:, 0:4], gt[:, 0:4], ALU.add, [r_bk4, r_gt], [r_gs])
                tt("dve", gs[:, 8:12], gs[:, 8:12], bk4[:, 8:12], ALU.subtract, [r_gs, r_bk4], [r_gs])
                act(gs[:, 8:12], gs[:, 8:12], AF.Exp, [r_gs], [r_gs])
                act(gs[:, 12:16], bk4[:, 8:12], AF.Exp, [r_bk4], [r_gs], scale=-1.0)
                if own:
                    tcopy("dve", gs[:, 4:8], bk4[:, 8:12], [r_bk4], [r_gs])
                    tt("dve", gs[:, 16:20], gs[:, 4:8], bk4[:, 0:4], ALU.subtract, [r_gs, r_bk4], [r_gs])
                    act(gs[:, 16:20], gs[:, 16:20], AF.Exp, [r_gs], [r_gs])
                wk = gs[:, 8:12]
                dd = gs[:, 12:16]
                wp = gs[:, 16:20]
                silu_tanh(cb, r_cb, kmtok, r_kmtok, CK)
                tt("dve", vt[:, :, 0:128], bkW[:, 0:512].rearrange("p (a b) -> p a b", a=4),
                   wk.unsqueeze(2).to_broadcast([128, 4, 128]), ALU.mult, [r_bkW, r_gs], [r_vt])
                tcopy("dve", vt[:, :, 128:129], wk.unsqueeze(2), [r_gs], [r_vt])
                tt("pool", Chat[:], Cst[:], dd.unsqueeze(2).to_broadcast([128, 4, 129]), ALU.mult,
                   [r_C, r_gs], [r_Chat])

                if own:
                    hTo = hT
                    bkq, r_bkq = proj(hTo, r_hT, QA, 512, 1)
                    bkqm, r_bkqm = proj(hTo, r_hT, QM, 512, 2)
                    bkom, r_bkom = proj(hTo, r_hT, OM, 512, 3)
                    conv_premul(bkqm, r_bkqm, 0, xwq[g % 2])
                    rope_tm(bkq, r_bkq, c_t[:, :], s_t[:, :], r_c, r_s, krot, r_krot)
                    cbq, r_cbq = conv_mm(xwq[g % 2], xwq[(g + 1) % 2], 7)
                    transpose8(krot, r_krot, ktT[:], r_ktT, 4, bidx=0)
                    dma("ktT", QT_d[:, :, u * 128:(u + 1) * 128].rearrange("h p t -> p h t"), ktT[:],
                        [r_ktT], [r_QT])
                    bkga0, r_bkga0 = proj(hTo, r_hT, GA, 512, 1)
                    bkga1, r_bkga1 = proj(hTo, r_hT, GA + 512, 512, 2)
                    bkgb0, r_bkgb0 = proj(hTo, r_hT, GB, 512, 5)
                    silu_tanh(cbq, r_cbq, qmtok, r_qmtok, CQ)
                    act(oms[:], bkom[:, 0:512], AF.Sigmoid, [r_bkom], [r_oms])
                    act(sgs[:, 0:512], bkga0[:, 0:512], AF.Sigmoid, [r_bkga0], [r_sgs])
                    act(sgs[:, 512:1024], bkga1[:, 0:512], AF.Sigmoid, [r_bkga1], [r_sgs])
                    act(sgs[:, 1024:1536], bkgb0[:, 0:512], AF.Sigmoid, [r_bkgb0], [r_sgs])
                    transpose8(qmtok, r_qmtok, qmT[:], r_qmT, 4, bidx=0)
                    transpose8(kmtok, r_kmtok, kmT[:], r_kmT, 4, bidx=0)
                    bkgb1, r_bkgb1 = proj(hTo, r_hT, GB + 512, 512, 3)
                    act(sgs[:, 1536:2048], bkgb1[:, 0:512], AF.Sigmoid, [r_bkgb1], [r_sgs])
                    dma("sgs", SG_d[u * 128:(u + 1) * 128, :], sgs[:], [r_sgs], [r_SG])
                    bk7, r_bk7 = bank(7)
                    for h in range(H):
                        mm(bk7[:, h * 128:(h + 1) * 128], kmT[:, h, :], qmT[:, h, :], True, True,
                           [r_kmT, r_qmT], [r_bk7])
                    tt("dve", ATs[:], bk7[:, 0:512].rearrange("p (a b) -> p a b", a=4),
                       tri[:, :].unsqueeze(1).to_broadcast([128, 4, 128]), ALU.mult, [r_bk7, r_tri], [r_ATs])
                    acopy(Chb[:], Chat[:], [r_Chat], [r_Chb])
                    obs = [bank(6), bank(4)]
                    for hp in range(2):
                        ob, r_ob = obs[hp]
                        for hh in range(2):
                            h = 2 * hp + hh
                            oap = ob[:, hh * 129:(hh + 1) * 129]
                            mm(oap, qmT[:, h, :], Chb[:, h, :], True, False, [r_qmT, r_Chb], [r_ob])
                            mm(oap, ATs[:, h, :], vt[:, h, :], False, True, [r_ATs, r_vt], [r_ob])
                    for hp in range(2):
                        ob, r_ob = obs[hp]
                        for hh in range(2):
                            h = 2 * hp + hh
                            oap = ob[:, hh * 129:(hh + 1) * 129]
                            act(rr[:, 4 * hp + hh:4 * hp + hh + 1], oap[:, 128:129], AF.Abs, [r_ob, r_gs], [r_rr],
                                scale=wp[:, h:h + 1])
                    for hp in range(2):
                        ob, r_ob = obs[hp]
                        for hh in range(2):
                            h = 2 * hp + hh
                            oap = ob[:, hh * 129:(hh + 1) * 129]
                            c0 = 4 * hp + hh
                            ts("dve", rr[:, c0:c0 + 1], rr[:, c0:c0 + 1], 1.0, None, ALU.max, None, [r_rr], [r_rr])
                            recip(rr[:, c0:c0 + 1], rr[:, c0:c0 + 1], [r_rr], [r_rr])
                            tt("dve", rr[:, c0 + 2:c0 + 3], rr[:, c0:c0 + 1], wp[:, h:h + 1], ALU.mult, [r_rr, r_gs], [r_rr])
                            ts("dve", hmf[:, h, :], oap[:, 0:128], rr[:, c0 + 2:c0 + 3], None, ALU.mult, None,
                               [r_ob, r_rr], [r_hmf])
                    for h in range(H):
                        act(junk[:, 0:128], hmf[:, h, :], AF.Copy, [r_hmf], [r_junk, r_mst],
                            accum_out=mst[:, h, 0:1])
                        act(junk[:, 0:128], hmf[:, h, :], AF.Square, [r_hmf], [r_junk, r_mst],
                            accum_out=mst[:, h, 1:2])
                    ts("dve", mst[:, :, 2:3], mst[:, :, 0:1], 1.0 / 128, None, ALU.mult, None, [r_mst], [r_mst])
                    tt("dve", mst[:, :, 3:4], mst[:, :, 2:3], mst[:, :, 2:3], ALU.mult, [r_mst], [r_mst])
                    stt("dve", mst[:, :, 4:5], mst[:, :, 1:2], 1.0 / 128, mst[:, :, 3:4], ALU.mult, ALU.subtract,
                        [r_mst], [r_mst])
                    act(mst[:, :, 4:5], mst[:, :, 4:5], AF.Ln, [r_mst], [r_mst], bias=EPS)
                    act(mst[:, :, 5:6], mst[:, :, 4:5], AF.Exp, [r_mst], [r_mst], scale=-0.5)
                    for h in range(H):
                        ts("dve", hmf[:, h, :], hmf[:, h, :], mst[:, h, 2:3], mst[:, h, 5:6], ALU.subtract, ALU.mult,
                           [r_hmf, r_mst], [r_hmf])
                    tt("dve", hmf[:], hmf[:], gmlb[:, :].unsqueeze(1).to_broadcast([128, 4, 128]), ALU.mult,
                       [r_hmf, r_gmlb], [r_hmf])
                    hmb, r_hmb = hmbs[g % 2]
                    tt("dve", hmb[:], hmf[:].rearrange("p a b -> p (a b)"), oms[:], ALU.mult, [r_hmf, r_oms], [r_hmb])

            stage_A1(0)
            if NT > 1:
                stage_A1(1)
            stage_A2(0)
            for g in range(NT):
                if g + 2 < NT:
                    stage_A1(g + 2)
                if g + 1 < NT:
                    stage_A2(g + 1)
                if g < NT - NOWN:
                    emit_prep(prep_per_tile)
                stage_B(g)
            stage_D(NT - 1)
            stage_C(NT - 1)
            emit_prep(len(prep))

        scale = 64.0 ** -0.5
        P.barrier()
        with ExitStack() as es2:
            biast, r_biast = sbuf(es2, "biast", [128, NSLOT], F32, ro=True)
            pat, r_pat = sbuf(es2, "pat", [128, 4, 512], BF16, ro=True)
            gdab, r_gdab = sbuf(es2, "gdab", [128, 128], F32, ro=True)
            qt_s = [sbuf(es2, f"qts{i}", [128, 512], BF16) for i in range(2)]
            kg = [sbuf(es2, f"kg{i}", [128, NTS * 128], BF16) for i in range(2)]
            vg = [sbuf(es2, f"vg{i}", [128, NTS, 130], BF16) for i in range(2)]
            pts = [sbuf(es2, f"pts{i}", [128, 2, 512], BF16) for i in range(3)]
            aof, r_aof = sbuf(es2, "aof", [128, BT, 4, 128], F32)
            aob, r_aob = sbuf(es2, "aob", [128, 512], BF16)
            aoTs, r_aoTs = sbuf(es2, "aoTs", [128, 4, 128], BF16)
            tmpa, r_tmpa = sbuf(es2, "tmpa", [128, 128], F32)
            rl, r_rl = sbuf(es2, "rl", [128, 8], F32)
            dma("biast", biast[:], biast_d, [], [r_biast])
            dma("pat", pat[:], pat_d, [], [r_pat])
            dma("gdab", gdab[:], gda_d.partition_broadcast(128), [], [r_gdab])
            ts("dve", gdab[:], gdab[:], 1.0 - LAM_INIT, None, ALU.mult, None, [r_gdab], [r_gdab])

            S_t = [bank2(0), bank2(2)]
            O_bk = [[bank(4), bank(5)], [bank(6), bank(7)]]
            ld_i = 0
            pt_i = 0
            s_i = 0
            for bi, tiles in enumerate(blocks):
                nq = len(tiles)
                N = nq * 128
                t0 = tiles[0]
                klist = []
                for kt in range(NT):
                    ku = kt - G0
                    if bi == 0:
                        if kt < G0:
                            klist.append((kt, None))
                        elif kt == G0:
                            klist.append((kt, 0))
                    else:
                        if ku < t0:
                            klist.append((kt, None))
                        elif ku < t0 + nq:
                            klist.append((kt, ku - t0))
                groups = {}
                for kt, kind in klist:
                    groups.setdefault(kt // NTS, []).append((kt, kind))
                gkeys = sorted(groups)
                for h in range(H):
                    q_s, r_q = qt_s[(bi * H + h) % 2]
                    dma(f"qts{(bi * H + h) % 2}", q_s[:, 0:N], QT_d[h, :, t0 * 128:t0 * 128 + N], [r_QT], [r_q])
                    total_k = len(klist)
                    loaded = {}
                    seq = []
                    for gk in gkeys:
                        for (kt, kind) in groups[gk]:
                            seq.append((gk, kt, kind))

                    def ensure_group(gk):
                        nonlocal ld_i
                        if gk not in loaded:
                            k_s, r_k = kg[ld_i % 2]
                            v_s, r_v = vg[ld_i % 2]
                            dma(f"kg{ld_i % 2}", k_s[:], KT_d[h, :, gk * NTS * 128:(gk + 1) * NTS * 128], [r_KT], [r_k])
                            dma(f"vg{ld_i % 2}", v_s[:], VV_d[h, :, gk * NTS:(gk + 1) * NTS, :], [r_VV], [r_v])
                            ld_i += 1
                            loaded[gk] = (k_s, r_k, v_s, r_v)
                        return loaded[gk]

                    def emit_s(idx):
                        nonlocal s_i, pt_i
                        gk, kt, kind = seq[idx]
                        k_s, r_k, v_s, r_v = ensure_group(gk)
                        kl = kt - gk * NTS
                        St, r_sa, r_sb = S_t[s_i % 2]
                        s_i += 1
                        mm(St[:, 0, 0:N], k_s[0:64, kl * 128:(kl + 1) * 128], q_s[0:64, 0:N], True, True,
                           [r_k, r_q], [r_sa])
                        mm(St[:, 1, 0:N], k_s[64:128, kl * 128:(kl + 1) * 128], q_s[64:128, 0:N], True, True,
                           [r_k, r_q], [r_sb])
                        p_s, r_p = pts[pt_i % 3]
                        pt_i += 1
                        act(p_s[:, :, 0:N], St[:, :, 0:N], AF.Exp, [r_sa, r_sb, r_biast], [r_p],
                            bias=biast[:, gk:gk + 1], scale=scale)
                        if kind is not None:
                            tt("dve", p_s[:, :, 0:N], p_s[:, :, 0:N],
                               pat[:, kind, 0:N].unsqueeze(1).to_broadcast([128, 2, N]), ALU.mult,
                               [r_p, r_pat], [r_p])
                        return p_s, r_p

                    def emit_pv(idx, p_s, r_p):
                        gk, kt, kind = seq[idx]
                        k_s, r_k, v_s, r_v = loaded[gk]
                        kl = kt - gk * NTS
                        for sub in range(nq):
                            for m in range(2):
                                ob, r_ob = O_bk[m][sub // 2]
                                mm(ob[:, (sub % 2) * 129:(sub % 2 + 1) * 129],
                                   p_s[:, m, sub * 128:(sub + 1) * 128], v_s[:, kl, 0:129],
                                   idx == 0 and (sub % 2 == 0), idx == total_k - 1, [r_p, r_v], [r_ob], skip=True)

                    pend = emit_s(0)
                    for idx in range(total_k):
                        nxt = emit_s(idx + 1) if idx + 1 < total_k else None
                        emit_pv(idx, *pend)
                        pend = nxt
                    for sub in range(nq):
                        o0, r_o0 = O_bk[0][sub // 2]
                        o1, r_o1 = O_bk[1][sub // 2]
                        o0 = o0[:, (sub % 2) * 129:(sub % 2 + 1) * 129]
                        o1 = o1[:, (sub % 2) * 129:(sub % 2 + 1) * 129]
                        ts("dve", rl[:, 0:1], o0[:, 128:129], 1e-30, None, ALU.max, None, [r_o0], [r_rl])
                        ts("dve", rl[:, 1:2], o1[:, 128:129], 1e-30, None, ALU.max, None, [r_o1], [r_rl])
                        recip(rl[:, 2:4], rl[:, 0:2], [r_rl], [r_rl])
                        tt("dve", rl[:, 4:5], rl[:, 3:4], nlam[:, 0:1], ALU.mult, [r_rl, r_nlam], [r_rl])
                        ts("dve", tmpa[:], o1[:, 0:128], rl[:, 4:5], None, ALU.mult, None, [r_o1, r_rl], [r_tmpa])
                        stt("dve", aof[:, sub, h, :], o0[:, 0:128], rl[:, 2:3], tmpa[:], ALU.mult, ALU.add,
                            [r_o0, r_rl, r_tmpa], [r_aof])
                for sub in range(nq):
                    for h in range(H):
                        act(junk[:, 0:128], aof[:, sub, h, :], AF.Square, [r_aof], [r_junk, r_ssq],
                            accum_out=ssq[:, 4 + h:5 + h])
                    act(ssq[:, 4:8], ssq[:, 4:8], AF.Sqrt, [r_ssq], [r_ssq], bias=EPS, scale=1.0 / 128)
                    recip(ssq[:, 4:8], ssq[:, 4:8], [r_ssq], [r_ssq])
                    for h in range(H):
                        stt("dve", aob[:, h * 128:(h + 1) * 128], aof[:, sub, h, :], ssq[:, 4 + h:5 + h], gdab[:],
                            ALU.mult, ALU.mult, [r_aof, r_ssq, r_gdab], [r_aob])
                    transpose8(aob, r_aob, aoTs[:], r_aoTs, 4, bidx=0)
                    u = t0 + sub
                    dma("aoTs", AOT_d[:, :, u * 128:(u + 1) * 128], aoTs[:], [r_aoTs], [r_AOT])

        P.barrier()
        with ExitStack() as es3:
            Wya, r_Wya = sbuf(es3, "Wya", [128, 4, D], BF16, ro=True)
            Wyb, r_Wyb = sbuf(es3, "Wyb", [128, 4, D], BF16, ro=True)
            Wo, r_Wo = sbuf(es3, "Wo", [128, 8, D], BF16, ro=True)
            Wpg, r_Wpg = sbuf(es3, "Wpg", [128, 8, D], BF16, ro=True)
            Wpl, r_Wpl = sbuf(es3, "Wpl", [128, 2, D], BF16, ro=True)
            wdn = [sbuf(es3, f"wdn{i}", [128, 2, D], BF16) for i in range(2)]
            g2b, r_g2b = sbuf(es3, "g2b", [128, D], F32, ro=True)
            cfw, r_cfw = sbuf(es3, "cfw", [128, NFC, 3], F32, ro=True)
            cfb, r_cfb = sbuf(es3, "cfb", [128, NFC], F32, ro=True)
            wup = [sbuf(es3, f"wup{i}", [128, 8, 2, 256], BF16) for i in range(2)]
            aoT, r_aoT = sbuf(es3, "aoT", [128, 4, 512], BF16)
            hmT, r_hmT = sbuf(es3, "hmT", [128, 4, 512], BF16)
            sg3s = [sbuf(es3, f"sg3_{i}", [128, 2 * D], BF16) for i in range(2)]
            x3t = [sbuf(es3, f"x3t{i}", [128, D], F32) for i in range(2)]
            m2, r_m2 = sbuf(es3, "m2", [128, D], F32)
            mbs = [sbuf(es3, f"mb{i}", [128, D], BF16) for i in range(2)]
            mT, r_mT = sbuf(es3, "mT", [128, 8, 128], BF16)
            x1b, r_x1b = sbuf(es3, "x1b", [128, BT, D], F32)
            h2bs = [sbuf(es3, f"h2b{i}", [128, D], BF16) for i in range(2)]
            h2T, r_h2T = sbuf(es3, "h2T", [128, 8, 512], BF16)
            aext, r_aext = sbuf(es3, "aext", [128, 2 + 512], F32)
            halo = [sbuf(es3, f"halo{i}", [128, NFC, 2], F32) for i in range(2)]
            acc, r_acc = sbuf(es3, "acc", [128, 512], F32)
            gel, r_gel = sbuf(es3, "gel", [128, 512], F32)
            uT, r_uT = sbuf(es3, "uT", [128, NFC, 512], BF16)
            x2bf, r_x2bf = sbuf(es3, "x2bf", [128, D], BF16)
            x2Ts = [sbuf(es3, f"x2T{i}", [128, 8, 128], BF16) for i in range(2)]
            gsb, r_gsb = sbuf(es3, "gsb", [128, D], F32)
            pf, r_pf = sbuf(es3, "pf", [128, 2, 128], F32)
            pbs = [sbuf(es3, f"pb{i}", [128, 2, 128], BF16) for i in range(2)]
            m1, r_m1 = gsb, r_gsb
            yo = x3t

            dma("Wya", Wya[:], WYA_d, [r_WYA], [r_Wya])
            dma("Wyb", Wyb[:], WYB_d, [r_WYB], [r_Wyb])
            dma("Wo", Wo[:], WO_d, [r_WO], [r_Wo])
            dma("Wpg", Wpg[:], WPG_d, [r_WPG], [r_Wpg])
            dma("Wpl", Wpl[:], WPL_d, [r_WPL], [r_Wpl])
            dma("g2b", g2b[:], g2_d.partition_broadcast(128), [], [r_g2b])
            dma("cfw", cfw[:], cfw_d, [], [r_cfw])
            dma("cfb", cfb[:], cfb_d, [], [r_cfb])
            for i in range(2):
                memset("pool", halo[i][0][:], 0.0, [halo[i][1]])

            wu_i = 0
            wd_i = 0
            yo_i = 0
            x_i = 0
            for bi, tiles in enumerate(blocks):
                nq = len(tiles)
                N = nq * 128
                t0 = tiles[0]
                last_is_halo = bi == 0
                dma("aoT", aoT[:, :, 0:N], AOT_d[:, :, t0 * 128:t0 * 128 + N], [r_AOT], [r_aoT])
                dma("hmT", hmT[:, :, 0:N], HMT_d[:, :, t0 * 128:t0 * 128 + N], [r_HMT], [r_hmT])
                xbuf = {}

                def stage_MA(ti):
                    nonlocal x_i
                    u = tiles[ti]
                    x_t, r_x = x3t[x_i % 2]
                    xbuf[ti] = (x_t, r_x)
                    dma(f"x3t{x_i % 2}", x_t[:], xa[(G0 + u) * 128:(G0 + u + 1) * 128, :], [], [r_x])
                    x_i += 1
                    sg3, r_sg3 = sg3s[ti % 2]
                    dma(f"sg3_{ti % 2}", sg3[:], SG_d[u * 128:(u + 1) * 128, :], [r_SG], [r_sg3])
                    ya, r_ya0, r_ya1 = bank2(0)
                    yb, r_yb0, r_yb1 = bank2(2)
                    for half, (ra_, rb_) in enumerate(((r_ya0, r_yb0), (r_ya1, r_yb1))):
                        for c in range(4):
                            mm(ya[:, half, :], aoT[:, c, ti * 128:(ti + 1) * 128], Wya[:, c, half * 512:(half + 1) * 512],
                               c == 0, c == 3, [r_aoT, r_Wya], [ra_])
                        for c in range(4):
                            mm(yb[:, half, :], hmT[:, c, ti * 128:(ti + 1) * 128], Wyb[:, c, half * 512:(half + 1) * 512],
                               c == 0, c == 3, [r_hmT, r_Wyb], [rb_])
                    mb, r_mb = mbs[ti % 2]
                    tt("dve", m1[:].rearrange("p (a b) -> p a b", a=2), ya[:, :, :],
                       sg3[:, 0:D].rearrange("p (a b) -> p a b", a=2), ALU.mult, [r_ya0, r_ya1, r_sg3], [r_m1])
                    tt("dve", m2[:].rearrange("p (a b) -> p a b", a=2), yb[:, :, :],
                       sg3[:, D:2 * D].rearrange("p (a b) -> p a b", a=2), ALU.mult, [r_yb0, r_yb1, r_sg3], [r_m2])
                    tt("dve", mb[:], m1[:], m2[:], ALU.add, [r_m1, r_m2], [r_mb])

                def stage_MB(ti):
                    mb, r_mb = mbs[ti % 2]
                    x_t, r_x = xbuf[ti]
                    h2b, r_h2b = h2bs[ti % 2]
                    transpose8(mb, r_mb, mT[:], r_mT, 8, bidx=4)
                    xo, r_xo0, r_xo1 = bank2(6)
                    for half, rr_ in enumerate((r_xo0, r_xo1)):
                        for kc in range(8):
                            mm(xo[:, half, :], mT[:, kc, :], Wo[:, kc, half * 512:(half + 1) * 512], kc == 0, kc == 7,
                               [r_mT, r_Wo], [rr_])
                    tt("dve", x1b[:, ti, :].rearrange("p (a b) -> p a b", a=2), xo[:, :, :],
                       x_t[:].rearrange("p (a b) -> p a b", a=2), ALU.add, [r_xo0, r_xo1, r_x], [r_x1b])
                    rstd = rms_stats(x1b[:, ti, :], D, r_x1b)
                    stt("dve", h2b[:], x1b[:, ti, :], rstd, g2b[:], ALU.mult, ALU.mult, [r_x1b, r_ssq, r_g2b], [r_h2b])

                def stage_MC(ti):
                    h2b, r_h2b = h2bs[ti % 2]
                    transpose8(h2b, r_h2b, h2T[:, :, ti * 128:(ti + 1) * 128], r_h2T, 8, bidx=5)

                stage_MA(0)
                for ti in range(nq):
                    if ti + 1 < nq:
                        stage_MA(ti + 1)
                    stage_MB(ti)
                    if ti >= 1:
                        stage_MC(ti - 1)
                stage_MC(nq - 1)

                h_in, r_hin = halo[bi % 2]
                h_out, r_hout = halo[(bi + 1) % 2]
                for fc0 in range(0, NFC, 2):
                    w_s, r_w = wup[wu_i % 2]
                    wu_i += 1
                    dma(f"wup{(wu_i - 1) % 2}", w_s[:, :, 0, :], WUP_d[:, :, fc0 * 128:fc0 * 128 + 256], [r_WUP], [r_w])
                    if not last_is_halo:
                        dma(f"wup{(wu_i - 1) % 2}", w_s[:, :, 1, :], WUP_d[:, :, DFF + fc0 * 128:DFF + fc0 * 128 + 256],
                            [r_WUP], [r_w])
                    for fi in range(2):
                        fc = fc0 + fi
                        ab, r_ab = bank(2 * fi)
                        for kc in range(8):
                            mm(ab[:, 0:N], w_s[:, kc, 0, fi * 128:(fi + 1) * 128], h2T[:, kc, 0:N], kc == 0, kc == 7,
                               [r_w, r_h2T], [r_ab])
                        acopy(aext[:, 2:2 + N], ab[:, 0:N], [r_ab], [r_aext])
                        tcopy("dve", aext[:, 0:2], h_in[:, fc, :], [r_hin], [r_aext])
                        tcopy("dve", h_out[:, fc, :], aext[:, N:N + 2], [r_aext], [r_hout])
                        if last_is_halo:
                            continue
                        gb_, r_gb = bank(2 * fi + 1)
                        for kc in range(8):
                            mm(gb_[:, 0:N], w_s[:, kc, 1, fi * 128:(fi + 1) * 128], h2T[:, kc, 0:N], kc == 0, kc == 7,
                               [r_w, r_h2T], [r_gb])
                        ts("dve", acc[:, 0:N], aext[:, 0:N], cfw[:, fc, 0:1], cfb[:, fc:fc + 1], ALU.mult, ALU.add,
                           [r_aext, r_cfw, r_cfb], [r_acc])
                        stt("dve", acc[:, 0:N], aext[:, 1:N + 1], cfw[:, fc, 1:2], acc[:, 0:N], ALU.mult, ALU.add,
                            [r_aext, r_cfw, r_acc], [r_acc])
                        stt("dve", acc[:, 0:N], aext[:, 2:N + 2], cfw[:, fc, 2:3], acc[:, 0:N], ALU.mult, ALU.add,
                            [r_aext, r_cfw, r_acc], [r_acc])
                        act(gel[:, 0:N], acc[:, 0:N], AF.Gelu_apprx_tanh, [r_acc], [r_gel])
                        tt("dve", uT[:, fc, 0:N], gel[:, 0:N], gb_[:, 0:N], ALU.mult, [r_gel, r_gb], [r_uT])
                if last_is_halo:
                    continue

                for fc0 in range(0, NFC, 2):
                    wd, r_wd = wdn[wd_i % 2]
                    wd_i += 1
                    dma(f"wdn{(wd_i - 1) % 2}", wd[:], WDN_d[:, fc0:fc0 + 2, :], [r_WDN], [r_wd])
                    for fi in range(2):
                        fc = fc0 + fi
                        for ti in range(nq):
                            for half in range(2):
                                bkx, r_bkx = bank(2 * ti + half)
                                mm(bkx[:, :], uT[:, fc, ti * 128:(ti + 1) * 128], wd[:, fi, half * 512:(half + 1) * 512],
                                   fc == 0, fc == NFC - 1, [r_uT, r_wd], [r_bkx])
                for ti in range(nq):
                    xd, r_xd0, r_xd1 = bank2(2 * ti)
                    tt("dve", x1b[:, ti, :].rearrange("p (a b) -> p a b", a=2), xd[:, :, :],
                       x1b[:, ti, :].rearrange("p (a b) -> p a b", a=2), ALU.add, [r_xd0, r_xd1, r_x1b], [r_x1b])
                def stage_PA(ti):
                    u = tiles[ti]
                    x2 = x1b[:, ti, :]
                    x2T, r_x2T = x2Ts[ti % 2]
                    pb, r_pb = pbs[ti % 2]
                    tcopy("act", x2bf[:], x2, [r_x1b], [r_x2bf])
                    transpose8(x2bf, r_x2bf, x2T[:], r_x2T, 8, bidx=6)
                    tok0 = (u - 1) * 128
                    dma("pf", pf[:], pT_d[:, :, tok0:tok0 + 128], [], [r_pf])
                    tcopy("dve", pb[:], pf[:], [r_pf], [r_pb])

                def stage_PB(ti):
                    nonlocal yo_i
                    u = tiles[ti]
                    x2 = x1b[:, ti, :]
                    r_x2 = r_x1b
                    x2T, r_x2T = x2Ts[ti % 2]
                    pb, r_pb = pbs[ti % 2]
                    tok0 = (u - 1) * 128
                    gp, r_gp0, r_gp1 = bank2(0)
                    pp, r_pp0, r_pp1 = bank2(2)
                    for half, (rg_, rp_) in enumerate(((r_gp0, r_pp0), (r_gp1, r_pp1))):
                        for kc in range(8):
                            mm(gp[:, half, :], x2T[:, kc, :], Wpg[:, kc, half * 512:(half + 1) * 512], kc == 0, kc == 7,
                               [r_x2T, r_Wpg], [rg_])
                        for c in range(2):
                            mm(pp[:, half, :], pb[:, c, :], Wpl[:, c, half * 512:(half + 1) * 512], c == 0, c == 1,
                               [r_pb, r_Wpl], [rp_])
                    act(gsb[:].rearrange("p (a b) -> p a b", a=2), gp[:, :, :], AF.Sigmoid, [r_gp0, r_gp1], [r_gsb])
                    tt("dve", gsb[:].rearrange("p (a b) -> p a b", a=2), pp[:, :, :],
                       gsb[:].rearrange("p (a b) -> p a b", a=2), ALU.mult, [r_pp0, r_pp1, r_gsb], [r_gsb])
                    tt("dve", x2, x2, gsb[:], ALU.add, [r_x2, r_gsb], [r_x2])
                    rstd = rms_stats(x2, D, r_x2)
                    y_t, r_yt = yo[yo_i % 2]
                    yo_i += 1
                    stt("dve", y_t[:], x2, rstd, gFb[:], ALU.mult, ALU.mult, [r_x2, r_ssq, r_gFb], [r_yt])
                    dma(f"x3t{(yo_i - 1) % 2}", y_d[tok0:tok0 + 128, :], y_t[:], [r_yt], [r_y])

                stage_PA(0)
                for ti in range(nq):
                    if ti + 1 < nq:
                        stage_PA(ti + 1)
                    stage_PB(ti)

            P.wait_all("sp", [r_y, yo[0][1], yo[1][1]])
            P.wait_all("pool", [r_y, yo[0][1], yo[1][1]])
            P.wait_all("act", [r_y, yo[0][1], yo[1][1]])
        P.run()
    return nc, P


def host_constants(NTS):
    NT = NSLOT * NTS
    bf = ml_dtypes.bfloat16
    c = {}
    c["identb"] = np.eye(128, dtype=np.float32).astype(bf)
    s = np.arange(128)
    c["tri"] = (s[:, None] <= s[None, :]).astype(np.float32)
    c["ones"] = np.ones((128, 128), np.float32)
    sh = np.zeros((128, 7, 128), np.float32)
    for j in range(4):
        for t in range(128):
            sidx = t - 3 + j
            if 0 <= sidx < 128:
                sh[sidx, j, t] = 1.0
    for j in range(3):
        for t in range(128):
            sidx = 128 + t - 3 + j
            if 0 <= sidx < 128:
                sh[sidx, 4 + j, t] = 1.0
    c["shifts"] = sh.astype(bf)
    pat = np.zeros((128, 4, 512), np.float32)
    k = np.arange(128)
    q = np.arange(512)
    for o in range(4):
        pat[:, o, :] = (((128 * o + k) // 64)[:, None] <= (q // 64)[None, :]).astype(np.float32)
    c["pat"] = pat.astype(bf)
    inv_freq = (10000.0 ** (-np.arange(0, 64, 2, dtype=np.float32) / 64)).astype(np.float32)
    pos = (np.arange(NT)[None, :] * 128 + np.arange(128)[:, None]).astype(np.float32)
    ang = pos[:, :, None] * inv_freq[None, None, :]
    c["ropec"] = np.cos(ang).astype(np.float32)
    c["ropes"] = np.sin(ang).astype(np.float32)
    return c


_PROG_CACHE = {}


def make_in_maps(inputs, NTS, cores=None):
    SLOT = NTS * 128
    S = NSLOT * SLOT
    NT = NSLOT * NTS
    f32 = np.float32
    x = np.asarray(inputs["x"], f32).reshape(S, D)
    p = np.asarray(inputs["p"], f32).reshape(S, PLE)
    consts = host_constants(NTS)
    shared = {
        "w_in": np.ascontiguousarray(np.asarray(inputs["w_in"], f32)[0]),
        "w_ya": np.ascontiguousarray(np.asarray(inputs["w_ya"], f32)[0]),
        "w_yb": np.ascontiguousarray(np.asarray(inputs["w_yb"], f32)[0]),
        "w_o": np.ascontiguousarray(np.asarray(inputs["w_o"], f32)[0]),
        "w_up": np.ascontiguousarray(np.asarray(inputs["w_up"], f32)[0]),
        "w_down": np.ascontiguousarray(np.asarray(inputs["w_down"], f32)[0]),
        "w_ple": np.ascontiguousarray(np.asarray(inputs["w_ple"], f32)[0]),
        "w_pg": np.ascontiguousarray(np.asarray(inputs["w_pg"], f32)[0]),
        "g1": np.asarray(inputs["norm1_g"], f32).reshape(1, D),
        "g2": np.asarray(inputs["norm2_g"], f32).reshape(1, D),
        "gF": np.asarray(inputs["final_g"], f32).reshape(1, D),
        "gda": np.asarray(inputs["da_norm_g"], f32).reshape(1, 128),
        "gml": np.asarray(inputs["ml_norm_g"], f32).reshape(1, 128),
        "b_if": np.asarray(inputs["b_if"], f32).reshape(1, 8),
        "lamv": np.stack([np.asarray(inputs[k], f32).reshape(64) for k in ("lam_q1", "lam_k1", "lam_q2", "lam_k2")]).reshape(1, 256),
        "conv_m_w": np.ascontiguousarray(np.asarray(inputs["conv_m_w"], f32)[0]).reshape(1, 4 * D),
        "cfw": np.ascontiguousarray(np.asarray(inputs["conv_f_w"], f32)[0].T.reshape(NFC, 128, 3).transpose(1, 0, 2)),
        "cfb": np.ascontiguousarray(np.asarray(inputs["conv_f_b"], f32)[0].reshape(NFC, 128).T),
    }
    shared.update(consts)
    in_maps = []
    for r in (range(NCORES) if cores is None else cores):
        xa = np.zeros((NT * 128, D), f32)
        lo = (r - (NSLOT - 1)) * SLOT
        hi = (r + 1) * SLOT
        src_lo = max(lo, 0)
        xa[src_lo - lo:, :] = x[src_lo:hi]
        pT = np.ascontiguousarray(p[r * SLOT:(r + 1) * SLOT].T.reshape(2, 128, SLOT).transpose(1, 0, 2))
        biast = np.zeros((128, NSLOT), f32)
        for s_ in range(NSLOT):
            if s_ < NSLOT - 1 - r:
                biast[:, s_] = NEG
        m = dict(shared)
        m["xa"] = xa
        m["pT"] = pT
        m["biast"] = biast
        in_maps.append(m)
    return in_maps


def kernel(**inputs):
    S = int(np.asarray(inputs["x"]).shape[1])
    NTS = S // (NSLOT * 128)
    if NTS not in _PROG_CACHE:
        _PROG_CACHE[NTS] = build_program(NTS)[0]
    nc = _PROG_CACHE[NTS]
    in_maps = make_in_maps(inputs, NTS)
    res = run_bass_kernel_spmd(nc, in_maps, core_ids=list(range(NCORES)))
    out = np.concatenate([np.asarray(r["y"], np.float32) for r in res.results], axis=0)
    return out.reshape(1, S, D)
```

```python
import math
from contextlib import ExitStack

import numpy as np
import ml_dtypes

import concourse.bass as bass
import concourse.mybir as mybir
from concourse.bass_utils import run_bass_kernel_spmd

F32 = mybir.dt.float32
BF16 = mybir.dt.bfloat16
AF = mybir.ActivationFunctionType
ALU = mybir.AluOpType

NCORES = 8
NSLOT = 8
D = 1024
NIN = 5640
DFF = 2816
NFC = DFF // 128
PLE = 256
H = 4
EPS = 1e-6
LAM_INIT = 0.8 - 0.6 * math.exp(-0.3 * 0)
QA, KA, VA, QM, KM, VM, OM, IFG, GA, GB = 0, 512, 1024, 1536, 2048, 2560, 3072, 3584, 3592, 4616
NEG = -30000.0


class Res:
    __slots__ = ("name", "w", "r", "ro")

    def __init__(self, name, ro=False):
        self.name = name
        self.w = None
        self.r = []
        self.ro = ro


class Prog:
    ENGS = ("pe", "act", "dve", "pool", "sp")

    def __init__(self, nc):
        self.nc = nc
        self.q = {e: [] for e in self.ENGS}
        self.sem = {e: nc.alloc_semaphore(f"c_{e}") for e in self.ENGS}
        self.cnt = {e: 0 for e in self.ENGS}
        self.seen = {e: {} for e in self.ENGS}
        self.dsem = {}
        self.dknow = {}
        self.hist = {e: {} for e in self.ENGS}
        self.n_wait = 0
        self.n_ins = 0

    def _merge(self, eng, know):
        se = self.seen[eng]
        for k, v in know.items():
            if se.get(k, 0) < v:
                se[k] = v

    def _need(self, eng, toks):
        se = self.seen[eng]
        cand = []
        for t in toks:
            if t is None:
                continue
            kind, key, val = t
            if kind == "d":
                val = self.dsem[key][1]
            cand.append((kind, key, val))
        cand.sort(key=lambda t: (t[0] == "c" and t[1] == eng, -t[2]))
        out = []
        for kind, key, val in cand:
            k = (kind, key)
            if se.get(k, 0) >= val:
                continue
            if kind == "c":
                sem = self.sem[key]
                know = self.hist[key].get(val)
            else:
                sem = self.dsem[key][0]
                know = self.dknow.get(key)
            se[k] = val
            if know:
                self._merge(eng, know)
            out.append((sem, val))
        return out

    def _deps(self, eng, reads, writes):
        toks = []
        for r in reads:
            toks.append(r.w)
        for w in writes:
            if not (eng == "pe" and w.w is not None and w.w[0] == "c" and w.w[1] == "pe"):
                toks.append(w.w)
            toks.extend(w.r)
        return toks

    def _record(self, tok, reads, writes):
        for r in reads:
            if not r.ro:
                r.r.append(tok)
        for w in writes:
            w.w = tok
            w.r = []

    def op(self, eng, fn, reads=(), writes=()):
        waits = self._need(eng, self._deps(eng, reads, writes))
        self.cnt[eng] += 1
        n = self.cnt[eng]
        tok = ("c", eng, n)
        snap = dict(self.seen[eng])
        snap[("c", eng)] = n
        self.hist[eng][n] = snap
        self.q[eng].append((waits, fn, self.sem[eng], 1))
        self.n_wait += len(waits)
        self.n_ins += 1
        self._record(tok, reads, writes)
        return tok

    def dma(self, eng, semname, fn, reads=(), writes=()):
        if semname not in self.dsem:
            self.dsem[semname] = [self.nc.alloc_semaphore(f"d_{semname}"), 0]
            self.dknow[semname] = {}
        waits = self._need(eng, self._deps(eng, reads, writes))
        ent = self.dsem[semname]
        ent[1] += 16
        tok = ("d", semname, ent[1])
        dk = self.dknow[semname]
        for k, v in self.seen[eng].items():
            if dk.get(k, 0) < v:
                dk[k] = v
        self.q[eng].append((waits, fn, ent[0], 16))
        self.n_wait += len(waits)
        self.n_ins += 1
        self._record(tok, reads, writes)
        return tok

    def wait_all(self, eng, ress):
        toks = []
        for r in ress:
            toks.append(r.w)
            toks.extend(r.r)
        waits = self._need(eng, toks)
        self.q[eng].append((waits, None, None, 0))

    def barrier(self):
        for eng in self.ENGS:
            toks = [("c", e2, self.cnt[e2]) for e2 in self.ENGS if self.cnt[e2] > 0]
            toks += [("d", name, ent[1]) for name, ent in self.dsem.items() if ent[1] > 0]
            waits = self._need(eng, toks)
            self.q[eng].append((waits, None, None, 0))

    def run(self):
        q = self.q

        def play(engobj, items):
            for waits, fn, sem, inc in items:
                if fn is None:
                    for (s, v) in waits:
                        engobj.wait_ge(s, v)
                    continue
                for (s, v) in waits[:-1]:
                    engobj.wait_ge(s, v)
                ins = fn(engobj)
                if waits:
                    ins._wait_ge(*waits[-1])
                if sem is not None:
                    ins.then_inc(sem, inc)

        with self.nc.Block() as block:
            @block.tensor
            def _(e):
                play(e, q["pe"])

            @block.scalar
            def _(e):
                play(e, q["act"])

            @block.vector
            def _(e):
                play(e, q["dve"])

            @block.gpsimd
            def _(e):
                play(e, q["pool"])

            @block.sync
            def _(e):
                play(e, q["sp"])


def build_program(NTS):
    NT = NSLOT * NTS
    NOWN = NTS + 1
    G0 = NT - NOWN
    SLOT = NTS * 128
    BT = min(4, NTS)
    blocks = [[0]] + [list(range(1 + b, 1 + b + BT)) for b in range(0, NTS, BT)]
    NOWNTOK = NOWN * 128

    nc = bass.Bass("TRN2", target_bir_lowering=False)
    P = Prog(nc)

    def din(name, shape, dt=F32):
        return nc.dram_tensor(name, list(shape), dt, kind="ExternalInput").ap()

    def dscr(name, shape, dt):
        return nc.dram_tensor(name, list(shape), dt).ap(), Res(name)

    xa = din("xa", [NT * 128, D])
    pT_d = din("pT", [128, 2, SLOT])
    w_in_d = din("w_in", [D, NIN])
    w_ya_d = din("w_ya", [512, D])
    w_yb_d = din("w_yb", [512, D])
    w_o_d = din("w_o", [D, D])
    w_up_d = din("w_up", [D, 2 * DFF])
    w_dn_d = din("w_down", [DFF, D])
    w_ple_d = din("w_ple", [PLE, D])
    w_pg_d = din("w_pg", [D, D])
    g1_d = din("g1", [1, D])
    g2_d = din("g2", [1, D])
    gF_d = din("gF", [1, D])
    gda_d = din("gda", [1, 128])
    gml_d = din("gml", [1, 128])
    bif_d = din("b_if", [1, 8])
    lam_d = din("lamv", [1, 256])
    cmw_d = din("conv_m_w", [1, 4 * D])
    cfw_d = din("cfw", [128, NFC, 3])
    cfb_d = din("cfb", [128, NFC])
    identb_d = din("identb", [128, 128], BF16)
    tri_d = din("tri", [128, 128])
    ones_d = din("ones", [128, 128])
    shifts_d = din("shifts", [128, 7, 128], BF16)
    pat_d = din("pat", [128, 4, 512], BF16)
    ropec_d = din("ropec", [128, NT, 32])
    ropes_d = din("ropes", [128, NT, 32])
    biast_d = din("biast", [128, NSLOT])
    y_d = nc.dram_tensor("y", [SLOT, D], F32, kind="ExternalOutput").ap()
    r_y = Res("y")

    KT_d, r_KT = dscr("KT", [H, 128, NT * 128], BF16)
    VV_d, r_VV = dscr("VV", [H, 128, NT, 130], BF16)
    QT_d, r_QT = dscr("QT", [H, 128, NOWNTOK], BF16)
    SG_d, r_SG = dscr("SG", [NOWNTOK, 2 * D], BF16)
    HMT_d, r_HMT = dscr("HMT", [128, H, NOWNTOK], BF16)
    AOT_d, r_AOT = dscr("AOT", [128, H, NOWNTOK], BF16)
    WUP_d, r_WUP = dscr("WUPb", [128, 8, 2 * DFF], BF16)
    WDN_d, r_WDN = dscr("WDNb", [128, NFC, D], BF16)
    WYA_d, r_WYA = dscr("WYAb", [128, 4, D], BF16)
    WYB_d, r_WYB = dscr("WYBb", [128, 4, D], BF16)
    WO_d, r_WO = dscr("WOb", [128, 8, D], BF16)
    WPG_d, r_WPG = dscr("WPGb", [128, 8, D], BF16)
    WPL_d, r_WPL = dscr("WPLb", [128, 2, D], BF16)

    def mm(out, lhsT, rhs, start, stop, reads, writes, skip=False):
        if skip:
            P.op("pe", lambda e: e.matmul(out, lhsT=lhsT, rhs=rhs, start=start, stop=stop, skip_group_check=True),
                 reads, writes)
        else:
            P.op("pe", lambda e: e.matmul(out, lhsT=lhsT, rhs=rhs, start=start, stop=stop), reads, writes)

    def tr(out, in_, ident, reads, writes):
        P.op("pe", lambda e: e.transpose(out=out, in_=in_, identity=ident), reads, writes)

    def act(out, in_, func, reads, writes, bias=0.0, scale=1.0, accum_out=None, eng="act"):
        if accum_out is None:
            P.op(eng, lambda e: e.activation(out=out, in_=in_, func=func, bias=bias, scale=scale), reads, writes)
        else:
            P.op(eng, lambda e: e.activation(out=out, in_=in_, func=func, bias=bias, scale=scale,
                                             accum_out=accum_out), reads, writes)

    def acopy(out, in_, reads, writes):
        P.op("act", lambda e: e.copy(out=out, in_=in_), reads, writes)

    def tcopy(eng, out, in_, reads, writes):
        if eng == "act":
            P.op(eng, lambda e: e.copy(out=out, in_=in_), reads, writes)
        else:
            P.op(eng, lambda e: e.tensor_copy(out=out, in_=in_), reads, writes)

    def tt(eng, out, in0, in1, op, reads, writes):
        P.op(eng, lambda e: e.tensor_tensor(out=out, in0=in0, in1=in1, op=op), reads, writes)

    def ts(eng, out, in0, s1, s2, op0, op1, reads, writes):
        if s2 is None:
            P.op(eng, lambda e: e.tensor_scalar(out=out, in0=in0, scalar1=s1, scalar2=None, op0=op0), reads, writes)
        else:
            P.op(eng, lambda e: e.tensor_scalar(out=out, in0=in0, scalar1=s1, scalar2=s2, op0=op0, op1=op1),
                 reads, writes)

    def stt(eng, out, in0, scalar, in1, op0, op1, reads, writes):
        P.op(eng, lambda e: e.scalar_tensor_tensor(out=out, in0=in0, scalar=scalar, in1=in1, op0=op0, op1=op1),
             reads, writes)

    def recip(out, in_, reads, writes):
        P.op("dve", lambda e: e.reciprocal(out=out, in_=in_), reads, writes)

    def memset(eng, ap, val, writes):
        P.op(eng, lambda e: e.memset(ap, val), (), writes)

    def dma(semname, out, in_, reads, writes, eng="sp"):
        P.dma(eng, semname, lambda e: e.dma_start(out=out, in_=in_), reads, writes)

    with ExitStack() as es_top:
        def sbuf(es, name, shape, dt=F32, ro=False):
            t = es.enter_context(nc.sbuf_tensor("s_" + name, list(shape), dt))
            return t, Res(name, ro=ro)

        banks = []
        for i in range(4):
            t = es_top.enter_context(nc.psum_tensor(f"pb{i}", [128, 2, 512], F32))
            banks.append((t, Res(f"pb{i}a")))
            banks.append((t, Res(f"pb{i}b")))

        def bank(i):
            t, r = banks[i]
            return t[:, i % 2, :], r

        def bank2(i):
            t, ra = banks[i]
            _, rb = banks[i + 1]
            return t, ra, rb

        esg = es_top
        identb, r_identb = sbuf(esg, "identb", [128, 128], BF16, ro=True)
        tri, r_tri = sbuf(esg, "tri", [128, 128], F32, ro=True)
        ones, r_ones = sbuf(esg, "ones", [128, 128], F32, ro=True)
        gFb, r_gFb = sbuf(esg, "gFb", [128, D], F32, ro=True)
        lam_t, r_lam = sbuf(esg, "lam_t", [128, 4, 64], F32)
        lam_s, r_lams = sbuf(esg, "lam_s", [128, 4], F32)
        nlam, r_nlam = sbuf(esg, "nlam", [128, 1], F32, ro=True)
        ssq, r_ssq = sbuf(esg, "ssq", [128, 8], F32)
        junk, r_junk = sbuf(esg, "junk", [128, D], F32)

        dma("identb", identb[:], identb_d, [], [r_identb])
        dma("tri", tri[:], tri_d, [], [r_tri])
        dma("ones", ones[:], ones_d, [], [r_ones])
        dma("gFb", gFb[:], gF_d.partition_broadcast(128), [], [r_gFb])
        dma("lam_t", lam_t[:].rearrange("p a b -> p (a b)"),
            lam_d.partition_broadcast(128), [], [r_lam])
        for i in range(2):
            tt("dve", lam_t[:, 2 * i, :], lam_t[:, 2 * i, :], lam_t[:, 2 * i + 1, :], ALU.mult, [r_lam], [r_lam])
            act(lam_t[:, 2 * i + 1, :], lam_t[:, 2 * i, :], AF.Copy, [r_lam], [r_lam, r_lams],
                accum_out=lam_s[:, i:i + 1])
        act(lam_s[:, 2:4], lam_s[:, 0:2], AF.Exp, [r_lams], [r_lams])
        tt("dve", lam_s[:, 0:1], lam_s[:, 3:4], lam_s[:, 2:3], ALU.subtract, [r_lams], [r_lams])
        ts("dve", nlam[:], lam_s[:, 0:1], -LAM_INIT, None, ALU.add, None, [r_lams], [r_nlam])

        def rms_stats(src_ap, n, r_src, col=0):
            act(junk[:, 0:n], src_ap, AF.Square, [r_src], [r_junk, r_ssq], accum_out=ssq[:, col:col + 1])
            act(ssq[:, col:col + 1], ssq[:, col:col + 1], AF.Sqrt, [r_ssq], [r_ssq], bias=EPS, scale=1.0 / n)
            recip(ssq[:, col:col + 1], ssq[:, col:col + 1], [r_ssq], [r_ssq])
            return ssq[:, col:col + 1]

        def transpose8(src_bf, r_src, dst_ap, r_dst, nblk, bidx=0):
            bk, r_bk = bank(bidx)
            bb = bk.bitcast(BF16)
            for k in range(nblk):
                tr(bb[:, k * 128:(k + 1) * 128], src_bf[:, k * 128:(k + 1) * 128], identb[:],
                   [r_src, r_identb], [r_bk])
            acopy(dst_ap, bb[:, 0:nblk * 128].rearrange("p (a b) -> p a b", a=nblk), [r_bk], [r_dst])

        with ExitStack() as es1:
            Wb, r_Wb = sbuf(es1, "Wb", [128, 8, NIN], BF16, ro=True)
            g1b, r_g1b = sbuf(es1, "g1b", [128, D], F32, ro=True)
            cmwb, r_cmwb = sbuf(es1, "cmwb", [128, 4, D], F32, ro=True)
            bifb, r_bifb = sbuf(es1, "bifb", [128, 8], F32, ro=True)
            gmlb, r_gmlb = sbuf(es1, "gmlb", [128, 128], F32, ro=True)
            shifts, r_shifts = sbuf(es1, "shifts", [128, 7, 128], BF16, ro=True)
            CU = 1024
            stage = [sbuf(es1, f"stage{i}", [128, CU], F32) for i in range(2)]
            stageb = [sbuf(es1, f"stageb{i}", [128, CU], BF16) for i in range(2)]
            NXT = 3
            xt = [sbuf(es1, f"xt{i}", [128, D], F32) for i in range(NXT)]
            rc = [sbuf(es1, f"rc{i}", [128, 32], F32) for i in range(NXT)]
            rs = [sbuf(es1, f"rs{i}", [128, 32], F32) for i in range(NXT)]
            st1 = [sbuf(es1, f"st1_{i}", [128, 2], F32) for i in range(NXT)]
            hbs = [sbuf(es1, f"hb{i}", [128, D], BF16) for i in range(2)]
            hTs = [sbuf(es1, f"hT{i}", [128, 8, 128], BF16) for i in range(2)]
            ra, r_ra = sbuf(es1, "ra", [128, 512], F32)
            rb1, r_rb1 = sbuf(es1, "rb1", [128, 8, 32], F32)
            rb2, r_rb2 = sbuf(es1, "rb2", [128, 8, 32], F32)
            krot, r_krot = sbuf(es1, "krot", [128, 512], BF16)
            ktT, r_ktT = sbuf(es1, "ktT", [128, 4, 128], BF16)
            vaug = [sbuf(es1, f"vaug{i}", [128, 4, 130], BF16) for i in range(2)]
            xwk = [(sbuf(es1, f"xwk{i}", [128, 4, 512], BF16)[0], [Res(f"xwk{i}_{j}") for j in range(4)]) for i in range(2)]
            xwq = [(sbuf(es1, f"xwq{i}", [128, 4, 512], BF16)[0], [Res(f"xwq{i}_{j}") for j in range(4)]) for i in range(2)]
            sig, r_sig = sbuf(es1, "sig", [128, 512], F32)
            kmtoks = [sbuf(es1, f"kmtok{i}", [128, 512], BF16) for i in range(2)]
            qmtok, r_qmtok = sbuf(es1, "qmtok", [128, 512], BF16)
            kmT, r_kmT = sbuf(es1, "kmT", [128, 4, 128], BF16)
            qmT, r_qmT = sbuf(es1, "qmT", [128, 4, 128], BF16)
            gt, r_gt = sbuf(es1, "gt", [128, 8], F32)
            gs, r_gs = sbuf(es1, "gs", [128, 24], F32)
            vts = [sbuf(es1, f"vt{i}", [128, 4, 129], BF16) for i in range(2)]
            Cst, r_C = sbuf(es1, "Cst", [128, 4, 129], F32)
            Chat, r_Chat = sbuf(es1, "Chat", [128, 4, 129], F32)
            Chb, r_Chb = sbuf(es1, "Chb", [128, 4, 129], BF16)
            ATs, r_ATs = sbuf(es1, "ATs", [128, 4, 128], BF16)
            hmf, r_hmf = sbuf(es1, "hmf", [128, 4, 128], F32)
            hmbs = [sbuf(es1, f"hmb{i}", [128, 512], BF16) for i in range(2)]
            hmTs, r_hmTs = sbuf(es1, "hmTs", [128, 4, 128], BF16)
            oms, r_oms = sbuf(es1, "oms", [128, 512], F32)
            sgs, r_sgs = sbuf(es1, "sgs", [128, 2 * D], BF16)
            mst, r_mst = sbuf(es1, "mst", [128, 4, 8], F32)
            rr, r_rr = sbuf(es1, "rr", [128, 8], F32)

            dma("g1b", g1b[:], g1_d.partition_broadcast(128), [], [r_g1b])
            dma("cmwb", cmwb[:].rearrange("p a b -> p (a b)"), cmw_d.partition_broadcast(128), [], [r_cmwb])
            dma("bifb", bifb[:], bif_d.partition_broadcast(128), [], [r_bifb])
            dma("gmlb", gmlb[:], gml_d.partition_broadcast(128), [], [r_gmlb])
            dma("shifts", shifts[:], shifts_d, [], [r_shifts])
            memset("pool", Cst[:], 0.0, [r_C])
            for i in range(2):
                memset("pool", xwk[i][0][:], 0.0, xwk[i][1])
                memset("pool", xwq[i][0][:], 0.0, xwq[i][1])
                memset("pool", vaug[i][0][:, :, 128:129], 1.0, [vaug[i][1]])
                memset("pool", vaug[i][0][:, :, 129:130], 0.0, [vaug[i][1]])

            cast_i = [0]

            def cast_unit(src_ap, ncols_free, dst_kind, dst_ap, r_dst, ceng="pool"):
                i = cast_i[0] % 2
                cast_i[0] += 1
                st, r_st = stage[i]
                a, b = src_ap.shape[1], src_ap.shape[2]
                stv = st[:, 0:ncols_free].rearrange("p (a b) -> p a b", a=a)
                dma(f"stage{i}", stv, src_ap, [], [r_st])
                if dst_kind == "sbuf":
                    tcopy(ceng, dst_ap, stv, [r_st], [r_dst])
                else:
                    sbt, r_sbt = stageb[i]
                    sbv = sbt[:, 0:ncols_free].rearrange("p (a b) -> p a b", a=a)
                    tcopy(ceng, sbv, stv, [r_st], [r_sbt])
                    dma(f"stageb{i}", dst_ap, sbv, [r_sbt], [r_dst])

            w_in_v = w_in_d.rearrange("(k p) n -> p k n", p=128)
            GW = CU // 8
            wgrp = {}
            col_groups = []
            for c0 in (KA, VA, IFG, KM, VM, QA, QM, OM, GA, GA + 512, GB, GB + 512):
                n_tot = 8 if c0 == IFG else 512
                for cc in range(c0, c0 + n_tot, GW):
                    col_groups.append((cc, min(GW, c0 + n_tot - cc)))
            engs = ["pool", "dve", "act"]
            for gi, (c0, n) in enumerate(col_groups):
                wgrp[c0] = (Res(f"Wb{c0}", ro=True), n)
                cast_unit(w_in_v[:, :, c0:c0 + n], 8 * n, "sbuf", Wb[:, :, c0:c0 + n], wgrp[c0][0],
                          ceng=engs[gi % 3] if gi < 40 else "pool")

            def wres(c0, n):
                return [r for c, (r, m) in wgrp.items() if c < c0 + n and c + m > c0]

            prep = []
            w_up_v = w_up_d.rearrange("(k p) n -> p k n", p=128)
            for c0 in range(0, 2 * DFF, GW):
                prep.append((w_up_v[:, :, c0:c0 + GW], CU, WUP_d[:, :, c0:c0 + GW], r_WUP))
            w_dn_v = w_dn_d.rearrange("(k p) n -> p k n", p=128)
            for k0 in range(NFC):
                prep.append((w_dn_v[:, k0:k0 + 1, :], D, WDN_d[:, k0:k0 + 1, :], r_WDN))
            w_ya_v = w_ya_d.rearrange("(k p) n -> p k n", p=128)
            w_yb_v = w_yb_d.rearrange("(k p) n -> p k n", p=128)
            for k0 in range(4):
                prep.append((w_ya_v[:, k0:k0 + 1, :], D, WYA_d[:, k0:k0 + 1, :], r_WYA))
                prep.append((w_yb_v[:, k0:k0 + 1, :], D, WYB_d[:, k0:k0 + 1, :], r_WYB))
            w_o_v = w_o_d.rearrange("(k p) n -> p k n", p=128)
            w_pg_v = w_pg_d.rearrange("(k p) n -> p k n", p=128)
            for k0 in range(8):
                prep.append((w_o_v[:, k0:k0 + 1, :], D, WO_d[:, k0:k0 + 1, :], r_WO))
                prep.append((w_pg_v[:, k0:k0 + 1, :], D, WPG_d[:, k0:k0 + 1, :], r_WPG))
            w_pl_v = w_ple_d.rearrange("(k p) n -> p k n", p=128)
            for k0 in range(2):
                prep.append((w_pl_v[:, k0:k0 + 1, :], D, WPL_d[:, k0:k0 + 1, :], r_WPL))
            prep_per_tile = -(-len(prep) // max(1, NT - NOWN))
            prep_pos = [0]

            def emit_prep(k):
                for _ in range(k):
                    if prep_pos[0] < len(prep):
                        pr = prep[prep_pos[0]]
                        prep_pos[0] += 1
                        cast_unit(pr[0], pr[1], "dram", pr[2], pr[3])

            def proj(hT_t, r_hT_, c0, n, bidx, col0=0):
                bk, r_bk = bank(bidx)
                for kc in range(8):
                    mm(bk[:, col0:col0 + n], hT_t[:, kc, :], Wb[:, kc, c0:c0 + n], kc == 0, kc == 7,
                       [r_hT_] + wres(c0, n), [r_bk])
                return bk, r_bk

            def rope_tm(bk, r_bk, cosap, sinap, r_c, r_s, out_bf, r_out):
                v4 = lambda ap: ap.rearrange("p (a b c) -> p a b c", a=8, b=2, c=32)
                kv = v4(bk[:, 0:512])
                av = v4(ra[:, :])
                ov = v4(out_bf[:, :])
                cb4 = cosap.unsqueeze(1).unsqueeze(1).to_broadcast([128, 8, 2, 32])
                sb3 = sinap.unsqueeze(1).to_broadcast([128, 8, 32])
                tt("dve", av, kv, cb4, ALU.mult, [r_bk, r_c], [r_ra])
                tt("dve", rb1[:], kv[:, :, 1, :], sb3, ALU.mult, [r_bk, r_s], [r_rb1])
                tt("dve", rb2[:], kv[:, :, 0, :], sb3, ALU.mult, [r_bk, r_s], [r_rb2])
                tt("dve", ov[:, :, 0, :], av[:, :, 0, :], rb1[:], ALU.subtract, [r_ra, r_rb1], [r_out])
                tt("dve", ov[:, :, 1, :], av[:, :, 1, :], rb2[:], ALU.add, [r_ra, r_rb2], [r_out])

            def conv_premul(bk, r_bk, wcol0, xw_cur):
                xc, r_xc = xw_cur
                for j in range(4):
                    tt("dve", xc[:, j, :], bk[:, 0:512], cmwb[:, j, wcol0:wcol0 + 512], ALU.mult,
                       [r_bk, r_cmwb], [r_xc[j]])

            def conv_mm(xw_cur, xw_prev, bidx_out):
                xc, r_xc = xw_cur
                xp, r_xp = xw_prev
                ob, r_ob = bank(bidx_out)
                for j in range(4):
                    mm(ob[:, 0:512], shifts[:, j, :], xc[:, j, :], j == 0, False, [r_shifts, r_xc[j]], [r_ob])
                for j in range(3):
                    mm(ob[:, 0:512], shifts[:, 4 + j, :], xp[:, j, :], False, j == 2, [r_shifts, r_xp[j]], [r_ob])
                return ob, r_ob

            def silu_tanh(cb, r_cb, out_bf, r_out, c):
                act(sig[:], cb[:, 0:512], AF.Exp, [r_cb], [r_sig], scale=-1.0 / c)
                act(sig[:], sig[:], AF.Ln, [r_sig], [r_sig], bias=1.0)
                act(sig[:], sig[:], AF.Exp, [r_sig], [r_sig], scale=-1.0)
                tt("dve", out_bf[:], cb[:, 0:512], sig[:], ALU.mult, [r_cb, r_sig], [r_out])

            km_scale = 128.0 ** -0.5
            CK = km_scale
            CQ = 1.0
            ts("dve", cmwb[:, :, 512:1024], cmwb[:, :, 512:1024], CK, None, ALU.mult, None, [r_cmwb], [r_cmwb])

            def stage_A1(g):
                x_t, r_x = xt[g % NXT]
                c_t, r_c = rc[g % NXT]
                s_t, r_s = rs[g % NXT]
                s1, r_s1 = st1[g % NXT]
                hb, r_hb = hbs[g % 2]
                dma(f"xt{g % NXT}", x_t[:], xa[g * 128:(g + 1) * 128, :], [], [r_x])
                dma(f"rc{g % NXT}", c_t[:], ropec_d[:, g, :], [], [r_c])
                dma(f"rs{g % NXT}", s_t[:], ropes_d[:, g, :], [], [r_s])
                act(junk[:, 0:D], x_t[:], AF.Square, [r_x], [r_junk, r_s1], accum_out=s1[:, 0:1])
                act(s1[:, 1:2], s1[:, 0:1], AF.Ln, [r_s1], [r_s1], bias=EPS, scale=1.0 / D)
                act(s1[:, 1:2], s1[:, 1:2], AF.Exp, [r_s1], [r_s1], scale=-0.5)
                stt("dve", hb[:], x_t[:], s1[:, 1:2], g1b[:], ALU.mult, ALU.mult, [r_x, r_s1, r_g1b], [r_hb])

            def stage_A2(g):
                hb, r_hb = hbs[g % 2]
                hT, r_hT = hTs[g % 2]
                transpose8(hb, r_hb, hT[:], r_hT, 8, bidx=0)

            def stage_D(g):
                u = g - G0
                hmb, r_hmb = hmbs[g % 2]
                transpose8(hmb, r_hmb, hmTs[:], r_hmTs, 4, bidx=0)
                dma("hmTs", HMT_d[:, :, u * 128:(u + 1) * 128], hmTs[:], [r_hmTs], [r_HMT])

            def stage_C(g):
                kmtok, r_kmtok = kmtoks[g % 2]
                vt, r_vt = vts[g % 2]
                for hp in range(2):
                    cbk, r_cbk = bank(5)
                    for hh in range(2):
                        h = 2 * hp + hh
                        mm(cbk[:, hh * 129:(hh + 1) * 129], kmtok[:, h * 128:(h + 1) * 128], vt[:, h, :], True, True,
                           [r_kmtok, r_vt], [r_cbk])
                    tt("dve", Cst[:, 2 * hp:2 * hp + 2, :], Chat[:, 2 * hp:2 * hp + 2, :],
                       cbk[:, 0:258].rearrange("p (a b) -> p a b", a=2), ALU.add, [r_Chat, r_cbk], [r_C])

            def stage_B(g):
                own = g >= G0
                u = g - G0
                hT, r_hT = hTs[g % 2]
                kmtok, r_kmtok = kmtoks[g % 2]
                vt, r_vt = vts[g % 2]
                c_t, r_c = rc[g % NXT]
                s_t, r_s = rs[g % NXT]
                bkG, r_bkG = proj(hT, r_hT, IFG, 8, 4, col0=16)
                tt("dve", gt[:], bkG[:, 16:24], bifb[:], ALU.add, [r_bkG, r_bifb], [r_gt])
                act(gs[:, 0:4], gt[:, 4:8], AF.Exp, [r_gt], [r_gs], scale=-1.0)
                act(gs[:, 0:4], gs[:, 0:4], AF.Ln, [r_gs], [r_gs], bias=1.0)
                bkM, r_bkM = proj(hT, r_hT, KM, 512, 3)
                bkK, r_bkK = proj(hT, r_hT, KA, 512, 1)
                bkV, r_bkV = proj(hT, r_hT, VA, 512, 2)
                bkW, r_bkW = proj(hT, r_hT, VM, 512, 6)
                if g - 1 >= G0:
                    stage_D(g - 1)
                conv_premul(bkM, r_bkM, 512, xwk[g % 2])
                rope_tm(bkK, r_bkK, c_t[:, :], s_t[:, :], r_c, r_s, krot, r_krot)
                va, r_va = vaug[g % 2]
                acopy(va[:, :, 0:128], bkV[:, 0:512].rearrange("p (a b) -> p a b", a=4), [r_bkV], [r_va])
                dma(f"vaug{g % 2}", VV_d[:, :, g, :].rearrange("h p c -> p h c"), va[:], [r_va], [r_VV])
                bk4, r_bk4 = bank(4)
                mm(bk4[:, 0:4], tri[:], gs[:, 0:4], True, True, [r_tri, r_gs], [r_bk4])
                mm(bk4[:, 8:12], ones[:], gs[:, 0:4], True, True, [r_ones, r_gs], [r_bk4])
                cb, r_cb = conv_mm(xwk[g % 2], xwk[(g + 1) % 2], 7)
                transpose8(krot, r_krot, ktT[:], r_ktT, 4, bidx=0)
                dma("ktT", KT_d[:, :, g * 128:(g + 1) * 128].rearrange("h p t -> p h t"), ktT[:], [r_ktT], [r_KT])
                if g > 0:
                    stage_C(g - 1)
                tt("dve", gs[:, 8:12], bk4[:, 0:4], gt[:, 0:4], ALU.add, [r_bk4, r_gt], [r_gs])
                tt("dve", gs[:, 8:12], gs[:, 8:12], bk4[:, 8:12], ALU.subtract, [r_gs, r_bk4], [r_gs])
                act(gs[:, 8:12], gs[:, 8:12], AF.Exp, [r_gs], [r_gs])
                act(gs[:, 12:16], bk4[:, 8:12], AF.Exp, [r_bk4], [r_gs], scale=-1.0)
                if own:
                    tcopy("dve", gs[:, 4:8], bk4[:, 8:12], [r_bk4], [r_gs])
                    tt("dve", gs[:, 16:20], gs[:, 4:8], bk4[:, 0:4], ALU.subtract, [r_gs, r_bk4], [r_gs])
                    act(gs[:, 16:20], gs[:, 16:20], AF.Exp, [r_gs], [r_gs])
                wk = gs[:, 8:12]
                dd = gs[:, 12:16]
                wp = gs[:, 16:20]
                silu_tanh(cb, r_cb, kmtok, r_kmtok, CK)
                tt("dve", vt[:, :, 0:128], bkW[:, 0:512].rearrange("p (a b) -> p a b", a=4),
                   wk.unsqueeze(2).to_broadcast([128, 4, 128]), ALU.mult, [r_bkW, r_gs], [r_vt])
                tcopy("dve", vt[:, :, 128:129], wk.unsqueeze(2), [r_gs], [r_vt])
                tt("pool", Chat[:], Cst[:], dd.unsqueeze(2).to_broadcast([128, 4, 129]), ALU.mult,
                   [r_C, r_gs], [r_Chat])

                if own:
                    hTo = hT
                    bkq, r_bkq = proj(hTo, r_hT, QA, 512, 1)
                    bkqm, r_bkqm = proj(hTo, r_hT, QM, 512, 2)
                    bkom, r_bkom = proj(hTo, r_hT, OM, 512, 3)
                    conv_premul(bkqm, r_bkqm, 0, xwq[g % 2])
                    rope_tm(bkq, r_bkq, c_t[:, :], s_t[:, :], r_c, r_s, krot, r_krot)
                    cbq, r_cbq = conv_mm(xwq[g % 2], xwq[(g + 1) % 2], 7)
                    transpose8(krot, r_krot, ktT[:], r_ktT, 4, bidx=0)
                    dma("ktT", QT_d[:, :, u * 128:(u + 1) * 128].rearrange("h p t -> p h t"), ktT[:],
                        [r_ktT], [r_QT])
                    bkga0, r_bkga0 = proj(hTo, r_hT, GA, 512, 1)
                    bkga1, r_bkga1 = proj(hTo, r_hT, GA + 512, 512, 2)
                    bkgb0, r_bkgb0 = proj(hTo, r_hT, GB, 512, 5)
                    silu_tanh(cbq, r_cbq, qmtok, r_qmtok, CQ)
                    act(oms[:], bkom[:, 0:512], AF.Sigmoid, [r_bkom], [r_oms])
                    act(sgs[:, 0:512], bkga0[:, 0:512], AF.Sigmoid, [r_bkga0], [r_sgs])
                    act(sgs[:, 512:1024], bkga1[:, 0:512], AF.Sigmoid, [r_bkga1], [r_sgs])
                    act(sgs[:, 1024:1536], bkgb0[:, 0:512], AF.Sigmoid, [r_bkgb0], [r_sgs])
                    transpose8(qmtok, r_qmtok, qmT[:], r_qmT, 4, bidx=0)
                    transpose8(kmtok, r_kmtok, kmT[:], r_kmT, 4, bidx=0)
                    bkgb1, r_bkgb1 = proj(hTo, r_hT, GB + 512, 512, 3)
                    act(sgs[:, 1536:2048], bkgb1[:, 0:512], AF.Sigmoid, [r_bkgb1], [r_sgs])
                    dma("sgs", SG_d[u * 128:(u + 1) * 128, :], sgs[:], [r_sgs], [r_SG])
                    bk7, r_bk7 = bank(7)
                    for h in range(H):
                        mm(bk7[:, h * 128:(h + 1) * 128], kmT[:, h, :], qmT[:, h, :], True, True,
                           [r_kmT, r_qmT], [r_bk7])
                    tt("dve", ATs[:], bk7[:, 0:512].rearrange("p (a b) -> p a b", a=4),
                       tri[:, :].unsqueeze(1).to_broadcast([128, 4, 128]), ALU.mult, [r_bk7, r_tri], [r_ATs])
                    acopy(Chb[:], Chat[:], [r_Chat], [r_Chb])
                    obs = [bank(6), bank(4)]
                    for hp in range(2):
                        ob, r_ob = obs[hp]
                        for hh in range(2):
                            h = 2 * hp + hh
                            oap = ob[:, hh * 129:(hh + 1) * 129]
                            mm(oap, qmT[:, h, :], Chb[:, h, :], True, False, [r_qmT, r_Chb], [r_ob])
                            mm(oap, ATs[:, h, :], vt[:, h, :], False, True, [r_ATs, r_vt], [r_ob])
                    for hp in range(2):
                        ob, r_ob = obs[hp]
                        for hh in range(2):
                            h = 2 * hp + hh
                            oap = ob[:, hh * 129:(hh + 1) * 129]
                            act(rr[:, 4 * hp + hh:4 * hp + hh + 1], oap[:, 128:129], AF.Abs, [r_ob, r_gs], [r_rr],
                                scale=wp[:, h:h + 1])
                    for hp in range(2):
                        ob, r_ob = obs[hp]
                        for hh in range(2):
                            h = 2 * hp + hh
                            oap = ob[:, hh * 129:(hh + 1) * 129]
                            c0 = 4 * hp + hh
                            ts("dve", rr[:, c0:c0 + 1], rr[:, c0:c0 + 1], 1.0, None, ALU.max, None, [r_rr], [r_rr])
                            recip(rr[:, c0:c0 + 1], rr[:, c0:c0 + 1], [r_rr], [r_rr])
                            tt("dve", rr[:, c0 + 2:c0 + 3], rr[:, c0:c0 + 1], wp[:, h:h + 1], ALU.mult, [r_rr, r_gs], [r_rr])
                            ts("dve", hmf[:, h, :], oap[:, 0:128], rr[:, c0 + 2:c0 + 3], None, ALU.mult, None,
                               [r_ob, r_rr], [r_hmf])
                    for h in range(H):
                        act(junk[:, 0:128], hmf[:, h, :], AF.Copy, [r_hmf], [r_junk, r_mst],
                            accum_out=mst[:, h, 0:1])
                        act(junk[:, 0:128], hmf[:, h, :], AF.Square, [r_hmf], [r_junk, r_mst],
                            accum_out=mst[:, h, 1:2])
                    ts("dve", mst[:, :, 2:3], mst[:, :, 0:1], 1.0 / 128, None, ALU.mult, None, [r_mst], [r_mst])
                    tt("dve", mst[:, :, 3:4], mst[:, :, 2:3], mst[:, :, 2:3], ALU.mult, [r_mst], [r_mst])
                    stt("dve", mst[:, :, 4:5], mst[:, :, 1:2], 1.0 / 128, mst[:, :, 3:4], ALU.mult, ALU.subtract,
                        [r_mst], [r_mst])
                    act(mst[:, :, 4:5], mst[:, :, 4:5], AF.Ln, [r_mst], [r_mst], bias=EPS)
                    act(mst[:, :, 5:6], mst[:, :, 4:5], AF.Exp, [r_mst], [r_mst], scale=-0.5)
                    for h in range(H):
                        ts("dve", hmf[:, h, :], hmf[:, h, :], mst[:, h, 2:3], mst[:, h, 5:6], ALU.subtract, ALU.mult,
                           [r_hmf, r_mst], [r_hmf])
                    tt("dve", hmf[:], hmf[:], gmlb[:, :].unsqueeze(1).to_broadcast([128, 4, 128]), ALU.mult,
                       [r_hmf, r_gmlb], [r_hmf])
                    hmb, r_hmb = hmbs[g % 2]
                    tt("dve", hmb[:], hmf[:].rearrange("p a b -> p (a b)"), oms[:], ALU.mult, [r_hmf, r_oms], [r_hmb])

            stage_A1(0)
            if NT > 1:
                stage_A1(1)
            stage_A2(0)
            for g in range(NT):
                if g + 2 < NT:
                    stage_A1(g + 2)
                if g + 1 < NT:
                    stage_A2(g + 1)
                if g < NT - NOWN:
                    emit_prep(prep_per_tile)
                stage_B(g)
            stage_D(NT - 1)
            stage_C(NT - 1)
            emit_prep(len(prep))

        scale = 64.0 ** -0.5
        P.barrier()
        with ExitStack() as es2:
            biast, r_biast = sbuf(es2, "biast", [128, NSLOT], F32, ro=True)
            pat, r_pat = sbuf(es2, "pat", [128, 4, 512], BF16, ro=True)
            gdab, r_gdab = sbuf(es2, "gdab", [128, 128], F32, ro=True)
            qt_s = [sbuf(es2, f"qts{i}", [128, 512], BF16) for i in range(2)]
            kg = [sbuf(es2, f"kg{i}", [128, NTS * 128], BF16) for i in range(2)]
            vg = [sbuf(es2, f"vg{i}", [128, NTS, 130], BF16) for i in range(2)]
            pts = [sbuf(es2, f"pts{i}", [128, 2, 512], BF16) for i in range(3)]
            aof, r_aof = sbuf(es2, "aof", [128, BT, 4, 128], F32)
            aob, r_aob = sbuf(es2, "aob", [128, 512], BF16)
            aoTs, r_aoTs = sbuf(es2, "aoTs", [128, 4, 128], BF16)
            tmpa, r_tmpa = sbuf(es2, "tmpa", [128, 128], F32)
            rl, r_rl = sbuf(es2, "rl", [128, 8], F32)
            osb, r_osb = sbuf(es2, "osb", [128, 2, 2, 258], F32)
            dma("biast", biast[:], biast_d, [], [r_biast])
            dma("pat", pat[:], pat_d, [], [r_pat])
            dma("gdab", gdab[:], gda_d.partition_broadcast(128), [], [r_gdab])
            ts("dve", gdab[:], gdab[:], 1.0 - LAM_INIT, None, ALU.mult, None, [r_gdab], [r_gdab])

            S_t = [bank2(0), bank2(2)]
            O_bk = [[bank(4), bank(5)], [bank(6), bank(7)]]
            ld_i = 0
            pt_i = 0
            s_i = 0
            for bi, tiles in enumerate(blocks):
                nq = len(tiles)
                N = nq * 128
                t0 = tiles[0]
                klist = []
                for kt in range(NT):
                    ku = kt - G0
                    if bi == 0:
                        if kt < G0:
                            klist.append((kt, None))
                        elif kt == G0:
                            klist.append((kt, 0))
                    else:
                        if ku < t0:
                            klist.append((kt, None))
                        elif ku < t0 + nq:
                            klist.append((kt, ku - t0))
                groups = {}
                for kt, kind in klist:
                    groups.setdefault(kt // NTS, []).append((kt, kind))
                gkeys = sorted(groups)
                for h in range(H):
                    q_s, r_q = qt_s[(bi * H + h) % 2]
                    dma(f"qts{(bi * H + h) % 2}", q_s[:, 0:N], QT_d[h, :, t0 * 128:t0 * 128 + N], [r_QT], [r_q])
                    total_k = len(klist)
                    loaded = {}
                    seq = []
                    for gk in gkeys:
                        for (kt, kind) in groups[gk]:
                            seq.append((gk, kt, kind))

                    def ensure_group(gk):
                        nonlocal ld_i
                        if gk not in loaded:
                            k_s, r_k = kg[ld_i % 2]
                            v_s, r_v = vg[ld_i % 2]
                            dma(f"kg{ld_i % 2}", k_s[:], KT_d[h, :, gk * NTS * 128:(gk + 1) * NTS * 128], [r_KT], [r_k])
                            dma(f"vg{ld_i % 2}", v_s[:], VV_d[h, :, gk * NTS:(gk + 1) * NTS, :], [r_VV], [r_v])
                            ld_i += 1
                            loaded[gk] = (k_s, r_k, v_s, r_v)
                        return loaded[gk]

                    def emit_s(idx):
                        nonlocal s_i, pt_i
                        gk, kt, kind = seq[idx]
                        k_s, r_k, v_s, r_v = ensure_group(gk)
                        kl = kt - gk * NTS
                        St, r_sa, r_sb = S_t[s_i % 2]
                        s_i += 1
                        mm(St[:, 0, 0:N], k_s[0:64, kl * 128:(kl + 1) * 128], q_s[0:64, 0:N], True, True,
                           [r_k, r_q], [r_sa])
                        mm(St[:, 1, 0:N], k_s[64:128, kl * 128:(kl + 1) * 128], q_s[64:128, 0:N], True, True,
                           [r_k, r_q], [r_sb])
                        p_s, r_p = pts[pt_i % 3]
                        pt_i += 1
                        act(p_s[:, :, 0:N], St[:, :, 0:N], AF.Exp, [r_sa, r_sb, r_biast], [r_p],
                            bias=biast[:, gk:gk + 1], scale=scale)
                        if kind is not None:
                            tt("dve", p_s[:, :, 0:N], p_s[:, :, 0:N],
                               pat[:, kind, 0:N].unsqueeze(1).to_broadcast([128, 2, N]), ALU.mult,
                               [r_p, r_pat], [r_p])
                        return p_s, r_p

                    def emit_pv(idx, p_s, r_p):
                        gk, kt, kind = seq[idx]
                        k_s, r_k, v_s, r_v = loaded[gk]
                        kl = kt - gk * NTS
                        for sub in range(nq):
                            for m in range(2):
                                ob, r_ob = O_bk[m][sub // 2]
                                mm(ob[:, (sub % 2) * 129:(sub % 2 + 1) * 129],
                                   p_s[:, m, sub * 128:(sub + 1) * 128], v_s[:, kl, 0:129],
                                   idx == 0 and (sub % 2 == 0), idx == total_k - 1, [r_p, r_v], [r_ob], skip=True)

                    pend = emit_s(0)
                    for idx in range(total_k):
                        nxt = emit_s(idx + 1) if idx + 1 < total_k else None
                        emit_pv(idx, *pend)
                        pend = nxt
                    for m in range(2):
                        for half in range((nq + 1) // 2):
                            obk, r_obk = O_bk[m][half]
                            ncol = min(2, nq - 2 * half) * 129
                            tcopy("dve", osb[:, m, half, 0:ncol], obk[:, 0:ncol], [r_obk], [r_osb])
                    for sub in range(nq):
                        r_o0 = r_osb
                        r_o1 = r_osb
                        o0 = osb[:, 0, sub // 2, (sub % 2) * 129:(sub % 2 + 1) * 129]
                        o1 = osb[:, 1, sub // 2, (sub % 2) * 129:(sub % 2 + 1) * 129]
                        ts("dve", rl[:, 0:1], o0[:, 128:129], 1e-30, None, ALU.max, None, [r_o0], [r_rl])
                        ts("dve", rl[:, 1:2], o1[:, 128:129], 1e-30, None, ALU.max, None, [r_o1], [r_rl])
                        recip(rl[:, 2:4], rl[:, 0:2], [r_rl], [r_rl])
                        tt("dve", rl[:, 4:5], rl[:, 3:4], nlam[:, 0:1], ALU.mult, [r_rl, r_nlam], [r_rl])
                        ts("dve", tmpa[:], o1[:, 0:128], rl[:, 4:5], None, ALU.mult, None, [r_o1, r_rl], [r_tmpa])
                        stt("dve", aof[:, sub, h, :], o0[:, 0:128], rl[:, 2:3], tmpa[:], ALU.mult, ALU.add,
                            [r_o0, r_rl, r_tmpa], [r_aof])
                for sub in range(nq):
                    for h in range(H):
                        act(junk[:, 0:128], aof[:, sub, h, :], AF.Square, [r_aof], [r_junk, r_ssq],
                            accum_out=ssq[:, 4 + h:5 + h])
                    act(ssq[:, 4:8], ssq[:, 4:8], AF.Sqrt, [r_ssq], [r_ssq], bias=EPS, scale=1.0 / 128)
                    recip(ssq[:, 4:8], ssq[:, 4:8], [r_ssq], [r_ssq])
                    for h in range(H):
                        stt("dve", aob[:, h * 128:(h + 1) * 128], aof[:, sub, h, :], ssq[:, 4 + h:5 + h], gdab[:],
                            ALU.mult, ALU.mult, [r_aof, r_ssq, r_gdab], [r_aob])
                    transpose8(aob, r_aob, aoTs[:], r_aoTs, 4, bidx=0)
                    u = t0 + sub
                    dma("aoTs", AOT_d[:, :, u * 128:(u + 1) * 128], aoTs[:], [r_aoTs], [r_AOT])

        P.barrier()
        with ExitStack() as es3:
            Wya, r_Wya = sbuf(es3, "Wya", [128, 4, D], BF16, ro=True)
            Wyb, r_Wyb = sbuf(es3, "Wyb", [128, 4, D], BF16, ro=True)
            Wo, r_Wo = sbuf(es3, "Wo", [128, 8, D], BF16, ro=True)
            Wpg, r_Wpg = sbuf(es3, "Wpg", [128, 8, D], BF16, ro=True)
            Wpl, r_Wpl = sbuf(es3, "Wpl", [128, 2, D], BF16, ro=True)
            wdn = [sbuf(es3, f"wdn{i}", [128, 2, D], BF16) for i in range(2)]
            g2b, r_g2b = sbuf(es3, "g2b", [128, D], F32, ro=True)
            cfw, r_cfw = sbuf(es3, "cfw", [128, NFC, 3], F32, ro=True)
            cfb, r_cfb = sbuf(es3, "cfb", [128, NFC], F32, ro=True)
            wup = [sbuf(es3, f"wup{i}", [128, 8, 2, 256], BF16) for i in range(2)]
            aoT, r_aoT = sbuf(es3, "aoT", [128, 4, 512], BF16)
            hmT, r_hmT = sbuf(es3, "hmT", [128, 4, 512], BF16)
            sg3s = [sbuf(es3, f"sg3_{i}", [128, 2 * D], BF16) for i in range(2)]
            x3t = [sbuf(es3, f"x3t{i}", [128, D], F32) for i in range(2)]
            m2, r_m2 = sbuf(es3, "m2", [128, D], F32)
            mbs = [sbuf(es3, f"mb{i}", [128, D], BF16) for i in range(2)]
            mT, r_mT = sbuf(es3, "mT", [128, 8, 128], BF16)
            x1b, r_x1b = sbuf(es3, "x1b", [128, BT, D], F32)
            h2bs = [sbuf(es3, f"h2b{i}", [128, D], BF16) for i in range(2)]
            h2T, r_h2T = sbuf(es3, "h2T", [128, 8, 512], BF16)
            aext, r_aext = sbuf(es3, "aext", [128, 2 + 512], F32)
            halo = [sbuf(es3, f"halo{i}", [128, NFC, 2], F32) for i in range(2)]
            acc, r_acc = sbuf(es3, "acc", [128, 512], F32)
            gel, r_gel = sbuf(es3, "gel", [128, 512], F32)
            uT, r_uT = sbuf(es3, "uT", [128, NFC, 512], BF16)
            x2bf, r_x2bf = sbuf(es3, "x2bf", [128, D], BF16)
            x2Ts = [sbuf(es3, f"x2T{i}", [128, 8, 128], BF16) for i in range(2)]
            gsb, r_gsb = sbuf(es3, "gsb", [128, D], F32)
            pf, r_pf = sbuf(es3, "pf", [128, 2, 128], F32)
            pbs = [sbuf(es3, f"pb{i}", [128, 2, 128], BF16) for i in range(2)]
            m1, r_m1 = gsb, r_gsb
            yo = x3t

            dma("Wya", Wya[:], WYA_d, [r_WYA], [r_Wya])
            dma("Wyb", Wyb[:], WYB_d, [r_WYB], [r_Wyb])
            dma("Wo", Wo[:], WO_d, [r_WO], [r_Wo])
            dma("Wpg", Wpg[:], WPG_d, [r_WPG], [r_Wpg])
            dma("Wpl", Wpl[:], WPL_d, [r_WPL], [r_Wpl])
            dma("g2b", g2b[:], g2_d.partition_broadcast(128), [], [r_g2b])
            dma("cfw", cfw[:], cfw_d, [], [r_cfw])
            dma("cfb", cfb[:], cfb_d, [], [r_cfb])
            for i in range(2):
                memset("pool", halo[i][0][:], 0.0, [halo[i][1]])

            wu_i = 0
            wd_i = 0
            yo_i = 0
            x_i = 0
            for bi, tiles in enumerate(blocks):
                nq = len(tiles)
                N = nq * 128
                t0 = tiles[0]
                last_is_halo = bi == 0
                dma("aoT", aoT[:, :, 0:N], AOT_d[:, :, t0 * 128:t0 * 128 + N], [r_AOT], [r_aoT])
                dma("hmT", hmT[:, :, 0:N], HMT_d[:, :, t0 * 128:t0 * 128 + N], [r_HMT], [r_hmT])
                xbuf = {}

                def stage_MA(ti):
                    nonlocal x_i
                    u = tiles[ti]
                    x_t, r_x = x3t[x_i % 2]
                    xbuf[ti] = (x_t, r_x)
                    dma(f"x3t{x_i % 2}", x_t[:], xa[(G0 + u) * 128:(G0 + u + 1) * 128, :], [], [r_x])
                    x_i += 1
                    sg3, r_sg3 = sg3s[ti % 2]
                    dma(f"sg3_{ti % 2}", sg3[:], SG_d[u * 128:(u + 1) * 128, :], [r_SG], [r_sg3])
                    ya, r_ya0, r_ya1 = bank2(0)
                    yb, r_yb0, r_yb1 = bank2(2)
                    for half, (ra_, rb_) in enumerate(((r_ya0, r_yb0), (r_ya1, r_yb1))):
                        for c in range(4):
                            mm(ya[:, half, :], aoT[:, c, ti * 128:(ti + 1) * 128], Wya[:, c, half * 512:(half + 1) * 512],
                               c == 0, c == 3, [r_aoT, r_Wya], [ra_])
                        for c in range(4):
                            mm(yb[:, half, :], hmT[:, c, ti * 128:(ti + 1) * 128], Wyb[:, c, half * 512:(half + 1) * 512],
                               c == 0, c == 3, [r_hmT, r_Wyb], [rb_])
                    mb, r_mb = mbs[ti % 2]
                    tt("dve", m1[:].rearrange("p (a b) -> p a b", a=2), ya[:, :, :],
                       sg3[:, 0:D].rearrange("p (a b) -> p a b", a=2), ALU.mult, [r_ya0, r_ya1, r_sg3], [r_m1])
                    tt("dve", m2[:].rearrange("p (a b) -> p a b", a=2), yb[:, :, :],
                       sg3[:, D:2 * D].rearrange("p (a b) -> p a b", a=2), ALU.mult, [r_yb0, r_yb1, r_sg3], [r_m2])
                    tt("dve", mb[:], m1[:], m2[:], ALU.add, [r_m1, r_m2], [r_mb])

                def stage_MB(ti):
                    mb, r_mb = mbs[ti % 2]
                    x_t, r_x = xbuf[ti]
                    h2b, r_h2b = h2bs[ti % 2]
                    transpose8(mb, r_mb, mT[:], r_mT, 8, bidx=4)
                    xo, r_xo0, r_xo1 = bank2(6)
                    for half, rr_ in enumerate((r_xo0, r_xo1)):
                        for kc in range(8):
                            mm(xo[:, half, :], mT[:, kc, :], Wo[:, kc, half * 512:(half + 1) * 512], kc == 0, kc == 7,
                               [r_mT, r_Wo], [rr_])
                    tt("dve", x1b[:, ti, :].rearrange("p (a b) -> p a b", a=2), xo[:, :, :],
                       x_t[:].rearrange("p (a b) -> p a b", a=2), ALU.add, [r_xo0, r_xo1, r_x], [r_x1b])
                    rstd = rms_stats(x1b[:, ti, :], D, r_x1b)
                    stt("dve", h2b[:], x1b[:, ti, :], rstd, g2b[:], ALU.mult, ALU.mult, [r_x1b, r_ssq, r_g2b], [r_h2b])

                def stage_MC(ti):
                    h2b, r_h2b = h2bs[ti % 2]
                    transpose8(h2b, r_h2b, h2T[:, :, ti * 128:(ti + 1) * 128], r_h2T, 8, bidx=5)

                stage_MA(0)
                for ti in range(nq):
                    if ti + 1 < nq:
                        stage_MA(ti + 1)
                    stage_MB(ti)
                    if ti >= 1:
                        stage_MC(ti - 1)
                stage_MC(nq - 1)

                h_in, r_hin = halo[bi % 2]
                h_out, r_hout = halo[(bi + 1) % 2]
                for fc0 in range(0, NFC, 2):
                    w_s, r_w = wup[wu_i % 2]
                    wu_i += 1
                    dma(f"wup{(wu_i - 1) % 2}", w_s[:, :, 0, :], WUP_d[:, :, fc0 * 128:fc0 * 128 + 256], [r_WUP], [r_w])
                    if not last_is_halo:
                        dma(f"wup{(wu_i - 1) % 2}", w_s[:, :, 1, :], WUP_d[:, :, DFF + fc0 * 128:DFF + fc0 * 128 + 256],
                            [r_WUP], [r_w])
                    for fi in range(2):
                        fc = fc0 + fi
                        ab, r_ab = bank(2 * fi)
                        for kc in range(8):
                            mm(ab[:, 0:N], w_s[:, kc, 0, fi * 128:(fi + 1) * 128], h2T[:, kc, 0:N], kc == 0, kc == 7,
                               [r_w, r_h2T], [r_ab])
                        acopy(aext[:, 2:2 + N], ab[:, 0:N], [r_ab], [r_aext])
                        tcopy("dve", aext[:, 0:2], h_in[:, fc, :], [r_hin], [r_aext])
                        tcopy("dve", h_out[:, fc, :], aext[:, N:N + 2], [r_aext], [r_hout])
                        if last_is_halo:
                            continue
                        gb_, r_gb = bank(2 * fi + 1)
                        for kc in range(8):
                            mm(gb_[:, 0:N], w_s[:, kc, 1, fi * 128:(fi + 1) * 128], h2T[:, kc, 0:N], kc == 0, kc == 7,
                               [r_w, r_h2T], [r_gb])
                        ts("dve", acc[:, 0:N], aext[:, 0:N], cfw[:, fc, 0:1], cfb[:, fc:fc + 1], ALU.mult, ALU.add,
                           [r_aext, r_cfw, r_cfb], [r_acc])
                        stt("dve", acc[:, 0:N], aext[:, 1:N + 1], cfw[:, fc, 1:2], acc[:, 0:N], ALU.mult, ALU.add,
                            [r_aext, r_cfw, r_acc], [r_acc])
                        stt("dve", acc[:, 0:N], aext[:, 2:N + 2], cfw[:, fc, 2:3], acc[:, 0:N], ALU.mult, ALU.add,
                            [r_aext, r_cfw, r_acc], [r_acc])
                        act(gel[:, 0:N], acc[:, 0:N], AF.Gelu_apprx_tanh, [r_acc], [r_gel])
                        tt("dve", uT[:, fc, 0:N], gel[:, 0:N], gb_[:, 0:N], ALU.mult, [r_gel, r_gb], [r_uT])
                if last_is_halo:
                    continue

                for fc0 in range(0, NFC, 2):
                    wd, r_wd = wdn[wd_i % 2]
                    wd_i += 1
                    dma(f"wdn{(wd_i - 1) % 2}", wd[:], WDN_d[:, fc0:fc0 + 2, :], [r_WDN], [r_wd])
                    for fi in range(2):
                        fc = fc0 + fi
                        for ti in range(nq):
                            for half in range(2):
                                bkx, r_bkx = bank(2 * ti + half)
                                mm(bkx[:, :], uT[:, fc, ti * 128:(ti + 1) * 128], wd[:, fi, half * 512:(half + 1) * 512],
                                   fc == 0, fc == NFC - 1, [r_uT, r_wd], [r_bkx])
                for ti in range(nq):
                    xd, r_xd0, r_xd1 = bank2(2 * ti)
                    tt("dve", x1b[:, ti, :].rearrange("p (a b) -> p a b", a=2), xd[:, :, :],
                       x1b[:, ti, :].rearrange("p (a b) -> p a b", a=2), ALU.add, [r_xd0, r_xd1, r_x1b], [r_x1b])
                def stage_PA(ti):
                    u = tiles[ti]
                    x2 = x1b[:, ti, :]
                    x2T, r_x2T = x2Ts[ti % 2]
                    pb, r_pb = pbs[ti % 2]
                    tcopy("act", x2bf[:], x2, [r_x1b], [r_x2bf])
                    transpose8(x2bf, r_x2bf, x2T[:], r_x2T, 8, bidx=6)
                    tok0 = (u - 1) * 128
                    dma("pf", pf[:], pT_d[:, :, tok0:tok0 + 128], [], [r_pf])
                    tcopy("dve", pb[:], pf[:], [r_pf], [r_pb])

                def stage_PB(ti):
                    nonlocal yo_i
                    u = tiles[ti]
                    x2 = x1b[:, ti, :]
                    r_x2 = r_x1b
                    x2T, r_x2T = x2Ts[ti % 2]
                    pb, r_pb = pbs[ti % 2]
                    tok0 = (u - 1) * 128
                    gp, r_gp0, r_gp1 = bank2(0)
                    pp, r_pp0, r_pp1 = bank2(2)
                    for half, (rg_, rp_) in enumerate(((r_gp0, r_pp0), (r_gp1, r_pp1))):
                        for kc in range(8):
                            mm(gp[:, half, :], x2T[:, kc, :], Wpg[:, kc, half * 512:(half + 1) * 512], kc == 0, kc == 7,
                               [r_x2T, r_Wpg], [rg_])
                        for c in range(2):
                            mm(pp[:, half, :], pb[:, c, :], Wpl[:, c, half * 512:(half + 1) * 512], c == 0, c == 1,
                               [r_pb, r_Wpl], [rp_])
                    act(gsb[:].rearrange("p (a b) -> p a b", a=2), gp[:, :, :], AF.Sigmoid, [r_gp0, r_gp1], [r_gsb])
                    tt("dve", gsb[:].rearrange("p (a b) -> p a b", a=2), pp[:, :, :],
                       gsb[:].rearrange("p (a b) -> p a b", a=2), ALU.mult, [r_pp0, r_pp1, r_gsb], [r_gsb])
                    tt("dve", x2, x2, gsb[:], ALU.add, [r_x2, r_gsb], [r_x2])
                    rstd = rms_stats(x2, D, r_x2)
                    y_t, r_yt = yo[yo_i % 2]
                    yo_i += 1
                    stt("dve", y_t[:], x2, rstd, gFb[:], ALU.mult, ALU.mult, [r_x2, r_ssq, r_gFb], [r_yt])
                    dma(f"x3t{(yo_i - 1) % 2}", y_d[tok0:tok0 + 128, :], y_t[:], [r_yt], [r_y])

                stage_PA(0)
                for ti in range(nq):
                    if ti + 1 < nq:
                        stage_PA(ti + 1)
                    stage_PB(ti)

            P.wait_all("sp", [r_y, yo[0][1], yo[1][1]])
            P.wait_all("pool", [r_y, yo[0][1], yo[1][1]])
            P.wait_all("act", [r_y, yo[0][1], yo[1][1]])
        P.run()
    return nc, P


def host_constants(NTS):
    NT = NSLOT * NTS
    bf = ml_dtypes.bfloat16
    c = {}
    c["identb"] = np.eye(128, dtype=np.float32).astype(bf)
    s = np.arange(128)
    c["tri"] = (s[:, None] <= s[None, :]).astype(np.float32)
    c["ones"] = np.ones((128, 128), np.float32)
    sh = np.zeros((128, 7, 128), np.float32)
    for j in range(4):
        for t in range(128):
            sidx = t - 3 + j
            if 0 <= sidx < 128:
                sh[sidx, j, t] = 1.0
    for j in range(3):
        for t in range(128):
            sidx = 128 + t - 3 + j
            if 0 <= sidx < 128:
                sh[sidx, 4 + j, t] = 1.0
    c["shifts"] = sh.astype(bf)
    pat = np.zeros((128, 4, 512), np.float32)
    k = np.arange(128)
    q = np.arange(512)
    for o in range(4):
        pat[:, o, :] = (((128 * o + k) // 64)[:, None] <= (q // 64)[None, :]).astype(np.float32)
    c["pat"] = pat.astype(bf)
    inv_freq = (10000.0 ** (-np.arange(0, 64, 2, dtype=np.float32) / 64)).astype(np.float32)
    pos = (np.arange(NT)[None, :] * 128 + np.arange(128)[:, None]).astype(np.float32)
    ang = pos[:, :, None] * inv_freq[None, None, :]
    c["ropec"] = np.cos(ang).astype(np.float32)
    c["ropes"] = np.sin(ang).astype(np.float32)
    return c


_PROG_CACHE = {}


def make_in_maps(inputs, NTS, cores=None):
    SLOT = NTS * 128
    S = NSLOT * SLOT
    NT = NSLOT * NTS
    f32 = np.float32
    x = np.asarray(inputs["x"], f32).reshape(S, D)
    p = np.asarray(inputs["p"], f32).reshape(S, PLE)
    consts = host_constants(NTS)
    shared = {
        "w_in": np.ascontiguousarray(np.asarray(inputs["w_in"], f32)[0]),
        "w_ya": np.ascontiguousarray(np.asarray(inputs["w_ya"], f32)[0]),
        "w_yb": np.ascontiguousarray(np.asarray(inputs["w_yb"], f32)[0]),
        "w_o": np.ascontiguousarray(np.asarray(inputs["w_o"], f32)[0]),
        "w_up": np.ascontiguousarray(np.asarray(inputs["w_up"], f32)[0]),
        "w_down": np.ascontiguousarray(np.asarray(inputs["w_down"], f32)[0]),
        "w_ple": np.ascontiguousarray(np.asarray(inputs["w_ple"], f32)[0]),
        "w_pg": np.ascontiguousarray(np.asarray(inputs["w_pg"], f32)[0]),
        "g1": np.asarray(inputs["norm1_g"], f32).reshape(1, D),
        "g2": np.asarray(inputs["norm2_g"], f32).reshape(1, D),
        "gF": np.asarray(inputs["final_g"], f32).reshape(1, D),
        "gda": np.asarray(inputs["da_norm_g"], f32).reshape(1, 128),
        "gml": np.asarray(inputs["ml_norm_g"], f32).reshape(1, 128),
        "b_if": np.asarray(inputs["b_if"], f32).reshape(1, 8),
        "lamv": np.stack([np.asarray(inputs[k], f32).reshape(64) for k in ("lam_q1", "lam_k1", "lam_q2", "lam_k2")]).reshape(1, 256),
        "conv_m_w": np.ascontiguousarray(np.asarray(inputs["conv_m_w"], f32)[0]).reshape(1, 4 * D),
        "cfw": np.ascontiguousarray(np.asarray(inputs["conv_f_w"], f32)[0].T.reshape(NFC, 128, 3).transpose(1, 0, 2)),
        "cfb": np.ascontiguousarray(np.asarray(inputs["conv_f_b"], f32)[0].reshape(NFC, 128).T),
    }
    shared.update(consts)
    in_maps = []
    for r in (range(NCORES) if cores is None else cores):
        xa = np.zeros((NT * 128, D), f32)
        lo = (r - (NSLOT - 1)) * SLOT
        hi = (r + 1) * SLOT
        src_lo = max(lo, 0)
        xa[src_lo - lo:, :] = x[src_lo:hi]
        pT = np.ascontiguousarray(p[r * SLOT:(r + 1) * SLOT].T.reshape(2, 128, SLOT).transpose(1, 0, 2))
        biast = np.zeros((128, NSLOT), f32)
        for s_ in range(NSLOT):
            if s_ < NSLOT - 1 - r:
                biast[:, s_] = NEG
        m = dict(shared)
        m["xa"] = xa
        m["pT"] = pT
        m["biast"] = biast
        in_maps.append(m)
    return in_maps


def kernel(**inputs):
    S = int(np.asarray(inputs["x"]).shape[1])
    NTS = S // (NSLOT * 128)
    if NTS not in _PROG_CACHE:
        _PROG_CACHE[NTS] = build_program(NTS)[0]
    nc = _PROG_CACHE[NTS]
    in_maps = make_in_maps(inputs, NTS)
    res = run_bass_kernel_spmd(nc, in_maps, core_ids=list(range(NCORES)))
    out = np.concatenate([np.asarray(r["y"], np.float32) for r in res.results], axis=0)
    return out.reshape(1, S, D)
```

```python
import math
from contextlib import ExitStack

import numpy as np
import ml_dtypes

import concourse.bass as bass
import concourse.mybir as mybir
from concourse.bass_utils import run_bass_kernel_spmd

F32 = mybir.dt.float32
BF16 = mybir.dt.bfloat16
AF = mybir.ActivationFunctionType
ALU = mybir.AluOpType

NCORES = 8
NSLOT = 8
D = 1024
NIN = 5640
DFF = 2816
NFC = DFF // 128
PLE = 256
H = 4
EPS = 1e-6
LAM_INIT = 0.8 - 0.6 * math.exp(-0.3 * 0)
QA, KA, VA, QM, KM, VM, OM, IFG, GA, GB = 0, 512, 1024, 1536, 2048, 2560, 3072, 3584, 3592, 4616
NEG = -30000.0


class Res:
    __slots__ = ("name", "w", "r", "ro")

    def __init__(self, name, ro=False):
        self.name = name
        self.w = None
        self.r = []
        self.ro = ro


class Prog:
    ENGS = ("pe", "act", "dve", "pool", "sp")

    def __init__(self, nc):
        self.nc = nc
        self.q = {e: [] for e in self.ENGS}
        self.sem = {e: nc.alloc_semaphore(f"c_{e}") for e in self.ENGS}
        self.cnt = {e: 0 for e in self.ENGS}
        self.seen = {e: {} for e in self.ENGS}
        self.dsem = {}
        self.dknow = {}
        self.hist = {e: {} for e in self.ENGS}
        self.n_wait = 0
        self.n_ins = 0

    def _merge(self, eng, know):
        se = self.seen[eng]
        for k, v in know.items():
            if se.get(k, 0) < v:
                se[k] = v

    def _need(self, eng, toks):
        se = self.seen[eng]
        cand = []
        for t in toks:
            if t is None:
                continue
            kind, key, val = t
            if kind == "d":
                val = self.dsem[key][1]
            cand.append((kind, key, val))
        cand.sort(key=lambda t: (t[0] == "c" and t[1] == eng, -t[2]))
        out = []
        for kind, key, val in cand:
            k = (kind, key)
            if se.get(k, 0) >= val:
                continue
            if kind == "c":
                sem = self.sem[key]
                know = self.hist[key].get(val)
            else:
                sem = self.dsem[key][0]
                know = self.dknow.get(key)
            se[k] = val
            if know:
                self._merge(eng, know)
            out.append((sem, val))
        return out

    def _deps(self, eng, reads, writes):
        toks = []
        for r in reads:
            toks.append(r.w)
        for w in writes:
            if not (eng == "pe" and w.w is not None and w.w[0] == "c" and w.w[1] == "pe"):
                toks.append(w.w)
            toks.extend(w.r)
        return toks

    def _record(self, tok, reads, writes):
        for r in reads:
            if not r.ro:
                r.r.append(tok)
        for w in writes:
            w.w = tok
            w.r = []

    def op(self, eng, fn, reads=(), writes=()):
        waits = self._need(eng, self._deps(eng, reads, writes))
        self.cnt[eng] += 1
        n = self.cnt[eng]
        tok = ("c", eng, n)
        snap = dict(self.seen[eng])
        snap[("c", eng)] = n
        self.hist[eng][n] = snap
        self.q[eng].append((waits, fn, self.sem[eng], 1))
        self.n_wait += len(waits)
        self.n_ins += 1
        self._record(tok, reads, writes)
        return tok

    def dma(self, eng, semname, fn, reads=(), writes=()):
        if semname not in self.dsem:
            self.dsem[semname] = [self.nc.alloc_semaphore(f"d_{semname}"), 0]
            self.dknow[semname] = {}
        waits = self._need(eng, self._deps(eng, reads, writes))
        ent = self.dsem[semname]
        ent[1] += 16
        tok = ("d", semname, ent[1])
        dk = self.dknow[semname]
        for k, v in self.seen[eng].items():
            if dk.get(k, 0) < v:
                dk[k] = v
        self.q[eng].append((waits, fn, ent[0], 16))
        self.n_wait += len(waits)
        self.n_ins += 1
        self._record(tok, reads, writes)
        return tok

    def wait_all(self, eng, ress):
        toks = []
        for r in ress:
            toks.append(r.w)
            toks.extend(r.r)
        waits = self._need(eng, toks)
        self.q[eng].append((waits, None, None, 0))

    def barrier(self):
        for eng in self.ENGS:
            toks = [("c", e2, self.cnt[e2]) for e2 in self.ENGS if self.cnt[e2] > 0]
            toks += [("d", name, ent[1]) for name, ent in self.dsem.items() if ent[1] > 0]
            waits = self._need(eng, toks)
            self.q[eng].append((waits, None, None, 0))

    def run(self):
        q = self.q

        def play(engobj, items):
            for waits, fn, sem, inc in items:
                if fn is None:
                    for (s, v) in waits:
                        engobj.wait_ge(s, v)
                    continue
                for (s, v) in waits[:-1]:
                    engobj.wait_ge(s, v)
                ins = fn(engobj)
                if waits:
                    ins._wait_ge(*waits[-1])
                if sem is not None:
                    ins.then_inc(sem, inc)

        with self.nc.Block() as block:
            @block.tensor
            def _(e):
                play(e, q["pe"])

            @block.scalar
            def _(e):
                play(e, q["act"])

            @block.vector
            def _(e):
                play(e, q["dve"])

            @block.gpsimd
            def _(e):
                play(e, q["pool"])

            @block.sync
            def _(e):
                play(e, q["sp"])


def build_program(NTS):
    NT = NSLOT * NTS
    NOWN = NTS + 1
    G0 = NT - NOWN
    SLOT = NTS * 128
    BT = min(4, NTS)
    blocks = [[0]] + [list(range(1 + b, 1 + b + BT)) for b in range(0, NTS, BT)]
    NOWNTOK = NOWN * 128

    nc = bass.Bass("TRN2", target_bir_lowering=False)
    P = Prog(nc)

    def din(name, shape, dt=F32):
        return nc.dram_tensor(name, list(shape), dt, kind="ExternalInput").ap()

    def dscr(name, shape, dt):
        return nc.dram_tensor(name, list(shape), dt).ap(), Res(name)

    xa = din("xa", [NT * 128, D])
    pT_d = din("pT", [128, 2, SLOT])
    w_in_d = din("w_in", [D, NIN])
    w_ya_d = din("w_ya", [512, D])
    w_yb_d = din("w_yb", [512, D])
    w_o_d = din("w_o", [D, D])
    w_up_d = din("w_up", [D, 2 * DFF])
    w_dn_d = din("w_down", [DFF, D])
    w_ple_d = din("w_ple", [PLE, D])
    w_pg_d = din("w_pg", [D, D])
    g1_d = din("g1", [1, D])
    g2_d = din("g2", [1, D])
    gF_d = din("gF", [1, D])
    gda_d = din("gda", [1, 128])
    gml_d = din("gml", [1, 128])
    bif_d = din("b_if", [1, 8])
    lam_d = din("lamv", [1, 256])
    cmw_d = din("conv_m_w", [1, 4 * D])
    cfw_d = din("cfw", [128, NFC, 3])
    cfb_d = din("cfb", [128, NFC])
    identb_d = din("identb", [128, 128], BF16)
    tri_d = din("tri", [128, 128])
    ones_d = din("ones", [128, 128])
    shifts_d = din("shifts", [128, 7, 128], BF16)
    pat_d = din("pat", [128, 4, 512], BF16)
    ropec_d = din("ropec", [128, NT, 32])
    ropes_d = din("ropes", [128, NT, 32])
    biast_d = din("biast", [128, NSLOT])
    y_d = nc.dram_tensor("y", [SLOT, D], F32, kind="ExternalOutput").ap()
    r_y = Res("y")

    KT_d, r_KT = dscr("KT", [H, 128, NT * 128], BF16)
    VV_d, r_VV = dscr("VV", [H, 128, NT, 130], BF16)
    QT_d, r_QT = dscr("QT", [H, 128, NOWNTOK], BF16)
    SG_d, r_SG = dscr("SG", [NOWNTOK, 2 * D], BF16)
    HMT_d, r_HMT = dscr("HMT", [128, H, NOWNTOK], BF16)
    AOT_d, r_AOT = dscr("AOT", [128, H, NOWNTOK], BF16)
    WUP_d, r_WUP = dscr("WUPb", [128, 8, 2 * DFF], BF16)
    WDN_d, r_WDN = dscr("WDNb", [128, NFC, D], BF16)
    WYA_d, r_WYA = dscr("WYAb", [128, 4, D], BF16)
    WYB_d, r_WYB = dscr("WYBb", [128, 4, D], BF16)
    WO_d, r_WO = dscr("WOb", [128, 8, D], BF16)
    WPG_d, r_WPG = dscr("WPGb", [128, 8, D], BF16)
    WPL_d, r_WPL = dscr("WPLb", [128, 2, D], BF16)

    def mm(out, lhsT, rhs, start, stop, reads, writes, skip=False):
        if skip:
            P.op("pe", lambda e: e.matmul(out, lhsT=lhsT, rhs=rhs, start=start, stop=stop, skip_group_check=True),
                 reads, writes)
        else:
            P.op("pe", lambda e: e.matmul(out, lhsT=lhsT, rhs=rhs, start=start, stop=stop), reads, writes)

    def tr(out, in_, ident, reads, writes):
        P.op("pe", lambda e: e.transpose(out=out, in_=in_, identity=ident), reads, writes)

    def act(out, in_, func, reads, writes, bias=0.0, scale=1.0, accum_out=None, eng="act"):
        if accum_out is None:
            P.op(eng, lambda e: e.activation(out=out, in_=in_, func=func, bias=bias, scale=scale), reads, writes)
        else:
            P.op(eng, lambda e: e.activation(out=out, in_=in_, func=func, bias=bias, scale=scale,
                                             accum_out=accum_out), reads, writes)

    def acopy(out, in_, reads, writes):
        P.op("act", lambda e: e.copy(out=out, in_=in_), reads, writes)

    def tcopy(eng, out, in_, reads, writes):
        if eng == "act":
            P.op(eng, lambda e: e.copy(out=out, in_=in_), reads, writes)
        else:
            P.op(eng, lambda e: e.tensor_copy(out=out, in_=in_), reads, writes)

    def tt(eng, out, in0, in1, op, reads, writes):
        P.op(eng, lambda e: e.tensor_tensor(out=out, in0=in0, in1=in1, op=op), reads, writes)

    def ts(eng, out, in0, s1, s2, op0, op1, reads, writes):
        if s2 is None:
            P.op(eng, lambda e: e.tensor_scalar(out=out, in0=in0, scalar1=s1, scalar2=None, op0=op0), reads, writes)
        else:
            P.op(eng, lambda e: e.tensor_scalar(out=out, in0=in0, scalar1=s1, scalar2=s2, op0=op0, op1=op1),
                 reads, writes)

    def stt(eng, out, in0, scalar, in1, op0, op1, reads, writes):
        P.op(eng, lambda e: e.scalar_tensor_tensor(out=out, in0=in0, scalar=scalar, in1=in1, op0=op0, op1=op1),
             reads, writes)

    def recip(out, in_, reads, writes):
        P.op("dve", lambda e: e.reciprocal(out=out, in_=in_), reads, writes)

    def memset(eng, ap, val, writes):
        P.op(eng, lambda e: e.memset(ap, val), (), writes)

    def dma(semname, out, in_, reads, writes, eng="sp"):
        P.dma(eng, semname, lambda e: e.dma_start(out=out, in_=in_), reads, writes)

    with ExitStack() as es_top:
        def sbuf(es, name, shape, dt=F32, ro=False):
            t = es.enter_context(nc.sbuf_tensor("s_" + name, list(shape), dt))
            return t, Res(name, ro=ro)

        banks = []
        for i in range(4):
            t = es_top.enter_context(nc.psum_tensor(f"pb{i}", [128, 2, 512], F32))
            banks.append((t, Res(f"pb{i}a")))
            banks.append((t, Res(f"pb{i}b")))

        def bank(i):
            t, r = banks[i]
            return t[:, i % 2, :], r

        def bank2(i):
            t, ra = banks[i]
            _, rb = banks[i + 1]
            return t, ra, rb

        esg = es_top
        identb, r_identb = sbuf(esg, "identb", [128, 128], BF16, ro=True)
        tri, r_tri = sbuf(esg, "tri", [128, 128], F32, ro=True)
        ones, r_ones = sbuf(esg, "ones", [128, 128], F32, ro=True)
        gFb, r_gFb = sbuf(esg, "gFb", [128, D], F32, ro=True)
        lam_t, r_lam = sbuf(esg, "lam_t", [128, 4, 64], F32)
        lam_s, r_lams = sbuf(esg, "lam_s", [128, 4], F32)
        nlam, r_nlam = sbuf(esg, "nlam", [128, 1], F32, ro=True)
        ssq, r_ssq = sbuf(esg, "ssq", [128, 8], F32)
        junk, r_junk = sbuf(esg, "junk", [128, D], F32)

        dma("identb", identb[:], identb_d, [], [r_identb])
        dma("tri", tri[:], tri_d, [], [r_tri])
        dma("ones", ones[:], ones_d, [], [r_ones])
        dma("gFb", gFb[:], gF_d.partition_broadcast(128), [], [r_gFb])
        dma("lam_t", lam_t[:].rearrange("p a b -> p (a b)"),
            lam_d.partition_broadcast(128), [], [r_lam])
        for i in range(2):
            tt("dve", lam_t[:, 2 * i, :], lam_t[:, 2 * i, :], lam_t[:, 2 * i + 1, :], ALU.mult, [r_lam], [r_lam])
            act(lam_t[:, 2 * i + 1, :], lam_t[:, 2 * i, :], AF.Copy, [r_lam], [r_lam, r_lams],
                accum_out=lam_s[:, i:i + 1])
        act(lam_s[:, 2:4], lam_s[:, 0:2], AF.Exp, [r_lams], [r_lams])
        tt("dve", lam_s[:, 0:1], lam_s[:, 3:4], lam_s[:, 2:3], ALU.subtract, [r_lams], [r_lams])
        ts("dve", nlam[:], lam_s[:, 0:1], -LAM_INIT, None, ALU.add, None, [r_lams], [r_nlam])

        def rms_stats(src_ap, n, r_src, col=0):
            act(junk[:, 0:n], src_ap, AF.Square, [r_src], [r_junk, r_ssq], accum_out=ssq[:, col:col + 1])
            act(ssq[:, col:col + 1], ssq[:, col:col + 1], AF.Sqrt, [r_ssq], [r_ssq], bias=EPS, scale=1.0 / n)
            recip(ssq[:, col:col + 1], ssq[:, col:col + 1], [r_ssq], [r_ssq])
            return ssq[:, col:col + 1]

        def transpose8(src_bf, r_src, dst_ap, r_dst, nblk, bidx=0):
            bk, r_bk = bank(bidx)
            bb = bk.bitcast(BF16)
            for k in range(nblk):
                tr(bb[:, k * 128:(k + 1) * 128], src_bf[:, k * 128:(k + 1) * 128], identb[:],
                   [r_src, r_identb], [r_bk])
            acopy(dst_ap, bb[:, 0:nblk * 128].rearrange("p (a b) -> p a b", a=nblk), [r_bk], [r_dst])

        with ExitStack() as es1:
            Wb, r_Wb = sbuf(es1, "Wb", [128, 8, NIN], BF16, ro=True)
            g1b, r_g1b = sbuf(es1, "g1b", [128, D], F32, ro=True)
            cmwb, r_cmwb = sbuf(es1, "cmwb", [128, 4, D], F32, ro=True)
            bifb, r_bifb = sbuf(es1, "bifb", [128, 8], F32, ro=True)
            gmlb, r_gmlb = sbuf(es1, "gmlb", [128, 128], F32, ro=True)
            shifts, r_shifts = sbuf(es1, "shifts", [128, 7, 128], BF16, ro=True)
            CU = 1024
            stage = [sbuf(es1, f"stage{i}", [128, CU], F32) for i in range(2)]
            stageb = [sbuf(es1, f"stageb{i}", [128, CU], BF16) for i in range(2)]
            NXT = 3
            xt = [sbuf(es1, f"xt{i}", [128, D], F32) for i in range(NXT)]
            rc = [sbuf(es1, f"rc{i}", [128, 32], F32) for i in range(NXT)]
            rs = [sbuf(es1, f"rs{i}", [128, 32], F32) for i in range(NXT)]
            st1 = [sbuf(es1, f"st1_{i}", [128, 2], F32) for i in range(NXT)]
            hbs = [sbuf(es1, f"hb{i}", [128, D], BF16) for i in range(2)]
            hTs = [sbuf(es1, f"hT{i}", [128, 8, 128], BF16) for i in range(2)]
            ra, r_ra = sbuf(es1, "ra", [128, 512], F32)
            rb1, r_rb1 = sbuf(es1, "rb1", [128, 8, 32], F32)
            rb2, r_rb2 = sbuf(es1, "rb2", [128, 8, 32], F32)
            krot, r_krot = sbuf(es1, "krot", [128, 512], BF16)
            ktT, r_ktT = sbuf(es1, "ktT", [128, 4, 128], BF16)
            vaug = [sbuf(es1, f"vaug{i}", [128, 4, 130], BF16) for i in range(2)]
            xwk = [(sbuf(es1, f"xwk{i}", [128, 4, 512], BF16)[0], [Res(f"xwk{i}_{j}") for j in range(4)]) for i in range(2)]
            xwq = [(sbuf(es1, f"xwq{i}", [128, 4, 512], BF16)[0], [Res(f"xwq{i}_{j}") for j in range(4)]) for i in range(2)]
            sig, r_sig = sbuf(es1, "sig", [128, 512], F32)
            kmtoks = [sbuf(es1, f"kmtok{i}", [128, 512], BF16) for i in range(2)]
            qmtok, r_qmtok = sbuf(es1, "qmtok", [128, 512], BF16)
            kmT, r_kmT = sbuf(es1, "kmT", [128, 4, 128], BF16)
            qmT, r_qmT = sbuf(es1, "qmT", [128, 4, 128], BF16)
            gt, r_gt = sbuf(es1, "gt", [128, 8], F32)
            gs, r_gs = sbuf(es1, "gs", [128, 24], F32)
            vts = [sbuf(es1, f"vt{i}", [128, 4, 129], BF16) for i in range(2)]
            Cst, r_C = sbuf(es1, "Cst", [128, 4, 129], F32)
            Chat, r_Chat = sbuf(es1, "Chat", [128, 4, 129], F32)
            Chb, r_Chb = sbuf(es1, "Chb", [128, 4, 129], BF16)
            ATs, r_ATs = sbuf(es1, "ATs", [128, 4, 128], BF16)
            hmf, r_hmf = sbuf(es1, "hmf", [128, 4, 128], F32)
            hmbs = [sbuf(es1, f"hmb{i}", [128, 512], BF16) for i in range(2)]
            hmTs, r_hmTs = sbuf(es1, "hmTs", [128, 4, 128], BF16)
            oms, r_oms = sbuf(es1, "oms", [128, 512], F32)
            sgs, r_sgs = sbuf(es1, "sgs", [128, 2 * D], BF16)
            mst, r_mst = sbuf(es1, "mst", [128, 4, 8], F32)
            rr, r_rr = sbuf(es1, "rr", [128, 8], F32)

            dma("g1b", g1b[:], g1_d.partition_broadcast(128), [], [r_g1b])
            dma("cmwb", cmwb[:].rearrange("p a b -> p (a b)"), cmw_d.partition_broadcast(128), [], [r_cmwb])
            dma("bifb", bifb[:], bif_d.partition_broadcast(128), [], [r_bifb])
            dma("gmlb", gmlb[:], gml_d.partition_broadcast(128), [], [r_gmlb])
            dma("shifts", shifts[:], shifts_d, [], [r_shifts])
            memset("pool", Cst[:], 0.0, [r_C])
            for i in range(2):
                memset("pool", xwk[i][0][:], 0.0, xwk[i][1])
                memset("pool", xwq[i][0][:], 0.0, xwq[i][1])
                memset("pool", vaug[i][0][:, :, 128:129], 1.0, [vaug[i][1]])
                memset("pool", vaug[i][0][:, :, 129:130], 0.0, [vaug[i][1]])

            cast_i = [0]

            def cast_unit(src_ap, ncols_free, dst_kind, dst_ap, r_dst, ceng="pool"):
                i = cast_i[0] % 2
                cast_i[0] += 1
                st, r_st = stage[i]
                a, b = src_ap.shape[1], src_ap.shape[2]
                stv = st[:, 0:ncols_free].rearrange("p (a b) -> p a b", a=a)
                dma(f"stage{i}", stv, src_ap, [], [r_st])
                if dst_kind == "sbuf":
                    tcopy(ceng, dst_ap, stv, [r_st], [r_dst])
                else:
                    sbt, r_sbt = stageb[i]
                    sbv = sbt[:, 0:ncols_free].rearrange("p (a b) -> p a b", a=a)
                    tcopy(ceng, sbv, stv, [r_st], [r_sbt])
                    dma(f"stageb{i}", dst_ap, sbv, [r_sbt], [r_dst])

            w_in_v = w_in_d.rearrange("(k p) n -> p k n", p=128)
            GW = CU // 8
            wgrp = {}
            col_groups = []
            for c0 in (KA, VA, IFG, KM, VM, QA, QM, OM, GA, GA + 512, GB, GB + 512):
                n_tot = 8 if c0 == IFG else 512
                for cc in range(c0, c0 + n_tot, GW):
                    col_groups.append((cc, min(GW, c0 + n_tot - cc)))
            engs = ["pool", "dve", "act"]
            for gi, (c0, n) in enumerate(col_groups):
                wgrp[c0] = (Res(f"Wb{c0}", ro=True), n)
                cast_unit(w_in_v[:, :, c0:c0 + n], 8 * n, "sbuf", Wb[:, :, c0:c0 + n], wgrp[c0][0],
                          ceng=engs[gi % 3] if gi < 40 else "pool")

            def wres(c0, n):
                return [r for c, (r, m) in wgrp.items() if c < c0 + n and c + m > c0]

            prep = []
            w_up_v = w_up_d.rearrange("(k p) n -> p k n", p=128)
            for c0 in range(0, 2 * DFF, GW):
                prep.append((w_up_v[:, :, c0:c0 + GW], CU, WUP_d[:, :, c0:c0 + GW], r_WUP))
            w_dn_v = w_dn_d.rearrange("(k p) n -> p k n", p=128)
            for k0 in range(NFC):
                prep.append((w_dn_v[:, k0:k0 + 1, :], D, WDN_d[:, k0:k0 + 1, :], r_WDN))
            w_ya_v = w_ya_d.rearrange("(k p) n -> p k n", p=128)
            w_yb_v = w_yb_d.rearrange("(k p) n -> p k n", p=128)
            for k0 in range(4):
                prep.append((w_ya_v[:, k0:k0 + 1, :], D, WYA_d[:, k0:k0 + 1, :], r_WYA))
                prep.append((w_yb_v[:, k0:k0 + 1, :], D, WYB_d[:, k0:k0 + 1, :], r_WYB))
            w_o_v = w_o_d.rearrange("(k p) n -> p k n", p=128)
            w_pg_v = w_pg_d.rearrange("(k p) n -> p k n", p=128)
            for k0 in range(8):
                prep.append((w_o_v[:, k0:k0 + 1, :], D, WO_d[:, k0:k0 + 1, :], r_WO))
                prep.append((w_pg_v[:, k0:k0 + 1, :], D, WPG_d[:, k0:k0 + 1, :], r_WPG))
            w_pl_v = w_ple_d.rearrange("(k p) n -> p k n", p=128)
            for k0 in range(2):
                prep.append((w_pl_v[:, k0:k0 + 1, :], D, WPL_d[:, k0:k0 + 1, :], r_WPL))
            prep_per_tile = -(-len(prep) // max(1, NT - NOWN))
            prep_pos = [0]

            def emit_prep(k):
                for _ in range(k):
                    if prep_pos[0] < len(prep):
                        pr = prep[prep_pos[0]]
                        prep_pos[0] += 1
                        cast_unit(pr[0], pr[1], "dram", pr[2], pr[3])

            def proj(hT_t, r_hT_, c0, n, bidx, col0=0):
                bk, r_bk = bank(bidx)
                for kc in range(8):
                    mm(bk[:, col0:col0 + n], hT_t[:, kc, :], Wb[:, kc, c0:c0 + n], kc == 0, kc == 7,
                       [r_hT_] + wres(c0, n), [r_bk])
                return bk, r_bk

            def rope_tm(bk, r_bk, cosap, sinap, r_c, r_s, out_bf, r_out):
                v4 = lambda ap: ap.rearrange("p (a b c) -> p a b c", a=8, b=2, c=32)
                kv = v4(bk[:, 0:512])
                av = v4(ra[:, :])
                ov = v4(out_bf[:, :])
                cb4 = cosap.unsqueeze(1).unsqueeze(1).to_broadcast([128, 8, 2, 32])
                sb3 = sinap.unsqueeze(1).to_broadcast([128, 8, 32])
                tt("dve", av, kv, cb4, ALU.mult, [r_bk, r_c], [r_ra])
                tt("dve", rb1[:], kv[:, :, 1, :], sb3, ALU.mult, [r_bk, r_s], [r_rb1])
                tt("dve", rb2[:], kv[:, :, 0, :], sb3, ALU.mult, [r_bk, r_s], [r_rb2])
                tt("dve", ov[:, :, 0, :], av[:, :, 0, :], rb1[:], ALU.subtract, [r_ra, r_rb1], [r_out])
                tt("dve", ov[:, :, 1, :], av[:, :, 1, :], rb2[:], ALU.add, [r_ra, r_rb2], [r_out])

            def conv_premul(bk, r_bk, wcol0, xw_cur):
                xc, r_xc = xw_cur
                for j in range(4):
                    tt("dve", xc[:, j, :], bk[:, 0:512], cmwb[:, j, wcol0:wcol0 + 512], ALU.mult,
                       [r_bk, r_cmwb], [r_xc[j]])

            def conv_mm(xw_cur, xw_prev, bidx_out):
                xc, r_xc = xw_cur
                xp, r_xp = xw_prev
                ob, r_ob = bank(bidx_out)
                for j in range(4):
                    mm(ob[:, 0:512], shifts[:, j, :], xc[:, j, :], j == 0, False, [r_shifts, r_xc[j]], [r_ob])
                for j in range(3):
                    mm(ob[:, 0:512], shifts[:, 4 + j, :], xp[:, j, :], False, j == 2, [r_shifts, r_xp[j]], [r_ob])
                return ob, r_ob

            def silu_tanh(cb, r_cb, out_bf, r_out, c):
                act(sig[:], cb[:, 0:512], AF.Exp, [r_cb], [r_sig], scale=-1.0 / c)
                act(sig[:], sig[:], AF.Ln, [r_sig], [r_sig], bias=1.0)
                act(sig[:], sig[:], AF.Exp, [r_sig], [r_sig], scale=-1.0)
                tt("dve", out_bf[:], cb[:, 0:512], sig[:], ALU.mult, [r_cb, r_sig], [r_out])

            km_scale = 128.0 ** -0.5
            CK = km_scale
            CQ = 1.0
            ts("dve", cmwb[:, :, 512:1024], cmwb[:, :, 512:1024], CK, None, ALU.mult, None, [r_cmwb], [r_cmwb])

            def stage_A1(g):
                x_t, r_x = xt[g % NXT]
                c_t, r_c = rc[g % NXT]
                s_t, r_s = rs[g % NXT]
                s1, r_s1 = st1[g % NXT]
                hb, r_hb = hbs[g % 2]
                dma(f"xt{g % NXT}", x_t[:], xa[g * 128:(g + 1) * 128, :], [], [r_x])
                dma(f"rc{g % NXT}", c_t[:], ropec_d[:, g, :], [], [r_c])
                dma(f"rs{g % NXT}", s_t[:], ropes_d[:, g, :], [], [r_s])
                act(junk[:, 0:D], x_t[:], AF.Square, [r_x], [r_junk, r_s1], accum_out=s1[:, 0:1])
                act(s1[:, 1:2], s1[:, 0:1], AF.Ln, [r_s1], [r_s1], bias=EPS, scale=1.0 / D)
                act(s1[:, 1:2], s1[:, 1:2], AF.Exp, [r_s1], [r_s1], scale=-0.5)
                stt("dve", hb[:], x_t[:], s1[:, 1:2], g1b[:], ALU.mult, ALU.mult, [r_x, r_s1, r_g1b], [r_hb])

            def stage_A2(g):
                hb, r_hb = hbs[g % 2]
                hT, r_hT = hTs[g % 2]
                transpose8(hb, r_hb, hT[:], r_hT, 8, bidx=0)

            def stage_D(g):
                u = g - G0
                hmb, r_hmb = hmbs[g % 2]
                transpose8(hmb, r_hmb, hmTs[:], r_hmTs, 4, bidx=0)
                dma("hmTs", HMT_d[:, :, u * 128:(u + 1) * 128], hmTs[:], [r_hmTs], [r_HMT])

            def stage_C(g):
                kmtok, r_kmtok = kmtoks[g % 2]
                vt, r_vt = vts[g % 2]
                for hp in range(2):
                    cbk, r_cbk = bank(5)
                    for hh in range(2):
                        h = 2 * hp + hh
                        mm(cbk[:, hh * 129:(hh + 1) * 129], kmtok[:, h * 128:(h + 1) * 128], vt[:, h, :], True, True,
                           [r_kmtok, r_vt], [r_cbk])
                    tt("dve", Cst[:, 2 * hp:2 * hp + 2, :], Chat[:, 2 * hp:2 * hp + 2, :],
                       cbk[:, 0:258].rearrange("p (a b) -> p a b", a=2), ALU.add, [r_Chat, r_cbk], [r_C])

            def stage_B(g):
                own = g >= G0
                u = g - G0
                hT, r_hT = hTs[g % 2]
                kmtok, r_kmtok = kmtoks[g % 2]
                vt, r_vt = vts[g % 2]
                c_t, r_c = rc[g % NXT]
                s_t, r_s = rs[g % NXT]
                bkG, r_bkG = proj(hT, r_hT, IFG, 8, 4, col0=16)
                tt("dve", gt[:], bkG[:, 16:24], bifb[:], ALU.add, [r_bkG, r_bifb], [r_gt])
                act(gs[:, 0:4], gt[:, 4:8], AF.Exp, [r_gt], [r_gs], scale=-1.0)
                act(gs[:, 0:4], gs[:, 0:4], AF.Ln, [r_gs], [r_gs], bias=1.0)
                bkM, r_bkM = proj(hT, r_hT, KM, 512, 3)
                bkK, r_bkK = proj(hT, r_hT, KA, 512, 1)
                bkV, r_bkV = proj(hT, r_hT, VA, 512, 2)
                bkW, r_bkW = proj(hT, r_hT, VM, 512, 6)
                if g - 1 >= G0:
                    stage_D(g - 1)
                conv_premul(bkM, r_bkM, 512, xwk[g % 2])
                rope_tm(bkK, r_bkK, c_t[:, :], s_t[:, :], r_c, r_s, krot, r_krot)
                va, r_va = vaug[g % 2]
                acopy(va[:, :, 0:128], bkV[:, 0:512].rearrange("p (a b) -> p a b", a=4), [r_bkV], [r_va])
                dma(f"vaug{g % 2}", VV_d[:, :, g, :].rearrange("h p c -> p h c"), va[:], [r_va], [r_VV])
                bk4, r_bk4 = bank(4)
                mm(bk4[:, 0:4], tri[:], gs[:, 0:4], True, True, [r_tri, r_gs], [r_bk4])
                mm(bk4[:, 8:12], ones[:], gs[:, 0:4], True, True, [r_ones, r_gs], [r_bk4])
                cb, r_cb = conv_mm(xwk[g % 2], xwk[(g + 1) % 2], 7)
                transpose8(krot, r_krot, ktT[:], r_ktT, 4, bidx=0)
                dma("ktT", KT_d[:, :, g * 128:(g + 1) * 128].rearrange("h p t -> p h t"), ktT[:], [r_ktT], [r_KT])
                if g > 0:
                    stage_C(g - 1)
                tt("dve", gs[:, 8:12], bk4[:, 0:4], gt[:, 0:4], ALU.add, [r_bk4, r_gt], [r_gs])
                tt("dve", gs[:, 8:12], gs[:, 8:12], bk4[:, 8:12], ALU.subtract, [r_gs, r_bk4], [r_gs])
                act(gs[:, 8:12], gs[:, 8:12], AF.Exp, [r_gs], [r_gs])
                act(gs[:, 12:16], bk4[:, 8:12], AF.Exp, [r_bk4], [r_gs], scale=-1.0)
                if own:
                    tcopy("dve", gs[:, 4:8], bk4[:, 8:12], [r_bk4], [r_gs])
                    tt("dve", gs[:, 16:20], gs[:, 4:8], bk4[:, 0:4], ALU.subtract, [r_gs, r_bk4], [r_gs])
                    act(gs[:, 16:20], gs[:, 16:20], AF.Exp, [r_gs], [r_gs])
                wk = gs[:, 8:12]
                dd = gs[:, 12:16]
                wp = gs[:, 16:20]
                silu_tanh(cb, r_cb, kmtok, r_kmtok, CK)
                tt("dve", vt[:, :, 0:128], bkW[:, 0:512].rearrange("p (a b) -> p a b", a=4),
                   wk.unsqueeze(2).to_broadcast([128, 4, 128]), ALU.mult, [r_bkW, r_gs], [r_vt])
                tcopy("dve", vt[:, :, 128:129], wk.unsqueeze(2), [r_gs], [r_vt])
                tt("pool", Chat[:], Cst[:], dd.unsqueeze(2).to_broadcast([128, 4, 129]), ALU.mult,
                   [r_C, r_gs], [r_Chat])

                if own:
                    hTo = hT
                    bkq, r_bkq = proj(hTo, r_hT, QA, 512, 1)
                    bkqm, r_bkqm = proj(hTo, r_hT, QM, 512, 2)
                    bkom, r_bkom = proj(hTo, r_hT, OM, 512, 3)
                    conv_premul(bkqm, r_bkqm, 0, xwq[g % 2])
                    rope_tm(bkq, r_bkq, c_t[:, :], s_t[:, :], r_c, r_s, krot, r_krot)
                    cbq, r_cbq = conv_mm(xwq[g % 2], xwq[(g + 1) % 2], 7)
                    transpose8(krot, r_krot, ktT[:], r_ktT, 4, bidx=0)
                    dma("ktT", QT_d[:, :, u * 128:(u + 1) * 128].rearrange("h p t -> p h t"), ktT[:],
                        [r_ktT], [r_QT])
                    bkga0, r_bkga0 = proj(hTo, r_hT, GA, 512, 1)
                    bkga1, r_bkga1 = proj(hTo, r_hT, GA + 512, 512, 2)
                    bkgb0, r_bkgb0 = proj(hTo, r_hT, GB, 512, 5)
                    silu_tanh(cbq, r_cbq, qmtok, r_qmtok, CQ)
                    act(oms[:], bkom[:, 0:512], AF.Sigmoid, [r_bkom], [r_oms])
                    act(sgs[:, 0:512], bkga0[:, 0:512], AF.Sigmoid, [r_bkga0], [r_sgs])
                    act(sgs[:, 512:1024], bkga1[:, 0:512], AF.Sigmoid, [r_bkga1], [r_sgs])
                    act(sgs[:, 1024:1536], bkgb0[:, 0:512], AF.Sigmoid, [r_bkgb0], [r_sgs])
                    transpose8(qmtok, r_qmtok, qmT[:], r_qmT, 4, bidx=0)
                    transpose8(kmtok, r_kmtok, kmT[:], r_kmT, 4, bidx=0)
                    bkgb1, r_bkgb1 = proj(hTo, r_hT, GB + 512, 512, 3)
                    act(sgs[:, 1536:2048], bkgb1[:, 0:512], AF.Sigmoid, [r_bkgb1], [r_sgs])
                    dma("sgs", SG_d[u * 128:(u + 1) * 128, :], sgs[:], [r_sgs], [r_SG])
                    bk7, r_bk7 = bank(7)
                    for h in range(H):
                        mm(bk7[:, h * 128:(h + 1) * 128], kmT[:, h, :], qmT[:, h, :], True, True,
                           [r_kmT, r_qmT], [r_bk7])
                    tt("dve", ATs[:], bk7[:, 0:512].rearrange("p (a b) -> p a b", a=4),
                       tri[:, :].unsqueeze(1).to_broadcast([128, 4, 128]), ALU.mult, [r_bk7, r_tri], [r_ATs])
                    acopy(Chb[:], Chat[:], [r_Chat], [r_Chb])
                    obs = [bank(6), bank(4)]
                    for hp in range(2):
                        ob, r_ob = obs[hp]
                        for hh in range(2):
                            h = 2 * hp + hh
                            oap = ob[:, hh * 129:(hh + 1) * 129]
                            mm(oap, qmT[:, h, :], Chb[:, h, :], True, False, [r_qmT, r_Chb], [r_ob])
                            mm(oap, ATs[:, h, :], vt[:, h, :], False, True, [r_ATs, r_vt], [r_ob])
                    for hp in range(2):
                        ob, r_ob = obs[hp]
                        for hh in range(2):
                            h = 2 * hp + hh
                            oap = ob[:, hh * 129:(hh + 1) * 129]
                            act(rr[:, 4 * hp + hh:4 * hp + hh + 1], oap[:, 128:129], AF.Abs, [r_ob, r_gs], [r_rr],
                                scale=wp[:, h:h + 1])
                    for hp in range(2):
                        ob, r_ob = obs[hp]
                        for hh in range(2):
                            h = 2 * hp + hh
                            oap = ob[:, hh * 129:(hh + 1) * 129]
                            c0 = 4 * hp + hh
                            ts("dve", rr[:, c0:c0 + 1], rr[:, c0:c0 + 1], 1.0, None, ALU.max, None, [r_rr], [r_rr])
                            recip(rr[:, c0:c0 + 1], rr[:, c0:c0 + 1], [r_rr], [r_rr])
                            tt("dve", rr[:, c0 + 2:c0 + 3], rr[:, c0:c0 + 1], wp[:, h:h + 1], ALU.mult, [r_rr, r_gs], [r_rr])
                            ts("dve", hmf[:, h, :], oap[:, 0:128], rr[:, c0 + 2:c0 + 3], None, ALU.mult, None,
                               [r_ob, r_rr], [r_hmf])
                    for h in range(H):
                        act(junk[:, 0:128], hmf[:, h, :], AF.Copy, [r_hmf], [r_junk, r_mst],
                            accum_out=mst[:, h, 0:1])
                        act(junk[:, 0:128], hmf[:, h, :], AF.Square, [r_hmf], [r_junk, r_mst],
                            accum_out=mst[:, h, 1:2])
                    ts("dve", mst[:, :, 2:3], mst[:, :, 0:1], 1.0 / 128, None, ALU.mult, None, [r_mst], [r_mst])
                    tt("dve", mst[:, :, 3:4], mst[:, :, 2:3], mst[:, :, 2:3], ALU.mult, [r_mst], [r_mst])
                    stt("dve", mst[:, :, 4:5], mst[:, :, 1:2], 1.0 / 128, mst[:, :, 3:4], ALU.mult, ALU.subtract,
                        [r_mst], [r_mst])
                    act(mst[:, :, 4:5], mst[:, :, 4:5], AF.Ln, [r_mst], [r_mst], bias=EPS)
                    act(mst[:, :, 5:6], mst[:, :, 4:5], AF.Exp, [r_mst], [r_mst], scale=-0.5)
                    for h in range(H):
                        ts("dve", hmf[:, h, :], hmf[:, h, :], mst[:, h, 2:3], mst[:, h, 5:6], ALU.subtract, ALU.mult,
                           [r_hmf, r_mst], [r_hmf])
                    tt("dve", hmf[:], hmf[:], gmlb[:, :].unsqueeze(1).to_broadcast([128, 4, 128]), ALU.mult,
                       [r_hmf, r_gmlb], [r_hmf])
                    hmb, r_hmb = hmbs[g % 2]
                    tt("dve", hmb[:], hmf[:].rearrange("p a b -> p (a b)"), oms[:], ALU.mult, [r_hmf, r_oms], [r_hmb])

            stage_A1(0)
            if NT > 1:
                stage_A1(1)
            stage_A2(0)
            for g in range(NT):
                if g + 2 < NT:
                    stage_A1(g + 2)
                if g + 1 < NT:
                    stage_A2(g + 1)
                if g < NT - NOWN:
                    emit_prep(prep_per_tile)
                stage_B(g)
            stage_D(NT - 1)
            stage_C(NT - 1)
            emit_prep(len(prep))

        scale = 64.0 ** -0.5
        P.barrier()
        es23 = ExitStack()
        es23.__enter__()
        Wya, r_Wya = sbuf(es23, "Wya", [128, 4, D], BF16, ro=True)
        Wyb, r_Wyb = sbuf(es23, "Wyb", [128, 4, D], BF16, ro=True)
        Wo, r_Wo = sbuf(es23, "Wo", [128, 8, D], BF16, ro=True)
        Wpg, r_Wpg = sbuf(es23, "Wpg", [128, 8, D], BF16, ro=True)
        Wpl, r_Wpl = sbuf(es23, "Wpl", [128, 2, D], BF16, ro=True)
        g2b, r_g2b = sbuf(es23, "g2b", [128, D], F32, ro=True)
        cfw, r_cfw = sbuf(es23, "cfw", [128, NFC, 3], F32, ro=True)
        cfb, r_cfb = sbuf(es23, "cfb", [128, NFC], F32, ro=True)
        dma("Wya", Wya[:], WYA_d, [r_WYA], [r_Wya])
        dma("Wyb", Wyb[:], WYB_d, [r_WYB], [r_Wyb])
        dma("Wo", Wo[:], WO_d, [r_WO], [r_Wo])
        dma("Wpg", Wpg[:], WPG_d, [r_WPG], [r_Wpg])
        dma("Wpl", Wpl[:], WPL_d, [r_WPL], [r_Wpl])
        dma("g2b", g2b[:], g2_d.partition_broadcast(128), [], [r_g2b])
        dma("cfw", cfw[:], cfw_d, [], [r_cfw])
        dma("cfb", cfb[:], cfb_d, [], [r_cfb])
        with ExitStack() as es2:
            biast, r_biast = sbuf(es2, "biast", [128, NSLOT], F32, ro=True)
            pat, r_pat = sbuf(es2, "pat", [128, 4, 512], BF16, ro=True)
            gdab, r_gdab = sbuf(es2, "gdab", [128, 128], F32, ro=True)
            qt_s = [sbuf(es2, f"qts{i}", [128, 512], BF16) for i in range(2)]
            kg = [sbuf(es2, f"kg{i}", [128, NTS * 128], BF16) for i in range(2)]
            vg = [sbuf(es2, f"vg{i}", [128, NTS, 130], BF16) for i in range(2)]
            pts = [sbuf(es2, f"pts{i}", [128, 2, 512], BF16) for i in range(3)]
            aof, r_aof = sbuf(es2, "aof", [128, BT, 4, 128], F32)
            aob, r_aob = sbuf(es2, "aob", [128, 512], BF16)
            aoTs, r_aoTs = sbuf(es2, "aoTs", [128, 4, 128], BF16)
            tmpa, r_tmpa = sbuf(es2, "tmpa", [128, 128], F32)
            rl, r_rl = sbuf(es2, "rl", [128, 8], F32)
            osb, r_osb = sbuf(es2, "osb", [128, 2, 2, 258], F32)
            dma("biast", biast[:], biast_d, [], [r_biast])
            dma("pat", pat[:], pat_d, [], [r_pat])
            dma("gdab", gdab[:], gda_d.partition_broadcast(128), [], [r_gdab])
            ts("dve", gdab[:], gdab[:], 1.0 - LAM_INIT, None, ALU.mult, None, [r_gdab], [r_gdab])

            S_t = [bank2(0), bank2(2)]
            O_bk = [[bank(4), bank(5)], [bank(6), bank(7)]]
            ld_i = 0
            pt_i = 0
            s_i = 0
            for bi, tiles in enumerate(blocks):
                nq = len(tiles)
                N = nq * 128
                t0 = tiles[0]
                klist = []
                for kt in range(NT):
                    ku = kt - G0
                    if bi == 0:
                        if kt < G0:
                            klist.append((kt, None))
                        elif kt == G0:
                            klist.append((kt, 0))
                    else:
                        if ku < t0:
                            klist.append((kt, None))
                        elif ku < t0 + nq:
                            klist.append((kt, ku - t0))
                groups = {}
                for kt, kind in klist:
                    groups.setdefault(kt // NTS, []).append((kt, kind))
                gkeys = sorted(groups)
                for h in range(H):
                    q_s, r_q = qt_s[(bi * H + h) % 2]
                    dma(f"qts{(bi * H + h) % 2}", q_s[:, 0:N], QT_d[h, :, t0 * 128:t0 * 128 + N], [r_QT], [r_q])
                    total_k = len(klist)
                    loaded = {}
                    seq = []
                    for gk in gkeys:
                        for (kt, kind) in groups[gk]:
                            seq.append((gk, kt, kind))

                    def ensure_group(gk):
                        nonlocal ld_i
                        if gk not in loaded:
                            k_s, r_k = kg[ld_i % 2]
                            v_s, r_v = vg[ld_i % 2]
                            dma(f"kg{ld_i % 2}", k_s[:], KT_d[h, :, gk * NTS * 128:(gk + 1) * NTS * 128], [r_KT], [r_k])
                            dma(f"vg{ld_i % 2}", v_s[:], VV_d[h, :, gk * NTS:(gk + 1) * NTS, :], [r_VV], [r_v])
                            ld_i += 1
                            loaded[gk] = (k_s, r_k, v_s, r_v)
                        return loaded[gk]

                    def emit_s(idx):
                        nonlocal s_i, pt_i
                        gk, kt, kind = seq[idx]
                        k_s, r_k, v_s, r_v = ensure_group(gk)
                        kl = kt - gk * NTS
                        St, r_sa, r_sb = S_t[s_i % 2]
                        s_i += 1
                        mm(St[:, 0, 0:N], k_s[0:64, kl * 128:(kl + 1) * 128], q_s[0:64, 0:N], True, True,
                           [r_k, r_q], [r_sa])
                        mm(St[:, 1, 0:N], k_s[64:128, kl * 128:(kl + 1) * 128], q_s[64:128, 0:N], True, True,
                           [r_k, r_q], [r_sb])
                        p_s, r_p = pts[pt_i % 3]
                        pt_i += 1
                        act(p_s[:, :, 0:N], St[:, :, 0:N], AF.Exp, [r_sa, r_sb, r_biast], [r_p],
                            bias=biast[:, gk:gk + 1], scale=scale)
                        if kind is not None:
                            tt("dve", p_s[:, :, 0:N], p_s[:, :, 0:N],
                               pat[:, kind, 0:N].unsqueeze(1).to_broadcast([128, 2, N]), ALU.mult,
                               [r_p, r_pat], [r_p])
                        return p_s, r_p

                    def emit_pv(idx, p_s, r_p):
                        gk, kt, kind = seq[idx]
                        k_s, r_k, v_s, r_v = loaded[gk]
                        kl = kt - gk * NTS
                        for sub in range(nq):
                            for m in range(2):
                                ob, r_ob = O_bk[m][sub // 2]
                                mm(ob[:, (sub % 2) * 129:(sub % 2 + 1) * 129],
                                   p_s[:, m, sub * 128:(sub + 1) * 128], v_s[:, kl, 0:129],
                                   idx == 0 and (sub % 2 == 0), idx == total_k - 1, [r_p, r_v], [r_ob], skip=True)

                    pend = emit_s(0)
                    for idx in range(total_k):
                        nxt = emit_s(idx + 1) if idx + 1 < total_k else None
                        emit_pv(idx, *pend)
                        pend = nxt
                    for m in range(2):
                        for half in range((nq + 1) // 2):
                            obk, r_obk = O_bk[m][half]
                            ncol = min(2, nq - 2 * half) * 129
                            tcopy("dve", osb[:, m, half, 0:ncol], obk[:, 0:ncol], [r_obk], [r_osb])
                    for sub in range(nq):
                        r_o0 = r_osb
                        r_o1 = r_osb
                        o0 = osb[:, 0, sub // 2, (sub % 2) * 129:(sub % 2 + 1) * 129]
                        o1 = osb[:, 1, sub // 2, (sub % 2) * 129:(sub % 2 + 1) * 129]
                        ts("dve", rl[:, 0:1], o0[:, 128:129], 1e-30, None, ALU.max, None, [r_o0], [r_rl])
                        ts("dve", rl[:, 1:2], o1[:, 128:129], 1e-30, None, ALU.max, None, [r_o1], [r_rl])
                        recip(rl[:, 2:4], rl[:, 0:2], [r_rl], [r_rl])
                        tt("dve", rl[:, 4:5], rl[:, 3:4], nlam[:, 0:1], ALU.mult, [r_rl, r_nlam], [r_rl])
                        ts("dve", tmpa[:], o1[:, 0:128], rl[:, 4:5], None, ALU.mult, None, [r_o1, r_rl], [r_tmpa])
                        stt("dve", aof[:, sub, h, :], o0[:, 0:128], rl[:, 2:3], tmpa[:], ALU.mult, ALU.add,
                            [r_o0, r_rl, r_tmpa], [r_aof])
                for sub in range(nq):
                    for h in range(H):
                        act(junk[:, 0:128], aof[:, sub, h, :], AF.Square, [r_aof], [r_junk, r_ssq],
                            accum_out=ssq[:, 4 + h:5 + h])
                    act(ssq[:, 4:8], ssq[:, 4:8], AF.Sqrt, [r_ssq], [r_ssq], bias=EPS, scale=1.0 / 128)
                    recip(ssq[:, 4:8], ssq[:, 4:8], [r_ssq], [r_ssq])
                    for h in range(H):
                        stt("dve", aob[:, h * 128:(h + 1) * 128], aof[:, sub, h, :], ssq[:, 4 + h:5 + h], gdab[:],
                            ALU.mult, ALU.mult, [r_aof, r_ssq, r_gdab], [r_aob])
                    transpose8(aob, r_aob, aoTs[:], r_aoTs, 4, bidx=0)
                    u = t0 + sub
                    dma("aoTs", AOT_d[:, :, u * 128:(u + 1) * 128], aoTs[:], [r_aoTs], [r_AOT])

        P.barrier()
        with ExitStack() as es3:
            wdn = [sbuf(es3, f"wdn{i}", [128, 2, D], BF16) for i in range(2)]
            wup = [sbuf(es3, f"wup{i}", [128, 8, 2, 256], BF16) for i in range(2)]
            aoT, r_aoT = sbuf(es3, "aoT", [128, 4, 512], BF16)
            hmT, r_hmT = sbuf(es3, "hmT", [128, 4, 512], BF16)
            sg3s = [sbuf(es3, f"sg3_{i}", [128, 2 * D], BF16) for i in range(2)]
            x3t = [sbuf(es3, f"x3t{i}", [128, D], F32) for i in range(2)]
            m2, r_m2 = sbuf(es3, "m2", [128, D], F32)
            mbs = [sbuf(es3, f"mb{i}", [128, D], BF16) for i in range(2)]
            mT, r_mT = sbuf(es3, "mT", [128, 8, 128], BF16)
            x1b, r_x1b = sbuf(es3, "x1b", [128, BT, D], F32)
            h2bs = [sbuf(es3, f"h2b{i}", [128, D], BF16) for i in range(2)]
            h2T, r_h2T = sbuf(es3, "h2T", [128, 8, 512], BF16)
            aext, r_aext = sbuf(es3, "aext", [128, 2 + 512], F32)
            halo = [sbuf(es3, f"halo{i}", [128, NFC, 2], F32) for i in range(2)]
            acc, r_acc = sbuf(es3, "acc", [128, 512], F32)
            gel, r_gel = sbuf(es3, "gel", [128, 512], F32)
            uT, r_uT = sbuf(es3, "uT", [128, NFC, 512], BF16)
            x2bf, r_x2bf = sbuf(es3, "x2bf", [128, D], BF16)
            x2Ts = [sbuf(es3, f"x2T{i}", [128, 8, 128], BF16) for i in range(2)]
            gsb, r_gsb = sbuf(es3, "gsb", [128, D], F32)
            pf, r_pf = sbuf(es3, "pf", [128, 2, 128], F32)
            pbs = [sbuf(es3, f"pb{i}", [128, 2, 128], BF16) for i in range(2)]
            m1, r_m1 = gsb, r_gsb
            yo = x3t

            for i in range(2):
                memset("pool", halo[i][0][:], 0.0, [halo[i][1]])

            wu_i = 0
            wd_i = 0
            yo_i = 0
            x_i = 0
            for bi, tiles in enumerate(blocks):
                nq = len(tiles)
                N = nq * 128
                t0 = tiles[0]
                last_is_halo = bi == 0
                dma("aoT", aoT[:, :, 0:N], AOT_d[:, :, t0 * 128:t0 * 128 + N], [r_AOT], [r_aoT])
                dma("hmT", hmT[:, :, 0:N], HMT_d[:, :, t0 * 128:t0 * 128 + N], [r_HMT], [r_hmT])
                xbuf = {}

                def stage_MA(ti):
                    nonlocal x_i
                    u = tiles[ti]
                    x_t, r_x = x3t[x_i % 2]
                    xbuf[ti] = (x_t, r_x)
                    dma(f"x3t{x_i % 2}", x_t[:], xa[(G0 + u) * 128:(G0 + u + 1) * 128, :], [], [r_x])
                    x_i += 1
                    sg3, r_sg3 = sg3s[ti % 2]
                    dma(f"sg3_{ti % 2}", sg3[:], SG_d[u * 128:(u + 1) * 128, :], [r_SG], [r_sg3])
                    ya, r_ya0, r_ya1 = bank2(0)
                    yb, r_yb0, r_yb1 = bank2(2)
                    for half, (ra_, rb_) in enumerate(((r_ya0, r_yb0), (r_ya1, r_yb1))):
                        for c in range(4):
                            mm(ya[:, half, :], aoT[:, c, ti * 128:(ti + 1) * 128], Wya[:, c, half * 512:(half + 1) * 512],
                               c == 0, c == 3, [r_aoT, r_Wya], [ra_])
                        for c in range(4):
                            mm(yb[:, half, :], hmT[:, c, ti * 128:(ti + 1) * 128], Wyb[:, c, half * 512:(half + 1) * 512],
                               c == 0, c == 3, [r_hmT, r_Wyb], [rb_])
                    mb, r_mb = mbs[ti % 2]
                    tt("dve", m1[:].rearrange("p (a b) -> p a b", a=2), ya[:, :, :],
                       sg3[:, 0:D].rearrange("p (a b) -> p a b", a=2), ALU.mult, [r_ya0, r_ya1, r_sg3], [r_m1])
                    tt("dve", m2[:].rearrange("p (a b) -> p a b", a=2), yb[:, :, :],
                       sg3[:, D:2 * D].rearrange("p (a b) -> p a b", a=2), ALU.mult, [r_yb0, r_yb1, r_sg3], [r_m2])
                    tt("dve", mb[:], m1[:], m2[:], ALU.add, [r_m1, r_m2], [r_mb])

                def stage_MB(ti):
                    mb, r_mb = mbs[ti % 2]
                    x_t, r_x = xbuf[ti]
                    h2b, r_h2b = h2bs[ti % 2]
                    transpose8(mb, r_mb, mT[:], r_mT, 8, bidx=4)
                    xo, r_xo0, r_xo1 = bank2(6)
                    for half, rr_ in enumerate((r_xo0, r_xo1)):
                        for kc in range(8):
                            mm(xo[:, half, :], mT[:, kc, :], Wo[:, kc, half * 512:(half + 1) * 512], kc == 0, kc == 7,
                               [r_mT, r_Wo], [rr_])
                    tt("dve", x1b[:, ti, :].rearrange("p (a b) -> p a b", a=2), xo[:, :, :],
                       x_t[:].rearrange("p (a b) -> p a b", a=2), ALU.add, [r_xo0, r_xo1, r_x], [r_x1b])
                    rstd = rms_stats(x1b[:, ti, :], D, r_x1b)
                    stt("dve", h2b[:], x1b[:, ti, :], rstd, g2b[:], ALU.mult, ALU.mult, [r_x1b, r_ssq, r_g2b], [r_h2b])

                def stage_MC(ti):
                    h2b, r_h2b = h2bs[ti % 2]
                    transpose8(h2b, r_h2b, h2T[:, :, ti * 128:(ti + 1) * 128], r_h2T, 8, bidx=5)

                stage_MA(0)
                for ti in range(nq):
                    if ti + 1 < nq:
                        stage_MA(ti + 1)
                    stage_MB(ti)
                    if ti >= 1:
                        stage_MC(ti - 1)
                stage_MC(nq - 1)

                h_in, r_hin = halo[bi % 2]
                h_out, r_hout = halo[(bi + 1) % 2]
                for fc0 in range(0, NFC, 2):
                    w_s, r_w = wup[wu_i % 2]
                    wu_i += 1
                    dma(f"wup{(wu_i - 1) % 2}", w_s[:, :, 0, :], WUP_d[:, :, fc0 * 128:fc0 * 128 + 256], [r_WUP], [r_w])
                    if not last_is_halo:
                        dma(f"wup{(wu_i - 1) % 2}", w_s[:, :, 1, :], WUP_d[:, :, DFF + fc0 * 128:DFF + fc0 * 128 + 256],
                            [r_WUP], [r_w])
                    for fi in range(2):
                        fc = fc0 + fi
                        ab, r_ab = bank(2 * fi)
                        for kc in range(8):
                            mm(ab[:, 0:N], w_s[:, kc, 0, fi * 128:(fi + 1) * 128], h2T[:, kc, 0:N], kc == 0, kc == 7,
                               [r_w, r_h2T], [r_ab])
                        acopy(aext[:, 2:2 + N], ab[:, 0:N], [r_ab], [r_aext])
                        tcopy("dve", aext[:, 0:2], h_in[:, fc, :], [r_hin], [r_aext])
                        tcopy("dve", h_out[:, fc, :], aext[:, N:N + 2], [r_aext], [r_hout])
                        if last_is_halo:
                            continue
                        gb_, r_gb = bank(2 * fi + 1)
                        for kc in range(8):
                            mm(gb_[:, 0:N], w_s[:, kc, 1, fi * 128:(fi + 1) * 128], h2T[:, kc, 0:N], kc == 0, kc == 7,
                               [r_w, r_h2T], [r_gb])
                        ts("dve", acc[:, 0:N], aext[:, 0:N], cfw[:, fc, 0:1], cfb[:, fc:fc + 1], ALU.mult, ALU.add,
                           [r_aext, r_cfw, r_cfb], [r_acc])
                        stt("dve", acc[:, 0:N], aext[:, 1:N + 1], cfw[:, fc, 1:2], acc[:, 0:N], ALU.mult, ALU.add,
                            [r_aext, r_cfw, r_acc], [r_acc])
                        stt("dve", acc[:, 0:N], aext[:, 2:N + 2], cfw[:, fc, 2:3], acc[:, 0:N], ALU.mult, ALU.add,
                            [r_aext, r_cfw, r_acc], [r_acc])
                        act(gel[:, 0:N], acc[:, 0:N], AF.Gelu_apprx_tanh, [r_acc], [r_gel])
                        tt("dve", uT[:, fc, 0:N], gel[:, 0:N], gb_[:, 0:N], ALU.mult, [r_gel, r_gb], [r_uT])
                if last_is_halo:
                    continue

                for fc0 in range(0, NFC, 2):
                    wd, r_wd = wdn[wd_i % 2]
                    wd_i += 1
                    dma(f"wdn{(wd_i - 1) % 2}", wd[:], WDN_d[:, fc0:fc0 + 2, :], [r_WDN], [r_wd])
                    for fi in range(2):
                        fc = fc0 + fi
                        for ti in range(nq):
                            for half in range(2):
                                bkx, r_bkx = bank(2 * ti + half)
                                mm(bkx[:, :], uT[:, fc, ti * 128:(ti + 1) * 128], wd[:, fi, half * 512:(half + 1) * 512],
                                   fc == 0, fc == NFC - 1, [r_uT, r_wd], [r_bkx])
                for ti in range(nq):
                    xd, r_xd0, r_xd1 = bank2(2 * ti)
                    tt("dve", x1b[:, ti, :].rearrange("p (a b) -> p a b", a=2), xd[:, :, :],
                       x1b[:, ti, :].rearrange("p (a b) -> p a b", a=2), ALU.add, [r_xd0, r_xd1, r_x1b], [r_x1b])
                def stage_PA(ti):
                    u = tiles[ti]
                    x2 = x1b[:, ti, :]
                    x2T, r_x2T = x2Ts[ti % 2]
                    pb, r_pb = pbs[ti % 2]
                    tcopy("act", x2bf[:], x2, [r_x1b], [r_x2bf])
                    transpose8(x2bf, r_x2bf, x2T[:], r_x2T, 8, bidx=6)
                    tok0 = (u - 1) * 128
                    dma("pf", pf[:], pT_d[:, :, tok0:tok0 + 128], [], [r_pf])
                    tcopy("dve", pb[:], pf[:], [r_pf], [r_pb])

                def stage_PB(ti):
                    nonlocal yo_i
                    u = tiles[ti]
                    x2 = x1b[:, ti, :]
                    r_x2 = r_x1b
                    x2T, r_x2T = x2Ts[ti % 2]
                    pb, r_pb = pbs[ti % 2]
                    tok0 = (u - 1) * 128
                    gp, r_gp0, r_gp1 = bank2(0)
                    pp, r_pp0, r_pp1 = bank2(2)
                    for half, (rg_, rp_) in enumerate(((r_gp0, r_pp0), (r_gp1, r_pp1))):
                        for kc in range(8):
                            mm(gp[:, half, :], x2T[:, kc, :], Wpg[:, kc, half * 512:(half + 1) * 512], kc == 0, kc == 7,
                               [r_x2T, r_Wpg], [rg_])
                        for c in range(2):
                            mm(pp[:, half, :], pb[:, c, :], Wpl[:, c, half * 512:(half + 1) * 512], c == 0, c == 1,
                               [r_pb, r_Wpl], [rp_])
                    act(gsb[:].rearrange("p (a b) -> p a b", a=2), gp[:, :, :], AF.Sigmoid, [r_gp0, r_gp1], [r_gsb])
                    tt("dve", gsb[:].rearrange("p (a b) -> p a b", a=2), pp[:, :, :],
                       gsb[:].rearrange("p (a b) -> p a b", a=2), ALU.mult, [r_pp0, r_pp1, r_gsb], [r_gsb])
                    tt("dve", x2, x2, gsb[:], ALU.add, [r_x2, r_gsb], [r_x2])
                    rstd = rms_stats(x2, D, r_x2)
                    y_t, r_yt = yo[yo_i % 2]
                    yo_i += 1
                    stt("dve", y_t[:], x2, rstd, gFb[:], ALU.mult, ALU.mult, [r_x2, r_ssq, r_gFb], [r_yt])
                    dma(f"x3t{(yo_i - 1) % 2}", y_d[tok0:tok0 + 128, :], y_t[:], [r_yt], [r_y])

                stage_PA(0)
                for ti in range(nq):
                    if ti + 1 < nq:
                        stage_PA(ti + 1)
                    stage_PB(ti)

            P.wait_all("sp", [r_y, yo[0][1], yo[1][1]])
            P.wait_all("pool", [r_y, yo[0][1], yo[1][1]])
            P.wait_all("act", [r_y, yo[0][1], yo[1][1]])
        es23.close()
        P.run()
    return nc, P


def host_constants(NTS):
    NT = NSLOT * NTS
    bf = ml_dtypes.bfloat16
    c = {}
    c["identb"] = np.eye(128, dtype=np.float32).astype(bf)
    s = np.arange(128)
    c["tri"] = (s[:, None] <= s[None, :]).astype(np.float32)
    c["ones"] = np.ones((128, 128), np.float32)
    sh = np.zeros((128, 7, 128), np.float32)
    for j in range(4):
        for t in range(128):
            sidx = t - 3 + j
            if 0 <= sidx < 128:
                sh[sidx, j, t] = 1.0
    for j in range(3):
        for t in range(128):
            sidx = 128 + t - 3 + j
            if 0 <= sidx < 128:
                sh[sidx, 4 + j, t] = 1.0
    c["shifts"] = sh.astype(bf)
    pat = np.zeros((128, 4, 512), np.float32)
    k = np.arange(128)
    q = np.arange(512)
    for o in range(4):
        pat[:, o, :] = (((128 * o + k) // 64)[:, None] <= (q // 64)[None, :]).astype(np.float32)
    c["pat"] = pat.astype(bf)
    inv_freq = (10000.0 ** (-np.arange(0, 64, 2, dtype=np.float32) / 64)).astype(np.float32)
    pos = (np.arange(NT)[None, :] * 128 + np.arange(128)[:, None]).astype(np.float32)
    ang = pos[:, :, None] * inv_freq[None, None, :]
    c["ropec"] = np.cos(ang).astype(np.float32)
    c["ropes"] = np.sin(ang).astype(np.float32)
    return c


_PROG_CACHE = {}


def make_in_maps(inputs, NTS, cores=None):
    SLOT = NTS * 128
    S = NSLOT * SLOT
    NT = NSLOT * NTS
    f32 = np.float32
    x = np.asarray(inputs["x"], f32).reshape(S, D)
    p = np.asarray(inputs["p"], f32).reshape(S, PLE)
    consts = host_constants(NTS)
    shared = {
        "w_in": np.ascontiguousarray(np.asarray(inputs["w_in"], f32)[0]),
        "w_ya": np.ascontiguousarray(np.asarray(inputs["w_ya"], f32)[0]),
        "w_yb": np.ascontiguousarray(np.asarray(inputs["w_yb"], f32)[0]),
        "w_o": np.ascontiguousarray(np.asarray(inputs["w_o"], f32)[0]),
        "w_up": np.ascontiguousarray(np.asarray(inputs["w_up"], f32)[0]),
        "w_down": np.ascontiguousarray(np.asarray(inputs["w_down"], f32)[0]),
        "w_ple": np.ascontiguousarray(np.asarray(inputs["w_ple"], f32)[0]),
        "w_pg": np.ascontiguousarray(np.asarray(inputs["w_pg"], f32)[0]),
        "g1": np.asarray(inputs["norm1_g"], f32).reshape(1, D),
        "g2": np.asarray(inputs["norm2_g"], f32).reshape(1, D),
        "gF": np.asarray(inputs["final_g"], f32).reshape(1, D),
        "gda": np.asarray(inputs["da_norm_g"], f32).reshape(1, 128),
        "gml": np.asarray(inputs["ml_norm_g"], f32).reshape(1, 128),
        "b_if": np.asarray(inputs["b_if"], f32).reshape(1, 8),
        "lamv": np.stack([np.asarray(inputs[k], f32).reshape(64) for k in ("lam_q1", "lam_k1", "lam_q2", "lam_k2")]).reshape(1, 256),
        "conv_m_w": np.ascontiguousarray(np.asarray(inputs["conv_m_w"], f32)[0]).reshape(1, 4 * D),
        "cfw": np.ascontiguousarray(np.asarray(inputs["conv_f_w"], f32)[0].T.reshape(NFC, 128, 3).transpose(1, 0, 2)),
        "cfb": np.ascontiguousarray(np.asarray(inputs["conv_f_b"], f32)[0].reshape(NFC, 128).T),
    }
    shared.update(consts)
    in_maps = []
    for r in (range(NCORES) if cores is None else cores):
        xa = np.zeros((NT * 128, D), f32)
        lo = (r - (NSLOT - 1)) * SLOT
        hi = (r + 1) * SLOT
        src_lo = max(lo, 0)
        xa[src_lo - lo:, :] = x[src_lo:hi]
        pT = np.ascontiguousarray(p[r * SLOT:(r + 1) * SLOT].T.reshape(2, 128, SLOT).transpose(1, 0, 2))
        biast = np.zeros((128, NSLOT), f32)
        for s_ in range(NSLOT):
            if s_ < NSLOT - 1 - r:
                biast[:, s_] = NEG
        m = dict(shared)
        m["xa"] = xa
        m["pT"] = pT
        m["biast"] = biast
        in_maps.append(m)
    return in_maps


def kernel(**inputs):
    S = int(np.asarray(inputs["x"]).shape[1])
    NTS = S // (NSLOT * 128)
    if NTS not in _PROG_CACHE:
        _PROG_CACHE[NTS] = build_program(NTS)[0]
    nc = _PROG_CACHE[NTS]
    in_maps = make_in_maps(inputs, NTS)
    res = run_bass_kernel_spmd(nc, in_maps, core_ids=list(range(NCORES)))
    out = np.concatenate([np.asarray(r["y"], np.float32) for r in res.results], axis=0)
    return out.reshape(1, S, D)
```

```python
import math
from contextlib import ExitStack

import numpy as np
import ml_dtypes

import concourse.bass as bass
import concourse.mybir as mybir
from concourse.bass_utils import run_bass_kernel_spmd

F32 = mybir.dt.float32
BF16 = mybir.dt.bfloat16
AF = mybir.ActivationFunctionType
ALU = mybir.AluOpType

NCORES = 8
NSLOT = 8
D = 1024
NIN = 5640
DFF = 2816
NFC = DFF // 128
PLE = 256
H = 4
EPS = 1e-6
LAM_INIT = 0.8 - 0.6 * math.exp(-0.3 * 0)
QA, KA, VA, QM, KM, VM, OM, IFG, GA, GB = 0, 512, 1024, 1536, 2048, 2560, 3072, 3584, 3592, 4616
NEG = -30000.0


class Res:
    __slots__ = ("name", "w", "r", "ro")

    def __init__(self, name, ro=False):
        self.name = name
        self.w = None
        self.r = []
        self.ro = ro


class Prog:
    ENGS = ("pe", "act", "dve", "pool", "sp")

    def __init__(self, nc):
        self.nc = nc
        self.q = {e: [] for e in self.ENGS}
        self.sem = {e: nc.alloc_semaphore(f"c_{e}") for e in self.ENGS}
        self.cnt = {e: 0 for e in self.ENGS}
        self.seen = {e: {} for e in self.ENGS}
        self.dsem = {}
        self.dknow = {}
        self.hist = {e: {} for e in self.ENGS}
        self.n_wait = 0
        self.n_ins = 0

    def _merge(self, eng, know):
        se = self.seen[eng]
        for k, v in know.items():
            if se.get(k, 0) < v:
                se[k] = v

    def _need(self, eng, toks):
        se = self.seen[eng]
        cand = []
        for t in toks:
            if t is None:
                continue
            kind, key, val = t
            if kind == "d":
                val = self.dsem[key][1]
            cand.append((kind, key, val))
        cand.sort(key=lambda t: (t[0] == "c" and t[1] == eng, -t[2]))
        out = []
        for kind, key, val in cand:
            k = (kind, key)
            if se.get(k, 0) >= val:
                continue
            if kind == "c":
                sem = self.sem[key]
                know = self.hist[key].get(val)
            else:
                sem = self.dsem[key][0]
                know = self.dknow.get(key)
            se[k] = val
            if know:
                self._merge(eng, know)
            out.append((sem, val))
        return out

    def _deps(self, eng, reads, writes):
        toks = []
        for r in reads:
            toks.append(r.w)
        for w in writes:
            if not (eng == "pe" and w.w is not None and w.w[0] == "c" and w.w[1] == "pe"):
                toks.append(w.w)
            toks.extend(w.r)
        return toks

    def _record(self, tok, reads, writes):
        for r in reads:
            if not r.ro:
                r.r.append(tok)
        for w in writes:
            w.w = tok
            w.r = []

    def op(self, eng, fn, reads=(), writes=()):
        waits = self._need(eng, self._deps(eng, reads, writes))
        self.cnt[eng] += 1
        n = self.cnt[eng]
        tok = ("c", eng, n)
        snap = dict(self.seen[eng])
        snap[("c", eng)] = n
        self.hist[eng][n] = snap
        self.q[eng].append((waits, fn, self.sem[eng], 1))
        self.n_wait += len(waits)
        self.n_ins += 1
        self._record(tok, reads, writes)
        return tok

    def dma(self, eng, semname, fn, reads=(), writes=()):
        if semname not in self.dsem:
            self.dsem[semname] = [self.nc.alloc_semaphore(f"d_{semname}"), 0]
            self.dknow[semname] = {}
        waits = self._need(eng, self._deps(eng, reads, writes))
        ent = self.dsem[semname]
        ent[1] += 16
        tok = ("d", semname, ent[1])
        dk = self.dknow[semname]
        for k, v in self.seen[eng].items():
            if dk.get(k, 0) < v:
                dk[k] = v
        self.q[eng].append((waits, fn, ent[0], 16))
        self.n_wait += len(waits)
        self.n_ins += 1
        self._record(tok, reads, writes)
        return tok

    def wait_all(self, eng, ress):
        toks = []
        for r in ress:
            toks.append(r.w)
            toks.extend(r.r)
        waits = self._need(eng, toks)
        self.q[eng].append((waits, None, None, 0))

    def barrier(self):
        for eng in self.ENGS:
            toks = [("c", e2, self.cnt[e2]) for e2 in self.ENGS if self.cnt[e2] > 0]
            toks += [("d", name, ent[1]) for name, ent in self.dsem.items() if ent[1] > 0]
            waits = self._need(eng, toks)
            self.q[eng].append((waits, None, None, 0))

    def run(self):
        q = self.q

        def play(engobj, items):
            for waits, fn, sem, inc in items:
                if fn is None:
                    for (s, v) in waits:
                        engobj.wait_ge(s, v)
                    continue
                for (s, v) in waits[:-1]:
                    engobj.wait_ge(s, v)
                ins = fn(engobj)
                if waits:
                    ins._wait_ge(*waits[-1])
                if sem is not None:
                    ins.then_inc(sem, inc)

        with self.nc.Block() as block:
            @block.tensor
            def _(e):
                play(e, q["pe"])

            @block.scalar
            def _(e):
                play(e, q["act"])

            @block.vector
            def _(e):
                play(e, q["dve"])

            @block.gpsimd
            def _(e):
                play(e, q["pool"])

            @block.sync
            def _(e):
                play(e, q["sp"])


def build_program(NTS):
    NT = NSLOT * NTS
    NOWN = NTS + 1
    G0 = NT - NOWN
    SLOT = NTS * 128
    BT = min(4, NTS)
    blocks = [[0]] + [list(range(1 + b, 1 + b + BT)) for b in range(0, NTS, BT)]
    NOWNTOK = NOWN * 128

    nc = bass.Bass("TRN2", target_bir_lowering=False)
    P = Prog(nc)

    def din(name, shape, dt=F32):
        return nc.dram_tensor(name, list(shape), dt, kind="ExternalInput").ap()

    def dscr(name, shape, dt):
        return nc.dram_tensor(name, list(shape), dt).ap(), Res(name)

    xa = din("xa", [NT * 128, D])
    pT_d = din("pT", [128, 2, SLOT])
    w_in_d = din("w_in", [D, NIN])
    w_ya_d = din("w_ya", [512, D])
    w_yb_d = din("w_yb", [512, D])
    w_o_d = din("w_o", [D, D])
    w_up_d = din("w_up", [D, 2 * DFF])
    w_dn_d = din("w_down", [DFF, D])
    w_ple_d = din("w_ple", [PLE, D])
    w_pg_d = din("w_pg", [D, D])
    g1_d = din("g1", [1, D])
    g2_d = din("g2", [1, D])
    gF_d = din("gF", [1, D])
    gda_d = din("gda", [1, 128])
    gml_d = din("gml", [1, 128])
    bif_d = din("b_if", [1, 8])
    lam_d = din("lamv", [1, 256])
    cmw_d = din("conv_m_w", [1, 4 * D])
    cfw_d = din("cfw", [128, NFC, 3])
    cfb_d = din("cfb", [128, NFC])
    identb_d = din("identb", [128, 128], BF16)
    tri_d = din("tri", [128, 128])
    ones_d = din("ones", [128, 128])
    shifts_d = din("shifts", [128, 7, 128], BF16)
    pat_d = din("pat", [128, 4, 512], BF16)
    ropec_d = din("ropec", [128, NT, 32])
    ropes_d = din("ropes", [128, NT, 32])
    biast_d = din("biast", [128, NSLOT])
    y_d = nc.dram_tensor("y", [SLOT, D], F32, kind="ExternalOutput").ap()
    r_y = Res("y")

    KT_d, r_KT = dscr("KT", [H, 128, NT * 128], BF16)
    VV_d, r_VV = dscr("VV", [H, 128, NT, 130], BF16)
    QT_d, r_QT = dscr("QT", [H, 128, NOWNTOK], BF16)
    SG_d, r_SG = dscr("SG", [NOWNTOK, 2 * D], BF16)
    HMT_d, r_HMT = dscr("HMT", [128, H, NOWNTOK], BF16)
    AOT_d, r_AOT = dscr("AOT", [128, H, NOWNTOK], BF16)
    WUP_d, r_WUP = dscr("WUPb", [128, 8, 2 * DFF], BF16)
    WDN_d, r_WDN = dscr("WDNb", [128, NFC, D], BF16)
    WYA_d, r_WYA = dscr("WYAb", [128, 4, D], BF16)
    WYB_d, r_WYB = dscr("WYBb", [128, 4, D], BF16)
    WO_d, r_WO = dscr("WOb", [128, 8, D], BF16)
    WPG_d, r_WPG = dscr("WPGb", [128, 8, D], BF16)
    WPL_d, r_WPL = dscr("WPLb", [128, 2, D], BF16)

    def mm(out, lhsT, rhs, start, stop, reads, writes, skip=False):
        if skip:
            P.op("pe", lambda e: e.matmul(out, lhsT=lhsT, rhs=rhs, start=start, stop=stop, skip_group_check=True),
                 reads, writes)
        else:
            P.op("pe", lambda e: e.matmul(out, lhsT=lhsT, rhs=rhs, start=start, stop=stop), reads, writes)

    def tr(out, in_, ident, reads, writes):
        P.op("pe", lambda e: e.transpose(out=out, in_=in_, identity=ident), reads, writes)

    def act(out, in_, func, reads, writes, bias=0.0, scale=1.0, accum_out=None, eng="act"):
        if accum_out is None:
            P.op(eng, lambda e: e.activation(out=out, in_=in_, func=func, bias=bias, scale=scale), reads, writes)
        else:
            P.op(eng, lambda e: e.activation(out=out, in_=in_, func=func, bias=bias, scale=scale,
                                             accum_out=accum_out), reads, writes)

    def acopy(out, in_, reads, writes):
        P.op("act", lambda e: e.copy(out=out, in_=in_), reads, writes)

    def tcopy(eng, out, in_, reads, writes):
        if eng == "act":
            P.op(eng, lambda e: e.copy(out=out, in_=in_), reads, writes)
        else:
            P.op(eng, lambda e: e.tensor_copy(out=out, in_=in_), reads, writes)

    def tt(eng, out, in0, in1, op, reads, writes):
        P.op(eng, lambda e: e.tensor_tensor(out=out, in0=in0, in1=in1, op=op), reads, writes)

    def ts(eng, out, in0, s1, s2, op0, op1, reads, writes):
        if s2 is None:
            P.op(eng, lambda e: e.tensor_scalar(out=out, in0=in0, scalar1=s1, scalar2=None, op0=op0), reads, writes)
        else:
            P.op(eng, lambda e: e.tensor_scalar(out=out, in0=in0, scalar1=s1, scalar2=s2, op0=op0, op1=op1),
                 reads, writes)

    def stt(eng, out, in0, scalar, in1, op0, op1, reads, writes):
        P.op(eng, lambda e: e.scalar_tensor_tensor(out=out, in0=in0, scalar=scalar, in1=in1, op0=op0, op1=op1),
             reads, writes)

    def recip(out, in_, reads, writes):
        P.op("dve", lambda e: e.reciprocal(out=out, in_=in_), reads, writes)

    def memset(eng, ap, val, writes):
        P.op(eng, lambda e: e.memset(ap, val), (), writes)

    def dma(semname, out, in_, reads, writes, eng="sp"):
        P.dma(eng, semname, lambda e: e.dma_start(out=out, in_=in_), reads, writes)

    with ExitStack() as es_top:
        def sbuf(es, name, shape, dt=F32, ro=False):
            t = es.enter_context(nc.sbuf_tensor("s_" + name, list(shape), dt))
            return t, Res(name, ro=ro)

        banks = []
        for i in range(4):
            t = es_top.enter_context(nc.psum_tensor(f"pb{i}", [128, 2, 512], F32))
            banks.append((t, Res(f"pb{i}a")))
            banks.append((t, Res(f"pb{i}b")))

        def bank(i):
            t, r = banks[i]
            return t[:, i % 2, :], r

        def bank2(i):
            t, ra = banks[i]
            _, rb = banks[i + 1]
            return t, ra, rb

        esg = es_top
        identb, r_identb = sbuf(esg, "identb", [128, 128], BF16, ro=True)
        tri, r_tri = sbuf(esg, "tri", [128, 128], F32, ro=True)
        ones, r_ones = sbuf(esg, "ones", [128, 128], F32, ro=True)
        gFb, r_gFb = sbuf(esg, "gFb", [128, D], F32, ro=True)
        lam_t, r_lam = sbuf(esg, "lam_t", [128, 4, 64], F32)
        lam_s, r_lams = sbuf(esg, "lam_s", [128, 4], F32)
        nlam, r_nlam = sbuf(esg, "nlam", [128, 1], F32, ro=True)
        ssq, r_ssq = sbuf(esg, "ssq", [128, 8], F32)
        junk, r_junk = sbuf(esg, "junk", [128, D], F32)

        dma("identb", identb[:], identb_d, [], [r_identb])
        dma("tri", tri[:], tri_d, [], [r_tri])
        dma("ones", ones[:], ones_d, [], [r_ones])
        dma("gFb", gFb[:], gF_d.partition_broadcast(128), [], [r_gFb])
        dma("lam_t", lam_t[:].rearrange("p a b -> p (a b)"),
            lam_d.partition_broadcast(128), [], [r_lam])
        for i in range(2):
            tt("dve", lam_t[:, 2 * i, :], lam_t[:, 2 * i, :], lam_t[:, 2 * i + 1, :], ALU.mult, [r_lam], [r_lam])
            act(lam_t[:, 2 * i + 1, :], lam_t[:, 2 * i, :], AF.Copy, [r_lam], [r_lam, r_lams],
                accum_out=lam_s[:, i:i + 1])
        act(lam_s[:, 2:4], lam_s[:, 0:2], AF.Exp, [r_lams], [r_lams])
        tt("dve", lam_s[:, 0:1], lam_s[:, 3:4], lam_s[:, 2:3], ALU.subtract, [r_lams], [r_lams])
        ts("dve", nlam[:], lam_s[:, 0:1], -LAM_INIT, None, ALU.add, None, [r_lams], [r_nlam])

        def rms_stats(src_ap, n, r_src, col=0):
            act(junk[:, 0:n], src_ap, AF.Square, [r_src], [r_junk, r_ssq], accum_out=ssq[:, col:col + 1])
            act(ssq[:, col:col + 1], ssq[:, col:col + 1], AF.Sqrt, [r_ssq], [r_ssq], bias=EPS, scale=1.0 / n)
            recip(ssq[:, col:col + 1], ssq[:, col:col + 1], [r_ssq], [r_ssq])
            return ssq[:, col:col + 1]

        def transpose8(src_bf, r_src, dst_ap, r_dst, nblk, bidx=0):
            bk, r_bk = bank(bidx)
            bb = bk.bitcast(BF16)
            for k in range(nblk):
                tr(bb[:, k * 128:(k + 1) * 128], src_bf[:, k * 128:(k + 1) * 128], identb[:],
                   [r_src, r_identb], [r_bk])
            acopy(dst_ap, bb[:, 0:nblk * 128].rearrange("p (a b) -> p a b", a=nblk), [r_bk], [r_dst])

        with ExitStack() as es1:
            Wb, r_Wb = sbuf(es1, "Wb", [128, 8, NIN], BF16, ro=True)
            g1b, r_g1b = sbuf(es1, "g1b", [128, D], F32, ro=True)
            cmwb, r_cmwb = sbuf(es1, "cmwb", [128, 4, D], F32, ro=True)
            bifb, r_bifb = sbuf(es1, "bifb", [128, 8], F32, ro=True)
            gmlb, r_gmlb = sbuf(es1, "gmlb", [128, 128], F32, ro=True)
            shifts, r_shifts = sbuf(es1, "shifts", [128, 7, 128], BF16, ro=True)
            CU = 1024
            stage = [sbuf(es1, f"stage{i}", [128, CU], F32) for i in range(2)]
            stageb = [sbuf(es1, f"stageb{i}", [128, CU], BF16) for i in range(2)]
            NXT = 3
            xt = [sbuf(es1, f"xt{i}", [128, D], F32) for i in range(NXT)]
            rc = [sbuf(es1, f"rc{i}", [128, 32], F32) for i in range(NXT)]
            rs = [sbuf(es1, f"rs{i}", [128, 32], F32) for i in range(NXT)]
            st1 = [sbuf(es1, f"st1_{i}", [128, 2], F32) for i in range(NXT)]
            hbs = [sbuf(es1, f"hb{i}", [128, D], BF16) for i in range(2)]
            hTs = [sbuf(es1, f"hT{i}", [128, 8, 128], BF16) for i in range(2)]
            ra, r_ra = sbuf(es1, "ra", [128, 512], F32)
            rb1, r_rb1 = sbuf(es1, "rb1", [128, 8, 32], F32)
            rb2, r_rb2 = sbuf(es1, "rb2", [128, 8, 32], F32)
            krot, r_krot = sbuf(es1, "krot", [128, 512], BF16)
            ktT, r_ktT = sbuf(es1, "ktT", [128, 4, 128], BF16)
            vaug = [sbuf(es1, f"vaug{i}", [128, 4, 130], BF16) for i in range(2)]
            xwk = [(sbuf(es1, f"xwk{i}", [128, 4, 512], BF16)[0], [Res(f"xwk{i}_{j}") for j in range(4)]) for i in range(2)]
            xwq = [(sbuf(es1, f"xwq{i}", [128, 4, 512], BF16)[0], [Res(f"xwq{i}_{j}") for j in range(4)]) for i in range(2)]
            sig, r_sig = sbuf(es1, "sig", [128, 512], F32)
            kmtoks = [sbuf(es1, f"kmtok{i}", [128, 512], BF16) for i in range(2)]
            qmtok, r_qmtok = sbuf(es1, "qmtok", [128, 512], BF16)
            kmT, r_kmT = sbuf(es1, "kmT", [128, 4, 128], BF16)
            qmT, r_qmT = sbuf(es1, "qmT", [128, 4, 128], BF16)
            gt, r_gt = sbuf(es1, "gt", [128, 8], F32)
            gs, r_gs = sbuf(es1, "gs", [128, 24], F32)
            vts = [sbuf(es1, f"vt{i}", [128, 4, 129], BF16) for i in range(2)]
            Cst, r_C = sbuf(es1, "Cst", [128, 4, 129], F32)
            Chat, r_Chat = sbuf(es1, "Chat", [128, 4, 129], F32)
            Chb, r_Chb = sbuf(es1, "Chb", [128, 4, 129], BF16)
            ATs, r_ATs = sbuf(es1, "ATs", [128, 4, 128], BF16)
            hmf, r_hmf = sbuf(es1, "hmf", [128, 4, 128], F32)
            hmbs = [sbuf(es1, f"hmb{i}", [128, 512], BF16) for i in range(2)]
            hmTs, r_hmTs = sbuf(es1, "hmTs", [128, 4, 128], BF16)
            oms, r_oms = sbuf(es1, "oms", [128, 512], F32)
            sgs, r_sgs = sbuf(es1, "sgs", [128, 2 * D], BF16)
            mst, r_mst = sbuf(es1, "mst", [128, 4, 8], F32)
            rr, r_rr = sbuf(es1, "rr", [128, 8], F32)

            dma("g1b", g1b[:], g1_d.partition_broadcast(128), [], [r_g1b])
            dma("cmwb", cmwb[:].rearrange("p a b -> p (a b)"), cmw_d.partition_broadcast(128), [], [r_cmwb])
            dma("bifb", bifb[:], bif_d.partition_broadcast(128), [], [r_bifb])
            dma("gmlb", gmlb[:], gml_d.partition_broadcast(128), [], [r_gmlb])
            dma("shifts", shifts[:], shifts_d, [], [r_shifts])
            memset("pool", Cst[:], 0.0, [r_C])
            for i in range(2):
                memset("pool", xwk[i][0][:], 0.0, xwk[i][1])
                memset("pool", xwq[i][0][:], 0.0, xwq[i][1])
                memset("pool", vaug[i][0][:, :, 128:129], 1.0, [vaug[i][1]])
                memset("pool", vaug[i][0][:, :, 129:130], 0.0, [vaug[i][1]])

            cast_i = [0]

            def cast_unit(src_ap, ncols_free, dst_kind, dst_ap, r_dst, ceng="pool"):
                i = cast_i[0] % 2
                cast_i[0] += 1
                st, r_st = stage[i]
                a, b = src_ap.shape[1], src_ap.shape[2]
                stv = st[:, 0:ncols_free].rearrange("p (a b) -> p a b", a=a)
                dma(f"stage{i}", stv, src_ap, [], [r_st])
                if dst_kind == "sbuf":
                    tcopy(ceng, dst_ap, stv, [r_st], [r_dst])
                else:
                    sbt, r_sbt = stageb[i]
                    sbv = sbt[:, 0:ncols_free].rearrange("p (a b) -> p a b", a=a)
                    tcopy(ceng, sbv, stv, [r_st], [r_sbt])
                    dma(f"stageb{i}", dst_ap, sbv, [r_sbt], [r_dst])

            w_in_v = w_in_d.rearrange("(k p) n -> p k n", p=128)
            GW = CU // 8
            wgrp = {}
            col_groups = []
            for c0 in (KA, VA, IFG, KM, VM, QA, QM, OM, GA, GA + 512, GB, GB + 512):
                n_tot = 8 if c0 == IFG else 512
                for cc in range(c0, c0 + n_tot, GW):
                    col_groups.append((cc, min(GW, c0 + n_tot - cc)))
            engs = ["pool", "dve", "act"]
            for gi, (c0, n) in enumerate(col_groups):
                wgrp[c0] = (Res(f"Wb{c0}", ro=True), n)
                cast_unit(w_in_v[:, :, c0:c0 + n], 8 * n, "sbuf", Wb[:, :, c0:c0 + n], wgrp[c0][0],
                          ceng=engs[gi % 3] if gi < 40 else "pool")

            def wres(c0, n):
                return [r for c, (r, m) in wgrp.items() if c < c0 + n and c + m > c0]

            prep = []
            w_up_v = w_up_d.rearrange("(k p) n -> p k n", p=128)
            for c0 in range(0, 2 * DFF, GW):
                prep.append((w_up_v[:, :, c0:c0 + GW], CU, WUP_d[:, :, c0:c0 + GW], r_WUP))
            w_dn_v = w_dn_d.rearrange("(k p) n -> p k n", p=128)
            for k0 in range(NFC):
                prep.append((w_dn_v[:, k0:k0 + 1, :], D, WDN_d[:, k0:k0 + 1, :], r_WDN))
            w_ya_v = w_ya_d.rearrange("(k p) n -> p k n", p=128)
            w_yb_v = w_yb_d.rearrange("(k p) n -> p k n", p=128)
            for k0 in range(4):
                prep.append((w_ya_v[:, k0:k0 + 1, :], D, WYA_d[:, k0:k0 + 1, :], r_WYA))
                prep.append((w_yb_v[:, k0:k0 + 1, :], D, WYB_d[:, k0:k0 + 1, :], r_WYB))
            w_o_v = w_o_d.rearrange("(k p) n -> p k n", p=128)
            w_pg_v = w_pg_d.rearrange("(k p) n -> p k n", p=128)
            for k0 in range(8):
                prep.append((w_o_v[:, k0:k0 + 1, :], D, WO_d[:, k0:k0 + 1, :], r_WO))
                prep.append((w_pg_v[:, k0:k0 + 1, :], D, WPG_d[:, k0:k0 + 1, :], r_WPG))
            w_pl_v = w_ple_d.rearrange("(k p) n -> p k n", p=128)
            for k0 in range(2):
                prep.append((w_pl_v[:, k0:k0 + 1, :], D, WPL_d[:, k0:k0 + 1, :], r_WPL))
            prep_per_tile = -(-len(prep) // max(1, NT - NOWN))
            prep_pos = [0]

            def emit_prep(k):
                for _ in range(k):
                    if prep_pos[0] < len(prep):
                        pr = prep[prep_pos[0]]
                        prep_pos[0] += 1
                        cast_unit(pr[0], pr[1], "dram", pr[2], pr[3])

            def proj(hT_t, r_hT_, c0, n, bidx, col0=0):
                bk, r_bk = bank(bidx)
                for kc in range(8):
                    mm(bk[:, col0:col0 + n], hT_t[:, kc, :], Wb[:, kc, c0:c0 + n], kc == 0, kc == 7,
                       [r_hT_] + wres(c0, n), [r_bk])
                return bk, r_bk

            def rope_tm(bk, r_bk, cosap, sinap, r_c, r_s, out_bf, r_out):
                v4 = lambda ap: ap.rearrange("p (a b c) -> p a b c", a=8, b=2, c=32)
                kv = v4(bk[:, 0:512])
                av = v4(ra[:, :])
                ov = v4(out_bf[:, :])
                cb4 = cosap.unsqueeze(1).unsqueeze(1).to_broadcast([128, 8, 2, 32])
                sb3 = sinap.unsqueeze(1).to_broadcast([128, 8, 32])
                tt("dve", av, kv, cb4, ALU.mult, [r_bk, r_c], [r_ra])
                tt("dve", rb1[:], kv[:, :, 1, :], sb3, ALU.mult, [r_bk, r_s], [r_rb1])
                tt("dve", rb2[:], kv[:, :, 0, :], sb3, ALU.mult, [r_bk, r_s], [r_rb2])
                tt("dve", ov[:, :, 0, :], av[:, :, 0, :], rb1[:], ALU.subtract, [r_ra, r_rb1], [r_out])
                tt("dve", ov[:, :, 1, :], av[:, :, 1, :], rb2[:], ALU.add, [r_ra, r_rb2], [r_out])

            def conv_premul(bk, r_bk, wcol0, xw_cur):
                xc, r_xc = xw_cur
                for j in range(4):
                    tt("dve", xc[:, j, :], bk[:, 0:512], cmwb[:, j, wcol0:wcol0 + 512], ALU.mult,
                       [r_bk, r_cmwb], [r_xc[j]])

            def conv_mm(xw_cur, xw_prev, bidx_out):
                xc, r_xc = xw_cur
                xp, r_xp = xw_prev
                ob, r_ob = bank(bidx_out)
                for j in range(4):
                    mm(ob[:, 0:512], shifts[:, j, :], xc[:, j, :], j == 0, False, [r_shifts, r_xc[j]], [r_ob])
                for j in range(3):
                    mm(ob[:, 0:512], shifts[:, 4 + j, :], xp[:, j, :], False, j == 2, [r_shifts, r_xp[j]], [r_ob])
                return ob, r_ob

            def silu_tanh(cb, r_cb, out_bf, r_out, c):
                act(sig[:], cb[:, 0:512], AF.Exp, [r_cb], [r_sig], scale=-1.0 / c)
                act(sig[:], sig[:], AF.Ln, [r_sig], [r_sig], bias=1.0)
                act(sig[:], sig[:], AF.Exp, [r_sig], [r_sig], scale=-1.0)
                tt("dve", out_bf[:], cb[:, 0:512], sig[:], ALU.mult, [r_cb, r_sig], [r_out])

            km_scale = 128.0 ** -0.5
            CK = km_scale
            CQ = 1.0
            ts("dve", cmwb[:, :, 512:1024], cmwb[:, :, 512:1024], CK, None, ALU.mult, None, [r_cmwb], [r_cmwb])

            def stage_A1(g):
                x_t, r_x = xt[g % NXT]
                c_t, r_c = rc[g % NXT]
                s_t, r_s = rs[g % NXT]
                s1, r_s1 = st1[g % NXT]
                hb, r_hb = hbs[g % 2]
                dma(f"xt{g % NXT}", x_t[:], xa[g * 128:(g + 1) * 128, :], [], [r_x])
                dma(f"rc{g % NXT}", c_t[:], ropec_d[:, g, :], [], [r_c])
                dma(f"rs{g % NXT}", s_t[:], ropes_d[:, g, :], [], [r_s])
                act(junk[:, 0:D], x_t[:], AF.Square, [r_x], [r_junk, r_s1], accum_out=s1[:, 0:1])
                act(s1[:, 1:2], s1[:, 0:1], AF.Ln, [r_s1], [r_s1], bias=EPS, scale=1.0 / D)
                act(s1[:, 1:2], s1[:, 1:2], AF.Exp, [r_s1], [r_s1], scale=-0.5)
                stt("dve", hb[:], x_t[:], s1[:, 1:2], g1b[:], ALU.mult, ALU.mult, [r_x, r_s1, r_g1b], [r_hb])

            def stage_A2(g):
                hb, r_hb = hbs[g % 2]
                hT, r_hT = hTs[g % 2]
                transpose8(hb, r_hb, hT[:], r_hT, 8, bidx=0)

            def stage_D(g):
                u = g - G0
                hmb, r_hmb = hmbs[g % 2]
                transpose8(hmb, r_hmb, hmTs[:], r_hmTs, 4, bidx=0)
                dma("hmTs", HMT_d[:, :, u * 128:(u + 1) * 128], hmTs[:], [r_hmTs], [r_HMT])

            def stage_C(g):
                kmtok, r_kmtok = kmtoks[g % 2]
                vt, r_vt = vts[g % 2]
                for hp in range(2):
                    cbk, r_cbk = bank(5)
                    for hh in range(2):
                        h = 2 * hp + hh
                        mm(cbk[:, hh * 129:(hh + 1) * 129], kmtok[:, h * 128:(h + 1) * 128], vt[:, h, :], True, True,
                           [r_kmtok, r_vt], [r_cbk])
                    tt("dve", Cst[:, 2 * hp:2 * hp + 2, :], Chat[:, 2 * hp:2 * hp + 2, :],
                       cbk[:, 0:258].rearrange("p (a b) -> p a b", a=2), ALU.add, [r_Chat, r_cbk], [r_C])

            def stage_B(g):
                own = g >= G0
                u = g - G0
                hT, r_hT = hTs[g % 2]
                kmtok, r_kmtok = kmtoks[g % 2]
                vt, r_vt = vts[g % 2]
                c_t, r_c = rc[g % NXT]
                s_t, r_s = rs[g % NXT]
                bkG, r_bkG = proj(hT, r_hT, IFG, 8, 4, col0=16)
                tt("dve", gt[:], bkG[:, 16:24], bifb[:], ALU.add, [r_bkG, r_bifb], [r_gt])
                act(gs[:, 0:4], gt[:, 4:8], AF.Exp, [r_gt], [r_gs], scale=-1.0)
                act(gs[:, 0:4], gs[:, 0:4], AF.Ln, [r_gs], [r_gs], bias=1.0)
                bkM, r_bkM = proj(hT, r_hT, KM, 512, 3)
                bkK, r_bkK = proj(hT, r_hT, KA, 512, 1)
                bkV, r_bkV = proj(hT, r_hT, VA, 512, 2)
                bkW, r_bkW = proj(hT, r_hT, VM, 512, 6)
                if g - 1 >= G0:
                    stage_D(g - 1)
                conv_premul(bkM, r_bkM, 512, xwk[g % 2])
                rope_tm(bkK, r_bkK, c_t[:, :], s_t[:, :], r_c, r_s, krot, r_krot)
                va, r_va = vaug[g % 2]
                acopy(va[:, :, 0:128], bkV[:, 0:512].rearrange("p (a b) -> p a b", a=4), [r_bkV], [r_va])
                dma(f"vaug{g % 2}", VV_d[:, :, g, :].rearrange("h p c -> p h c"), va[:], [r_va], [r_VV])
                bk4, r_bk4 = bank(4)
                mm(bk4[:, 0:4], tri[:], gs[:, 0:4], True, True, [r_tri, r_gs], [r_bk4])
                mm(bk4[:, 8:12], ones[:], gs[:, 0:4], True, True, [r_ones, r_gs], [r_bk4])
                cb, r_cb = conv_mm(xwk[g % 2], xwk[(g + 1) % 2], 7)
                transpose8(krot, r_krot, ktT[:], r_ktT, 4, bidx=0)
                dma("ktT", KT_d[:, :, g * 128:(g + 1) * 128].rearrange("h p t -> p h t"), ktT[:], [r_ktT], [r_KT])
                if g > 0:
                    stage_C(g - 1)
                tt("dve", gs[:, 8:12], bk4[:, 0:4], gt[:, 0:4], ALU.add, [r_bk4, r_gt], [r_gs])
                tt("dve", gs[:, 8:12], gs[:, 8:12], bk4[:, 8:12], ALU.subtract, [r_gs, r_bk4], [r_gs])
                act(gs[:, 8:12], gs[:, 8:12], AF.Exp, [r_gs], [r_gs])
                act(gs[:, 12:16], bk4[:, 8:12], AF.Exp, [r_bk4], [r_gs], scale=-1.0)
                if own:
                    tcopy("dve", gs[:, 4:8], bk4[:, 8:12], [r_bk4], [r_gs])
                    tt("dve", gs[:, 16:20], gs[:, 4:8], bk4[:, 0:4], ALU.subtract, [r_gs, r_bk4], [r_gs])
                    act(gs[:, 16:20], gs[:, 16:20], AF.Exp, [r_gs], [r_gs])
                wk = gs[:, 8:12]
                dd = gs[:, 12:16]
                wp = gs[:, 16:20]
                silu_tanh(cb, r_cb, kmtok, r_kmtok, CK)
                tt("dve", vt[:, :, 0:128], bkW[:, 0:512].rearrange("p (a b) -> p a b", a=4),
                   wk.unsqueeze(2).to_broadcast([128, 4, 128]), ALU.mult, [r_bkW, r_gs], [r_vt])
                tcopy("dve", vt[:, :, 128:129], wk.unsqueeze(2), [r_gs], [r_vt])
                tt("pool", Chat[:], Cst[:], dd.unsqueeze(2).to_broadcast([128, 4, 129]), ALU.mult,
                   [r_C, r_gs], [r_Chat])

                if own:
                    hTo = hT
                    bkq, r_bkq = proj(hTo, r_hT, QA, 512, 1)
                    bkqm, r_bkqm = proj(hTo, r_hT, QM, 512, 2)
                    bkom, r_bkom = proj(hTo, r_hT, OM, 512, 3)
                    conv_premul(bkqm, r_bkqm, 0, xwq[g % 2])
                    rope_tm(bkq, r_bkq, c_t[:, :], s_t[:, :], r_c, r_s, krot, r_krot)
                    cbq, r_cbq = conv_mm(xwq[g % 2], xwq[(g + 1) % 2], 7)
                    transpose8(krot, r_krot, ktT[:], r_ktT, 4, bidx=0)
                    dma("ktT", QT_d[:, :, u * 128:(u + 1) * 128].rearrange("h p t -> p h t"), ktT[:],
                        [r_ktT], [r_QT])
                    bkga0, r_bkga0 = proj(hTo, r_hT, GA, 512, 1)
                    bkga1, r_bkga1 = proj(hTo, r_hT, GA + 512, 512, 2)
                    bkgb0, r_bkgb0 = proj(hTo, r_hT, GB, 512, 5)
                    silu_tanh(cbq, r_cbq, qmtok, r_qmtok, CQ)
                    act(oms[:], bkom[:, 0:512], AF.Sigmoid, [r_bkom], [r_oms])
                    act(sgs[:, 0:512], bkga0[:, 0:512], AF.Sigmoid, [r_bkga0], [r_sgs])
                    act(sgs[:, 512:1024], bkga1[:, 0:512], AF.Sigmoid, [r_bkga1], [r_sgs])
                    act(sgs[:, 1024:1536], bkgb0[:, 0:512], AF.Sigmoid, [r_bkgb0], [r_sgs])
                    transpose8(qmtok, r_qmtok, qmT[:], r_qmT, 4, bidx=0)
                    transpose8(kmtok, r_kmtok, kmT[:], r_kmT, 4, bidx=0)
                    bkgb1, r_bkgb1 = proj(hTo, r_hT, GB + 512, 512, 3)
                    act(sgs[:, 1536:2048], bkgb1[:, 0:512], AF.Sigmoid, [r_bkgb1], [r_sgs])
                    dma("sgs", SG_d[u * 128:(u + 1) * 128, :], sgs[:], [r_sgs], [r_SG])
                    bk7, r_bk7 = bank(7)
                    for h in range(H):
                        mm(bk7[:, h * 128:(h + 1) * 128], kmT[:, h, :], qmT[:, h, :], True, True,
                           [r_kmT, r_qmT], [r_bk7])
                    tt("dve", ATs[:], bk7[:, 0:512].rearrange("p (a b) -> p a b", a=4),
                       tri[:, :].unsqueeze(1).to_broadcast([128, 4, 128]), ALU.mult, [r_bk7, r_tri], [r_ATs])
                    acopy(Chb[:], Chat[:], [r_Chat], [r_Chb])
                    obs = [bank(6), bank(4)]
                    for hp in range(2):
                        ob, r_ob = obs[hp]
                        for hh in range(2):
                            h = 2 * hp + hh
                            oap = ob[:, hh * 129:(hh + 1) * 129]
                            mm(oap, qmT[:, h, :], Chb[:, h, :], True, False, [r_qmT, r_Chb], [r_ob])
                            mm(oap, ATs[:, h, :], vt[:, h, :], False, True, [r_ATs, r_vt], [r_ob])
                    for hp in range(2):
                        ob, r_ob = obs[hp]
                        for hh in range(2):
                            h = 2 * hp + hh
                            oap = ob[:, hh * 129:(hh + 1) * 129]
                            act(rr[:, 4 * hp + hh:4 * hp + hh + 1], oap[:, 128:129], AF.Abs, [r_ob, r_gs], [r_rr],
                                scale=wp[:, h:h + 1])
                    for hp in range(2):
                        ob, r_ob = obs[hp]
                        for hh in range(2):
                            h = 2 * hp + hh
                            oap = ob[:, hh * 129:(hh + 1) * 129]
                            c0 = 4 * hp + hh
                            ts("dve", rr[:, c0:c0 + 1], rr[:, c0:c0 + 1], 1.0, None, ALU.max, None, [r_rr], [r_rr])
                            recip(rr[:, c0:c0 + 1], rr[:, c0:c0 + 1], [r_rr], [r_rr])
                            tt("dve", rr[:, c0 + 2:c0 + 3], rr[:, c0:c0 + 1], wp[:, h:h + 1], ALU.mult, [r_rr, r_gs], [r_rr])
                            ts("dve", hmf[:, h, :], oap[:, 0:128], rr[:, c0 + 2:c0 + 3], None, ALU.mult, None,
                               [r_ob, r_rr], [r_hmf])
                    for h in range(H):
                        act(junk[:, 0:128], hmf[:, h, :], AF.Copy, [r_hmf], [r_junk, r_mst],
                            accum_out=mst[:, h, 0:1])
                        act(junk[:, 0:128], hmf[:, h, :], AF.Square, [r_hmf], [r_junk, r_mst],
                            accum_out=mst[:, h, 1:2])
                    ts("dve", mst[:, :, 2:3], mst[:, :, 0:1], 1.0 / 128, None, ALU.mult, None, [r_mst], [r_mst])
                    tt("dve", mst[:, :, 3:4], mst[:, :, 2:3], mst[:, :, 2:3], ALU.mult, [r_mst], [r_mst])
                    stt("dve", mst[:, :, 4:5], mst[:, :, 1:2], 1.0 / 128, mst[:, :, 3:4], ALU.mult, ALU.subtract,
                        [r_mst], [r_mst])
                    act(mst[:, :, 4:5], mst[:, :, 4:5], AF.Ln, [r_mst], [r_mst], bias=EPS)
                    act(mst[:, :, 5:6], mst[:, :, 4:5], AF.Exp, [r_mst], [r_mst], scale=-0.5)
                    for h in range(H):
                        ts("dve", hmf[:, h, :], hmf[:, h, :], mst[:, h, 2:3], mst[:, h, 5:6], ALU.subtract, ALU.mult,
                           [r_hmf, r_mst], [r_hmf])
                    tt("dve", hmf[:], hmf[:], gmlb[:, :].unsqueeze(1).to_broadcast([128, 4, 128]), ALU.mult,
                       [r_hmf, r_gmlb], [r_hmf])
                    hmb, r_hmb = hmbs[g % 2]
                    tt("dve", hmb[:], hmf[:].rearrange("p a b -> p (a b)"), oms[:], ALU.mult, [r_hmf, r_oms], [r_hmb])

            stage_A1(0)
            if NT > 1:
                stage_A1(1)
            stage_A2(0)
            for g in range(NT):
                if g + 2 < NT:
                    stage_A1(g + 2)
                if g + 1 < NT:
                    stage_A2(g + 1)
                if g < NT - NOWN:
                    emit_prep(prep_per_tile)
                stage_B(g)
            stage_D(NT - 1)
            stage_C(NT - 1)
            emit_prep(len(prep))

        scale = 64.0 ** -0.5
        P.barrier()
        with ExitStack() as es2:
            biast, r_biast = sbuf(es2, "biast", [128, NSLOT], F32, ro=True)
            pat, r_pat = sbuf(es2, "pat", [128, 4, 512], BF16, ro=True)
            gdab, r_gdab = sbuf(es2, "gdab", [128, 128], F32, ro=True)
            qt_s = [sbuf(es2, f"qts{i}", [128, 512], BF16) for i in range(2)]
            kg = [sbuf(es2, f"kg{i}", [128, NTS * 128], BF16) for i in range(2)]
            vg = [sbuf(es2, f"vg{i}", [128, NTS, 130], BF16) for i in range(2)]
            pts = [sbuf(es2, f"pts{i}", [128, 2, 512], BF16) for i in range(4)]
            aof, r_aof = sbuf(es2, "aof", [128, BT, 4, 128], F32)
            aob, r_aob = sbuf(es2, "aob", [128, 512], BF16)
            aoTs, r_aoTs = sbuf(es2, "aoTs", [128, 4, 128], BF16)
            tmpa, r_tmpa = sbuf(es2, "tmpa", [128, 128], F32)
            rl, r_rl = sbuf(es2, "rl", [128, 8], F32)
            osb, r_osb = sbuf(es2, "osb", [128, 2, 2, 258], F32)
            dma("biast", biast[:], biast_d, [], [r_biast])
            dma("pat", pat[:], pat_d, [], [r_pat])
            dma("gdab", gdab[:], gda_d.partition_broadcast(128), [], [r_gdab])
            ts("dve", gdab[:], gdab[:], 1.0 - LAM_INIT, None, ALU.mult, None, [r_gdab], [r_gdab])

            S_t = [bank2(0), bank2(2)]
            O_bk = [[bank(4), bank(5)], [bank(6), bank(7)]]
            ld_i = 0
            pt_i = 0
            s_i = 0
            for bi, tiles in enumerate(blocks):
                nq = len(tiles)
                N = nq * 128
                t0 = tiles[0]
                klist = []
                for kt in range(NT):
                    ku = kt - G0
                    if bi == 0:
                        if kt < G0:
                            klist.append((kt, None))
                        elif kt == G0:
                            klist.append((kt, 0))
                    else:
                        if ku < t0:
                            klist.append((kt, None))
                        elif ku < t0 + nq:
                            klist.append((kt, ku - t0))
                groups = {}
                for kt, kind in klist:
                    groups.setdefault(kt // NTS, []).append((kt, kind))
                gkeys = sorted(groups)
                for h in range(H):
                    q_s, r_q = qt_s[(bi * H + h) % 2]
                    dma(f"qts{(bi * H + h) % 2}", q_s[:, 0:N], QT_d[h, :, t0 * 128:t0 * 128 + N], [r_QT], [r_q])
                    total_k = len(klist)
                    loaded = {}
                    seq = []
                    for gk in gkeys:
                        for (kt, kind) in groups[gk]:
                            seq.append((gk, kt, kind))

                    def ensure_group(gk):
                        nonlocal ld_i
                        if gk not in loaded:
                            k_s, r_k = kg[ld_i % 2]
                            v_s, r_v = vg[ld_i % 2]
                            dma(f"kg{ld_i % 2}", k_s[:], KT_d[h, :, gk * NTS * 128:(gk + 1) * NTS * 128], [r_KT], [r_k])
                            dma(f"vg{ld_i % 2}", v_s[:], VV_d[h, :, gk * NTS:(gk + 1) * NTS, :], [r_VV], [r_v])
                            ld_i += 1
                            loaded[gk] = (k_s, r_k, v_s, r_v)
                        return loaded[gk]

                    def emit_s(idx):
                        nonlocal s_i, pt_i
                        gk, kt, kind = seq[idx]
                        k_s, r_k, v_s, r_v = ensure_group(gk)
                        kl = kt - gk * NTS
                        St, r_sa, r_sb = S_t[s_i % 2]
                        s_i += 1
                        mm(St[:, 0, 0:N], k_s[0:64, kl * 128:(kl + 1) * 128], q_s[0:64, 0:N], True, True,
                           [r_k, r_q], [r_sa])
                        mm(St[:, 1, 0:N], k_s[64:128, kl * 128:(kl + 1) * 128], q_s[64:128, 0:N], True, True,
                           [r_k, r_q], [r_sb])
                        p_s, r_p = pts[pt_i % 4]
                        pt_i += 1
                        act(p_s[:, :, 0:N], St[:, :, 0:N], AF.Exp, [r_sa, r_sb, r_biast], [r_p],
                            bias=biast[:, gk:gk + 1], scale=scale)
                        if kind is not None:
                            tt("dve", p_s[:, :, 0:N], p_s[:, :, 0:N],
                               pat[:, kind, 0:N].unsqueeze(1).to_broadcast([128, 2, N]), ALU.mult,
                               [r_p, r_pat], [r_p])
                        return p_s, r_p

                    def emit_pv(idx, p_s, r_p):
                        gk, kt, kind = seq[idx]
                        k_s, r_k, v_s, r_v = loaded[gk]
                        kl = kt - gk * NTS
                        for sub in range(nq):
                            for m in range(2):
                                ob, r_ob = O_bk[m][sub // 2]
                                mm(ob[:, (sub % 2) * 129:(sub % 2 + 1) * 129],
                                   p_s[:, m, sub * 128:(sub + 1) * 128], v_s[:, kl, 0:129],
                                   idx == 0 and (sub % 2 == 0), idx == total_k - 1, [r_p, r_v], [r_ob], skip=True)

                    pend = emit_s(0)
                    for idx in range(total_k):
                        nxt = emit_s(idx + 1) if idx + 1 < total_k else None
                        emit_pv(idx, *pend)
                        pend = nxt
                    for m in range(2):
                        for half in range((nq + 1) // 2):
                            obk, r_obk = O_bk[m][half]
                            ncol = min(2, nq - 2 * half) * 129
                            tcopy("dve", osb[:, m, half, 0:ncol], obk[:, 0:ncol], [r_obk], [r_osb])
                    for sub in range(nq):
                        r_o0 = r_osb
                        r_o1 = r_osb
                        o0 = osb[:, 0, sub // 2, (sub % 2) * 129:(sub % 2 + 1) * 129]
                        o1 = osb[:, 1, sub // 2, (sub % 2) * 129:(sub % 2 + 1) * 129]
                        ts("dve", rl[:, 0:1], o0[:, 128:129], 1e-30, None, ALU.max, None, [r_o0], [r_rl])
                        ts("dve", rl[:, 1:2], o1[:, 128:129], 1e-30, None, ALU.max, None, [r_o1], [r_rl])
                        recip(rl[:, 2:4], rl[:, 0:2], [r_rl], [r_rl])
                        tt("dve", rl[:, 4:5], rl[:, 3:4], nlam[:, 0:1], ALU.mult, [r_rl, r_nlam], [r_rl])
                        ts("dve", tmpa[:], o1[:, 0:128], rl[:, 4:5], None, ALU.mult, None, [r_o1, r_rl], [r_tmpa])
                        stt("dve", aof[:, sub, h, :], o0[:, 0:128], rl[:, 2:3], tmpa[:], ALU.mult, ALU.add,
                            [r_o0, r_rl, r_tmpa], [r_aof])
                for sub in range(nq):
                    for h in range(H):
                        act(junk[:, 0:128], aof[:, sub, h, :], AF.Square, [r_aof], [r_junk, r_ssq],
                            accum_out=ssq[:, 4 + h:5 + h])
                    act(ssq[:, 4:8], ssq[:, 4:8], AF.Sqrt, [r_ssq], [r_ssq], bias=EPS, scale=1.0 / 128)
                    recip(ssq[:, 4:8], ssq[:, 4:8], [r_ssq], [r_ssq])
                    for h in range(H):
                        stt("dve", aob[:, h * 128:(h + 1) * 128], aof[:, sub, h, :], ssq[:, 4 + h:5 + h], gdab[:],
                            ALU.mult, ALU.mult, [r_aof, r_ssq, r_gdab], [r_aob])
                    transpose8(aob, r_aob, aoTs[:], r_aoTs, 4, bidx=0)
                    u = t0 + sub
                    dma("aoTs", AOT_d[:, :, u * 128:(u + 1) * 128], aoTs[:], [r_aoTs], [r_AOT])

        P.barrier()
        with ExitStack() as es3:
            Wya, r_Wya = sbuf(es3, "Wya", [128, 4, D], BF16, ro=True)
            Wyb, r_Wyb = sbuf(es3, "Wyb", [128, 4, D], BF16, ro=True)
            Wo, r_Wo = sbuf(es3, "Wo", [128, 8, D], BF16, ro=True)
            Wpg, r_Wpg = sbuf(es3, "Wpg", [128, 8, D], BF16, ro=True)
            Wpl, r_Wpl = sbuf(es3, "Wpl", [128, 2, D], BF16, ro=True)
            wdn = [sbuf(es3, f"wdn{i}", [128, 2, D], BF16) for i in range(2)]
            g2b, r_g2b = sbuf(es3, "g2b", [128, D], F32, ro=True)
            cfw, r_cfw = sbuf(es3, "cfw", [128, NFC, 3], F32, ro=True)
            cfb, r_cfb = sbuf(es3, "cfb", [128, NFC], F32, ro=True)
            wup = [sbuf(es3, f"wup{i}", [128, 8, 2, 256], BF16) for i in range(2)]
            aoT, r_aoT = sbuf(es3, "aoT", [128, 4, 512], BF16)
            hmT, r_hmT = sbuf(es3, "hmT", [128, 4, 512], BF16)
            sg3s = [sbuf(es3, f"sg3_{i}", [128, 2 * D], BF16) for i in range(2)]
            x3t = [sbuf(es3, f"x3t{i}", [128, D], F32) for i in range(2)]
            m2, r_m2 = sbuf(es3, "m2", [128, D], F32)
            mbs = [sbuf(es3, f"mb{i}", [128, D], BF16) for i in range(2)]
            mT, r_mT = sbuf(es3, "mT", [128, 8, 128], BF16)
            x1b, r_x1b = sbuf(es3, "x1b", [128, BT, D], F32)
            h2bs = [sbuf(es3, f"h2b{i}", [128, D], BF16) for i in range(2)]
            h2T, r_h2T = sbuf(es3, "h2T", [128, 8, 512], BF16)
            aext, r_aext = sbuf(es3, "aext", [128, 2 + 512], F32)
            halo = [sbuf(es3, f"halo{i}", [128, NFC, 2], F32) for i in range(2)]
            acc, r_acc = sbuf(es3, "acc", [128, 512], F32)
            gel, r_gel = sbuf(es3, "gel", [128, 512], F32)
            uT, r_uT = sbuf(es3, "uT", [128, NFC, 512], BF16)
            x2bf, r_x2bf = sbuf(es3, "x2bf", [128, D], BF16)
            x2Ts = [sbuf(es3, f"x2T{i}", [128, 8, 128], BF16) for i in range(2)]
            gsb, r_gsb = sbuf(es3, "gsb", [128, D], F32)
            pf, r_pf = sbuf(es3, "pf", [128, 2, 128], F32)
            pbs = [sbuf(es3, f"pb{i}", [128, 2, 128], BF16) for i in range(2)]
            m1, r_m1 = gsb, r_gsb
            yo = x3t

            dma("Wya", Wya[:], WYA_d, [r_WYA], [r_Wya])
            dma("Wyb", Wyb[:], WYB_d, [r_WYB], [r_Wyb])
            dma("Wo", Wo[:], WO_d, [r_WO], [r_Wo])
            dma("Wpg", Wpg[:], WPG_d, [r_WPG], [r_Wpg])
            dma("Wpl", Wpl[:], WPL_d, [r_WPL], [r_Wpl])
            dma("g2b", g2b[:], g2_d.partition_broadcast(128), [], [r_g2b])
            dma("cfw", cfw[:], cfw_d, [], [r_cfw])
            dma("cfb", cfb[:], cfb_d, [], [r_cfb])
            for i in range(2):
                memset("pool", halo[i][0][:], 0.0, [halo[i][1]])

            wu_i = 0
            wd_i = 0
            yo_i = 0
            x_i = 0
            for bi, tiles in enumerate(blocks):
                nq = len(tiles)
                N = nq * 128
                t0 = tiles[0]
                last_is_halo = bi == 0
                dma("aoT", aoT[:, :, 0:N], AOT_d[:, :, t0 * 128:t0 * 128 + N], [r_AOT], [r_aoT])
                dma("hmT", hmT[:, :, 0:N], HMT_d[:, :, t0 * 128:t0 * 128 + N], [r_HMT], [r_hmT])
                xbuf = {}

                def stage_MA(ti):
                    nonlocal x_i
                    u = tiles[ti]
                    x_t, r_x = x3t[x_i % 2]
                    xbuf[ti] = (x_t, r_x)
                    dma(f"x3t{x_i % 2}", x_t[:], xa[(G0 + u) * 128:(G0 + u + 1) * 128, :], [], [r_x])
                    x_i += 1
                    sg3, r_sg3 = sg3s[ti % 2]
                    dma(f"sg3_{ti % 2}", sg3[:], SG_d[u * 128:(u + 1) * 128, :], [r_SG], [r_sg3])
                    ya, r_ya0, r_ya1 = bank2(0)
                    yb, r_yb0, r_yb1 = bank2(2)
                    for half, (ra_, rb_) in enumerate(((r_ya0, r_yb0), (r_ya1, r_yb1))):
                        for c in range(4):
                            mm(ya[:, half, :], aoT[:, c, ti * 128:(ti + 1) * 128], Wya[:, c, half * 512:(half + 1) * 512],
                               c == 0, c == 3, [r_aoT, r_Wya], [ra_])
                        for c in range(4):
                            mm(yb[:, half, :], hmT[:, c, ti * 128:(ti + 1) * 128], Wyb[:, c, half * 512:(half + 1) * 512],
                               c == 0, c == 3, [r_hmT, r_Wyb], [rb_])
                    mb, r_mb = mbs[ti % 2]
                    tt("dve", m1[:].rearrange("p (a b) -> p a b", a=2), ya[:, :, :],
                       sg3[:, 0:D].rearrange("p (a b) -> p a b", a=2), ALU.mult, [r_ya0, r_ya1, r_sg3], [r_m1])
                    tt("dve", m2[:].rearrange("p (a b) -> p a b", a=2), yb[:, :, :],
                       sg3[:, D:2 * D].rearrange("p (a b) -> p a b", a=2), ALU.mult, [r_yb0, r_yb1, r_sg3], [r_m2])
                    tt("dve", mb[:], m1[:], m2[:], ALU.add, [r_m1, r_m2], [r_mb])

                def stage_MB(ti):
                    mb, r_mb = mbs[ti % 2]
                    x_t, r_x = xbuf[ti]
                    h2b, r_h2b = h2bs[ti % 2]
                    transpose8(mb, r_mb, mT[:], r_mT, 8, bidx=4)
                    xo, r_xo0, r_xo1 = bank2(6)
                    for half, rr_ in enumerate((r_xo0, r_xo1)):
                        for kc in range(8):
                            mm(xo[:, half, :], mT[:, kc, :], Wo[:, kc, half * 512:(half + 1) * 512], kc == 0, kc == 7,
                               [r_mT, r_Wo], [rr_])
                    tt("dve", x1b[:, ti, :].rearrange("p (a b) -> p a b", a=2), xo[:, :, :],
                       x_t[:].rearrange("p (a b) -> p a b", a=2), ALU.add, [r_xo0, r_xo1, r_x], [r_x1b])
                    rstd = rms_stats(x1b[:, ti, :], D, r_x1b)
                    stt("dve", h2b[:], x1b[:, ti, :], rstd, g2b[:], ALU.mult, ALU.mult, [r_x1b, r_ssq, r_g2b], [r_h2b])

                def stage_MC(ti):
                    h2b, r_h2b = h2bs[ti % 2]
                    transpose8(h2b, r_h2b, h2T[:, :, ti * 128:(ti + 1) * 128], r_h2T, 8, bidx=5)

                stage_MA(0)
                for ti in range(nq):
                    if ti + 1 < nq:
                        stage_MA(ti + 1)
                    stage_MB(ti)
                    if ti >= 1:
                        stage_MC(ti - 1)
                stage_MC(nq - 1)

                h_in, r_hin = halo[bi % 2]
                h_out, r_hout = halo[(bi + 1) % 2]
                for fc0 in range(0, NFC, 2):
                    w_s, r_w = wup[wu_i % 2]
                    wu_i += 1
                    dma(f"wup{(wu_i - 1) % 2}", w_s[:, :, 0, :], WUP_d[:, :, fc0 * 128:fc0 * 128 + 256], [r_WUP], [r_w])
                    if not last_is_halo:
                        dma(f"wup{(wu_i - 1) % 2}", w_s[:, :, 1, :], WUP_d[:, :, DFF + fc0 * 128:DFF + fc0 * 128 + 256],
                            [r_WUP], [r_w])
                    for fi in range(2):
                        fc = fc0 + fi
                        ab, r_ab = bank(2 * fi)
                        for kc in range(8):
                            mm(ab[:, 0:N], w_s[:, kc, 0, fi * 128:(fi + 1) * 128], h2T[:, kc, 0:N], kc == 0, kc == 7,
                               [r_w, r_h2T], [r_ab])
                        acopy(aext[:, 2:2 + N], ab[:, 0:N], [r_ab], [r_aext])
                        tcopy("dve", aext[:, 0:2], h_in[:, fc, :], [r_hin], [r_aext])
                        tcopy("dve", h_out[:, fc, :], aext[:, N:N + 2], [r_aext], [r_hout])
                        if last_is_halo:
                            continue
                        gb_, r_gb = bank(2 * fi + 1)
                        for kc in range(8):
                            mm(gb_[:, 0:N], w_s[:, kc, 1, fi * 128:(fi + 1) * 128], h2T[:, kc, 0:N], kc == 0, kc == 7,
                               [r_w, r_h2T], [r_gb])
                        ts("dve", acc[:, 0:N], aext[:, 0:N], cfw[:, fc, 0:1], cfb[:, fc:fc + 1], ALU.mult, ALU.add,
                           [r_aext, r_cfw, r_cfb], [r_acc])
                        stt("dve", acc[:, 0:N], aext[:, 1:N + 1], cfw[:, fc, 1:2], acc[:, 0:N], ALU.mult, ALU.add,
                            [r_aext, r_cfw, r_acc], [r_acc])
                        stt("dve", acc[:, 0:N], aext[:, 2:N + 2], cfw[:, fc, 2:3], acc[:, 0:N], ALU.mult, ALU.add,
                            [r_aext, r_cfw, r_acc], [r_acc])
                        act(gel[:, 0:N], acc[:, 0:N], AF.Gelu_apprx_tanh, [r_acc], [r_gel])
                        tt("dve", uT[:, fc, 0:N], gel[:, 0:N], gb_[:, 0:N], ALU.mult, [r_gel, r_gb], [r_uT])
                if last_is_halo:
                    continue

                for fc0 in range(0, NFC, 2):
                    wd, r_wd = wdn[wd_i % 2]
                    wd_i += 1
                    dma(f"wdn{(wd_i - 1) % 2}", wd[:], WDN_d[:, fc0:fc0 + 2, :], [r_WDN], [r_wd])
                    for fi in range(2):
                        fc = fc0 + fi
                        for ti in range(nq):
                            for half in range(2):
                                bkx, r_bkx = bank(2 * ti + half)
                                mm(bkx[:, :], uT[:, fc, ti * 128:(ti + 1) * 128], wd[:, fi, half * 512:(half + 1) * 512],
                                   fc == 0, fc == NFC - 1, [r_uT, r_wd], [r_bkx])
                for ti in range(nq):
                    xd, r_xd0, r_xd1 = bank2(2 * ti)
                    tt("dve", x1b[:, ti, :].rearrange("p (a b) -> p a b", a=2), xd[:, :, :],
                       x1b[:, ti, :].rearrange("p (a b) -> p a b", a=2), ALU.add, [r_xd0, r_xd1, r_x1b], [r_x1b])
                def stage_PA(ti):
                    u = tiles[ti]
                    x2 = x1b[:, ti, :]
                    x2T, r_x2T = x2Ts[ti % 2]
                    pb, r_pb = pbs[ti % 2]
                    tcopy("act", x2bf[:], x2, [r_x1b], [r_x2bf])
                    transpose8(x2bf, r_x2bf, x2T[:], r_x2T, 8, bidx=6)
                    tok0 = (u - 1) * 128
                    dma("pf", pf[:], pT_d[:, :, tok0:tok0 + 128], [], [r_pf])
                    tcopy("dve", pb[:], pf[:], [r_pf], [r_pb])

                def stage_PB(ti):
                    nonlocal yo_i
                    u = tiles[ti]
                    x2 = x1b[:, ti, :]
                    r_x2 = r_x1b
                    x2T, r_x2T = x2Ts[ti % 2]
                    pb, r_pb = pbs[ti % 2]
                    tok0 = (u - 1) * 128
                    gp, r_gp0, r_gp1 = bank2(0)
                    pp, r_pp0, r_pp1 = bank2(2)
                    for half, (rg_, rp_) in enumerate(((r_gp0, r_pp0), (r_gp1, r_pp1))):
                        for kc in range(8):
                            mm(gp[:, half, :], x2T[:, kc, :], Wpg[:, kc, half * 512:(half + 1) * 512], kc == 0, kc == 7,
                               [r_x2T, r_Wpg], [rg_])
                        for c in range(2):
                            mm(pp[:, half, :], pb[:, c, :], Wpl[:, c, half * 512:(half + 1) * 512], c == 0, c == 1,
                               [r_pb, r_Wpl], [rp_])
                    act(gsb[:].rearrange("p (a b) -> p a b", a=2), gp[:, :, :], AF.Sigmoid, [r_gp0, r_gp1], [r_gsb])
                    tt("dve", gsb[:].rearrange("p (a b) -> p a b", a=2), pp[:, :, :],
                       gsb[:].rearrange("p (a b) -> p a b", a=2), ALU.mult, [r_pp0, r_pp1, r_gsb], [r_gsb])
                    tt("dve", x2, x2, gsb[:], ALU.add, [r_x2, r_gsb], [r_x2])
                    rstd = rms_stats(x2, D, r_x2)
                    y_t, r_yt = yo[yo_i % 2]
                    yo_i += 1
                    stt("dve", y_t[:], x2, rstd, gFb[:], ALU.mult, ALU.mult, [r_x2, r_ssq, r_gFb], [r_yt])
                    dma(f"x3t{(yo_i - 1) % 2}", y_d[tok0:tok0 + 128, :], y_t[:], [r_yt], [r_y])

                stage_PA(0)
                for ti in range(nq):
                    if ti + 1 < nq:
                        stage_PA(ti + 1)
                    stage_PB(ti)

            P.wait_all("sp", [r_y, yo[0][1], yo[1][1]])
            P.wait_all("pool", [r_y, yo[0][1], yo[1][1]])
            P.wait_all("act", [r_y, yo[0][1], yo[1][1]])
        P.run()
    return nc, P


def host_constants(NTS):
    NT = NSLOT * NTS
    bf = ml_dtypes.bfloat16
    c = {}
    c["identb"] = np.eye(128, dtype=np.float32).astype(bf)
    s = np.arange(128)
    c["tri"] = (s[:, None] <= s[None, :]).astype(np.float32)
    c["ones"] = np.ones((128, 128), np.float32)
    sh = np.zeros((128, 7, 128), np.float32)
    for j in range(4):
        for t in range(128):
            sidx = t - 3 + j
            if 0 <= sidx < 128:
                sh[sidx, j, t] = 1.0
    for j in range(3):
        for t in range(128):
            sidx = 128 + t - 3 + j
            if 0 <= sidx < 128:
                sh[sidx, 4 + j, t] = 1.0
    c["shifts"] = sh.astype(bf)
    pat = np.zeros((128, 4, 512), np.float32)
    k = np.arange(128)
    q = np.arange(512)
    for o in range(4):
        pat[:, o, :] = (((128 * o + k) // 64)[:, None] <= (q // 64)[None, :]).astype(np.float32)
    c["pat"] = pat.astype(bf)
    inv_freq = (10000.0 ** (-np.arange(0, 64, 2, dtype=np.float32) / 64)).astype(np.float32)
    pos = (np.arange(NT)[None, :] * 128 + np.arange(128)[:, None]).astype(np.float32)
    ang = pos[:, :, None] * inv_freq[None, None, :]
    c["ropec"] = np.cos(ang).astype(np.float32)
    c["ropes"] = np.sin(ang).astype(np.float32)
    return c


_PROG_CACHE = {}


def make_in_maps(inputs, NTS, cores=None):
    SLOT = NTS * 128
    S = NSLOT * SLOT
    NT = NSLOT * NTS
    f32 = np.float32
    x = np.asarray(inputs["x"], f32).reshape(S, D)
    p = np.asarray(inputs["p"], f32).reshape(S, PLE)
    consts = host_constants(NTS)
    shared = {
        "w_in": np.ascontiguousarray(np.asarray(inputs["w_in"], f32)[0]),
        "w_ya": np.ascontiguousarray(np.asarray(inputs["w_ya"], f32)[0]),
        "w_yb": np.ascontiguousarray(np.asarray(inputs["w_yb"], f32)[0]),
        "w_o": np.ascontiguousarray(np.asarray(inputs["w_o"], f32)[0]),
        "w_up": np.ascontiguousarray(np.asarray(inputs["w_up"], f32)[0]),
        "w_down": np.ascontiguousarray(np.asarray(inputs["w_down"], f32)[0]),
        "w_ple": np.ascontiguousarray(np.asarray(inputs["w_ple"], f32)[0]),
        "w_pg": np.ascontiguousarray(np.asarray(inputs["w_pg"], f32)[0]),
        "g1": np.asarray(inputs["norm1_g"], f32).reshape(1, D),
        "g2": np.asarray(inputs["norm2_g"], f32).reshape(1, D),
        "gF": np.asarray(inputs["final_g"], f32).reshape(1, D),
        "gda": np.asarray(inputs["da_norm_g"], f32).reshape(1, 128),
        "gml": np.asarray(inputs["ml_norm_g"], f32).reshape(1, 128),
        "b_if": np.asarray(inputs["b_if"], f32).reshape(1, 8),
        "lamv": np.stack([np.asarray(inputs[k], f32).reshape(64) for k in ("lam_q1", "lam_k1", "lam_q2", "lam_k2")]).reshape(1, 256),
        "conv_m_w": np.ascontiguousarray(np.asarray(inputs["conv_m_w"], f32)[0]).reshape(1, 4 * D),
        "cfw": np.ascontiguousarray(np.asarray(inputs["conv_f_w"], f32)[0].T.reshape(NFC, 128, 3).transpose(1, 0, 2)),
        "cfb": np.ascontiguousarray(np.asarray(inputs["conv_f_b"], f32)[0].reshape(NFC, 128).T),
    }
    shared.update(consts)
    in_maps = []
    for r in (range(NCORES) if cores is None else cores):
        xa = np.zeros((NT * 128, D), f32)
        lo = (r - (NSLOT - 1)) * SLOT
        hi = (r + 1) * SLOT
        src_lo = max(lo, 0)
        xa[src_lo - lo:, :] = x[src_lo:hi]
        pT = np.ascontiguousarray(p[r * SLOT:(r + 1) * SLOT].T.reshape(2, 128, SLOT).transpose(1, 0, 2))
        biast = np.zeros((128, NSLOT), f32)
        for s_ in range(NSLOT):
            if s_ < NSLOT - 1 - r:
                biast[:, s_] = NEG
        m = dict(shared)
        m["xa"] = xa
        m["pT"] = pT
        m["biast"] = biast
        in_maps.append(m)
    return in_maps


def kernel(**inputs):
    S = int(np.asarray(inputs["x"]).shape[1])
    NTS = S // (NSLOT * 128)
    if NTS not in _PROG_CACHE:
        _PROG_CACHE[NTS] = build_program(NTS)[0]
    nc = _PROG_CACHE[NTS]
    in_maps = make_in_maps(inputs, NTS)
    res = run_bass_kernel_spmd(nc, in_maps, core_ids=list(range(NCORES)))
    out = np.concatenate([np.asarray(r["y"], np.float32) for r in res.results], axis=0)
    return out.reshape(1, S, D)
```
